# Optimizing a Trainium2 kernel written in Bass

```python
import jax
import jax.numpy as jnp
from jax import lax
import numpy as np

D_MODEL = 1024
BATCH = 4
SEQ = 8192
DEPTH = 4
DEC_BATCH = 8
DEC_SEQ = 64
PAST_LEN = 1024

CHUNK = 64
D_MIX = D_MODEL
H_A = 4
DV_A = D_MIX // 2 // H_A
DK_A = DV_A // 2
R_A = 16
TAU_A = 16.0
D_B = D_MIX // 2
W_B = 31
D_C = D_MIX // 2
H_C = 8
DH_C = D_C // H_C
LRU_C = 8.0
H_D = 4
DK_D = 128
DV_D = D_MIX // 2 // H_D
W_S = 4
D_FF = 2688
W_F = 3
N_EVEN = (DEPTH + 1) // 2
N_ODD = DEPTH // 2
ALPHA = (2 * DEPTH) ** 0.25
BETA = (8 * DEPTH) ** -0.25
EPS = 1e-5
EVEN_SPLITS = (H_A * DK_A, H_A * DK_A, H_A * DV_A, H_A * DV_A, R_A, 2 * D_B)
E_IN = sum(EVEN_SPLITS)
CONV_ODD = D_C + 2 * H_D * DK_D + H_D * DV_D
ODD_SPLITS = (CONV_ODD, D_C, H_D * DV_D, H_D, H_D)
O_IN = sum(ODD_SPLITS)

kernel_name = 'hybrid_streaming_encoder_step'


def _split(t, sizes):
    out, start = [], 0
    for s in sizes:
        out.append(t[..., start:start + s])
        start += s
    return out


def _layernorm(x, g, b):
    xf = x.astype(jnp.float32)
    mu = jnp.mean(xf, -1, keepdims=True)
    var = jnp.mean(jnp.square(xf - mu), -1, keepdims=True)
    return ((xf - mu) * lax.rsqrt(var + EPS) * g + b).astype(x.dtype)


def _rmsnorm_heads(x, g):
    xf = x.astype(jnp.float32)
    y = xf * lax.rsqrt(jnp.mean(jnp.square(xf), -1, keepdims=True) + EPS)
    return y.reshape(y.shape[:-2] + (-1,)) * g


def _l2norm(t):
    return t * lax.rsqrt(jnp.sum(t * t, -1, keepdims=True) + 1e-6)


def _causal_dwconv(x, buf, w, b):
    xp = jnp.concatenate([buf.astype(x.dtype), x], axis=1)
    y = lax.conv_general_dilated(xp, w[:, None, :].astype(x.dtype), window_strides=(1,), padding='VALID',
                                 dimension_numbers=('NWC', 'WIO', 'NWC'), feature_group_count=x.shape[-1])
    return y + b.astype(y.dtype), xp[:, -(w.shape[0] - 1):]


def _to_chunks(t, n, c):
    return jnp.moveaxis(t.reshape(t.shape[:2] + (n, c) + t.shape[3:]), 2, 0)


def _from_chunks(t):
    t = jnp.moveaxis(t, 0, 2)
    return t.reshape(t.shape[:2] + (-1, t.shape[-1]))


def _gla(q, k, v, log_a, s0):
    L = q.shape[2]
    c = min(L, CHUNK)
    n = L // c
    causal = jnp.tril(jnp.ones((c, c), bool))

    def step(s, inp):
        qc, kc, vc, ac = inp
        bcum = jnp.cumsum(ac, axis=2)
        o_inter = jnp.einsum('bhtk,bhkv->bhtv', qc * jnp.exp(bcum), s)
        diff = bcum[:, :, :, None, :] - bcum[:, :, None, :, :]
        decay = jnp.exp(jnp.where(causal[:, :, None], diff, -jnp.inf))
        scores = jnp.einsum('bhtk,bhsk,bhtsk->bhts', qc, kc, decay)
        o = o_inter + jnp.einsum('bhts,bhsv->bhtv', scores, vc)
        blast = bcum[:, :, -1:, :]
        s_new = jnp.exp(blast[:, :, 0, :])[..., None] * s + jnp.einsum(
            'bhsk,bhsv->bhkv', kc * jnp.exp(blast - bcum), vc)
        return s_new, o

    s_fin, o = lax.scan(step, s0, (_to_chunks(q, n, c), _to_chunks(k, n, c), _to_chunks(v, n, c),
                                   _to_chunks(log_a, n, c)))
    return _from_chunks(o), s_fin


def _gated_delta(q, k, v, beta, g, s0):
    L = q.shape[2]
    c = min(L, CHUNK)
    n = L // c
    dv = v.shape[-1]
    causal = jnp.tril(jnp.ones((c, c), bool))
    strict = jnp.tril(jnp.ones((c, c), bool), -1)
    eye = jnp.eye(c, dtype=jnp.float32)

    def step(s, inp):
        qc, kc, vc, bc, gc = inp
        gcum = jnp.cumsum(gc, axis=-1)
        decay = jnp.exp(jnp.where(causal, gcum[..., :, None] - gcum[..., None, :], -jnp.inf))
        kb = kc * bc[..., None]
        lower = jnp.where(strict, jnp.einsum('bhtk,bhsk->bhts', kb, kc) * decay, 0.0)
        rhs = jnp.concatenate([vc * bc[..., None], kb * jnp.exp(gcum)[..., None]], axis=-1)
        sol = lax.linalg.triangular_solve(eye + lower, rhs, left_side=True, lower=True, unit_diagonal=True)
        u, w = sol[..., :dv], sol[..., dv:]
        v_new = u - jnp.einsum('bhtk,bhkv->bhtv', w, s)
        attn = jnp.where(causal, jnp.einsum('bhtk,bhsk->bhts', qc, kc) * decay, 0.0)
        o = jnp.einsum('bhtk,bhkv->bhtv', qc * jnp.exp(gcum)[..., None], s) + jnp.einsum(
            'bhts,bhsv->bhtv', attn, v_new)
        glast = gcum[..., -1:]
        s_new = jnp.exp(glast)[..., None] * s + jnp.einsum(
            'bhsk,bhsv->bhkv', kc * jnp.exp(glast - gcum)[..., None], v_new)
        return s_new, o

    s_fin, o = lax.scan(step, s0, (_to_chunks(q, n, c), _to_chunks(k, n, c), _to_chunks(v, n, c),
                                   _to_chunks(beta, n, c), _to_chunks(g, n, c)))
    return _from_chunks(o), s_fin


def _rglru(xc, r, i, log_a_base, h0):
    log_a = LRU_C * r * log_a_base
    a = jnp.exp(log_a)
    bx = jnp.sqrt(-jnp.expm1(2.0 * log_a)) * (i * xc)
    bx = bx.at[:, 0].add(a[:, 0] * h0)

    def combine(e1, e2):
        a1, b1 = e1
        a2, b2 = e2
        return a1 * a2, a2 * b1 + b2

    _, h = lax.associative_scan(combine, (a, bx), axis=1)
    return h, h[:, -1]


def _even_mixer(x, s_gla, buf_b, w_in, w_lr, b_lr, g_gla, w_dw, b_dw, g_cn, b_cn, w_out):
    f32 = jnp.float32
    bsz, L, _ = x.shape
    q, k, v, gate, lr, glu = _split(x @ w_in, EVEN_SPLITS)
    heads = lambda t, d: t.astype(f32).reshape(bsz, L, H_A, d).transpose(0, 2, 1, 3)
    log_a = jax.nn.log_sigmoid((lr @ w_lr).astype(f32) + b_lr) / TAU_A
    o_a, s_new = _gla(heads(q, DK_A) * DK_A ** -0.5, heads(k, DK_A), heads(v, DV_A), heads(log_a, DK_A),
                      s_gla.astype(f32))
    o_a = _rmsnorm_heads(o_a.transpose(0, 2, 1, 3), g_gla) * jax.nn.silu(gate.astype(f32))
    u = glu[..., :D_B] * jax.nn.sigmoid(glu[..., D_B:])
    cv, buf_new = _causal_dwconv(u, buf_b, w_dw, b_dw)
    o_b = jax.nn.silu(_layernorm(cv, g_cn, b_cn))
    y = jnp.concatenate([o_a.astype(x.dtype), o_b.astype(x.dtype)], axis=-1) @ w_out
    return y, s_new.astype(x.dtype), buf_new


def _odd_mixer(x, h_lru, s_delta, buf, w_in, w_conv, b_conv, w_rg, b_rg, w_ig, b_ig, lam, a_log, dt_bias,
               g_delta, w_out):
    f32 = jnp.float32
    bsz, L, _ = x.shape
    conv_in, gate_c, z, beta_raw, a_raw = _split(x @ w_in, ODD_SPLITS)
    cv, buf_new = _causal_dwconv(conv_in, buf, w_conv, b_conv)
    xc = cv[..., :D_C].astype(f32)
    q, k, v = _split(jax.nn.silu(cv[..., D_C:].astype(f32)), (H_D * DK_D, H_D * DK_D, H_D * DV_D))
    xb = xc.reshape(bsz, L, H_C, DH_C)
    r = jax.nn.sigmoid(jnp.einsum('blhi,hij->blhj', xb, w_rg).reshape(bsz, L, D_C) + b_rg)
    ig = jax.nn.sigmoid(jnp.einsum('blhi,hij->blhj', xb, w_ig).reshape(bsz, L, D_C) + b_ig)
    h, h_last = _rglru(xc, r, ig, jax.nn.log_sigmoid(lam.astype(f32)), h_lru.astype(f32))
    o_c = h * jax.nn.gelu(gate_c.astype(f32))
    heads = lambda t, d: t.reshape(bsz, L, H_D, d).transpose(0, 2, 1, 3)
    qh = _l2norm(heads(q, DK_D)) * DK_D ** -0.5
    kh = _l2norm(heads(k, DK_D))
    vh = heads(v, DV_D)
    beta = jax.nn.sigmoid(beta_raw.astype(f32)).transpose(0, 2, 1)
    g = (-jnp.exp(a_log.astype(f32)) * jax.nn.softplus(a_raw.astype(f32) + dt_bias)).transpose(0, 2, 1)
    o_d, s_new = _gated_delta(qh, kh, vh, beta, g, s_delta.astype(f32))
    o_d = _rmsnorm_heads(o_d.transpose(0, 2, 1, 3), g_delta) * jax.nn.silu(z.astype(f32))
    y = jnp.concatenate([o_c, o_d], axis=-1).astype(x.dtype) @ w_out
    return y, h_last.astype(x.dtype), s_new.astype(x.dtype), buf_new


def _conv_ffn(x, buf, w_up, w_dw, b_dw, w_down):
    hu = x @ w_up
    cv, buf_new = _causal_dwconv(hu[..., :D_FF], buf, w_dw, b_dw)
    return (jax.nn.gelu(cv) * hu[..., D_FF:]) @ w_down, buf_new


def _zero_states(bsz, dtype):
    states = []
    for l in range(DEPTH):
        if l % 2 == 0:
            states.append((jnp.zeros((bsz, H_A, DK_A, DV_A), dtype), jnp.zeros((bsz, W_B - 1, D_B), dtype),
                           jnp.zeros((bsz, W_F - 1, D_FF), dtype)))
        else:
            states.append((jnp.zeros((bsz, D_C), dtype), jnp.zeros((bsz, H_D, DK_D, DV_D), dtype),
                           jnp.zeros((bsz, W_S - 1, CONV_ODD), dtype), jnp.zeros((bsz, W_F - 1, D_FF), dtype)))
    return states


def _trunk(x, states, even_w, odd_w, ffn_w):
    new_states = []
    for l in range(DEPTH):
        st = states[l]
        if l % 2 == 0:
            y, *mix_new = _even_mixer(x, st[0], st[1], *[w[l // 2] for w in even_w])
        else:
            y, *mix_new = _odd_mixer(x, st[0], st[1], st[2], *[w[l // 2] for w in odd_w])
        w_up, w_fdw, b_fdw, w_down, ln1_g, ln1_b, ln2_g, ln2_b = [w[l] for w in ffn_w]
        x = _layernorm(ALPHA * x + y, ln1_g, ln1_b)
        f, ffn_buf = _conv_ffn(x, st[-1], w_up, w_fdw, b_fdw, w_down)
        x = _layernorm(ALPHA * x + f, ln2_g, ln2_b)
        new_states.append((*mix_new, ffn_buf))
    return x, new_states


def setup_inputs(seed: int = 0) -> dict:
    key = jax.random.key(seed)
    keys = iter(jax.random.split(key, 64))
    nrm = lambda shape, s: jax.random.normal(next(keys), shape, jnp.float32) * s
    uni = lambda shape, lo, hi: jax.random.uniform(next(keys), shape, jnp.float32, lo, hi)
    gain = lambda shape: 1.0 + nrm(shape, 0.02)
    inp = {}
    inp['x_prompt'] = nrm((BATCH, SEQ, D_MODEL), 1.0)
    inp['x_sample'] = nrm((DEC_BATCH, DEC_SEQ, D_MODEL), 1.0)
    for l in range(DEPTH):
        if l % 2 == 0:
            inp[f'state_l{l}_gla'] = nrm((DEC_BATCH, H_A, DK_A, DV_A), 0.1)
            inp[f'cache_l{l}_dwconv'] = nrm((DEC_BATCH, W_B - 1, D_B), 0.5)
        else:
            inp[f'state_l{l}_lru'] = nrm((DEC_BATCH, D_C), 0.5)
            inp[f'state_l{l}_delta'] = nrm((DEC_BATCH, H_D, DK_D, DV_D), 0.1)
            inp[f'cache_l{l}_conv'] = nrm((DEC_BATCH, W_S - 1, CONV_ODD), 1.0)
        inp[f'cache_l{l}_ffn'] = nrm((DEC_BATCH, W_F - 1, D_FF), 1.0)
    inp['we_in'] = nrm((N_EVEN, D_MODEL, E_IN), D_MODEL ** -0.5)
    inp['we_lr'] = nrm((N_EVEN, R_A, H_A * DK_A), R_A ** -0.5)
    inp['be_lr'] = nrm((N_EVEN, H_A * DK_A), 0.1)
    inp['ge_gla'] = gain((N_EVEN, H_A * DV_A))
    inp['we_dw'] = nrm((N_EVEN, W_B, D_B), W_B ** -0.5)
    inp['be_dw'] = nrm((N_EVEN, D_B), 0.02)
    inp['ge_cn'] = gain((N_EVEN, D_B))
    inp['be_cn'] = nrm((N_EVEN, D_B), 0.02)
    inp['we_out'] = nrm((N_EVEN, D_MIX, D_MODEL), D_MIX ** -0.5 * BETA)
    inp['wo_in'] = nrm((N_ODD, D_MODEL, O_IN), D_MODEL ** -0.5)
    inp['wo_conv'] = nrm((N_ODD, W_S, CONV_ODD), W_S ** -0.5)
    inp['bo_conv'] = nrm((N_ODD, CONV_ODD), 0.02)
    inp['wo_rg'] = nrm((N_ODD, H_C, DH_C, DH_C), DH_C ** -0.5)
    inp['bo_rg'] = nrm((N_ODD, D_C), 0.02)
    inp['wo_ig'] = nrm((N_ODD, H_C, DH_C, DH_C), DH_C ** -0.5)
    inp['bo_ig'] = nrm((N_ODD, D_C), 0.02)
    a_pow = uni((N_ODD, D_C), 0.9, 0.999) ** (1.0 / LRU_C)
    inp['lam_lru'] = jnp.log(a_pow) - jnp.log1p(-a_pow)
    inp['a_log'] = jnp.log(uni((N_ODD, H_D), 1.0, 16.0))
    dt = jnp.exp(uni((N_ODD, H_D), float(np.log(1e-3)), float(np.log(1e-1))))
    inp['dt_bias'] = dt + jnp.log(-jnp.expm1(-dt))
    inp['go_delta'] = gain((N_ODD, H_D * DV_D))
    inp['wo_out'] = nrm((N_ODD, D_MIX, D_MODEL), D_MIX ** -0.5 * BETA)
    inp['w_up'] = nrm((DEPTH, D_MODEL, 2 * D_FF), D_MODEL ** -0.5)
    inp['w_fdw'] = nrm((DEPTH, W_F, D_FF), W_F ** -0.5)
    inp['b_fdw'] = nrm((DEPTH, D_FF), 0.02)
    inp['w_down'] = nrm((DEPTH, D_FF, D_MODEL), D_FF ** -0.5 * BETA)
    inp['ln1_g'] = gain((DEPTH, D_MODEL))
    inp['ln1_b'] = nrm((DEPTH, D_MODEL), 0.02)
    inp['ln2_g'] = gain((DEPTH, D_MODEL))
    inp['ln2_b'] = nrm((DEPTH, D_MODEL), 0.02)
    return inp


def reference(x_prompt, x_sample,
              state_l0_gla, cache_l0_dwconv, cache_l0_ffn,
              state_l1_lru, state_l1_delta, cache_l1_conv, cache_l1_ffn,
              state_l2_gla, cache_l2_dwconv, cache_l2_ffn,
              state_l3_lru, state_l3_delta, cache_l3_conv, cache_l3_ffn,
              we_in, we_lr, be_lr, ge_gla, we_dw, be_dw, ge_cn, be_cn, we_out,
              wo_in, wo_conv, bo_conv, wo_rg, bo_rg, wo_ig, bo_ig, lam_lru, a_log, dt_bias, go_delta, wo_out,
              w_up, w_fdw, b_fdw, w_down, ln1_g, ln1_b, ln2_g, ln2_b):
    even_w = (we_in, we_lr, be_lr, ge_gla, we_dw, be_dw, ge_cn, be_cn, we_out)
    odd_w = (wo_in, wo_conv, bo_conv, wo_rg, bo_rg, wo_ig, bo_ig, lam_lru, a_log, dt_bias, go_delta, wo_out)
    ffn_w = (w_up, w_fdw, b_fdw, w_down, ln1_g, ln1_b, ln2_g, ln2_b)
    y_prompt, new_p = _trunk(x_prompt, _zero_states(x_prompt.shape[0], x_prompt.dtype), even_w, odd_w, ffn_w)
    sample_states = [(state_l0_gla, cache_l0_dwconv, cache_l0_ffn),
                     (state_l1_lru, state_l1_delta, cache_l1_conv, cache_l1_ffn),
                     (state_l2_gla, cache_l2_dwconv, cache_l2_ffn),
                     (state_l3_lru, state_l3_delta, cache_l3_conv, cache_l3_ffn)]
    y_sample, new_s = _trunk(x_sample, sample_states, even_w, odd_w, ffn_w)
    (p0_gla, p0_dw, p0_ffn), (p1_lru, p1_delta, p1_conv, p1_ffn), (p2_gla, p2_dw, p2_ffn), \
        (p3_lru, p3_delta, p3_conv, p3_ffn) = new_p
    (s0_gla, s0_dw, s0_ffn), (s1_lru, s1_delta, s1_conv, s1_ffn), (s2_gla, s2_dw, s2_ffn), \
        (s3_lru, s3_delta, s3_conv, s3_ffn) = new_s
    return (y_prompt, y_sample,
            p0_gla, p0_dw, p0_ffn, p1_lru, p1_delta, p1_conv, p1_ffn,
            p2_gla, p2_dw, p2_ffn, p3_lru, p3_delta, p3_conv, p3_ffn,
            s0_gla, s0_dw, s0_ffn, s1_lru, s1_delta, s1_conv, s1_ffn,
            s2_gla, s2_dw, s2_ffn, s3_lru, s3_delta, s3_conv, s3_ffn)
```

```python
import numpy as np
import concourse.bass as bass
import concourse.mybir as mybir
from concourse.bass_utils import run_bass_kernel_spmd

F32 = mybir.dt.float32
BF16 = mybir.dt.bfloat16
AF = mybir.ActivationFunctionType
ALU = mybir.AluOpType

D_MODEL = 1024
DEPTH = 4
H_A, DK_A, DV_A, R_A = 4, 64, 128, 16
D_B, W_B = 512, 31
D_C, H_C, DH_C = 512, 8, 64
H_D, DK_D, DV_D = 4, 128, 128
W_S = 4
D_FF, W_F = 2688, 3
NFF = D_FF // 128
ALPHA = (2 * DEPTH) ** 0.25
EPS = 1e-5
LRU_C = 8.0
SLAB = 4096
NSLOT = 5
NDMASEM = 40


class Cut(Exception):
    pass


class Unit:
    __slots__ = ("w", "r")

    def __init__(self):
        self.w = None
        self.r = {}


class V:
    __slots__ = ("ap", "us")

    def __init__(self, ap, us):
        self.ap = ap
        self.us = tuple(us)

    def __getitem__(self, idx):
        return V(self.ap[idx], self.us)

    def re(self, s, **kw):
        return V(self.ap.rearrange(s, **kw), self.us)


def newV(ap):
    return V(ap, (Unit(),))


class Prog:
    ENG = ("pe", "act", "dve", "pool", "sp")

    def __init__(self, nc):
        self.nc = nc
        self.q = {e: [] for e in self.ENG}
        self.cnt = {e: 0 for e in self.ENG}
        self.waited = {e: {} for e in self.ENG}
        self.dma_val = [0] * NDMASEM
        self.dma_rr = 0
        self.out_tokens = []
        self.ninstr = 0
        self.epoch = 0

    def _wait(self, eng, key, val):
        if self.waited[eng].get(key, 0) >= val:
            return
        self.waited[eng][key] = val
        self.q[eng].append(("w", key, val))

    def _deps(self, eng, reads, writes):
        for v in reads:
            for u in v.us:
                if u.w is not None:
                    self._wait(eng, u.w[0], u.w[1])
        for v in writes:
            for u in v.us:
                if u.w is not None and u.w[0][0] != eng:
                    self._wait(eng, u.w[0], u.w[1])
                for k, val in u.r.items():
                    if k[0] != eng:
                        self._wait(eng, k, val)

    def _mark(self, tok, reads, writes):
        for v in reads:
            for u in v.us:
                if u.r.get(tok[0], 0) < tok[1]:
                    u.r[tok[0]] = tok[1]
        for v in writes:
            for u in v.us:
                u.w = tok
                u.r = {}

    def op(self, eng, fn, reads, writes, inc=True):
        self._deps(eng, reads, writes)
        key = (eng, self.epoch)
        if inc:
            self.cnt[eng] += 1
            tok = (key, self.cnt[eng])
        else:
            tok = (key, self.cnt[eng] + 1)
        self.q[eng].append(("i", fn, inc, key))
        self._mark(tok, reads, writes)
        self.ninstr += 1

    def new_epoch(self):
        self.fence()
        self.epoch += 1
        for e in self.ENG:
            self.cnt[e] = 0

    def dma(self, eng, out, in_, reads, writes, is_output=False, **kw):
        i = self.dma_rr
        self.dma_rr = (self.dma_rr + 1) % NDMASEM
        key = ("d", i)
        if self.dma_val[i] > 0:
            self._wait(eng, key, self.dma_val[i])
        self._deps(eng, reads, writes)
        self.dma_val[i] += 16
        tok = (key, self.dma_val[i])
        self.q[eng].append(("d", out, in_, i, kw))
        self._mark(tok, reads, writes)
        if is_output:
            self.out_tokens.append(tok)
        self.ninstr += 1

    def fence(self):
        comp = ("pe", "act", "dve", "pool")
        for e in comp:
            for f in comp:
                if e != f and self.cnt[f] > 0:
                    self._wait(e, (f, self.epoch), self.cnt[f])

    def finish(self):
        for key, val in self.out_tokens:
            self._wait("sp", key, val)

    def emit(self):
        nc = self.nc
        handles = {"pe": nc.tensor, "act": nc.scalar, "dve": nc.vector, "pool": nc.gpsimd, "sp": nc.sync}
        import contextlib
        with contextlib.ExitStack() as st:
            sems = {}
            for e in self.ENG:
                for ep in range(self.epoch + 1):
                    sems[(e, ep)] = st.enter_context(nc.semaphore("s_%s_%d" % (e, ep)))
            for i in range(NDMASEM):
                sems[("d", i)] = st.enter_context(nc.semaphore("sd%d" % i))
            block = st.enter_context(nc.Block())

            def run(e, h):
                for it in self.q[e]:
                    if it[0] == "w":
                        h.wait_ge(sems[it[1]], it[2])
                    elif it[0] == "i":
                        ins = it[1](h)
                        if it[2]:
                            ins.then_inc(sems[it[3]], 1)
                    else:
                        h.dma_start(out=it[1], in_=it[2], **it[4]).then_inc(sems[("d", it[3])], 16)

            @block.tensor
            def _(h):
                run("pe", h)

            @block.scalar
            def _(h):
                run("act", h)

            @block.vector
            def _(h):
                run("dve", h)

            @block.gpsimd
            def _(h):
                run("pool", h)

            @block.sync
            def _(h):
                run("sp", h)


class Arena:
    def __init__(self, tens, size):
        self.t = tens
        self.size = size
        self.off = 0

    def reset(self):
        self.off = 0

    def mark(self):
        return self.off

    def release(self, m):
        self.off = m

    def alloc(self, parts, shape):
        n = int(np.prod(shape))
        assert self.off + n <= self.size, ("arena overflow", self.off, n, self.size)
        ap = self.t[0:parts, self.off:self.off + n]
        self.off += n
        if len(shape) == 2:
            ap = ap.rearrange("p (a b) -> p a b", a=shape[0])
        elif len(shape) == 3:
            ap = ap.rearrange("p (a b c) -> p a b c", a=shape[0], b=shape[1])
        return newV(ap)


def _slab_in(w_cols):
    n = w_cols.shape[1]
    a = np.zeros((8, 128, 512), np.float32)
    a[:, :, :n] = w_cols.reshape(8, 128, n)
    return np.ascontiguousarray(a.transpose(1, 0, 2)).reshape(128, SLAB)


def _slab_down(w_cols):
    a = np.zeros((128, SLAB), np.float32)
    a[:, :NFF * 128] = w_cols.reshape(NFF, 128, 128).transpose(1, 0, 2).reshape(128, NFF * 128)
    return a


def _fm(vec):
    return np.ascontiguousarray(vec.reshape(-1, 128).T)


class ParamPack:
    def __init__(self):
        self.cols = []
        self.off = {}
        self.n = 0

    def add(self, name, arr):
        arr = np.asarray(arr, np.float32)
        assert arr.shape[0] == 128
        arr = arr.reshape(128, -1)
        self.off[name] = (self.n, arr.shape[1])
        self.cols.append(arr)
        self.n += arr.shape[1]

    def array(self):
        return np.ascontiguousarray(np.concatenate(self.cols, axis=1))


def layer_slab_names(l):
    names = []
    if l % 2 == 0:
        names += ["qk", "v", "gate", "glua", "glub", "out0", "out1"]
    else:
        names += ["xl", "q", "k", "v", "gc", "z", "out0", "out1"]
    names += ["up%d" % s for s in range(11)]
    names += ["dn%d" % n for n in range(8)]
    return names


def host_weights(inp, depth):
    slabs = []
    pp = ParamPack()
    small = {}
    for l in range(depth):
        if l % 2 == 0:
            e = l // 2
            w = inp["we_in"][e]
            slabs += [_slab_in(w[:, 0:512]), _slab_in(w[:, 512:1024]), _slab_in(w[:, 1024:1536]),
                      _slab_in(w[:, 1552:2064]), _slab_in(w[:, 2064:2576])]
            wo = inp["we_out"][e]
            slabs += [_slab_in(wo[:, 0:512]), _slab_in(wo[:, 512:1024])]
            small["wlrin%d" % l] = np.ascontiguousarray(
                w[:, 1536:1552].reshape(8, 128, 16).transpose(1, 0, 2)).reshape(128, 128)
            small["wlraug%d" % l] = np.ascontiguousarray(
                np.concatenate([inp["we_lr"][e], inp["be_lr"][e][None, :]], axis=0))
            pp.add("g_gla%d" % l, _fm(inp["ge_gla"][e]))
            pp.add("w_dw%d" % l, inp["we_dw"][e].T.reshape(4, 128, W_B).transpose(1, 0, 2))
            pp.add("b_dw%d" % l, _fm(inp["be_dw"][e]))
            pp.add("g_cn%d" % l, _fm(inp["ge_cn"][e]))
            pp.add("b_cn%d" % l, _fm(inp["be_cn"][e]))
        else:
            o = l // 2
            w = inp["wo_in"][o]
            slabs += [_slab_in(w[:, 0:512]), _slab_in(w[:, 512:1024]), _slab_in(w[:, 1024:1536]),
                      _slab_in(w[:, 1536:2048]), _slab_in(w[:, 2048:2560]), _slab_in(w[:, 2560:3072])]
            wo = inp["wo_out"][o]
            slabs += [_slab_in(wo[:, 0:512]), _slab_in(wo[:, 512:1024])]
            small["wbain%d" % l] = np.ascontiguousarray(
                w[:, 3072:3080].reshape(8, 128, 8).transpose(1, 0, 2)).reshape(128, 64)
            for nm, key in (("wrg", "wo_rg"), ("wig", "wo_ig")):
                g = inp[key][o]
                bd = np.zeros((4, 128, 128), np.float32)
                for hh in range(8):
                    c, r = hh // 2, (hh % 2) * 64
                    bd[c, r:r + 64, r:r + 64] = g[hh]
                small["%s%d" % (nm, l)] = np.ascontiguousarray(bd.transpose(1, 0, 2)).reshape(128, 512)
            pp.add("w_cv%d" % l, inp["wo_conv"][o].T.reshape(16, 128, W_S).transpose(1, 0, 2))
            pp.add("b_cv%d" % l, _fm(inp["bo_conv"][o]))
            pp.add("b_rg%d" % l, _fm(inp["bo_rg"][o]))
            pp.add("b_ig%d" % l, _fm(inp["bo_ig"][o]))
            pp.add("lam%d" % l, _fm(inp["lam_lru"][o]))
            col = np.zeros((128, 2), np.float32)
            col[0:H_D, 0] = inp["a_log"][o]
            col[0:H_D, 1] = inp["dt_bias"][o]
            pp.add("hd%d" % l, col)
            pp.add("g_dl%d" % l, _fm(inp["go_delta"][o]))
        wu = inp["w_up"][l]
        for s in range(11):
            cols = np.zeros((1024, 512), np.float32)
            for jj in range(2):
                j = 2 * s + jj
                if j < NFF:
                    cols[:, jj * 128:(jj + 1) * 128] = wu[:, j * 128:(j + 1) * 128]
                    cols[:, (2 + jj) * 128:(3 + jj) * 128] = wu[:, D_FF + j * 128:D_FF + (j + 1) * 128]
            slabs.append(_slab_in(cols))
        wd = inp["w_down"][l]
        for n in range(8):
            slabs.append(_slab_down(wd[:, n * 128:(n + 1) * 128]))
        pp.add("w_fdw%d" % l, inp["w_fdw"][l].T.reshape(NFF, 128, W_F).transpose(1, 0, 2))
        pp.add("b_fdw%d" % l, _fm(inp["b_fdw"][l]))
        for nm in ("ln1_g", "ln1_b", "ln2_g", "ln2_b"):
            pp.add("%s%d" % (nm, l), _fm(inp[nm][l]))
    return np.stack(slabs, 0), pp, small


NCONST = 128 + 256 * 4 + 512 * 2 + 512


def host_consts():
    ident = np.eye(128, dtype=np.float32)
    s = np.arange(64)
    U = (s[:, None] <= s[None, :]).astype(np.float32)
    Us = (s[:, None] < s[None, :]).astype(np.float32)
    Ls = (s[:, None] > s[None, :]).astype(np.float32)
    c = np.zeros((128, NCONST), np.float32)
    c[:, 0:128] = ident
    c[0:64, 128:384] = np.tile(U, (1, 4))
    c[0:64, 384:640] = np.tile(Us, (1, 4))
    c[0:64, 640:896] = np.tile(Ls, (1, 4))
    c[0:64, 896:1152] = np.tile(np.eye(64, dtype=np.float32), (1, 4))
    cm = np.ones(512, np.float32)
    cm[::64] = 0.0
    c[0:4, 1152:1664] = cm[None, :]
    for h in range(4):
        c[h, 1664 + h * 128:1664 + (h + 1) * 128] = 1.0
    c[0:64, 2176:2432] = np.tile((U - 1.0) * 30000.0, (1, 4))
    c[0:64, 2432:2688] = np.tile((U.T - 1.0) * 30000.0, (1, 4))
    return c


class Builder:
    def __init__(self, cfg, pp_off, npf, nslab_total, small_shapes):
        self.cfg = cfg
        self.depth = cfg["depth"]
        self.T = cfg["T"]
        self.S = cfg["seq"]
        self.has_sample = cfg["sample"]
        self.pp_off = pp_off
        nc = self.nc = bass.Bass("TRN2", target_bir_lowering=False)
        self.P = Prog(nc)
        T = self.T
        d = self.dram = {}
        depth = self.depth

        def din(name, shape):
            d[name] = nc.dram_tensor(name, list(shape), F32, kind="ExternalInput").ap()

        def dout(name, shape):
            d[name] = nc.dram_tensor(name, list(shape), F32, kind="ExternalOutput").ap()
        self.outs = []
        din("wslabs", (nslab_total, 128, SLAB))
        din("pf", (128, npf))
        din("consts", (128, NCONST))
        for k, shp in small_shapes.items():
            din(k, shp)
        if self.S:
            din("xp", (D_MODEL, self.S))
            dout("yp", (D_MODEL, self.S))
        if self.has_sample:
            din("xs", (D_MODEL, 64))
            dout("ys", (D_MODEL, 64))
        for grp in (["p"] if self.S else []) + (["s"] if self.has_sample else []):
            for l in range(depth):
                if l % 2 == 0:
                    dout("%s%d_gla" % (grp, l), (H_A, DK_A, DV_A))
                    dout("%s%d_dw" % (grp, l), (128, 4, W_B - 1))
                else:
                    dout("%s%d_lru" % (grp, l), (128, 4))
                    dout("%s%d_delta" % (grp, l), (H_D, DK_D, DV_D))
                    dout("%s%d_conv" % (grp, l), (128, 16, W_S - 1))
                dout("%s%d_ffn" % (grp, l), (128, NFF, W_F - 1))
        if self.has_sample:
            for l in range(depth):
                if l % 2 == 0:
                    din("i%d_gla" % l, (H_A, DK_A, DV_A))
                    din("i%d_dw" % l, (128, 4, W_B - 1))
                else:
                    din("i%d_lru" % l, (128, 4))
                    din("i%d_delta" % l, (H_D, DK_D, DV_D))
                    din("i%d_conv" % l, (128, 16, W_S - 1))
                din("i%d_ffn" % l, (128, NFF, W_F - 1))
        if cfg.get('dbg'):
            dout('dbg', (128, 8, self.T))
        self.small_shapes = small_shapes
        self.npf = npf
        self.nslab_total = nslab_total

    def mm(self, out, lhsT, rhs, start, stop):
        self.P.op("pe", lambda h, o=out.ap, a=lhsT.ap, b=rhs.ap, s=start, e=stop: h.matmul(o, a, b, start=s, stop=e),
                  [lhsT, rhs], [out], inc=True)

    def act(self, out, in_, func, scale=1.0, bias=0.0, extra=()):
        sc = scale.ap if isinstance(scale, V) else scale
        bi = bias.ap if isinstance(bias, V) else bias
        rd = [in_] + [x for x in (scale, bias) if isinstance(x, V)] + list(extra)
        self.P.op("act", lambda h, o=out.ap, i=in_.ap, f=func, s=sc, b=bi: h.activation(out=o, in_=i, func=f, bias=b, scale=s),
                  rd, [out])

    def tt(self, out, in0, in1, op, eng="dve"):
        self.P.op(eng, lambda h, o=out.ap, a=in0.ap, b=in1.ap, p=op: h.tensor_tensor(out=o, in0=a, in1=b, op=p),
                  [in0, in1], [out])

    def ts(self, out, in0, s1, s2, op0, op1=None, eng="dve"):
        a1 = s1.ap if isinstance(s1, V) else s1
        a2 = s2.ap if isinstance(s2, V) else s2
        rd = [in0] + [x for x in (s1, s2) if isinstance(x, V)]
        if op1 is None:
            self.P.op(eng, lambda h, o=out.ap, a=in0.ap, x=a1, p=op0: h.tensor_scalar(out=o, in0=a, scalar1=x, scalar2=None, op0=p),
                      rd, [out])
        else:
            self.P.op(eng, lambda h, o=out.ap, a=in0.ap, x=a1, y=a2, p=op0, q=op1: h.tensor_scalar(out=o, in0=a, scalar1=x, scalar2=y, op0=p, op1=q),
                      rd, [out])

    def stt(self, out, in0, sc, in1, op0, op1, eng="dve"):
        a1 = sc.ap if isinstance(sc, V) else sc
        rd = [in0, in1] + ([sc] if isinstance(sc, V) else [])
        self.P.op(eng, lambda h, o=out.ap, a=in0.ap, x=a1, b=in1.ap, p=op0, q=op1: h.scalar_tensor_tensor(out=o, in0=a, scalar=x, in1=b, op0=p, op1=q),
                  rd, [out])

    def cp(self, out, in_, eng="dve"):
        self.P.op(eng, lambda h, o=out.ap, i=in_.ap: h.tensor_copy(out=o, in_=i), [in_], [out])

    def recip(self, out, in_):
        self.P.op("dve", lambda h, o=out.ap, i=in_.ap: h.reciprocal(out=o, in_=i), [in_], [out])

    def memset(self, out, val, eng="dve"):
        self.P.op(eng, lambda h, o=out.ap, v=val: h.memset(o, v), [], [out])

    def scan(self, out, d0, d1, init, op0, op1):
        ia = init.ap if isinstance(init, V) else init
        rd = [d0, d1] + ([init] if isinstance(init, V) else [])
        self.P.op("dve", lambda h, o=out.ap, a=d0.ap, b=d1.ap, i=ia, p=op0, q=op1: h.tensor_tensor_scan(out=o, data0=a, data1=b, initial=i, op0=p, op1=q),
                  rd, [out])

    def dma(self, eng, out, in_, reads=(), writes=(), is_output=False, **kw):
        oa = out.ap if isinstance(out, V) else out
        ia = in_.ap if isinstance(in_, V) else in_
        rd = list(reads) + ([in_] if isinstance(in_, V) else [])
        wr = list(writes) + ([out] if isinstance(out, V) else [])
        self.P.dma(eng, oa, ia, rd, wr, is_output=is_output, **kw)

    def ck(self, lvl):
        if self.cfg.get('cut', 99) == lvl:
            raise Cut()

    def bank(self):
        b = self.banks[self.bank_rr]
        self.bank_rr = (self.bank_rr + 1) % 8
        return b

    def pfv(self, name, l):
        off, n = self.pp_off["%s%d" % (name, l)]
        return self.pf[:, off:off + n]

    def plan_slabs(self, ntile_calls):
        order = []
        base = 0
        self.layer_base = []
        for l in range(self.depth):
            self.layer_base.append(base)
            base += len(layer_slab_names(l))
        for _ in range(ntile_calls):
            for l in range(self.depth):
                for i, nm in enumerate(layer_slab_names(l)):
                    order.append((l, nm, self.layer_base[l] + i))
        self.slab_order = order
        self.slab_issued = 0
        self.slab_next = 0

    def slab(self, l, name):
        k = self.slab_next
        ol, onm, _ = self.slab_order[k]
        assert (ol, onm) == (l, name), ((ol, onm), (l, name))
        lim = min(len(self.slab_order), k + NSLOT)
        while self.slab_issued < lim:
            n = self.slab_issued
            _, _, gi = self.slab_order[n]
            self.dma("pool", self.slots[n % NSLOT], self.dram["wslabs"][gi])
            self.slab_issued += 1
        self.slab_next += 1
        return self.slots[k % NSLOT]

    def layernorm(self, X, l, which, T):
        g = self.pfv(which + "_g", l)
        bb = self.pfv(which + "_b", l)
        A, Ab = self.af, self.ab
        assert 2 * T <= 512
        pss = self.bank()
        psm, psq = pss[:, 0:T], pss[:, T:2 * T]
        zzs = [Ab.alloc(128, [2, T]) for _ in range(2)]
        for n in range(8):
            zz = zzs[n % 2]
            self.act(zz[:, 0, :], X[n], AF.Copy)
            self.act(zz[:, 1, :], X[n], AF.Square)
            self.mm(pss[:, 0:2 * T], self.ones_b, zz.re("p a b -> p (a b)"), n == 0, n == 7)
        mu = A.alloc(128, [T])
        msq = A.alloc(128, [T])
        var = A.alloc(128, [T])
        rs = A.alloc(128, [T])
        self.act(mu, psm, AF.Copy, scale=1.0 / D_MODEL)
        self.tt(msq, mu, mu, ALU.mult)
        self.stt(var, psq, 1.0 / D_MODEL, msq, ALU.mult, ALU.subtract)
        self.act(var, var, AF.Sqrt, bias=self.eps_c)
        self.recip(rs, var)
        t1 = [A.alloc(128, [T]) for _ in range(2)]
        for n in range(8):
            t = t1[n % 2]
            self.tt(t, X[n], mu, ALU.subtract)
            self.tt(t, t, rs, ALU.mult)
            self.act(X[n], t, AF.Identity, scale=g[:, n:n + 1], bias=bb[:, n:n + 1])
            self.cp(self.xb[n][:, 0:T], X[n])

    def ffn(self, X, l, T, st):
        A, Ab = self.af, self.ab
        wf = self.pfv("w_fdw", l)
        bf = self.pfv("b_fdw", l)
        ghal = st["ghal"][l]
        hbuf = [Ab.alloc(128, [T]) for _ in range(NFF)]
        gb = [A.alloc(128, [T + 2]) for _ in range(2)]
        acc = [A.alloc(128, [T]) for _ in range(2)]
        ge = [A.alloc(128, [T]) for _ in range(2)]
        for s in range(11):
            slot = self.slab(l, "up%d" % s).re("p (k n) -> p k n", k=8)
            for jj in range(2):
                j = 2 * s + jj
                if j >= NFF:
                    continue
                psg, psu = self.bank(), self.bank()
                for k in range(8):
                    self.mm(psg[:, 0:T], slot[:, k, jj * 128:(jj + 1) * 128], self.xb[k][:, 0:T], k == 0, k == 7)
                for k in range(8):
                    self.mm(psu[:, 0:T], slot[:, k, (2 + jj) * 128:(3 + jj) * 128], self.xb[k][:, 0:T], k == 0, k == 7)
                g_, a_, e_ = gb[j % 2], acc[j % 2], ge[j % 2]
                self.cp(g_[:, 0:2], ghal[:, j, :], eng="dve")
                self.act(g_[:, 2:T + 2], psg[:, 0:T], AF.Copy)
                self.cp(ghal[:, j, :], g_[:, T:T + 2], eng="dve")
                self.ts(a_, g_[:, 0:T], wf[:, 3 * j:3 * j + 1], bf[:, j:j + 1], ALU.mult, ALU.add)
                self.stt(a_, g_[:, 1:T + 1], wf[:, 3 * j + 1:3 * j + 2], a_, ALU.mult, ALU.add)
                self.stt(a_, g_[:, 2:T + 2], wf[:, 3 * j + 2:3 * j + 3], a_, ALU.mult, ALU.add)
                self.act(e_, a_, AF.Gelu_apprx_tanh)
                self.tt(hbuf[j], e_, psu[:, 0:T], ALU.mult)
        for n in range(8):
            slot = self.slab(l, "dn%d" % n)
            ps = self.bank()
            for j in range(NFF):
                self.mm(ps[:, 0:T], slot[:, j * 128:(j + 1) * 128], hbuf[j], j == 0, j == NFF - 1)
            self.stt(X[n], X[n], ALPHA, ps[:, 0:T], ALU.mult, ALU.add)

    def even_mixer(self, X, l, T, st):
        A, Ab = self.af, self.ab
        NCH = T // 64
        xb = self.xb
        S_f, S_b, uhal = st["S_f"][l], st["S_b"][l], st["uhal"][l]
        slot = self.slab(l, "qk").re("p (k n) -> p k n", k=8)
        qT = A.alloc(64, [4, T])
        kT = A.alloc(64, [4, T])
        for i in range(8):
            ps = self.bank()
            for k in range(8):
                self.mm(ps[0:64, 0:T], slot[:, k, i * 64:(i + 1) * 64], xb[k][:, 0:T], k == 0, k == 7)
            if i < 4:
                self.act(qT[:, i, :], ps[0:64, 0:T], AF.Copy, scale=DK_A ** -0.5)
            else:
                self.cp(kT[:, i - 4, :], ps[0:64, 0:T])
        self.ck(1)
        lrT = Ab.alloc(17, [T])
        self.memset(lrT, 1.0)
        ps = self.bank()
        wl = self.wlrin[l].re("p (k n) -> p k n", k=8)
        for k in range(8):
            self.mm(ps[0:16, 0:T], wl[:, k, :], xb[k][:, 0:T], k == 0, k == 7)
        self.cp(lrT[0:16, :], ps[0:16, 0:T])
        self.ck(2)
        slot = self.slab(l, "v").re("p (k n) -> p k n", k=8)
        vtok = [Ab.alloc(64, [512]) for _ in range(NCH)]
        sp_tok = [A.alloc(64, [256]) for _ in range(2)]
        e1 = [A.alloc(64, [256]) for _ in range(2)]
        oT = A.alloc(128, [4, T])
        ep = [A.alloc(64, [4, 64]) for _ in range(2)]
        en = [A.alloc(64, [4, 64]) for _ in range(2)]
        qd = [Ab.alloc(64, [4, 64]) for _ in range(2)]
        kd = [Ab.alloc(64, [4, 64]) for _ in range(2)]
        kk = [Ab.alloc(64, [4, 64]) for _ in range(2)]
        scm = [Ab.alloc(64, [4, 64]) for _ in range(2)]
        kkt = [Ab.alloc(64, [256]) for _ in range(2)]
        for c in range(NCH):
            cs = slice(c * 64, (c + 1) * 64)
            r = c % 2
            ps = self.bank()
            for k in range(8):
                self.mm(ps[0:64, 0:512], xb[k][:, cs], slot[:, k, :], k == 0, k == 7)
            self.act(vtok[c], ps[0:64, 0:512], AF.Copy)
            self.ck(3)
            ps2 = self.bank()
            self.mm(ps2[0:64, 0:256], lrT[0:17, cs], self.wlraug[l], True, True)
            self.act(e1[r], ps2[0:64, 0:256], AF.Exp, scale=-1.0)
            self.act(sp_tok[r], e1[r], AF.Ln, bias=1.0)
            self.ck(4)
            ps3 = self.bank()
            for h in range(4):
                self.mm(ps3[0:64, h * 64:(h + 1) * 64], sp_tok[r][:, h * 64:(h + 1) * 64], self.U_f, True, True)
            p3 = ps3[0:64, 0:256].re("p (a b) -> p a b", a=4)
            self.act(ep[r], p3, AF.Exp, scale=-1.0 / 16.0)
            self.act(en[r], p3, AF.Exp, scale=1.0 / 16.0)
            self.ck(5)
            self.tt(qd[r], qT[:, :, cs], ep[r], ALU.mult)
            self.tt(kd[r], kT[:, :, cs], en[r], ALU.mult)
            for h in range(4):
                self.ts(kk[r][:, h, :], kd[r][:, h, :], ep[r][:, h, 63:64], None, ALU.mult)
            self.ck(6)
            ps4 = self.bank()
            for h in range(4):
                self.mm(ps4[0:64, h * 64:(h + 1) * 64], kd[r][:, h, :], qd[r][:, h, :], True, True)
            self.tt(scm[r], ps4[0:64, 0:256].re("p (a b) -> p a b", a=4), self.mask4.re("p (a b) -> p a b", a=4), ALU.mult)
            ps5 = self.bank()
            for h in range(4):
                self.mm(ps5[0:64, h * 64:(h + 1) * 64], kk[r][:, h, :], self.ident_b[0:64, 0:64], True, True)
            self.act(kkt[r], ps5[0:64, 0:256], AF.Copy)
            self.ck(7)
            ps6 = self.bank()
            for h in range(4):
                self.mm(ps6[:, h * 64:(h + 1) * 64], S_b[:, h, :], qd[r][:, h, :], True, False)
                self.mm(ps6[:, h * 64:(h + 1) * 64], vtok[c][:, h * 128:(h + 1) * 128], scm[r][:, h, :], False, True)
            self.act(oT[:, :, cs], ps6[:, 0:256].re("p (a b) -> p a b", a=4), AF.Copy)
            ps7 = self.bank()
            for h in range(4):
                self.mm(ps7[0:64, h * 128:(h + 1) * 128], kkt[r][:, h * 64:(h + 1) * 64], vtok[c][:, h * 128:(h + 1) * 128], True, True)
            for h in range(4):
                self.stt(S_f[:, h, :], S_f[:, h, :], ep[r][:, h, 63:64], ps7[0:64, h * 128:(h + 1) * 128], ALU.mult, ALU.add)
            self.act(S_b, S_f, AF.Copy)
            self.ck(8)
        slot = self.slab(l, "gate").re("p (k n) -> p k n", k=8)
        sg = A.alloc(128, [4, T])
        for h in range(4):
            ps = self.bank()
            for k in range(8):
                self.mm(ps[:, 0:T], slot[:, k, h * 128:(h + 1) * 128], xb[k][:, 0:T], k == 0, k == 7)
            self.act(sg[:, h, :], ps[:, 0:T], AF.Silu)
        gg = self.pfv("g_gla", l)
        R = [Ab.alloc(128, [T]) for _ in range(8)]
        sq = [Ab.alloc(128, [T]) for _ in range(2)]
        sd = [A.alloc(128, [T]) for _ in range(2)]
        for h in range(4):
            self.act(sq[h % 2], oT[:, h, :], AF.Square)
            ps = self.bank()
            self.mm(ps[:, 0:T], self.ones_b, sq[h % 2], True, True)
            self.act(sd[h % 2], ps[:, 0:T], AF.Sqrt, scale=1.0 / DV_A, bias=self.eps_c)
            self.recip(sd[h % 2], sd[h % 2])
            self.stt(sd[h % 2], oT[:, h, :], gg[:, h:h + 1], sd[h % 2], ALU.mult, ALU.mult)
            self.tt(R[h], sd[h % 2], sg[:, h, :], ALU.mult)
        self.ck(9)
        up = A.alloc(128, [4, T + 30])
        self.cp(up[:, :, 0:30], uhal)
        self.ck(91)
        slot = self.slab(l, "glua").re("p (k n) -> p k n", k=8)
        for j in range(4):
            ps = self.bank()
            for k in range(8):
                self.mm(ps[:, 0:T], slot[:, k, j * 128:(j + 1) * 128], xb[k][:, 0:T], k == 0, k == 7)
            self.act(up[:, j, 30:30 + T], ps[:, 0:T], AF.Copy)
        slot = self.slab(l, "glub").re("p (k n) -> p k n", k=8)
        sgb = [A.alloc(128, [T]) for _ in range(2)]
        for j in range(4):
            ps = self.bank()
            for k in range(8):
                self.mm(ps[:, 0:T], slot[:, k, j * 128:(j + 1) * 128], xb[k][:, 0:T], k == 0, k == 7)
            self.act(sgb[j % 2], ps[:, 0:T], AF.Sigmoid)
            self.tt(up[:, j, 30:30 + T], up[:, j, 30:30 + T], sgb[j % 2], ALU.mult)
        self.cp(uhal, up[:, :, T:T + 30])
        self.ck(92)
        wdw = self.pfv("w_dw", l)
        bdw = self.pfv("b_dw", l)
        cv = [A.alloc(128, [T]) for _ in range(4)]
        pss = self.bank()
        psm, psq = pss[:, 0:T], pss[:, T:2 * T]
        cvz = [Ab.alloc(128, [2, T]) for _ in range(2)]
        for j in range(4):
            self.ts(cv[j], up[:, j, 0:T], wdw[:, j * 31:j * 31 + 1], bdw[:, j:j + 1], ALU.mult, ALU.add)
            self.ck(93)
            for tap in range(1, W_B):
                self.stt(cv[j], up[:, j, tap:tap + T], wdw[:, j * 31 + tap:j * 31 + tap + 1], cv[j], ALU.mult, ALU.add)
                self.ck(94)
            self.ck(945)
            self.act(cvz[j % 2][:, 0, :], cv[j], AF.Copy)
            self.act(cvz[j % 2][:, 1, :], cv[j], AF.Square)
            self.mm(pss[:, 0:2 * T], self.ones_b, cvz[j % 2].re("p a b -> p (a b)"), j == 0, j == 3)
        self.ck(95)
        mu = A.alloc(128, [T])
        msq = A.alloc(128, [T])
        var = A.alloc(128, [T])
        self.act(mu, psm, AF.Copy, scale=1.0 / D_B)
        self.tt(msq, mu, mu, ALU.mult)
        self.stt(var, psq, 1.0 / D_B, msq, ALU.mult, ALU.subtract)
        self.act(var, var, AF.Sqrt, bias=self.eps_c)
        self.recip(var, var)
        gcn = self.pfv("g_cn", l)
        self.ck(96)
        bcn = self.pfv("b_cn", l)
        for j in range(4):
            self.tt(cv[j], cv[j], mu, ALU.subtract)
            self.tt(cv[j], cv[j], var, ALU.mult)
            self.act(R[4 + j], cv[j], AF.Silu, scale=gcn[:, j:j + 1], bias=bcn[:, j:j + 1])
        self.ck(10)
        self.out_proj(X, l, T, R)

    def out_proj(self, X, l, T, R):
        for s in range(2):
            slot = self.slab(l, "out%d" % s).re("p (k n) -> p k n", k=8)
            for jj in range(4):
                n = s * 4 + jj
                ps = self.bank()
                for k in range(8):
                    self.mm(ps[:, 0:T], slot[:, k, jj * 128:(jj + 1) * 128], R[k], k == 0, k == 7)
                self.stt(X[n], X[n], ALPHA, ps[:, 0:T], ALU.mult, ALU.add)

    def log1p_series(self, A, e, parts, shape):
        den = A.alloc(parts, shape)
        s_ = A.alloc(parts, shape)
        s2 = A.alloc(parts, shape)
        p = A.alloc(parts, shape)
        self.ts(den, e, 2.0, None, ALU.add)
        self.recip(den, den)
        self.tt(s_, e, den, ALU.mult)
        self.tt(s2, s_, s_, ALU.mult)
        self.ts(p, s2, 1.0 / 11.0, 1.0 / 9.0, ALU.mult, ALU.add)
        for cst in (1.0 / 7.0, 1.0 / 5.0, 1.0 / 3.0, 1.0):
            self.tt(p, p, s2, ALU.mult)
            self.ts(p, p, cst, None, ALU.add)
        self.tt(p, p, s_, ALU.mult)
        return p

    def odd_prologue(self, l, sbf):
        A = self.af
        lam = self.pfv("lam", l)
        e = A.alloc(128, [4])
        self.act(e, lam, AF.Exp, scale=-1.0)
        p = self.log1p_series(A, e, 128, [4])
        L4 = newV(sbf("L4_%d" % l, [128, 4], F32)[:, :])
        self.ts(L4, p, -8.0, None, ALU.mult)
        self.L4[l] = L4
        hd = self.pfv("hd", l)
        negA = newV(sbf("negA_%d" % l, [4, 1], F32)[:, :])
        self.act(negA, hd[0:4, 0:1], AF.Exp)
        self.ts(negA, negA, -1.0, None, ALU.mult)
        self.negA[l] = negA

    def rms_gate(self, oT, gname, slabname, l, T, R, base, dv):
        A, Ab = self.af, self.ab
        slot = self.slab(l, slabname).re("p (k n) -> p k n", k=8)
        sg = A.alloc(128, [4, T])
        for h in range(4):
            ps = self.bank()
            for k in range(8):
                self.mm(ps[:, 0:T], slot[:, k, h * 128:(h + 1) * 128], self.xb[k][:, 0:T], k == 0, k == 7)
            self.act(sg[:, h, :], ps[:, 0:T], AF.Silu)
        gg = self.pfv(gname, l)
        sq = [Ab.alloc(128, [T]) for _ in range(2)]
        sd = [A.alloc(128, [T]) for _ in range(2)]
        for h in range(4):
            self.act(sq[h % 2], oT[:, h, :], AF.Square)
            ps = self.bank()
            self.mm(ps[:, 0:T], self.ones_b, sq[h % 2], True, True)
            self.act(sd[h % 2], ps[:, 0:T], AF.Sqrt, scale=1.0 / dv, bias=self.eps_c)
            self.recip(sd[h % 2], sd[h % 2])
            self.stt(sd[h % 2], oT[:, h, :], gg[:, h:h + 1], sd[h % 2], ALU.mult, ALU.mult)
            self.tt(R[base + h], sd[h % 2], sg[:, h, :], ALU.mult)

    def odd_mixer(self, X, l, T, st):
        A, Ab = self.af, self.ab
        NCH = T // 64
        xb = self.xb
        chal, hl, Sd = st["chal"][l], st["h_lru"][l], st["Sd_f"][l]
        wcv = self.pfv("w_cv", l)
        bcv = self.pfv("b_cv", l)
        R = [Ab.alloc(128, [T]) for _ in range(8)]
        cvo = {nm: A.alloc(128, [4, T]) for nm in ("xl", "q", "k", "v")}
        KA = A.alloc(128, [4, T])
        KBN = A.alloc(128, [4, T])
        QA = A.alloc(128, [4, T])
        oT = A.alloc(128, [4, T])
        eG = A.alloc(128, [4, NCH])
        Gb = A.alloc(64, [4, T])
        Gbn = A.alloc(64, [4, T])
        gcol = A.alloc(64, [NCH, 4])
        m0 = A.mark()
        cin = [A.alloc(128, [4, T + 3]) for _ in range(2)]
        for si, nm in enumerate(("xl", "q", "k", "v")):
            slot = self.slab(l, nm).re("p (k n) -> p k n", k=8)
            ci = cin[si % 2]
            self.cp(ci[:, :, 0:3], chal[:, si * 4:(si + 1) * 4, :])
            for j in range(4):
                ps = self.bank()
                for k in range(8):
                    self.mm(ps[:, 0:T], slot[:, k, j * 128:(j + 1) * 128], xb[k][:, 0:T], k == 0, k == 7)
                self.act(ci[:, j, 3:3 + T], ps[:, 0:T], AF.Copy)
            self.cp(chal[:, si * 4:(si + 1) * 4, :], ci[:, :, T:T + 3])
            dst = cvo[nm]
            for j in range(4):
                jj = si * 4 + j
                self.ts(dst[:, j, :], ci[:, j, 0:T], wcv[:, jj * 4:jj * 4 + 1], bcv[:, jj:jj + 1], ALU.mult, ALU.add)
                for tap in range(1, W_S):
                    self.stt(dst[:, j, :], ci[:, j, tap:tap + T], wcv[:, jj * 4 + tap:jj * 4 + tap + 1], dst[:, j, :], ALU.mult, ALU.add)
                if si > 0:
                    self.act(dst[:, j, :], dst[:, j, :], AF.Silu)
        self.ck(21)
        xc = cvo["xl"]
        xcb = Ab.alloc(128, [4, T])
        self.cp(xcb, xc)
        L4 = self.L4[l]
        wrg = self.wrg[l].re("p (c n) -> p c n", c=4)
        wig = self.wig[l].re("p (c n) -> p c n", c=4)
        brg = self.pfv("b_rg", l)
        big = self.pfv("b_ig", l)
        slot = self.slab(l, "gc").re("p (k n) -> p k n", k=8)
        tb = [[A.alloc(128, [T]) for _ in range(2)] for _ in range(7)]
        for c in range(4):
            r_, ig_, t_, rd_, a_, om_, h_ = [tb[i][c % 2] for i in range(7)]
            ps = self.bank()
            self.mm(ps[:, 0:T], wrg[:, c, :], xcb[:, c, :], True, True)
            self.act(r_, ps[:, 0:T], AF.Sigmoid, bias=brg[:, c:c + 1])
            ps = self.bank()
            self.mm(ps[:, 0:T], wig[:, c, :], xcb[:, c, :], True, True)
            self.act(ig_, ps[:, 0:T], AF.Sigmoid, bias=big[:, c:c + 1])
            self.act(t_, r_, AF.Tanh, scale=L4[:, c:c + 1])
            self.ts(rd_, t_, -1.0, 1.0, ALU.mult, ALU.add)
            self.recip(rd_, rd_)
            self.stt(a_, t_, 1.0, rd_, ALU.add, ALU.mult)
            self.stt(om_, t_, -4.0, rd_, ALU.mult, ALU.mult)
            self.tt(om_, om_, rd_, ALU.mult)
            self.act(om_, om_, AF.Sqrt)
            self.tt(om_, om_, ig_, ALU.mult)
            self.tt(om_, om_, xc[:, c, :], ALU.mult)
            self.scan(h_, a_, om_, hl[:, c:c + 1], ALU.mult, ALU.add)
            self.cp(hl[:, c:c + 1], h_[:, T - 1:T])
            ps = self.bank()
            for k in range(8):
                self.mm(ps[:, 0:T], slot[:, k, c * 128:(c + 1) * 128], xb[k][:, 0:T], k == 0, k == 7)
            self.act(r_, ps[:, 0:T], AF.Gelu_apprx_tanh)
            self.tt(R[c], h_, r_, ALU.mult)
        self.ck(22)
        self.P.fence()
        A.release(m0)
        qs, ks, vs = cvo["q"], cvo["k"], cvo["v"]
        wba = self.wbain[l].re("p (k n) -> p k n", k=8)
        psb, psa = self.bank(), self.bank()
        for k in range(8):
            self.mm(psb[0:4, 0:T], wba[:, k, 0:4], xb[k][:, 0:T], k == 0, k == 7)
        for k in range(8):
            self.mm(psa[0:4, 0:T], wba[:, k, 4:8], xb[k][:, 0:T], k == 0, k == 7)
        ROWS = A.alloc(4, [4, T])
        hd = self.pfv("hd", l)
        self.act(ROWS[:, 3, :], psb[0:4, 0:T], AF.Sigmoid)
        y = A.alloc(4, [T])
        ay = A.alloc(4, [T])
        self.ts(y, psa[0:4, 0:T], hd[0:4, 1:2], None, ALU.add)
        self.act(ay, y, AF.Abs)
        self.act(ay, ay, AF.Exp, scale=-1.0)
        p = self.log1p_series(A, ay, 4, [T])
        self.ts(y, y, 0.0, None, ALU.max)
        self.stt(y, p, 2.0, y, ALU.mult, ALU.add)
        self.ts(y, y, self.negA[l][0:4, 0:1], None, ALU.mult)
        gc = ROWS[:, 1, :]
        self.scan(gc, self.cmask[:, 0:T], y, 0.0, ALU.mult, ALU.add)
        self.act(ROWS[:, 0, :], gc, AF.Exp)
        self.tt(ROWS[:, 2, :], ROWS[:, 3, :], ROWS[:, 0, :], ALU.mult)
        self.ck(23)
        for c in range(NCH):
            psT = self.bank()
            self.mm(psT[0:64, 0:4], ROWS[:, 1, c * 64:(c + 1) * 64], self.ident_f[0:4, 0:4], True, True)
            self.cp(gcol[:, c, :], psT[0:64, 0:4])
        sqb = [Ab.alloc(128, [T]) for _ in range(2)]
        rn = [A.alloc(128, [T]) for _ in range(2)]
        for h in range(4):
            selh = self.sel[:, h * 128:(h + 1) * 128]
            for i, src in enumerate((qs, ks)):
                self.act(sqb[i], src[:, h, :], AF.Square)
                ps = self.bank()
                self.mm(ps[:, 0:T], self.ones_b, sqb[i], True, True)
                self.act(rn[i], ps[:, 0:T], AF.Sqrt, bias=self.eps6_c)
                self.recip(rn[i], rn[i])
            self.stt(qs[:, h, :], qs[:, h, :], DK_D ** -0.5, rn[0], ALU.mult, ALU.mult)
            self.tt(ks[:, h, :], ks[:, h, :], rn[1], ALU.mult)
            psE = self.bank()
            self.mm(psE[:, 0:T], selh, ROWS[:, 0, :], True, True)
            self.act(eG[:, h, :], psE[:, 0:T].re("p (c t) -> p c t", t=64)[:, :, 63], AF.Copy)
            self.tt(QA[:, h, :], qs[:, h, :], psE[:, 0:T], ALU.mult)
            psB = self.bank()
            self.mm(psB[:, 0:T], selh, ROWS[:, 2, :], True, True)
            self.tt(KA[:, h, :], ks[:, h, :], psB[:, 0:T], ALU.mult)
            psb2 = self.bank()
            self.mm(psb2[:, 0:T], selh, ROWS[:, 3, :], True, True)
            self.tt(vs[:, h, :], vs[:, h, :], psb2[:, 0:T], ALU.mult)
            self.tt(KBN[:, h, :], ks[:, h, :], psb2[:, 0:T], ALU.mult)
            psG = self.bank()
            self.mm(psG[:, 0:T], selh, ROWS[:, 1, :], True, True)
            self.act(Gb[:, h, :], psG[0:64, 0:T], AF.Copy)
            self.act(Gbn[:, h, :], psG[0:64, 0:T], AF.Copy, scale=-1.0)
        self.ck(24)
        self.P.fence()
        A.release(m0)
        NB = 16
        nb = [A.alloc(64, [256]) for _ in range(NB)]
        big_ = [A.alloc(64, [512]) for _ in range(5)]
        WT = A.alloc(128, [256])
        nbi = [0]
        BV, KN, QN = vs, ks, qs
        DT2 = [A.alloc(64, [256]) for _ in range(2)]
        AT2 = [A.alloc(64, [256]) for _ in range(2)]

        def nbuf():
            v = nb[nbi[0] % NB]
            nbi[0] += 1
            return v
        for c in range(NCH):
            cs = slice(c * 64, (c + 1) * 64)
            DT, D, DTs, Ds = DT2[c % 2], nbuf(), nbuf(), nbuf()
            for h in range(4):
                hc = slice(h * 64, (h + 1) * 64)
                self.stt(DT[:, hc], Gb[:, h, cs], gcol[:, c, h:h + 1], self.negU4[:, hc], ALU.subtract, ALU.add)
                self.stt(D[:, hc], Gbn[:, h, cs], gcol[:, c, h:h + 1], self.negL4[:, hc], ALU.add, ALU.add)
            self.act(DT, DT, AF.Exp)
            self.act(D, D, AF.Exp)
            self.tt(DTs, DT, self.ident4, ALU.subtract)
            self.tt(Ds, D, self.ident4, ALU.subtract)
            psNT, psN, psAT = self.bank(), self.bank(), self.bank()
            for h in range(4):
                hc = slice(h * 64, (h + 1) * 64)
                self.mm(psNT[0:64, hc], KN[:, h, cs], KBN[:, h, cs], True, True)
                self.mm(psN[0:64, hc], KBN[:, h, cs], KN[:, h, cs], True, True)
                self.mm(psAT[0:64, hc], KN[:, h, cs], QN[:, h, cs], True, True)
            NT, N, AT, PT = nbuf(), nbuf(), AT2[c % 2], nbuf()
            self.stt(NT, psNT[0:64, 0:256], -1.0, DTs, ALU.mult, ALU.mult)
            self.stt(N, psN[0:64, 0:256], -1.0, Ds, ALU.mult, ALU.mult)
            self.tt(AT, psAT[0:64, 0:256], DT, ALU.mult)
            self.tt(PT, NT, self.ident4, ALU.add)
            for lev in range(5):
                psN2 = self.bank()
                for h in range(4):
                    hc = slice(h * 64, (h + 1) * 64)
                    self.mm(psN2[0:64, hc], NT[:, hc], N[:, hc], True, True)
                N2 = nbuf()
                self.act(N2, psN2[0:64, 0:256], AF.Copy)
                if lev < 4:
                    psNT2 = self.bank()
                    for h in range(4):
                        hc = slice(h * 64, (h + 1) * 64)
                        self.mm(psNT2[0:64, hc], N[:, hc], NT[:, hc], True, True)
                    NT2 = nbuf()
                    self.cp(NT2, psNT2[0:64, 0:256])
                else:
                    NT2 = None
                psP = self.bank()
                for h in range(4):
                    hc = slice(h * 64, (h + 1) * 64)
                    self.mm(psP[0:64, hc], N2[:, hc], PT[:, hc], True, True)
                PT2 = nbuf()
                self.tt(PT2, PT, psP[0:64, 0:256], ALU.add)
                N, NT, PT = N2, NT2, PT2
            BVt, KAt, KBt, U_sb, VN = big_
            for src, dst, eng in ((BV, BVt, "act"), (KA, KAt, "dve")):
                psT = self.bank()
                for h in range(4):
                    self.mm(psT[0:64, h * 128:(h + 1) * 128], src[:, h, cs], self.ident_f, True, True)
                if eng == "act":
                    self.act(dst, psT[0:64, 0:512], AF.Copy)
                else:
                    self.cp(dst, psT[0:64, 0:512])
            psT = self.bank()
            for h in range(4):
                self.mm(psT[0:64, h * 128:(h + 1) * 128], KN[:, h, cs], self.ident_f, True, True)
            for h in range(4):
                self.ts(KBt[:, h * 128:(h + 1) * 128], psT[0:64, h * 128:(h + 1) * 128], DT[:, h * 64 + 63:h * 64 + 64], None, ALU.mult)
            psU = self.bank()
            for h in range(4):
                self.mm(psU[0:64, h * 128:(h + 1) * 128], PT[:, h * 64:(h + 1) * 64], BVt[:, h * 128:(h + 1) * 128], True, True)
            self.act(U_sb, psU[0:64, 0:512], AF.Copy)
            psW = self.bank()
            for h in range(4):
                self.mm(psW[:, h * 64:(h + 1) * 64], KAt[:, h * 128:(h + 1) * 128], PT[:, h * 64:(h + 1) * 64], True, True)
            self.cp(WT, psW[:, 0:256])
            psWS = self.bank()
            for h in range(4):
                self.mm(psWS[0:64, h * 128:(h + 1) * 128], WT[:, h * 64:(h + 1) * 64], Sd[:, h, :], True, True)
            self.tt(VN, U_sb, psWS[0:64, 0:512], ALU.subtract)
            psO, psO2 = self.bank(), self.bank()
            for h in range(4):
                hc = slice(h * 64, (h + 1) * 64)
                self.mm(psO[:, hc], Sd[:, h, :], QA[:, h, cs], True, True)
                self.mm(psO2[:, hc], VN[:, h * 128:(h + 1) * 128], AT[:, hc], True, True)
            self.act(oT[:, :, cs], psO[:, 0:256].re("p (a b) -> p a b", a=4), AF.Copy)
            self.tt(oT[:, :, cs], oT[:, :, cs], psO2[:, 0:256].re("p (a b) -> p a b", a=4), ALU.add)
            psS = self.bank()
            for h in range(4):
                self.mm(psS[:, h * 128:(h + 1) * 128], KBt[:, h * 128:(h + 1) * 128], VN[:, h * 128:(h + 1) * 128], True, True)
            for h in range(4):
                self.stt(Sd[:, h, :], Sd[:, h, :], eG[:, h, c:c + 1], psS[:, h * 128:(h + 1) * 128], ALU.mult, ALU.add)
            self.ck(25)
        self.P.fence()
        A.release(m0)
        self._dbg_oT = oT
        self._dbg_QA = QA
        self.rms_gate(oT, "g_dl", "z", l, T, R, 4, DV_D)
        self.ck(26)
        if self.cfg.get('dbg'):
            for n in range(4):
                self.cp(self.dbgbuf[:, n, 0:T], R[n])
            for n in range(4):
                self.cp(self.dbgbuf[:, 4 + n, 0:T], self._dbg_oT[:, n, :])
            self.dma('sp', self.dram['dbg'], self.dbgbuf, is_output=True)
        self.out_proj(X, l, T, R)

    def build(self):
        import contextlib
        nc, T, depth = self.nc, self.T, self.depth
        ntiles = self.S // T
        self.plan_slabs(ntiles + (1 if self.has_sample else 0))
        with contextlib.ExitStack() as es:
            def sb(name, shape, dt):
                return es.enter_context(nc.sbuf_tensor("t_" + name, list(shape), dt))
            AF_SZ = self.cfg.get("arena_f", 18500 * self.T // 256)
            AB_SZ = self.cfg.get("arena_b", 9700 * self.T // 256)
            self.af = Arena(sb("arena_f", [128, AF_SZ], F32), AF_SZ)
            self.ab = Arena(sb("arena_b", [128, AB_SZ], BF16), AB_SZ)
            slots_t = sb("slots", [128, NSLOT, SLAB], BF16)
            self.slots = [newV(slots_t[:, i, :]) for i in range(NSLOT)]
            Xt = [sb("X%d" % i, [128, 8, T], F32) for i in range(2)]
            Xs = [[newV(Xt[i][:, n, :]) for n in range(8)] for i in range(2)]
            xb_t = sb("xb", [128, 8, T], BF16)
            self.xb = [newV(xb_t[:, n, :]) for n in range(8)]
            self.pf = newV(sb("pf", [128, self.npf], F32)[:, :])
            cst = newV(sb("cst", [128, NCONST], F32)[:, :])
            cst_b = newV(sb("cst_b", [128, NCONST], BF16)[:, :])
            self.ones_b = newV(sb("ones_b", [128, 128], BF16)[:, :])
            self.eps_c = newV(sb("eps_c", [128, 1], F32)[:, :])
            self.ident_f = cst[:, 0:128]
            self.ident_b = cst_b[:, 0:128]
            self.U_f = cst[0:64, 128:192]
            self.mask4 = cst[0:64, 128:384]
            self.smask4 = cst[0:64, 384:640]
            self.lmask4 = cst[0:64, 640:896]
            self.ident4 = cst[0:64, 896:1152]
            self.cmask = cst[0:4, 1152:1664]
            self.sel = cst[0:4, 1664:2176]
            self.negU4 = cst[0:64, 2176:2432]
            self.negL4 = cst[0:64, 2432:2688]
            self.eps6_c = newV(sb("eps6_c", [128, 1], F32)[:, :])
            if self.cfg.get('dbg'):
                self.dbgbuf = newV(sb('dbgbuf', [128, 8, T], F32)[:, :, :])
            self.banks = [newV(es.enter_context(nc.psum_tensor("ps%d" % i, [128, 512], F32))[:, :]) for i in range(8)]
            self.bank_rr = 0
            self._ln_zb = [None, None]
            self._ln_zs = [None, None]
            self.wlrin, self.wlraug, self.wbain, self.wrg, self.wig = {}, {}, {}, {}, {}
            self.L4, self.negA = {}, {}
            stage = {}
            for k, shp in self.small_shapes.items():
                stage[k] = newV(sb("st_" + k, list(shp), F32)[:, :])
            st = {"S_f": {}, "S_b": {}, "uhal": {}, "ghal": {}, "h_lru": {}, "Sd_f": {}, "Sd_b": {}, "chal": {}}
            for l in range(depth):
                st["ghal"][l] = newV(sb("ghal%d" % l, [128, NFF, 2], F32)[:, :, :])
                if l % 2 == 0:
                    st["S_f"][l] = newV(sb("S_f%d" % l, [64, 4, 128], F32)[:, :, :])
                    st["S_b"][l] = newV(sb("S_b%d" % l, [64, 4, 128], BF16)[:, :, :])
                    st["uhal"][l] = newV(sb("uhal%d" % l, [128, 4, 30], F32)[:, :, :])
                else:
                    st["h_lru"][l] = newV(sb("hlru%d" % l, [128, 4], F32)[:, :])
                    st["Sd_f"][l] = newV(sb("Sd_f%d" % l, [128, 4, 128], F32)[:, :, :])
                    st["Sd_b"][l] = newV(sb("Sd_b%d" % l, [128, 4, 128], BF16)[:, :, :])
                    st["chal"][l] = newV(sb("chal%d" % l, [128, 16, 3], F32)[:, :, :])
            self.st = st
            d = self.dram
            self.dma("sp", self.pf, d["pf"])
            self.dma("sp", cst, d["consts"])
            self.cp(cst_b, cst)
            self.memset(self.ones_b, 1.0)
            self.memset(self.eps_c, EPS)
            self.memset(self.eps6_c, 1e-6)
            for k in self.small_shapes:
                self.dma("sp", stage[k], d[k])
                shp = self.small_shapes[k]
                bt = newV(sb("sb_" + k, list(shp), BF16)[:, :])
                self.cp(bt, stage[k])
                l = int(k[-1])
                if k.startswith("wlrin"):
                    self.wlrin[l] = bt
                elif k.startswith("wlraug"):
                    self.wlraug[l] = bt
                elif k.startswith("wbain"):
                    self.wbain[l] = bt
                elif k.startswith("wrg"):
                    self.wrg[l] = bt
                elif k.startswith("wig"):
                    self.wig[l] = bt

            for l in range(depth):
                if l % 2 == 1:
                    self.odd_prologue(l, sb)

            def zero_states():
                for l in range(depth):
                    self.memset(st["ghal"][l], 0.0)
                    if l % 2 == 0:
                        self.memset(st["S_f"][l], 0.0)
                        self.memset(st["S_b"][l], 0.0)
                        self.memset(st["uhal"][l], 0.0)
                    else:
                        self.memset(st["h_lru"][l], 0.0)
                        self.memset(st["Sd_f"][l], 0.0)
                        self.memset(st["Sd_b"][l], 0.0)
                        self.memset(st["chal"][l], 0.0)

            def load_states():
                for l in range(depth):
                    self.dma("sp", st["ghal"][l], d["i%d_ffn" % l])
                    if l % 2 == 0:
                        self.dma("sp", st["S_f"][l], d["i%d_gla" % l].rearrange("h k v -> k h v"))
                        self.act(st["S_b"][l], st["S_f"][l], AF.Copy)
                        self.dma("sp", st["uhal"][l], d["i%d_dw" % l])
                    else:
                        self.dma("sp", st["h_lru"][l], d["i%d_lru" % l])
                        self.dma("sp", st["Sd_f"][l], d["i%d_delta" % l].rearrange("h k v -> k h v"))
                        self.act(st["Sd_b"][l], st["Sd_f"][l], AF.Copy)
                        self.dma("sp", st["chal"][l], d["i%d_conv" % l])

            def store_states(grp):
                for l in range(depth):
                    self.dma("sp", d["%s%d_ffn" % (grp, l)], st["ghal"][l], is_output=True)
                    if l % 2 == 0:
                        self.dma("sp", d["%s%d_gla" % (grp, l)].rearrange("h k v -> k h v"), st["S_f"][l], is_output=True)
                        self.dma("sp", d["%s%d_dw" % (grp, l)], st["uhal"][l], is_output=True)
                    else:
                        self.dma("sp", d["%s%d_lru" % (grp, l)], st["h_lru"][l], is_output=True)
                        self.dma("sp", d["%s%d_delta" % (grp, l)].rearrange("h k v -> k h v"), st["Sd_f"][l], is_output=True)
                        self.dma("sp", d["%s%d_conv" % (grp, l)], st["chal"][l], is_output=True)

            def run_tile(X, Tt):
                for n in range(8):
                    self.cp(self.xb[n][:, 0:Tt], X[n][:, 0:Tt])
                for l in range(depth):
                    self.P.fence()
                    self.af.reset()
                    self.ab.reset()
                    Xv = [x[:, 0:Tt] for x in X]
                    if l % 2 == 0:
                        self.even_mixer(Xv, l, Tt, st)
                    else:
                        self.odd_mixer(Xv, l, Tt, st)
                    self.ck(11)
                    self.layernorm(Xv, l, "ln1", Tt)
                    self.ck(12)
                    self.P.fence()
                    self.af.reset()
                    self.ab.reset()
                    self.ffn(Xv, l, Tt, st)
                    self.ck(13)
                    self.layernorm(Xv, l, "ln2", Tt)

            tcount = 0
            try:
                self.main_body(ntiles, d, Xt, Xs, zero_states, load_states, store_states, run_tile)
            except Cut:
                pass
            self.P.finish()
            self.P.emit()
        return nc

    def main_body(self, ntiles, d, Xt, Xs, zero_states, load_states, store_states, run_tile):
        T = self.T
        tcount = 0
        if True:
            if ntiles:
                zero_states()
                xp = d["xp"].rearrange("(k p) s -> p k s", p=128)
                yp = d["yp"].rearrange("(k p) s -> p k s", p=128)
                Xall = [V(Xt[i][:, :, :], [u for x in Xs[i] for u in x.us]) for i in range(2)]
                self.dma("sp", Xall[0], xp[:, :, 0:T])
                for i in range(ntiles):
                    if i + 1 < ntiles:
                        self.dma("sp", Xall[(i + 1) % 2], xp[:, :, (i + 1) * T:(i + 2) * T])
                    if i > 0 and i % 6 == 0:
                        self.P.new_epoch()
                    run_tile(Xs[i % 2], T)
                    self.dma("sp", yp[:, :, i * T:(i + 1) * T], Xall[i % 2], is_output=True)
                    tcount += 1
                store_states("p")
            if self.has_sample:
                xs = d["xs"].rearrange("(k p) s -> p k s", p=128)
                ys = d["ys"].rearrange("(k p) s -> p k s", p=128)
                Xi = tcount % 2
                Xsv = V(Xt[Xi][:, :, 0:64], [u for x in Xs[Xi] for u in x.us])
                load_states()
                self.dma("sp", Xsv, xs)
                run_tile(Xs[Xi], 64)
                self.dma("sp", ys, Xsv, is_output=True)
                store_states("s")


def run_config(inp, cfg, xp_list, xs_list, states_list):
    depth = cfg["depth"]
    wslabs, pp, small = host_weights(inp, depth)
    pf = pp.array()
    small_shapes = {k: v.shape for k, v in small.items()}
    b = Builder(cfg, pp.off, pf.shape[1], wslabs.shape[0], small_shapes)
    nc = b.build()
    consts = host_consts()
    ncores = len(xp_list)
    in_maps = []
    for c in range(ncores):
        m = {"wslabs": wslabs, "pf": pf, "consts": consts}
        m.update(small)
        if cfg["seq"]:
            m["xp"] = np.ascontiguousarray(xp_list[c].T)
        if cfg["sample"]:
            m["xs"] = np.ascontiguousarray(xs_list[c].T)
            stt = states_list[c]
            for l in range(depth):
                if l % 2 == 0:
                    m["i%d_gla" % l] = np.ascontiguousarray(stt["gla%d" % l])
                    m["i%d_dw" % l] = np.ascontiguousarray(stt["dw%d" % l].T.reshape(4, 128, W_B - 1).transpose(1, 0, 2))
                else:
                    m["i%d_lru" % l] = _fm(stt["lru%d" % l])
                    m["i%d_delta" % l] = np.ascontiguousarray(stt["delta%d" % l])
                    m["i%d_conv" % l] = np.ascontiguousarray(stt["conv%d" % l].T.reshape(16, 128, W_S - 1).transpose(1, 0, 2))
                m["i%d_ffn" % l] = np.ascontiguousarray(stt["ffn%d" % l].T.reshape(NFF, 128, W_F - 1).transpose(1, 0, 2))
        in_maps.append(m)
    res = run_bass_kernel_spmd(nc, in_maps, core_ids=list(range(ncores)))
    return res.results, b


def unpack_state(r, grp, l):
    out = {}
    if l % 2 == 0:
        out["gla"] = r["%s%d_gla" % (grp, l)]
        out["dw"] = np.ascontiguousarray(r["%s%d_dw" % (grp, l)].transpose(2, 1, 0).reshape(W_B - 1, D_B))
    else:
        out["lru"] = np.ascontiguousarray(r["%s%d_lru" % (grp, l)].T.reshape(D_C))
        out["delta"] = r["%s%d_delta" % (grp, l)]
        out["conv"] = np.ascontiguousarray(r["%s%d_conv" % (grp, l)].transpose(2, 1, 0).reshape(W_S - 1, 2048))
    out["ffn"] = np.ascontiguousarray(r["%s%d_ffn" % (grp, l)].transpose(2, 1, 0).reshape(W_F - 1, D_FF))
    return out


def kernel(**inputs):
    inp = {k: np.asarray(v) for k, v in inputs.items()}
    cfg = {"depth": DEPTH, "T": 256, "seq": 8192, "sample": True}
    xp_list = [inp["x_prompt"][c % 4] for c in range(8)]
    xs_list = [inp["x_sample"][c] for c in range(8)]
    states = []
    for c in range(8):
        s = {}
        for l in range(DEPTH):
            if l % 2 == 0:
                s["gla%d" % l] = inp["state_l%d_gla" % l][c]
                s["dw%d" % l] = inp["cache_l%d_dwconv" % l][c]
            else:
                s["lru%d" % l] = inp["state_l%d_lru" % l][c]
                s["delta%d" % l] = inp["state_l%d_delta" % l][c]
                s["conv%d" % l] = inp["cache_l%d_conv" % l][c]
            s["ffn%d" % l] = inp["cache_l%d_ffn" % l][c]
        states.append(s)
    results, _ = run_config(inp, cfg, xp_list, xs_list, states)
    y_prompt = np.stack([results[c]["yp"].T for c in range(4)], 0)
    y_sample = np.stack([results[c]["ys"].T for c in range(8)], 0)
    outs = [y_prompt, y_sample]
    for grp, cores in (("p", range(4)), ("s", range(8))):
        per = [[unpack_state(results[c], grp, l) for l in range(DEPTH)] for c in cores]
        for l in range(DEPTH):
            keys = ("gla", "dw", "ffn") if l % 2 == 0 else ("lru", "delta", "conv", "ffn")
            for k in keys:
                outs.append(np.stack([per[i][l][k] for i in range(len(per))], 0))
    return tuple(np.ascontiguousarray(o, dtype=np.float32) for o in outs)
```

```python
import numpy as np
import concourse.bass as bass
import concourse.mybir as mybir
from concourse.bass_utils import run_bass_kernel_spmd

F32 = mybir.dt.float32
BF16 = mybir.dt.bfloat16
AF = mybir.ActivationFunctionType
ALU = mybir.AluOpType

D_MODEL = 1024
DEPTH = 4
H_A, DK_A, DV_A, R_A = 4, 64, 128, 16
D_B, W_B = 512, 31
D_C, H_C, DH_C = 512, 8, 64
H_D, DK_D, DV_D = 4, 128, 128
W_S = 4
D_FF, W_F = 2688, 3
NFF = D_FF // 128
ALPHA = (2 * DEPTH) ** 0.25
EPS = 1e-5
LRU_C = 8.0
SLAB = 4096
NSLOT = 5
NDMASEM = 40


class Cut(Exception):
    pass


class Unit:
    __slots__ = ("w", "r")

    def __init__(self):
        self.w = None
        self.r = {}


class V:
    __slots__ = ("ap", "us")

    def __init__(self, ap, us):
        self.ap = ap
        self.us = tuple(us)

    def __getitem__(self, idx):
        return V(self.ap[idx], self.us)

    def re(self, s, **kw):
        return V(self.ap.rearrange(s, **kw), self.us)


def newV(ap):
    return V(ap, (Unit(),))


class Prog:
    ENG = ("pe", "act", "dve", "pool", "sp")

    def __init__(self, nc):
        self.nc = nc
        self.q = {e: [] for e in self.ENG}
        self.cnt = {e: 0 for e in self.ENG}
        self.waited = {e: {} for e in self.ENG}
        self.dma_val = [0] * NDMASEM
        self.dma_rr = 0
        self.out_tokens = []
        self.ninstr = 0
        self.epoch = 0

    def _wait(self, eng, key, val):
        if self.waited[eng].get(key, 0) >= val:
            return
        self.waited[eng][key] = val
        self.q[eng].append(("w", key, val))

    def _deps(self, eng, reads, writes):
        for v in reads:
            for u in v.us:
                if u.w is not None:
                    self._wait(eng, u.w[0], u.w[1])
        for v in writes:
            for u in v.us:
                if u.w is not None and u.w[0][0] != eng:
                    self._wait(eng, u.w[0], u.w[1])
                for k, val in u.r.items():
                    if k[0] != eng:
                        self._wait(eng, k, val)

    def _mark(self, tok, reads, writes):
        for v in reads:
            for u in v.us:
                if u.r.get(tok[0], 0) < tok[1]:
                    u.r[tok[0]] = tok[1]
        for v in writes:
            for u in v.us:
                u.w = tok
                u.r = {}

    def op(self, eng, fn, reads, writes, inc=True):
        self._deps(eng, reads, writes)
        key = (eng, self.epoch)
        if inc:
            self.cnt[eng] += 1
            tok = (key, self.cnt[eng])
        else:
            tok = (key, self.cnt[eng] + 1)
        self.q[eng].append(("i", fn, inc, key))
        self._mark(tok, reads, writes)
        self.ninstr += 1

    def new_epoch(self):
        self.fence()
        self.epoch += 1
        for e in self.ENG:
            self.cnt[e] = 0

    def dma(self, eng, out, in_, reads, writes, is_output=False, **kw):
        i = self.dma_rr
        self.dma_rr = (self.dma_rr + 1) % NDMASEM
        key = ("d", i)
        if self.dma_val[i] > 0:
            self._wait(eng, key, self.dma_val[i])
        self._deps(eng, reads, writes)
        self.dma_val[i] += 16
        tok = (key, self.dma_val[i])
        self.q[eng].append(("d", out, in_, i, kw))
        self._mark(tok, reads, writes)
        if is_output:
            self.out_tokens.append(tok)
        self.ninstr += 1

    def fence(self):
        comp = ("pe", "act", "dve", "pool")
        for e in comp:
            for f in comp:
                if e != f and self.cnt[f] > 0:
                    self._wait(e, (f, self.epoch), self.cnt[f])

    def finish(self):
        for key, val in self.out_tokens:
            self._wait("sp", key, val)

    def emit(self):
        nc = self.nc
        handles = {"pe": nc.tensor, "act": nc.scalar, "dve": nc.vector, "pool": nc.gpsimd, "sp": nc.sync}
        import contextlib
        with contextlib.ExitStack() as st:
            sems = {}
            for e in self.ENG:
                for ep in range(self.epoch + 1):
                    sems[(e, ep)] = st.enter_context(nc.semaphore("s_%s_%d" % (e, ep)))
            for i in range(NDMASEM):
                sems[("d", i)] = st.enter_context(nc.semaphore("sd%d" % i))
            block = st.enter_context(nc.Block())

            def run(e, h):
                for it in self.q[e]:
                    if it[0] == "w":
                        h.wait_ge(sems[it[1]], it[2])
                    elif it[0] == "i":
                        ins = it[1](h)
                        if it[2]:
                            ins.then_inc(sems[it[3]], 1)
                    else:
                        h.dma_start(out=it[1], in_=it[2], **it[4]).then_inc(sems[("d", it[3])], 16)

            @block.tensor
            def _(h):
                run("pe", h)

            @block.scalar
            def _(h):
                run("act", h)

            @block.vector
            def _(h):
                run("dve", h)

            @block.gpsimd
            def _(h):
                run("pool", h)

            @block.sync
            def _(h):
                run("sp", h)


class Arena:
    def __init__(self, tens, size):
        self.t = tens
        self.size = size
        self.off = 0

    def reset(self):
        self.off = 0

    def mark(self):
        return self.off

    def release(self, m):
        self.off = m

    def alloc(self, parts, shape):
        n = int(np.prod(shape))
        assert self.off + n <= self.size, ("arena overflow", self.off, n, self.size)
        ap = self.t[0:parts, self.off:self.off + n]
        self.off += n
        if len(shape) == 2:
            ap = ap.rearrange("p (a b) -> p a b", a=shape[0])
        elif len(shape) == 3:
            ap = ap.rearrange("p (a b c) -> p a b c", a=shape[0], b=shape[1])
        return newV(ap)


def _slab_in(w_cols):
    n = w_cols.shape[1]
    a = np.zeros((8, 128, 512), np.float32)
    a[:, :, :n] = w_cols.reshape(8, 128, n)
    return np.ascontiguousarray(a.transpose(1, 0, 2)).reshape(128, SLAB)


def _slab_down(w_cols):
    a = np.zeros((128, SLAB), np.float32)
    a[:, :NFF * 128] = w_cols.reshape(NFF, 128, 128).transpose(1, 0, 2).reshape(128, NFF * 128)
    return a


def _fm(vec):
    return np.ascontiguousarray(vec.reshape(-1, 128).T)


class ParamPack:
    def __init__(self):
        self.cols = []
        self.off = {}
        self.n = 0

    def add(self, name, arr):
        arr = np.asarray(arr, np.float32)
        assert arr.shape[0] == 128
        arr = arr.reshape(128, -1)
        self.off[name] = (self.n, arr.shape[1])
        self.cols.append(arr)
        self.n += arr.shape[1]

    def array(self):
        return np.ascontiguousarray(np.concatenate(self.cols, axis=1))


def layer_slab_names(l):
    names = []
    if l % 2 == 0:
        names += ["qk", "v", "gate", "glua", "glub", "out0", "out1"]
    else:
        names += ["xl", "q", "k", "v", "gc", "z", "out0", "out1"]
    names += ["up%d" % s for s in range(11)]
    names += ["dn%d" % n for n in range(8)]
    return names


def host_weights(inp, depth):
    slabs = []
    pp = ParamPack()
    small = {}
    for l in range(depth):
        if l % 2 == 0:
            e = l // 2
            w = inp["we_in"][e]
            slabs += [_slab_in(w[:, 0:512]), _slab_in(w[:, 512:1024]), _slab_in(w[:, 1024:1536]),
                      _slab_in(w[:, 1552:2064]), _slab_in(w[:, 2064:2576])]
            wo = inp["we_out"][e]
            slabs += [_slab_in(wo[:, 0:512]), _slab_in(wo[:, 512:1024])]
            small["wlrin%d" % l] = np.ascontiguousarray(
                w[:, 1536:1552].reshape(8, 128, 16).transpose(1, 0, 2)).reshape(128, 128)
            small["wlraug%d" % l] = np.ascontiguousarray(
                np.concatenate([inp["we_lr"][e], inp["be_lr"][e][None, :]], axis=0))
            pp.add("g_gla%d" % l, _fm(inp["ge_gla"][e]))
            pp.add("w_dw%d" % l, inp["we_dw"][e].T.reshape(4, 128, W_B).transpose(1, 0, 2))
            pp.add("b_dw%d" % l, _fm(inp["be_dw"][e]))
            pp.add("g_cn%d" % l, _fm(inp["ge_cn"][e]))
            pp.add("b_cn%d" % l, _fm(inp["be_cn"][e]))
        else:
            o = l // 2
            w = inp["wo_in"][o]
            slabs += [_slab_in(w[:, 0:512]), _slab_in(w[:, 512:1024]), _slab_in(w[:, 1024:1536]),
                      _slab_in(w[:, 1536:2048]), _slab_in(w[:, 2048:2560]), _slab_in(w[:, 2560:3072])]
            wo = inp["wo_out"][o]
            slabs += [_slab_in(wo[:, 0:512]), _slab_in(wo[:, 512:1024])]
            small["wbain%d" % l] = np.ascontiguousarray(
                w[:, 3072:3080].reshape(8, 128, 8).transpose(1, 0, 2)).reshape(128, 64)
            for nm, key in (("wrg", "wo_rg"), ("wig", "wo_ig")):
                g = inp[key][o]
                bd = np.zeros((4, 128, 128), np.float32)
                for hh in range(8):
                    c, r = hh // 2, (hh % 2) * 64
                    bd[c, r:r + 64, r:r + 64] = g[hh]
                small["%s%d" % (nm, l)] = np.ascontiguousarray(bd.transpose(1, 0, 2)).reshape(128, 512)
            pp.add("w_cv%d" % l, inp["wo_conv"][o].T.reshape(16, 128, W_S).transpose(1, 0, 2))
            pp.add("b_cv%d" % l, _fm(inp["bo_conv"][o]))
            pp.add("b_rg%d" % l, _fm(inp["bo_rg"][o]))
            pp.add("b_ig%d" % l, _fm(inp["bo_ig"][o]))
            pp.add("lam%d" % l, _fm(inp["lam_lru"][o]))
            col = np.zeros((128, 2), np.float32)
            col[0:H_D, 0] = inp["a_log"][o]
            col[0:H_D, 1] = inp["dt_bias"][o]
            pp.add("hd%d" % l, col)
            pp.add("g_dl%d" % l, _fm(inp["go_delta"][o]))
        wu = inp["w_up"][l]
        for s in range(11):
            cols = np.zeros((1024, 512), np.float32)
            for jj in range(2):
                j = 2 * s + jj
                if j < NFF:
                    cols[:, jj * 128:(jj + 1) * 128] = wu[:, j * 128:(j + 1) * 128]
                    cols[:, (2 + jj) * 128:(3 + jj) * 128] = wu[:, D_FF + j * 128:D_FF + (j + 1) * 128]
            slabs.append(_slab_in(cols))
        wd = inp["w_down"][l]
        for n in range(8):
            slabs.append(_slab_down(wd[:, n * 128:(n + 1) * 128]))
        pp.add("w_fdw%d" % l, inp["w_fdw"][l].T.reshape(NFF, 128, W_F).transpose(1, 0, 2))
        pp.add("b_fdw%d" % l, _fm(inp["b_fdw"][l]))
        for nm in ("ln1_g", "ln1_b", "ln2_g", "ln2_b"):
            pp.add("%s%d" % (nm, l), _fm(inp[nm][l]))
    return np.stack(slabs, 0), pp, small


NCONST = 128 + 256 * 4 + 512 * 2 + 512


def host_consts():
    ident = np.eye(128, dtype=np.float32)
    s = np.arange(64)
    U = (s[:, None] <= s[None, :]).astype(np.float32)
    Us = (s[:, None] < s[None, :]).astype(np.float32)
    Ls = (s[:, None] > s[None, :]).astype(np.float32)
    c = np.zeros((128, NCONST), np.float32)
    c[:, 0:128] = ident
    c[0:64, 128:384] = np.tile(U, (1, 4))
    c[0:64, 384:640] = np.tile(Us, (1, 4))
    c[0:64, 640:896] = np.tile(Ls, (1, 4))
    c[0:64, 896:1152] = np.tile(np.eye(64, dtype=np.float32), (1, 4))
    cm = np.ones(512, np.float32)
    cm[::64] = 0.0
    c[0:4, 1152:1664] = cm[None, :]
    for h in range(4):
        c[h, 1664 + h * 128:1664 + (h + 1) * 128] = 1.0
    c[0:64, 2176:2432] = np.tile((U - 1.0) * 30000.0, (1, 4))
    c[0:64, 2432:2688] = np.tile((U.T - 1.0) * 30000.0, (1, 4))
    return c


class Builder:
    def __init__(self, cfg, pp_off, npf, nslab_total, small_shapes):
        self.cfg = cfg
        self.depth = cfg["depth"]
        self.T = cfg["T"]
        self.S = cfg["seq"]
        self.has_sample = cfg["sample"]
        self.pp_off = pp_off
        nc = self.nc = bass.Bass("TRN2", target_bir_lowering=False)
        self.P = Prog(nc)
        T = self.T
        d = self.dram = {}
        depth = self.depth

        def din(name, shape):
            d[name] = nc.dram_tensor(name, list(shape), F32, kind="ExternalInput").ap()

        def dout(name, shape):
            d[name] = nc.dram_tensor(name, list(shape), F32, kind="ExternalOutput").ap()
        self.outs = []
        din("wslabs", (nslab_total, 128, SLAB))
        self.wbf = nc.dram_tensor("wbf", [nslab_total, 128, SLAB], BF16, kind="Internal").ap()
        self.wbf_v = [newV(self.wbf[g]) for g in range(nslab_total)]
        din("pf", (128, npf))
        din("consts", (128, NCONST))
        for k, shp in small_shapes.items():
            din(k, shp)
        if self.S:
            din("xp", (D_MODEL, self.S))
            dout("yp", (D_MODEL, self.S))
        if self.has_sample:
            din("xs", (D_MODEL, 64))
            dout("ys", (D_MODEL, 64))
        for grp in (["p"] if self.S else []) + (["s"] if self.has_sample else []):
            for l in range(depth):
                if l % 2 == 0:
                    dout("%s%d_gla" % (grp, l), (H_A, DK_A, DV_A))
                    dout("%s%d_dw" % (grp, l), (128, 4, W_B - 1))
                else:
                    dout("%s%d_lru" % (grp, l), (128, 4))
                    dout("%s%d_delta" % (grp, l), (H_D, DK_D, DV_D))
                    dout("%s%d_conv" % (grp, l), (128, 16, W_S - 1))
                dout("%s%d_ffn" % (grp, l), (128, NFF, W_F - 1))
        if self.has_sample:
            for l in range(depth):
                if l % 2 == 0:
                    din("i%d_gla" % l, (H_A, DK_A, DV_A))
                    din("i%d_dw" % l, (128, 4, W_B - 1))
                else:
                    din("i%d_lru" % l, (128, 4))
                    din("i%d_delta" % l, (H_D, DK_D, DV_D))
                    din("i%d_conv" % l, (128, 16, W_S - 1))
                din("i%d_ffn" % l, (128, NFF, W_F - 1))
        if cfg.get('dbg'):
            dout('dbg', (128, 8, self.T))
        self.small_shapes = small_shapes
        self.npf = npf
        self.nslab_total = nslab_total

    def mm(self, out, lhsT, rhs, start, stop):
        self.P.op("pe", lambda h, o=out.ap, a=lhsT.ap, b=rhs.ap, s=start, e=stop: h.matmul(o, a, b, start=s, stop=e),
                  [lhsT, rhs], [out], inc=True)

    def act(self, out, in_, func, scale=1.0, bias=0.0, extra=()):
        sc = scale.ap if isinstance(scale, V) else scale
        bi = bias.ap if isinstance(bias, V) else bias
        rd = [in_] + [x for x in (scale, bias) if isinstance(x, V)] + list(extra)
        self.P.op("act", lambda h, o=out.ap, i=in_.ap, f=func, s=sc, b=bi: h.activation(out=o, in_=i, func=f, bias=b, scale=s),
                  rd, [out])

    def tt(self, out, in0, in1, op, eng="dve"):
        self.P.op(eng, lambda h, o=out.ap, a=in0.ap, b=in1.ap, p=op: h.tensor_tensor(out=o, in0=a, in1=b, op=p),
                  [in0, in1], [out])

    def ts(self, out, in0, s1, s2, op0, op1=None, eng="dve"):
        a1 = s1.ap if isinstance(s1, V) else s1
        a2 = s2.ap if isinstance(s2, V) else s2
        rd = [in0] + [x for x in (s1, s2) if isinstance(x, V)]
        if op1 is None:
            self.P.op(eng, lambda h, o=out.ap, a=in0.ap, x=a1, p=op0: h.tensor_scalar(out=o, in0=a, scalar1=x, scalar2=None, op0=p),
                      rd, [out])
        else:
            self.P.op(eng, lambda h, o=out.ap, a=in0.ap, x=a1, y=a2, p=op0, q=op1: h.tensor_scalar(out=o, in0=a, scalar1=x, scalar2=y, op0=p, op1=q),
                      rd, [out])

    def stt(self, out, in0, sc, in1, op0, op1, eng="dve"):
        a1 = sc.ap if isinstance(sc, V) else sc
        rd = [in0, in1] + ([sc] if isinstance(sc, V) else [])
        self.P.op(eng, lambda h, o=out.ap, a=in0.ap, x=a1, b=in1.ap, p=op0, q=op1: h.scalar_tensor_tensor(out=o, in0=a, scalar=x, in1=b, op0=p, op1=q),
                  rd, [out])

    def cp(self, out, in_, eng="dve"):
        self.P.op(eng, lambda h, o=out.ap, i=in_.ap: h.tensor_copy(out=o, in_=i), [in_], [out])

    def recip(self, out, in_):
        self.P.op("dve", lambda h, o=out.ap, i=in_.ap: h.reciprocal(out=o, in_=i), [in_], [out])

    def memset(self, out, val, eng="dve"):
        self.P.op(eng, lambda h, o=out.ap, v=val: h.memset(o, v), [], [out])

    def scan(self, out, d0, d1, init, op0, op1):
        ia = init.ap if isinstance(init, V) else init
        rd = [d0, d1] + ([init] if isinstance(init, V) else [])
        self.P.op("dve", lambda h, o=out.ap, a=d0.ap, b=d1.ap, i=ia, p=op0, q=op1: h.tensor_tensor_scan(out=o, data0=a, data1=b, initial=i, op0=p, op1=q),
                  rd, [out])

    def dma(self, eng, out, in_, reads=(), writes=(), is_output=False, **kw):
        oa = out.ap if isinstance(out, V) else out
        ia = in_.ap if isinstance(in_, V) else in_
        rd = list(reads) + ([in_] if isinstance(in_, V) else [])
        wr = list(writes) + ([out] if isinstance(out, V) else [])
        self.P.dma(eng, oa, ia, rd, wr, is_output=is_output, **kw)

    def ck(self, lvl):
        if self.cfg.get('cut', 99) == lvl:
            raise Cut()

    def bank(self):
        b = self.banks[self.bank_rr]
        self.bank_rr = (self.bank_rr + 1) % 8
        return b

    def pfv(self, name, l):
        off, n = self.pp_off["%s%d" % (name, l)]
        return self.pf[:, off:off + n]

    def plan_slabs(self, ntile_calls):
        order = []
        base = 0
        self.layer_base = []
        for l in range(self.depth):
            self.layer_base.append(base)
            base += len(layer_slab_names(l))
        for _ in range(ntile_calls):
            for l in range(self.depth):
                for i, nm in enumerate(layer_slab_names(l)):
                    order.append((l, nm, self.layer_base[l] + i))
        self.slab_order = order
        self.slab_issued = 0
        self.slab_next = 0

    def slab(self, l, name):
        k = self.slab_next
        ol, onm, _ = self.slab_order[k]
        assert (ol, onm) == (l, name), ((ol, onm), (l, name))
        lim = min(len(self.slab_order), k + NSLOT)
        while self.slab_issued < lim:
            n = self.slab_issued
            _, _, gi = self.slab_order[n]
            self.dma("sp", self.slots[n % NSLOT], self.wbf_v[gi])
            self.slab_issued += 1
        self.slab_next += 1
        return self.slots[k % NSLOT]

    def layernorm(self, X, l, which, T):
        g = self.pfv(which + "_g", l)
        bb = self.pfv(which + "_b", l)
        A, Ab = self.af, self.ab
        assert 2 * T <= 512
        pss = self.bank()
        psm, psq = pss[:, 0:T], pss[:, T:2 * T]
        zzs = [Ab.alloc(128, [2, T]) for _ in range(2)]
        for n in range(8):
            zz = zzs[n % 2]
            self.act(zz[:, 0, :], X[n], AF.Copy)
            self.act(zz[:, 1, :], X[n], AF.Square)
            self.mm(pss[:, 0:2 * T], self.ones_b, zz.re("p a b -> p (a b)"), n == 0, n == 7)
        mu = A.alloc(128, [T])
        msq = A.alloc(128, [T])
        var = A.alloc(128, [T])
        rs = A.alloc(128, [T])
        self.act(mu, psm, AF.Copy, scale=1.0 / D_MODEL)
        self.tt(msq, mu, mu, ALU.mult)
        self.stt(var, psq, 1.0 / D_MODEL, msq, ALU.mult, ALU.subtract)
        self.act(var, var, AF.Sqrt, bias=self.eps_c)
        self.recip(rs, var)
        t1 = [A.alloc(128, [T]) for _ in range(2)]
        for n in range(8):
            t = t1[n % 2]
            self.tt(t, X[n], mu, ALU.subtract)
            self.tt(t, t, rs, ALU.mult)
            self.act(X[n], t, AF.Identity, scale=g[:, n:n + 1], bias=bb[:, n:n + 1])
            self.cp(self.xb[n][:, 0:T], X[n])

    def ffn(self, X, l, T, st):
        A, Ab = self.af, self.ab
        wf = self.pfv("w_fdw", l)
        bf = self.pfv("b_fdw", l)
        ghal = st["ghal"][l]
        hbuf = [Ab.alloc(128, [T]) for _ in range(NFF)]
        gb = [A.alloc(128, [T + 2]) for _ in range(2)]
        acc = [A.alloc(128, [T]) for _ in range(2)]
        ge = [A.alloc(128, [T]) for _ in range(2)]
        for s in range(11):
            slot = self.slab(l, "up%d" % s).re("p (k n) -> p k n", k=8)
            for jj in range(2):
                j = 2 * s + jj
                if j >= NFF:
                    continue
                psg, psu = self.bank(), self.bank()
                for k in range(8):
                    self.mm(psg[:, 0:T], slot[:, k, jj * 128:(jj + 1) * 128], self.xb[k][:, 0:T], k == 0, k == 7)
                for k in range(8):
                    self.mm(psu[:, 0:T], slot[:, k, (2 + jj) * 128:(3 + jj) * 128], self.xb[k][:, 0:T], k == 0, k == 7)
                g_, a_, e_ = gb[j % 2], acc[j % 2], ge[j % 2]
                self.cp(g_[:, 0:2], ghal[:, j, :], eng="dve")
                self.act(g_[:, 2:T + 2], psg[:, 0:T], AF.Copy)
                self.cp(ghal[:, j, :], g_[:, T:T + 2], eng="dve")
                self.ts(a_, g_[:, 0:T], wf[:, 3 * j:3 * j + 1], bf[:, j:j + 1], ALU.mult, ALU.add)
                self.stt(a_, g_[:, 1:T + 1], wf[:, 3 * j + 1:3 * j + 2], a_, ALU.mult, ALU.add)
                self.stt(a_, g_[:, 2:T + 2], wf[:, 3 * j + 2:3 * j + 3], a_, ALU.mult, ALU.add)
                self.act(e_, a_, AF.Gelu_apprx_tanh)
                self.tt(hbuf[j], e_, psu[:, 0:T], ALU.mult)
        for n in range(8):
            slot = self.slab(l, "dn%d" % n)
            ps = self.bank()
            for j in range(NFF):
                self.mm(ps[:, 0:T], slot[:, j * 128:(j + 1) * 128], hbuf[j], j == 0, j == NFF - 1)
            self.stt(X[n], X[n], ALPHA, ps[:, 0:T], ALU.mult, ALU.add)

    def even_mixer(self, X, l, T, st):
        A, Ab = self.af, self.ab
        NCH = T // 64
        xb = self.xb
        S_f, S_b, uhal = st["S_f"][l], st["S_b"][l], st["uhal"][l]
        slot = self.slab(l, "qk").re("p (k n) -> p k n", k=8)
        qT = A.alloc(64, [4, T])
        kT = A.alloc(64, [4, T])
        for i in range(8):
            ps = self.bank()
            for k in range(8):
                self.mm(ps[0:64, 0:T], slot[:, k, i * 64:(i + 1) * 64], xb[k][:, 0:T], k == 0, k == 7)
            if i < 4:
                self.act(qT[:, i, :], ps[0:64, 0:T], AF.Copy, scale=DK_A ** -0.5)
            else:
                self.cp(kT[:, i - 4, :], ps[0:64, 0:T])
        self.ck(1)
        lrT = Ab.alloc(17, [T])
        self.memset(lrT, 1.0)
        ps = self.bank()
        wl = self.wlrin[l].re("p (k n) -> p k n", k=8)
        for k in range(8):
            self.mm(ps[0:16, 0:T], wl[:, k, :], xb[k][:, 0:T], k == 0, k == 7)
        self.cp(lrT[0:16, :], ps[0:16, 0:T])
        self.ck(2)
        slot = self.slab(l, "v").re("p (k n) -> p k n", k=8)
        vtok = [Ab.alloc(64, [512]) for _ in range(NCH)]
        sp_tok = [A.alloc(64, [256]) for _ in range(2)]
        e1 = [A.alloc(64, [256]) for _ in range(2)]
        oT = A.alloc(128, [4, T])
        ep = [A.alloc(64, [4, 64]) for _ in range(2)]
        en = [A.alloc(64, [4, 64]) for _ in range(2)]
        qd = [Ab.alloc(64, [4, 64]) for _ in range(2)]
        kd = [Ab.alloc(64, [4, 64]) for _ in range(2)]
        kk = [Ab.alloc(64, [4, 64]) for _ in range(2)]
        scm = [Ab.alloc(64, [4, 64]) for _ in range(2)]
        kkt = [Ab.alloc(64, [256]) for _ in range(2)]
        for c in range(NCH):
            cs = slice(c * 64, (c + 1) * 64)
            r = c % 2
            ps = self.bank()
            for k in range(8):
                self.mm(ps[0:64, 0:512], xb[k][:, cs], slot[:, k, :], k == 0, k == 7)
            self.act(vtok[c], ps[0:64, 0:512], AF.Copy)
            self.ck(3)
            ps2 = self.bank()
            self.mm(ps2[0:64, 0:256], lrT[0:17, cs], self.wlraug[l], True, True)
            self.act(e1[r], ps2[0:64, 0:256], AF.Exp, scale=-1.0)
            self.act(sp_tok[r], e1[r], AF.Ln, bias=1.0)
            self.ck(4)
            ps3 = self.bank()
            for h in range(4):
                self.mm(ps3[0:64, h * 64:(h + 1) * 64], sp_tok[r][:, h * 64:(h + 1) * 64], self.U_f, True, True)
            p3 = ps3[0:64, 0:256].re("p (a b) -> p a b", a=4)
            self.act(ep[r], p3, AF.Exp, scale=-1.0 / 16.0)
            self.act(en[r], p3, AF.Exp, scale=1.0 / 16.0)
            self.ck(5)
            self.tt(qd[r], qT[:, :, cs], ep[r], ALU.mult)
            self.tt(kd[r], kT[:, :, cs], en[r], ALU.mult)
            for h in range(4):
                self.ts(kk[r][:, h, :], kd[r][:, h, :], ep[r][:, h, 63:64], None, ALU.mult)
            self.ck(6)
            ps4 = self.bank()
            for h in range(4):
                self.mm(ps4[0:64, h * 64:(h + 1) * 64], kd[r][:, h, :], qd[r][:, h, :], True, True)
            self.tt(scm[r], ps4[0:64, 0:256].re("p (a b) -> p a b", a=4), self.mask4.re("p (a b) -> p a b", a=4), ALU.mult)
            ps5 = self.bank()
            for h in range(4):
                self.mm(ps5[0:64, h * 64:(h + 1) * 64], kk[r][:, h, :], self.ident_b[0:64, 0:64], True, True)
            self.act(kkt[r], ps5[0:64, 0:256], AF.Copy)
            self.ck(7)
            ps6 = self.bank()
            for h in range(4):
                self.mm(ps6[:, h * 64:(h + 1) * 64], S_b[:, h, :], qd[r][:, h, :], True, False)
                self.mm(ps6[:, h * 64:(h + 1) * 64], vtok[c][:, h * 128:(h + 1) * 128], scm[r][:, h, :], False, True)
            self.act(oT[:, :, cs], ps6[:, 0:256].re("p (a b) -> p a b", a=4), AF.Copy)
            ps7 = self.bank()
            for h in range(4):
                self.mm(ps7[0:64, h * 128:(h + 1) * 128], kkt[r][:, h * 64:(h + 1) * 64], vtok[c][:, h * 128:(h + 1) * 128], True, True)
            for h in range(4):
                self.stt(S_f[:, h, :], S_f[:, h, :], ep[r][:, h, 63:64], ps7[0:64, h * 128:(h + 1) * 128], ALU.mult, ALU.add)
            self.act(S_b, S_f, AF.Copy)
            self.ck(8)
        slot = self.slab(l, "gate").re("p (k n) -> p k n", k=8)
        sg = A.alloc(128, [4, T])
        for h in range(4):
            ps = self.bank()
            for k in range(8):
                self.mm(ps[:, 0:T], slot[:, k, h * 128:(h + 1) * 128], xb[k][:, 0:T], k == 0, k == 7)
            self.act(sg[:, h, :], ps[:, 0:T], AF.Silu)
        gg = self.pfv("g_gla", l)
        R = [Ab.alloc(128, [T]) for _ in range(8)]
        sq = [Ab.alloc(128, [T]) for _ in range(2)]
        sd = [A.alloc(128, [T]) for _ in range(2)]
        for h in range(4):
            self.act(sq[h % 2], oT[:, h, :], AF.Square)
            ps = self.bank()
            self.mm(ps[:, 0:T], self.ones_b, sq[h % 2], True, True)
            self.act(sd[h % 2], ps[:, 0:T], AF.Sqrt, scale=1.0 / DV_A, bias=self.eps_c)
            self.recip(sd[h % 2], sd[h % 2])
            self.stt(sd[h % 2], oT[:, h, :], gg[:, h:h + 1], sd[h % 2], ALU.mult, ALU.mult)
            self.tt(R[h], sd[h % 2], sg[:, h, :], ALU.mult)
        self.ck(9)
        up = A.alloc(128, [4, T + 30])
        self.cp(up[:, :, 0:30], uhal)
        self.ck(91)
        slot = self.slab(l, "glua").re("p (k n) -> p k n", k=8)
        for j in range(4):
            ps = self.bank()
            for k in range(8):
                self.mm(ps[:, 0:T], slot[:, k, j * 128:(j + 1) * 128], xb[k][:, 0:T], k == 0, k == 7)
            self.act(up[:, j, 30:30 + T], ps[:, 0:T], AF.Copy)
        slot = self.slab(l, "glub").re("p (k n) -> p k n", k=8)
        sgb = [A.alloc(128, [T]) for _ in range(2)]
        for j in range(4):
            ps = self.bank()
            for k in range(8):
                self.mm(ps[:, 0:T], slot[:, k, j * 128:(j + 1) * 128], xb[k][:, 0:T], k == 0, k == 7)
            self.act(sgb[j % 2], ps[:, 0:T], AF.Sigmoid)
            self.tt(up[:, j, 30:30 + T], up[:, j, 30:30 + T], sgb[j % 2], ALU.mult)
        self.cp(uhal, up[:, :, T:T + 30])
        self.ck(92)
        wdw = self.pfv("w_dw", l)
        bdw = self.pfv("b_dw", l)
        cv = [A.alloc(128, [T]) for _ in range(4)]
        pss = self.bank()
        psm, psq = pss[:, 0:T], pss[:, T:2 * T]
        cvz = [Ab.alloc(128, [2, T]) for _ in range(2)]
        for j in range(4):
            self.ts(cv[j], up[:, j, 0:T], wdw[:, j * 31:j * 31 + 1], bdw[:, j:j + 1], ALU.mult, ALU.add)
            self.ck(93)
            for tap in range(1, W_B):
                self.stt(cv[j], up[:, j, tap:tap + T], wdw[:, j * 31 + tap:j * 31 + tap + 1], cv[j], ALU.mult, ALU.add)
                self.ck(94)
            self.ck(945)
            self.act(cvz[j % 2][:, 0, :], cv[j], AF.Copy)
            self.act(cvz[j % 2][:, 1, :], cv[j], AF.Square)
            self.mm(pss[:, 0:2 * T], self.ones_b, cvz[j % 2].re("p a b -> p (a b)"), j == 0, j == 3)
        self.ck(95)
        mu = A.alloc(128, [T])
        msq = A.alloc(128, [T])
        var = A.alloc(128, [T])
        self.act(mu, psm, AF.Copy, scale=1.0 / D_B)
        self.tt(msq, mu, mu, ALU.mult)
        self.stt(var, psq, 1.0 / D_B, msq, ALU.mult, ALU.subtract)
        self.act(var, var, AF.Sqrt, bias=self.eps_c)
        self.recip(var, var)
        gcn = self.pfv("g_cn", l)
        self.ck(96)
        bcn = self.pfv("b_cn", l)
        for j in range(4):
            self.tt(cv[j], cv[j], mu, ALU.subtract)
            self.tt(cv[j], cv[j], var, ALU.mult)
            self.act(R[4 + j], cv[j], AF.Silu, scale=gcn[:, j:j + 1], bias=bcn[:, j:j + 1])
        self.ck(10)
        self.out_proj(X, l, T, R)

    def out_proj(self, X, l, T, R):
        for s in range(2):
            slot = self.slab(l, "out%d" % s).re("p (k n) -> p k n", k=8)
            for jj in range(4):
                n = s * 4 + jj
                ps = self.bank()
                for k in range(8):
                    self.mm(ps[:, 0:T], slot[:, k, jj * 128:(jj + 1) * 128], R[k], k == 0, k == 7)
                self.stt(X[n], X[n], ALPHA, ps[:, 0:T], ALU.mult, ALU.add)

    def log1p_series(self, A, e, parts, shape):
        den = A.alloc(parts, shape)
        s_ = A.alloc(parts, shape)
        s2 = A.alloc(parts, shape)
        p = A.alloc(parts, shape)
        self.ts(den, e, 2.0, None, ALU.add)
        self.recip(den, den)
        self.tt(s_, e, den, ALU.mult)
        self.tt(s2, s_, s_, ALU.mult)
        self.ts(p, s2, 1.0 / 11.0, 1.0 / 9.0, ALU.mult, ALU.add)
        for cst in (1.0 / 7.0, 1.0 / 5.0, 1.0 / 3.0, 1.0):
            self.tt(p, p, s2, ALU.mult)
            self.ts(p, p, cst, None, ALU.add)
        self.tt(p, p, s_, ALU.mult)
        return p

    def odd_prologue(self, l, sbf):
        A = self.af
        lam = self.pfv("lam", l)
        e = A.alloc(128, [4])
        self.act(e, lam, AF.Exp, scale=-1.0)
        p = self.log1p_series(A, e, 128, [4])
        L4 = newV(sbf("L4_%d" % l, [128, 4], F32)[:, :])
        self.ts(L4, p, -8.0, None, ALU.mult)
        self.L4[l] = L4
        hd = self.pfv("hd", l)
        negA = newV(sbf("negA_%d" % l, [4, 1], F32)[:, :])
        self.act(negA, hd[0:4, 0:1], AF.Exp)
        self.ts(negA, negA, -1.0, None, ALU.mult)
        self.negA[l] = negA

    def rms_gate(self, oT, gname, slabname, l, T, R, base, dv):
        A, Ab = self.af, self.ab
        slot = self.slab(l, slabname).re("p (k n) -> p k n", k=8)
        sg = A.alloc(128, [4, T])
        for h in range(4):
            ps = self.bank()
            for k in range(8):
                self.mm(ps[:, 0:T], slot[:, k, h * 128:(h + 1) * 128], self.xb[k][:, 0:T], k == 0, k == 7)
            self.act(sg[:, h, :], ps[:, 0:T], AF.Silu)
        gg = self.pfv(gname, l)
        sq = [Ab.alloc(128, [T]) for _ in range(2)]
        sd = [A.alloc(128, [T]) for _ in range(2)]
        for h in range(4):
            self.act(sq[h % 2], oT[:, h, :], AF.Square)
            ps = self.bank()
            self.mm(ps[:, 0:T], self.ones_b, sq[h % 2], True, True)
            self.act(sd[h % 2], ps[:, 0:T], AF.Sqrt, scale=1.0 / dv, bias=self.eps_c)
            self.recip(sd[h % 2], sd[h % 2])
            self.stt(sd[h % 2], oT[:, h, :], gg[:, h:h + 1], sd[h % 2], ALU.mult, ALU.mult)
            self.tt(R[base + h], sd[h % 2], sg[:, h, :], ALU.mult)

    def odd_mixer(self, X, l, T, st):
        A, Ab = self.af, self.ab
        NCH = T // 64
        xb = self.xb
        chal, hl, Sd = st["chal"][l], st["h_lru"][l], st["Sd_f"][l]
        wcv = self.pfv("w_cv", l)
        bcv = self.pfv("b_cv", l)
        R = [Ab.alloc(128, [T]) for _ in range(8)]
        cvo = {nm: A.alloc(128, [4, T]) for nm in ("xl", "q", "k", "v")}
        KA = A.alloc(128, [4, T])
        KBN = A.alloc(128, [4, T])
        QA = A.alloc(128, [4, T])
        oT = A.alloc(128, [4, T])
        eG = A.alloc(128, [4, NCH])
        Gb = A.alloc(64, [4, T])
        Gbn = A.alloc(64, [4, T])
        gcol = A.alloc(64, [NCH, 4])
        m0 = A.mark()
        cin = [A.alloc(128, [4, T + 3]) for _ in range(2)]
        for si, nm in enumerate(("xl", "q", "k", "v")):
            slot = self.slab(l, nm).re("p (k n) -> p k n", k=8)
            ci = cin[si % 2]
            self.cp(ci[:, :, 0:3], chal[:, si * 4:(si + 1) * 4, :])
            for j in range(4):
                ps = self.bank()
                for k in range(8):
                    self.mm(ps[:, 0:T], slot[:, k, j * 128:(j + 1) * 128], xb[k][:, 0:T], k == 0, k == 7)
                self.act(ci[:, j, 3:3 + T], ps[:, 0:T], AF.Copy)
            self.cp(chal[:, si * 4:(si + 1) * 4, :], ci[:, :, T:T + 3])
            dst = cvo[nm]
            for j in range(4):
                jj = si * 4 + j
                self.ts(dst[:, j, :], ci[:, j, 0:T], wcv[:, jj * 4:jj * 4 + 1], bcv[:, jj:jj + 1], ALU.mult, ALU.add)
                for tap in range(1, W_S):
                    self.stt(dst[:, j, :], ci[:, j, tap:tap + T], wcv[:, jj * 4 + tap:jj * 4 + tap + 1], dst[:, j, :], ALU.mult, ALU.add)
                if si > 0:
                    self.act(dst[:, j, :], dst[:, j, :], AF.Silu)
        self.ck(21)
        xc = cvo["xl"]
        xcb = Ab.alloc(128, [4, T])
        self.cp(xcb, xc)
        L4 = self.L4[l]
        wrg = self.wrg[l].re("p (c n) -> p c n", c=4)
        wig = self.wig[l].re("p (c n) -> p c n", c=4)
        brg = self.pfv("b_rg", l)
        big = self.pfv("b_ig", l)
        slot = self.slab(l, "gc").re("p (k n) -> p k n", k=8)
        tb = [[A.alloc(128, [T]) for _ in range(2)] for _ in range(7)]
        for c in range(4):
            r_, ig_, t_, rd_, a_, om_, h_ = [tb[i][c % 2] for i in range(7)]
            ps = self.bank()
            self.mm(ps[:, 0:T], wrg[:, c, :], xcb[:, c, :], True, True)
            self.act(r_, ps[:, 0:T], AF.Sigmoid, bias=brg[:, c:c + 1])
            ps = self.bank()
            self.mm(ps[:, 0:T], wig[:, c, :], xcb[:, c, :], True, True)
            self.act(ig_, ps[:, 0:T], AF.Sigmoid, bias=big[:, c:c + 1])
            self.act(t_, r_, AF.Tanh, scale=L4[:, c:c + 1])
            self.ts(rd_, t_, -1.0, 1.0, ALU.mult, ALU.add)
            self.recip(rd_, rd_)
            self.stt(a_, t_, 1.0, rd_, ALU.add, ALU.mult)
            self.stt(om_, t_, -4.0, rd_, ALU.mult, ALU.mult)
            self.tt(om_, om_, rd_, ALU.mult)
            self.act(om_, om_, AF.Sqrt)
            self.tt(om_, om_, ig_, ALU.mult)
            self.tt(om_, om_, xc[:, c, :], ALU.mult)
            self.scan(h_, a_, om_, hl[:, c:c + 1], ALU.mult, ALU.add)
            self.cp(hl[:, c:c + 1], h_[:, T - 1:T])
            ps = self.bank()
            for k in range(8):
                self.mm(ps[:, 0:T], slot[:, k, c * 128:(c + 1) * 128], xb[k][:, 0:T], k == 0, k == 7)
            self.act(r_, ps[:, 0:T], AF.Gelu_apprx_tanh)
            self.tt(R[c], h_, r_, ALU.mult)
        self.ck(22)
        self.P.fence()
        A.release(m0)
        qs, ks, vs = cvo["q"], cvo["k"], cvo["v"]
        wba = self.wbain[l].re("p (k n) -> p k n", k=8)
        psb, psa = self.bank(), self.bank()
        for k in range(8):
            self.mm(psb[0:4, 0:T], wba[:, k, 0:4], xb[k][:, 0:T], k == 0, k == 7)
        for k in range(8):
            self.mm(psa[0:4, 0:T], wba[:, k, 4:8], xb[k][:, 0:T], k == 0, k == 7)
        ROWS = A.alloc(4, [4, T])
        hd = self.pfv("hd", l)
        self.act(ROWS[:, 3, :], psb[0:4, 0:T], AF.Sigmoid)
        y = A.alloc(4, [T])
        ay = A.alloc(4, [T])
        self.ts(y, psa[0:4, 0:T], hd[0:4, 1:2], None, ALU.add)
        self.act(ay, y, AF.Abs)
        self.act(ay, ay, AF.Exp, scale=-1.0)
        p = self.log1p_series(A, ay, 4, [T])
        self.ts(y, y, 0.0, None, ALU.max)
        self.stt(y, p, 2.0, y, ALU.mult, ALU.add)
        self.ts(y, y, self.negA[l][0:4, 0:1], None, ALU.mult)
        gc = ROWS[:, 1, :]
        self.scan(gc, self.cmask[:, 0:T], y, 0.0, ALU.mult, ALU.add)
        self.act(ROWS[:, 0, :], gc, AF.Exp)
        self.tt(ROWS[:, 2, :], ROWS[:, 3, :], ROWS[:, 0, :], ALU.mult)
        self.ck(23)
        for c in range(NCH):
            psT = self.bank()
            self.mm(psT[0:64, 0:4], ROWS[:, 1, c * 64:(c + 1) * 64], self.ident_f[0:4, 0:4], True, True)
            self.cp(gcol[:, c, :], psT[0:64, 0:4])
        sqb = [Ab.alloc(128, [T]) for _ in range(2)]
        rn = [A.alloc(128, [T]) for _ in range(2)]
        for h in range(4):
            selh = self.sel[:, h * 128:(h + 1) * 128]
            for i, src in enumerate((qs, ks)):
                self.act(sqb[i], src[:, h, :], AF.Square)
                ps = self.bank()
                self.mm(ps[:, 0:T], self.ones_b, sqb[i], True, True)
                self.act(rn[i], ps[:, 0:T], AF.Sqrt, bias=self.eps6_c)
                self.recip(rn[i], rn[i])
            self.stt(qs[:, h, :], qs[:, h, :], DK_D ** -0.5, rn[0], ALU.mult, ALU.mult)
            self.tt(ks[:, h, :], ks[:, h, :], rn[1], ALU.mult)
            psE = self.bank()
            self.mm(psE[:, 0:T], selh, ROWS[:, 0, :], True, True)
            self.act(eG[:, h, :], psE[:, 0:T].re("p (c t) -> p c t", t=64)[:, :, 63], AF.Copy)
            self.tt(QA[:, h, :], qs[:, h, :], psE[:, 0:T], ALU.mult)
            psB = self.bank()
            self.mm(psB[:, 0:T], selh, ROWS[:, 2, :], True, True)
            self.tt(KA[:, h, :], ks[:, h, :], psB[:, 0:T], ALU.mult)
            psb2 = self.bank()
            self.mm(psb2[:, 0:T], selh, ROWS[:, 3, :], True, True)
            self.tt(vs[:, h, :], vs[:, h, :], psb2[:, 0:T], ALU.mult)
            self.tt(KBN[:, h, :], ks[:, h, :], psb2[:, 0:T], ALU.mult)
            psG = self.bank()
            self.mm(psG[:, 0:T], selh, ROWS[:, 1, :], True, True)
            self.act(Gb[:, h, :], psG[0:64, 0:T], AF.Copy)
            self.act(Gbn[:, h, :], psG[0:64, 0:T], AF.Copy, scale=-1.0)
        self.ck(24)
        self.P.fence()
        A.release(m0)
        NB = 16
        nb = [A.alloc(64, [256]) for _ in range(NB)]
        big_ = [A.alloc(64, [512]) for _ in range(5)]
        WT = A.alloc(128, [256])
        nbi = [0]
        BV, KN, QN = vs, ks, qs
        DT2 = [A.alloc(64, [256]) for _ in range(2)]
        AT2 = [A.alloc(64, [256]) for _ in range(2)]

        def nbuf():
            v = nb[nbi[0] % NB]
            nbi[0] += 1
            return v
        for c in range(NCH):
            cs = slice(c * 64, (c + 1) * 64)
            DT, D, DTs, Ds = DT2[c % 2], nbuf(), nbuf(), nbuf()
            for h in range(4):
                hc = slice(h * 64, (h + 1) * 64)
                self.stt(DT[:, hc], Gb[:, h, cs], gcol[:, c, h:h + 1], self.negU4[:, hc], ALU.subtract, ALU.add)
                self.stt(D[:, hc], Gbn[:, h, cs], gcol[:, c, h:h + 1], self.negL4[:, hc], ALU.add, ALU.add)
            self.act(DT, DT, AF.Exp)
            self.act(D, D, AF.Exp)
            self.tt(DTs, DT, self.ident4, ALU.subtract)
            self.tt(Ds, D, self.ident4, ALU.subtract)
            psNT, psN, psAT = self.bank(), self.bank(), self.bank()
            for h in range(4):
                hc = slice(h * 64, (h + 1) * 64)
                self.mm(psNT[0:64, hc], KN[:, h, cs], KBN[:, h, cs], True, True)
                self.mm(psN[0:64, hc], KBN[:, h, cs], KN[:, h, cs], True, True)
                self.mm(psAT[0:64, hc], KN[:, h, cs], QN[:, h, cs], True, True)
            NT, N, AT, PT = nbuf(), nbuf(), AT2[c % 2], nbuf()
            self.stt(NT, psNT[0:64, 0:256], -1.0, DTs, ALU.mult, ALU.mult)
            self.stt(N, psN[0:64, 0:256], -1.0, Ds, ALU.mult, ALU.mult)
            self.tt(AT, psAT[0:64, 0:256], DT, ALU.mult)
            self.tt(PT, NT, self.ident4, ALU.add)
            for lev in range(5):
                psN2 = self.bank()
                for h in range(4):
                    hc = slice(h * 64, (h + 1) * 64)
                    self.mm(psN2[0:64, hc], NT[:, hc], N[:, hc], True, True)
                N2 = nbuf()
                self.act(N2, psN2[0:64, 0:256], AF.Copy)
                if lev < 4:
                    psNT2 = self.bank()
                    for h in range(4):
                        hc = slice(h * 64, (h + 1) * 64)
                        self.mm(psNT2[0:64, hc], N[:, hc], NT[:, hc], True, True)
                    NT2 = nbuf()
                    self.cp(NT2, psNT2[0:64, 0:256])
                else:
                    NT2 = None
                psP = self.bank()
                for h in range(4):
                    hc = slice(h * 64, (h + 1) * 64)
                    self.mm(psP[0:64, hc], N2[:, hc], PT[:, hc], True, True)
                PT2 = nbuf()
                self.tt(PT2, PT, psP[0:64, 0:256], ALU.add)
                N, NT, PT = N2, NT2, PT2
            BVt, KAt, KBt, U_sb, VN = big_
            for src, dst, eng in ((BV, BVt, "act"), (KA, KAt, "dve")):
                psT = self.bank()
                for h in range(4):
                    self.mm(psT[0:64, h * 128:(h + 1) * 128], src[:, h, cs], self.ident_f, True, True)
                if eng == "act":
                    self.act(dst, psT[0:64, 0:512], AF.Copy)
                else:
                    self.cp(dst, psT[0:64, 0:512])
            psT = self.bank()
            for h in range(4):
                self.mm(psT[0:64, h * 128:(h + 1) * 128], KN[:, h, cs], self.ident_f, True, True)
            for h in range(4):
                self.ts(KBt[:, h * 128:(h + 1) * 128], psT[0:64, h * 128:(h + 1) * 128], DT[:, h * 64 + 63:h * 64 + 64], None, ALU.mult)
            psU = self.bank()
            for h in range(4):
                self.mm(psU[0:64, h * 128:(h + 1) * 128], PT[:, h * 64:(h + 1) * 64], BVt[:, h * 128:(h + 1) * 128], True, True)
            self.act(U_sb, psU[0:64, 0:512], AF.Copy)
            psW = self.bank()
            for h in range(4):
                self.mm(psW[:, h * 64:(h + 1) * 64], KAt[:, h * 128:(h + 1) * 128], PT[:, h * 64:(h + 1) * 64], True, True)
            self.cp(WT, psW[:, 0:256])
            psWS = self.bank()
            for h in range(4):
                self.mm(psWS[0:64, h * 128:(h + 1) * 128], WT[:, h * 64:(h + 1) * 64], Sd[:, h, :], True, True)
            self.tt(VN, U_sb, psWS[0:64, 0:512], ALU.subtract)
            psO, psO2 = self.bank(), self.bank()
            for h in range(4):
                hc = slice(h * 64, (h + 1) * 64)
                self.mm(psO[:, hc], Sd[:, h, :], QA[:, h, cs], True, True)
                self.mm(psO2[:, hc], VN[:, h * 128:(h + 1) * 128], AT[:, hc], True, True)
            self.act(oT[:, :, cs], psO[:, 0:256].re("p (a b) -> p a b", a=4), AF.Copy)
            self.tt(oT[:, :, cs], oT[:, :, cs], psO2[:, 0:256].re("p (a b) -> p a b", a=4), ALU.add)
            psS = self.bank()
            for h in range(4):
                self.mm(psS[:, h * 128:(h + 1) * 128], KBt[:, h * 128:(h + 1) * 128], VN[:, h * 128:(h + 1) * 128], True, True)
            for h in range(4):
                self.stt(Sd[:, h, :], Sd[:, h, :], eG[:, h, c:c + 1], psS[:, h * 128:(h + 1) * 128], ALU.mult, ALU.add)
            self.ck(25)
        self.P.fence()
        A.release(m0)
        self._dbg_oT = oT
        self._dbg_QA = QA
        self.rms_gate(oT, "g_dl", "z", l, T, R, 4, DV_D)
        self.ck(26)
        if self.cfg.get('dbg'):
            for n in range(4):
                self.cp(self.dbgbuf[:, n, 0:T], R[n])
            for n in range(4):
                self.cp(self.dbgbuf[:, 4 + n, 0:T], self._dbg_oT[:, n, :])
            self.dma('sp', self.dram['dbg'], self.dbgbuf, is_output=True)
        self.out_proj(X, l, T, R)

    def build(self):
        import contextlib
        nc, T, depth = self.nc, self.T, self.depth
        ntiles = self.S // T
        self.plan_slabs(ntiles + (1 if self.has_sample else 0))
        with contextlib.ExitStack() as es:
            def sb(name, shape, dt):
                return es.enter_context(nc.sbuf_tensor("t_" + name, list(shape), dt))
            AF_SZ = self.cfg.get("arena_f", 18500 * self.T // 256)
            AB_SZ = self.cfg.get("arena_b", 9700 * self.T // 256)
            self.af = Arena(sb("arena_f", [128, AF_SZ], F32), AF_SZ)
            self.ab = Arena(sb("arena_b", [128, AB_SZ], BF16), AB_SZ)
            slots_t = sb("slots", [128, NSLOT, SLAB], BF16)
            self.slots = [newV(slots_t[:, i, :]) for i in range(NSLOT)]
            Xt = [sb("X%d" % i, [128, 8, T], F32) for i in range(2)]
            Xs = [[newV(Xt[i][:, n, :]) for n in range(8)] for i in range(2)]
            xb_t = sb("xb", [128, 8, T], BF16)
            self.xb = [newV(xb_t[:, n, :]) for n in range(8)]
            self.pf = newV(sb("pf", [128, self.npf], F32)[:, :])
            cst = newV(sb("cst", [128, NCONST], F32)[:, :])
            cst_b = newV(sb("cst_b", [128, NCONST], BF16)[:, :])
            self.ones_b = newV(sb("ones_b", [128, 128], BF16)[:, :])
            self.eps_c = newV(sb("eps_c", [128, 1], F32)[:, :])
            self.ident_f = cst[:, 0:128]
            self.ident_b = cst_b[:, 0:128]
            self.U_f = cst[0:64, 128:192]
            self.mask4 = cst[0:64, 128:384]
            self.smask4 = cst[0:64, 384:640]
            self.lmask4 = cst[0:64, 640:896]
            self.ident4 = cst[0:64, 896:1152]
            self.cmask = cst[0:4, 1152:1664]
            self.sel = cst[0:4, 1664:2176]
            self.negU4 = cst[0:64, 2176:2432]
            self.negL4 = cst[0:64, 2432:2688]
            self.eps6_c = newV(sb("eps6_c", [128, 1], F32)[:, :])
            if self.cfg.get('dbg'):
                self.dbgbuf = newV(sb('dbgbuf', [128, 8, T], F32)[:, :, :])
            self.banks = [newV(es.enter_context(nc.psum_tensor("ps%d" % i, [128, 512], F32))[:, :]) for i in range(8)]
            self.bank_rr = 0
            self._ln_zb = [None, None]
            self._ln_zs = [None, None]
            self.wlrin, self.wlraug, self.wbain, self.wrg, self.wig = {}, {}, {}, {}, {}
            self.L4, self.negA = {}, {}
            stage = {}
            for k, shp in self.small_shapes.items():
                stage[k] = newV(sb("st_" + k, list(shp), F32)[:, :])
            st = {"S_f": {}, "S_b": {}, "uhal": {}, "ghal": {}, "h_lru": {}, "Sd_f": {}, "Sd_b": {}, "chal": {}}
            for l in range(depth):
                st["ghal"][l] = newV(sb("ghal%d" % l, [128, NFF, 2], F32)[:, :, :])
                if l % 2 == 0:
                    st["S_f"][l] = newV(sb("S_f%d" % l, [64, 4, 128], F32)[:, :, :])
                    st["S_b"][l] = newV(sb("S_b%d" % l, [64, 4, 128], BF16)[:, :, :])
                    st["uhal"][l] = newV(sb("uhal%d" % l, [128, 4, 30], F32)[:, :, :])
                else:
                    st["h_lru"][l] = newV(sb("hlru%d" % l, [128, 4], F32)[:, :])
                    st["Sd_f"][l] = newV(sb("Sd_f%d" % l, [128, 4, 128], F32)[:, :, :])
                    st["Sd_b"][l] = newV(sb("Sd_b%d" % l, [128, 4, 128], BF16)[:, :, :])
                    st["chal"][l] = newV(sb("chal%d" % l, [128, 16, 3], F32)[:, :, :])
            self.st = st
            d = self.dram
            for g in range(self.nslab_total):
                self.dma("pool", self.wbf_v[g], d["wslabs"][g])
            self.dma("sp", self.pf, d["pf"])
            self.dma("sp", cst, d["consts"])
            self.cp(cst_b, cst)
            self.memset(self.ones_b, 1.0)
            self.memset(self.eps_c, EPS)
            self.memset(self.eps6_c, 1e-6)
            for k in self.small_shapes:
                self.dma("sp", stage[k], d[k])
                shp = self.small_shapes[k]
                bt = newV(sb("sb_" + k, list(shp), BF16)[:, :])
                self.cp(bt, stage[k])
                l = int(k[-1])
                if k.startswith("wlrin"):
                    self.wlrin[l] = bt
                elif k.startswith("wlraug"):
                    self.wlraug[l] = bt
                elif k.startswith("wbain"):
                    self.wbain[l] = bt
                elif k.startswith("wrg"):
                    self.wrg[l] = bt
                elif k.startswith("wig"):
                    self.wig[l] = bt

            for l in range(depth):
                if l % 2 == 1:
                    self.odd_prologue(l, sb)

            def zero_states():
                for l in range(depth):
                    self.memset(st["ghal"][l], 0.0)
                    if l % 2 == 0:
                        self.memset(st["S_f"][l], 0.0)
                        self.memset(st["S_b"][l], 0.0)
                        self.memset(st["uhal"][l], 0.0)
                    else:
                        self.memset(st["h_lru"][l], 0.0)
                        self.memset(st["Sd_f"][l], 0.0)
                        self.memset(st["Sd_b"][l], 0.0)
                        self.memset(st["chal"][l], 0.0)

            def load_states():
                for l in range(depth):
                    self.dma("sp", st["ghal"][l], d["i%d_ffn" % l])
                    if l % 2 == 0:
                        self.dma("sp", st["S_f"][l], d["i%d_gla" % l].rearrange("h k v -> k h v"))
                        self.act(st["S_b"][l], st["S_f"][l], AF.Copy)
                        self.dma("sp", st["uhal"][l], d["i%d_dw" % l])
                    else:
                        self.dma("sp", st["h_lru"][l], d["i%d_lru" % l])
                        self.dma("sp", st["Sd_f"][l], d["i%d_delta" % l].rearrange("h k v -> k h v"))
                        self.act(st["Sd_b"][l], st["Sd_f"][l], AF.Copy)
                        self.dma("sp", st["chal"][l], d["i%d_conv" % l])

            def store_states(grp):
                for l in range(depth):
                    self.dma("sp", d["%s%d_ffn" % (grp, l)], st["ghal"][l], is_output=True)
                    if l % 2 == 0:
                        self.dma("sp", d["%s%d_gla" % (grp, l)].rearrange("h k v -> k h v"), st["S_f"][l], is_output=True)
                        self.dma("sp", d["%s%d_dw" % (grp, l)], st["uhal"][l], is_output=True)
                    else:
                        self.dma("sp", d["%s%d_lru" % (grp, l)], st["h_lru"][l], is_output=True)
                        self.dma("sp", d["%s%d_delta" % (grp, l)].rearrange("h k v -> k h v"), st["Sd_f"][l], is_output=True)
                        self.dma("sp", d["%s%d_conv" % (grp, l)], st["chal"][l], is_output=True)

            def run_tile(X, Tt):
                for n in range(8):
                    self.cp(self.xb[n][:, 0:Tt], X[n][:, 0:Tt])
                for l in range(depth):
                    self.P.fence()
                    self.af.reset()
                    self.ab.reset()
                    Xv = [x[:, 0:Tt] for x in X]
                    if l % 2 == 0:
                        self.even_mixer(Xv, l, Tt, st)
                    else:
                        self.odd_mixer(Xv, l, Tt, st)
                    self.ck(11)
                    self.layernorm(Xv, l, "ln1", Tt)
                    self.ck(12)
                    self.P.fence()
                    self.af.reset()
                    self.ab.reset()
                    self.ffn(Xv, l, Tt, st)
                    self.ck(13)
                    self.layernorm(Xv, l, "ln2", Tt)

            tcount = 0
            try:
                self.main_body(ntiles, d, Xt, Xs, zero_states, load_states, store_states, run_tile)
            except Cut:
                pass
            self.P.finish()
            self.P.emit()
        return nc

    def main_body(self, ntiles, d, Xt, Xs, zero_states, load_states, store_states, run_tile):
        T = self.T
        tcount = 0
        if True:
            if ntiles:
                zero_states()
                xp = d["xp"].rearrange("(k p) s -> p k s", p=128)
                yp = d["yp"].rearrange("(k p) s -> p k s", p=128)
                Xall = [V(Xt[i][:, :, :], [u for x in Xs[i] for u in x.us]) for i in range(2)]
                self.dma("pool", Xall[0], xp[:, :, 0:T])
                for i in range(ntiles):
                    if i + 1 < ntiles:
                        self.dma("pool", Xall[(i + 1) % 2], xp[:, :, (i + 1) * T:(i + 2) * T])
                    if i > 0 and i % 6 == 0:
                        self.P.new_epoch()
                    run_tile(Xs[i % 2], T)
                    self.dma("pool", yp[:, :, i * T:(i + 1) * T], Xall[i % 2], is_output=True)
                    tcount += 1
                store_states("p")
            if self.has_sample:
                xs = d["xs"].rearrange("(k p) s -> p k s", p=128)
                ys = d["ys"].rearrange("(k p) s -> p k s", p=128)
                Xi = tcount % 2
                Xsv = V(Xt[Xi][:, :, 0:64], [u for x in Xs[Xi] for u in x.us])
                load_states()
                self.dma("sp", Xsv, xs)
                run_tile(Xs[Xi], 64)
                self.dma("sp", ys, Xsv, is_output=True)
                store_states("s")


def run_config(inp, cfg, xp_list, xs_list, states_list):
    depth = cfg["depth"]
    wslabs, pp, small = host_weights(inp, depth)
    pf = pp.array()
    small_shapes = {k: v.shape for k, v in small.items()}
    b = Builder(cfg, pp.off, pf.shape[1], wslabs.shape[0], small_shapes)
    nc = b.build()
    consts = host_consts()
    ncores = len(xp_list)
    in_maps = []
    for c in range(ncores):
        m = {"wslabs": wslabs, "pf": pf, "consts": consts}
        m.update(small)
        if cfg["seq"]:
            m["xp"] = np.ascontiguousarray(xp_list[c].T)
        if cfg["sample"]:
            m["xs"] = np.ascontiguousarray(xs_list[c].T)
            stt = states_list[c]
            for l in range(depth):
                if l % 2 == 0:
                    m["i%d_gla" % l] = np.ascontiguousarray(stt["gla%d" % l])
                    m["i%d_dw" % l] = np.ascontiguousarray(stt["dw%d" % l].T.reshape(4, 128, W_B - 1).transpose(1, 0, 2))
                else:
                    m["i%d_lru" % l] = _fm(stt["lru%d" % l])
                    m["i%d_delta" % l] = np.ascontiguousarray(stt["delta%d" % l])
                    m["i%d_conv" % l] = np.ascontiguousarray(stt["conv%d" % l].T.reshape(16, 128, W_S - 1).transpose(1, 0, 2))
                m["i%d_ffn" % l] = np.ascontiguousarray(stt["ffn%d" % l].T.reshape(NFF, 128, W_F - 1).transpose(1, 0, 2))
        in_maps.append(m)
    res = run_bass_kernel_spmd(nc, in_maps, core_ids=list(range(ncores)))
    return res.results, b


def unpack_state(r, grp, l):
    out = {}
    if l % 2 == 0:
        out["gla"] = r["%s%d_gla" % (grp, l)]
        out["dw"] = np.ascontiguousarray(r["%s%d_dw" % (grp, l)].transpose(2, 1, 0).reshape(W_B - 1, D_B))
    else:
        out["lru"] = np.ascontiguousarray(r["%s%d_lru" % (grp, l)].T.reshape(D_C))
        out["delta"] = r["%s%d_delta" % (grp, l)]
        out["conv"] = np.ascontiguousarray(r["%s%d_conv" % (grp, l)].transpose(2, 1, 0).reshape(W_S - 1, 2048))
    out["ffn"] = np.ascontiguousarray(r["%s%d_ffn" % (grp, l)].transpose(2, 1, 0).reshape(W_F - 1, D_FF))
    return out


def kernel(**inputs):
    inp = {k: np.asarray(v) for k, v in inputs.items()}
    cfg = {"depth": DEPTH, "T": 256, "seq": 8192, "sample": True}
    xp_list = [inp["x_prompt"][c % 4] for c in range(8)]
    xs_list = [inp["x_sample"][c] for c in range(8)]
    states = []
    for c in range(8):
        s = {}
        for l in range(DEPTH):
            if l % 2 == 0:
                s["gla%d" % l] = inp["state_l%d_gla" % l][c]
                s["dw%d" % l] = inp["cache_l%d_dwconv" % l][c]
            else:
                s["lru%d" % l] = inp["state_l%d_lru" % l][c]
                s["delta%d" % l] = inp["state_l%d_delta" % l][c]
                s["conv%d" % l] = inp["cache_l%d_conv" % l][c]
            s["ffn%d" % l] = inp["cache_l%d_ffn" % l][c]
        states.append(s)
    results, _ = run_config(inp, cfg, xp_list, xs_list, states)
    y_prompt = np.stack([results[c]["yp"].T for c in range(4)], 0)
    y_sample = np.stack([results[c]["ys"].T for c in range(8)], 0)
    outs = [y_prompt, y_sample]
    for grp, cores in (("p", range(4)), ("s", range(8))):
        per = [[unpack_state(results[c], grp, l) for l in range(DEPTH)] for c in cores]
        for l in range(DEPTH):
            keys = ("gla", "dw", "ffn") if l % 2 == 0 else ("lru", "delta", "conv", "ffn")
            for k in keys:
                outs.append(np.stack([per[i][l][k] for i in range(len(per))], 0))
    return tuple(np.ascontiguousarray(o, dtype=np.float32) for o in outs)
```

```python
import numpy as np
import concourse.bass as bass
import concourse.mybir as mybir
from concourse.bass_utils import run_bass_kernel_spmd

F32 = mybir.dt.float32
BF16 = mybir.dt.bfloat16
AF = mybir.ActivationFunctionType
ALU = mybir.AluOpType

D_MODEL = 1024
DEPTH = 4
H_A, DK_A, DV_A, R_A = 4, 64, 128, 16
D_B, W_B = 512, 31
D_C, H_C, DH_C = 512, 8, 64
H_D, DK_D, DV_D = 4, 128, 128
W_S = 4
D_FF, W_F = 2688, 3
NFF = D_FF // 128
ALPHA = (2 * DEPTH) ** 0.25
EPS = 1e-5
LRU_C = 8.0
SLAB = 4096
NSLOT = 5
NDMASEM = 40


class Cut(Exception):
    pass


class Unit:
    __slots__ = ("w", "r")

    def __init__(self):
        self.w = None
        self.r = {}


class V:
    __slots__ = ("ap", "us")

    def __init__(self, ap, us):
        self.ap = ap
        self.us = tuple(us)

    def __getitem__(self, idx):
        return V(self.ap[idx], self.us)

    def re(self, s, **kw):
        return V(self.ap.rearrange(s, **kw), self.us)


def newV(ap):
    return V(ap, (Unit(),))


class Prog:
    ENG = ("pe", "act", "dve", "pool", "sp")

    def __init__(self, nc):
        self.nc = nc
        self.q = {e: [] for e in self.ENG}
        self.cnt = {e: 0 for e in self.ENG}
        self.waited = {e: {} for e in self.ENG}
        self.dma_val = [0] * NDMASEM
        self.dma_rr = 0
        self.out_tokens = []
        self.ninstr = 0
        self.epoch = 0

    def _wait(self, eng, key, val):
        if self.waited[eng].get(key, 0) >= val:
            return
        self.waited[eng][key] = val
        self.q[eng].append(("w", key, val))

    def _deps(self, eng, reads, writes):
        for v in reads:
            for u in v.us:
                if u.w is not None:
                    self._wait(eng, u.w[0], u.w[1])
        for v in writes:
            for u in v.us:
                if u.w is not None and u.w[0][0] != eng:
                    self._wait(eng, u.w[0], u.w[1])
                for k, val in u.r.items():
                    if k[0] != eng:
                        self._wait(eng, k, val)

    def _mark(self, tok, reads, writes):
        for v in reads:
            for u in v.us:
                if u.r.get(tok[0], 0) < tok[1]:
                    u.r[tok[0]] = tok[1]
        for v in writes:
            for u in v.us:
                u.w = tok
                u.r = {}

    def op(self, eng, fn, reads, writes, inc=True):
        self._deps(eng, reads, writes)
        key = (eng, self.epoch)
        if inc:
            self.cnt[eng] += 1
            tok = (key, self.cnt[eng])
        else:
            tok = (key, self.cnt[eng] + 1)
        self.q[eng].append(("i", fn, inc, key))
        self._mark(tok, reads, writes)
        self.ninstr += 1

    def new_epoch(self):
        self.fence()
        self.epoch += 1
        for e in self.ENG:
            self.cnt[e] = 0

    def dma(self, eng, out, in_, reads, writes, is_output=False, **kw):
        i = self.dma_rr
        self.dma_rr = (self.dma_rr + 1) % NDMASEM
        key = ("d", i)
        if self.dma_val[i] > 0:
            self._wait(eng, key, self.dma_val[i])
        self._deps(eng, reads, writes)
        self.dma_val[i] += 16
        tok = (key, self.dma_val[i])
        self.q[eng].append(("d", out, in_, i, kw))
        self._mark(tok, reads, writes)
        if is_output:
            self.out_tokens.append(tok)
        self.ninstr += 1

    def fence(self):
        comp = ("pe", "act", "dve", "pool")
        for e in comp:
            for f in comp:
                if e != f and self.cnt[f] > 0:
                    self._wait(e, (f, self.epoch), self.cnt[f])

    def finish(self):
        for key, val in self.out_tokens:
            self._wait("sp", key, val)

    def emit(self):
        nc = self.nc
        handles = {"pe": nc.tensor, "act": nc.scalar, "dve": nc.vector, "pool": nc.gpsimd, "sp": nc.sync}
        import contextlib
        with contextlib.ExitStack() as st:
            sems = {}
            for e in self.ENG:
                for ep in range(self.epoch + 1):
                    sems[(e, ep)] = st.enter_context(nc.semaphore("s_%s_%d" % (e, ep)))
            for i in range(NDMASEM):
                sems[("d", i)] = st.enter_context(nc.semaphore("sd%d" % i))
            block = st.enter_context(nc.Block())

            def run(e, h):
                for it in self.q[e]:
                    if it[0] == "w":
                        h.wait_ge(sems[it[1]], it[2])
                    elif it[0] == "i":
                        ins = it[1](h)
                        if it[2]:
                            ins.then_inc(sems[it[3]], 1)
                    else:
                        h.dma_start(out=it[1], in_=it[2], **it[4]).then_inc(sems[("d", it[3])], 16)

            @block.tensor
            def _(h):
                run("pe", h)

            @block.scalar
            def _(h):
                run("act", h)

            @block.vector
            def _(h):
                run("dve", h)

            @block.gpsimd
            def _(h):
                run("pool", h)

            @block.sync
            def _(h):
                run("sp", h)


class Arena:
    def __init__(self, tens, size):
        self.t = tens
        self.size = size
        self.off = 0

    def reset(self):
        self.off = 0

    def mark(self):
        return self.off

    def release(self, m):
        self.off = m

    def alloc(self, parts, shape):
        n = int(np.prod(shape))
        assert self.off + n <= self.size, ("arena overflow", self.off, n, self.size)
        ap = self.t[0:parts, self.off:self.off + n]
        self.off += n
        if len(shape) == 2:
            ap = ap.rearrange("p (a b) -> p a b", a=shape[0])
        elif len(shape) == 3:
            ap = ap.rearrange("p (a b c) -> p a b c", a=shape[0], b=shape[1])
        return newV(ap)


def _slab_in(w_cols):
    n = w_cols.shape[1]
    a = np.zeros((8, 128, 512), np.float32)
    a[:, :, :n] = w_cols.reshape(8, 128, n)
    return np.ascontiguousarray(a.transpose(1, 0, 2)).reshape(128, SLAB)


def _slab_down(w_cols):
    a = np.zeros((128, SLAB), np.float32)
    a[:, :NFF * 128] = w_cols.reshape(NFF, 128, 128).transpose(1, 0, 2).reshape(128, NFF * 128)
    return a


def _fm(vec):
    return np.ascontiguousarray(vec.reshape(-1, 128).T)


class ParamPack:
    def __init__(self):
        self.cols = []
        self.off = {}
        self.n = 0

    def add(self, name, arr):
        arr = np.asarray(arr, np.float32)
        assert arr.shape[0] == 128
        arr = arr.reshape(128, -1)
        self.off[name] = (self.n, arr.shape[1])
        self.cols.append(arr)
        self.n += arr.shape[1]

    def array(self):
        return np.ascontiguousarray(np.concatenate(self.cols, axis=1))


def layer_slab_names(l):
    names = []
    if l % 2 == 0:
        names += ["qk", "v", "gate", "glua", "glub", "out0", "out1"]
    else:
        names += ["xl", "q", "k", "v", "gc", "z", "out0", "out1"]
    names += ["up%d" % s for s in range(11)]
    names += ["dn%d" % n for n in range(8)]
    return names


def host_weights(inp, depth):
    slabs = []
    pp = ParamPack()
    small = {}
    for l in range(depth):
        if l % 2 == 0:
            e = l // 2
            w = inp["we_in"][e]
            slabs += [_slab_in(w[:, 0:512]), _slab_in(w[:, 512:1024]), _slab_in(w[:, 1024:1536]),
                      _slab_in(w[:, 1552:2064]), _slab_in(w[:, 2064:2576])]
            wo = inp["we_out"][e]
            slabs += [_slab_in(wo[:, 0:512]), _slab_in(wo[:, 512:1024])]
            small["wlrin%d" % l] = np.ascontiguousarray(
                w[:, 1536:1552].reshape(8, 128, 16).transpose(1, 0, 2)).reshape(128, 128)
            small["wlraug%d" % l] = np.ascontiguousarray(
                np.concatenate([inp["we_lr"][e], inp["be_lr"][e][None, :]], axis=0))
            pp.add("g_gla%d" % l, _fm(inp["ge_gla"][e]))
            pp.add("w_dw%d" % l, inp["we_dw"][e].T.reshape(4, 128, W_B).transpose(1, 0, 2))
            pp.add("b_dw%d" % l, _fm(inp["be_dw"][e]))
            pp.add("g_cn%d" % l, _fm(inp["ge_cn"][e]))
            pp.add("b_cn%d" % l, _fm(inp["be_cn"][e]))
        else:
            o = l // 2
            w = inp["wo_in"][o]
            slabs += [_slab_in(w[:, 0:512]), _slab_in(w[:, 512:1024]), _slab_in(w[:, 1024:1536]),
                      _slab_in(w[:, 1536:2048]), _slab_in(w[:, 2048:2560]), _slab_in(w[:, 2560:3072])]
            wo = inp["wo_out"][o]
            slabs += [_slab_in(wo[:, 0:512]), _slab_in(wo[:, 512:1024])]
            small["wbain%d" % l] = np.ascontiguousarray(
                w[:, 3072:3080].reshape(8, 128, 8).transpose(1, 0, 2)).reshape(128, 64)
            for nm, key in (("wrg", "wo_rg"), ("wig", "wo_ig")):
                g = inp[key][o]
                bd = np.zeros((4, 128, 128), np.float32)
                for hh in range(8):
                    c, r = hh // 2, (hh % 2) * 64
                    bd[c, r:r + 64, r:r + 64] = g[hh]
                small["%s%d" % (nm, l)] = np.ascontiguousarray(bd.transpose(1, 0, 2)).reshape(128, 512)
            pp.add("w_cv%d" % l, inp["wo_conv"][o].T.reshape(16, 128, W_S).transpose(1, 0, 2))
            pp.add("b_cv%d" % l, _fm(inp["bo_conv"][o]))
            pp.add("b_rg%d" % l, _fm(inp["bo_rg"][o]))
            pp.add("b_ig%d" % l, _fm(inp["bo_ig"][o]))
            pp.add("lam%d" % l, _fm(inp["lam_lru"][o]))
            col = np.zeros((128, 2), np.float32)
            col[0:H_D, 0] = inp["a_log"][o]
            col[0:H_D, 1] = inp["dt_bias"][o]
            pp.add("hd%d" % l, col)
            pp.add("g_dl%d" % l, _fm(inp["go_delta"][o]))
        wu = inp["w_up"][l]
        for s in range(11):
            cols = np.zeros((1024, 512), np.float32)
            for jj in range(2):
                j = 2 * s + jj
                if j < NFF:
                    cols[:, jj * 128:(jj + 1) * 128] = wu[:, j * 128:(j + 1) * 128]
                    cols[:, (2 + jj) * 128:(3 + jj) * 128] = wu[:, D_FF + j * 128:D_FF + (j + 1) * 128]
            slabs.append(_slab_in(cols))
        wd = inp["w_down"][l]
        for n in range(8):
            slabs.append(_slab_down(wd[:, n * 128:(n + 1) * 128]))
        pp.add("w_fdw%d" % l, inp["w_fdw"][l].T.reshape(NFF, 128, W_F).transpose(1, 0, 2))
        pp.add("b_fdw%d" % l, _fm(inp["b_fdw"][l]))
        for nm in ("ln1_g", "ln1_b", "ln2_g", "ln2_b"):
            pp.add("%s%d" % (nm, l), _fm(inp[nm][l]))
    return np.stack(slabs, 0), pp, small


NCONST = 128 + 256 * 4 + 512 * 2 + 512


def host_consts():
    ident = np.eye(128, dtype=np.float32)
    s = np.arange(64)
    U = (s[:, None] <= s[None, :]).astype(np.float32)
    Us = (s[:, None] < s[None, :]).astype(np.float32)
    Ls = (s[:, None] > s[None, :]).astype(np.float32)
    c = np.zeros((128, NCONST), np.float32)
    c[:, 0:128] = ident
    c[0:64, 128:384] = np.tile(U, (1, 4))
    c[0:64, 384:640] = np.tile(Us, (1, 4))
    c[0:64, 640:896] = np.tile(Ls, (1, 4))
    c[0:64, 896:1152] = np.tile(np.eye(64, dtype=np.float32), (1, 4))
    cm = np.ones(512, np.float32)
    cm[::64] = 0.0
    c[0:4, 1152:1664] = cm[None, :]
    for h in range(4):
        c[h, 1664 + h * 128:1664 + (h + 1) * 128] = 1.0
    c[0:64, 2176:2432] = np.tile((U - 1.0) * 30000.0, (1, 4))
    c[0:64, 2432:2688] = np.tile((U.T - 1.0) * 30000.0, (1, 4))
    return c


class Builder:
    def __init__(self, cfg, pp_off, npf, nslab_total, small_shapes):
        self.cfg = cfg
        self.depth = cfg["depth"]
        self.T = cfg["T"]
        self.S = cfg["seq"]
        self.has_sample = cfg["sample"]
        self.pp_off = pp_off
        nc = self.nc = bass.Bass("TRN2", target_bir_lowering=False)
        self.P = Prog(nc)
        T = self.T
        d = self.dram = {}
        depth = self.depth

        def din(name, shape):
            d[name] = nc.dram_tensor(name, list(shape), F32, kind="ExternalInput").ap()

        def dout(name, shape):
            d[name] = nc.dram_tensor(name, list(shape), F32, kind="ExternalOutput").ap()
        self.outs = []
        din("wslabs", (nslab_total, 128, SLAB))
        self.wbf = nc.dram_tensor("wbf", [nslab_total, 128, SLAB], BF16, kind="Internal").ap()
        self.wbf_v = [newV(self.wbf[g]) for g in range(nslab_total)]
        din("pf", (128, npf))
        din("consts", (128, NCONST))
        for k, shp in small_shapes.items():
            din(k, shp)
        if self.S:
            din("xp", (D_MODEL, self.S))
            dout("yp", (D_MODEL, self.S))
        if self.has_sample:
            din("xs", (D_MODEL, 64))
            dout("ys", (D_MODEL, 64))
        for grp in (["p"] if self.S else []) + (["s"] if self.has_sample else []):
            for l in range(depth):
                if l % 2 == 0:
                    dout("%s%d_gla" % (grp, l), (H_A, DK_A, DV_A))
                    dout("%s%d_dw" % (grp, l), (128, 4, W_B - 1))
                else:
                    dout("%s%d_lru" % (grp, l), (128, 4))
                    dout("%s%d_delta" % (grp, l), (H_D, DK_D, DV_D))
                    dout("%s%d_conv" % (grp, l), (128, 16, W_S - 1))
                dout("%s%d_ffn" % (grp, l), (128, NFF, W_F - 1))
        if self.has_sample:
            for l in range(depth):
                if l % 2 == 0:
                    din("i%d_gla" % l, (H_A, DK_A, DV_A))
                    din("i%d_dw" % l, (128, 4, W_B - 1))
                else:
                    din("i%d_lru" % l, (128, 4))
                    din("i%d_delta" % l, (H_D, DK_D, DV_D))
                    din("i%d_conv" % l, (128, 16, W_S - 1))
                din("i%d_ffn" % l, (128, NFF, W_F - 1))
        if cfg.get('dbg'):
            dout('dbg', (128, 8, self.T))
        self.small_shapes = small_shapes
        self.npf = npf
        self.nslab_total = nslab_total

    def mm(self, out, lhsT, rhs, start, stop):
        self.P.op("pe", lambda h, o=out.ap, a=lhsT.ap, b=rhs.ap, s=start, e=stop: h.matmul(o, a, b, start=s, stop=e),
                  [lhsT, rhs], [out], inc=True)

    def act(self, out, in_, func, scale=1.0, bias=0.0, extra=()):
        sc = scale.ap if isinstance(scale, V) else scale
        bi = bias.ap if isinstance(bias, V) else bias
        rd = [in_] + [x for x in (scale, bias) if isinstance(x, V)] + list(extra)
        self.P.op("act", lambda h, o=out.ap, i=in_.ap, f=func, s=sc, b=bi: h.activation(out=o, in_=i, func=f, bias=b, scale=s),
                  rd, [out])

    def tt(self, out, in0, in1, op, eng="dve"):
        self.P.op(eng, lambda h, o=out.ap, a=in0.ap, b=in1.ap, p=op: h.tensor_tensor(out=o, in0=a, in1=b, op=p),
                  [in0, in1], [out])

    def ts(self, out, in0, s1, s2, op0, op1=None, eng="dve"):
        a1 = s1.ap if isinstance(s1, V) else s1
        a2 = s2.ap if isinstance(s2, V) else s2
        rd = [in0] + [x for x in (s1, s2) if isinstance(x, V)]
        if op1 is None:
            self.P.op(eng, lambda h, o=out.ap, a=in0.ap, x=a1, p=op0: h.tensor_scalar(out=o, in0=a, scalar1=x, scalar2=None, op0=p),
                      rd, [out])
        else:
            self.P.op(eng, lambda h, o=out.ap, a=in0.ap, x=a1, y=a2, p=op0, q=op1: h.tensor_scalar(out=o, in0=a, scalar1=x, scalar2=y, op0=p, op1=q),
                      rd, [out])

    def stt(self, out, in0, sc, in1, op0, op1, eng="dve"):
        a1 = sc.ap if isinstance(sc, V) else sc
        rd = [in0, in1] + ([sc] if isinstance(sc, V) else [])
        self.P.op(eng, lambda h, o=out.ap, a=in0.ap, x=a1, b=in1.ap, p=op0, q=op1: h.scalar_tensor_tensor(out=o, in0=a, scalar=x, in1=b, op0=p, op1=q),
                  rd, [out])

    def cp(self, out, in_, eng="dve"):
        self.P.op(eng, lambda h, o=out.ap, i=in_.ap: h.tensor_copy(out=o, in_=i), [in_], [out])

    def recip(self, out, in_):
        self.P.op("dve", lambda h, o=out.ap, i=in_.ap: h.reciprocal(out=o, in_=i), [in_], [out])

    def memset(self, out, val, eng="dve"):
        self.P.op(eng, lambda h, o=out.ap, v=val: h.memset(o, v), [], [out])

    def scan(self, out, d0, d1, init, op0, op1):
        ia = init.ap if isinstance(init, V) else init
        rd = [d0, d1] + ([init] if isinstance(init, V) else [])
        self.P.op("dve", lambda h, o=out.ap, a=d0.ap, b=d1.ap, i=ia, p=op0, q=op1: h.tensor_tensor_scan(out=o, data0=a, data1=b, initial=i, op0=p, op1=q),
                  rd, [out])

    def dma(self, eng, out, in_, reads=(), writes=(), is_output=False, **kw):
        oa = out.ap if isinstance(out, V) else out
        ia = in_.ap if isinstance(in_, V) else in_
        rd = list(reads) + ([in_] if isinstance(in_, V) else [])
        wr = list(writes) + ([out] if isinstance(out, V) else [])
        self.P.dma(eng, oa, ia, rd, wr, is_output=is_output, **kw)

    def ck(self, lvl):
        if self.cfg.get('cut', 99) == lvl:
            raise Cut()

    def bank(self):
        b = self.banks[self.bank_rr]
        self.bank_rr = (self.bank_rr + 1) % 8
        return b

    def pfv(self, name, l):
        off, n = self.pp_off["%s%d" % (name, l)]
        return self.pf[:, off:off + n]

    def plan_slabs(self, ntile_calls):
        order = []
        base = 0
        self.layer_base = []
        for l in range(self.depth):
            self.layer_base.append(base)
            base += len(layer_slab_names(l))
        for _ in range(ntile_calls):
            for l in range(self.depth):
                for i, nm in enumerate(layer_slab_names(l)):
                    order.append((l, nm, self.layer_base[l] + i))
        self.slab_order = order
        self.slab_issued = 0
        self.slab_next = 0

    def slab(self, l, name):
        k = self.slab_next
        ol, onm, _ = self.slab_order[k]
        assert (ol, onm) == (l, name), ((ol, onm), (l, name))
        lim = min(len(self.slab_order), k + NSLOT)
        while self.slab_issued < lim:
            n = self.slab_issued
            _, _, gi = self.slab_order[n]
            self.dma("sp", self.slots[n % NSLOT], self.wbf_v[gi])
            self.slab_issued += 1
        self.slab_next += 1
        return self.slots[k % NSLOT]

    def layernorm(self, X, l, which, T):
        g = self.pfv(which + "_g", l)
        bb = self.pfv(which + "_b", l)
        A, Ab = self.af, self.ab
        assert 2 * T <= 512
        pss = self.bank()
        psm, psq = pss[:, 0:T], pss[:, T:2 * T]
        zzs = [Ab.alloc(128, [2, T]) for _ in range(2)]
        for n in range(8):
            zz = zzs[n % 2]
            self.act(zz[:, 0, :], X[n], AF.Copy)
            self.act(zz[:, 1, :], X[n], AF.Square)
            self.mm(pss[:, 0:2 * T], self.ones_b, zz.re("p a b -> p (a b)"), n == 0, n == 7)
        mu = A.alloc(128, [T])
        msq = A.alloc(128, [T])
        var = A.alloc(128, [T])
        rs = A.alloc(128, [T])
        self.act(mu, psm, AF.Copy, scale=1.0 / D_MODEL)
        self.tt(msq, mu, mu, ALU.mult)
        self.stt(var, psq, 1.0 / D_MODEL, msq, ALU.mult, ALU.subtract)
        self.act(var, var, AF.Sqrt, bias=self.eps_c)
        self.recip(rs, var)
        t1 = [A.alloc(128, [T]) for _ in range(2)]
        for n in range(8):
            t = t1[n % 2]
            self.tt(t, X[n], mu, ALU.subtract)
            self.tt(t, t, rs, ALU.mult)
            self.act(self.xb[n][:, 0:T], t, AF.Identity, scale=g[:, n:n + 1], bias=bb[:, n:n + 1])
            self.act(X[n], t, AF.Identity, scale=g[:, n:n + 1], bias=bb[:, n:n + 1])

    def ffn(self, X, l, T, st):
        A, Ab = self.af, self.ab
        wf = self.pfv("w_fdw", l)
        bf = self.pfv("b_fdw", l)
        ghal = st["ghal"][l]
        hbuf = [Ab.alloc(128, [T]) for _ in range(NFF)]
        gb = [A.alloc(128, [T + 2]) for _ in range(2)]
        acc = [A.alloc(128, [T]) for _ in range(2)]
        ge = [A.alloc(128, [T]) for _ in range(3)]
        pend = None
        for s in range(11):
            slot = self.slab(l, "up%d" % s).re("p (k n) -> p k n", k=8)
            for jj in range(2):
                j = 2 * s + jj
                if j >= NFF:
                    continue
                psg, psu = self.bank(), self.bank()
                for k in range(8):
                    self.mm(psg[:, 0:T], slot[:, k, jj * 128:(jj + 1) * 128], self.xb[k][:, 0:T], k == 0, k == 7)
                for k in range(8):
                    self.mm(psu[:, 0:T], slot[:, k, (2 + jj) * 128:(3 + jj) * 128], self.xb[k][:, 0:T], k == 0, k == 7)
                g_, a_, e_ = gb[j % 2], acc[j % 2], ge[j % 3]
                self.cp(g_[:, 0:2], ghal[:, j, :], eng="pool")
                self.act(g_[:, 2:T + 2], psg[:, 0:T], AF.Copy)
                self.cp(ghal[:, j, :], g_[:, T:T + 2], eng="pool")
                self.act(a_, g_[:, 0:T], AF.Identity, scale=wf[:, 3 * j:3 * j + 1], bias=bf[:, j:j + 1])
                self.stt(a_, g_[:, 1:T + 1], wf[:, 3 * j + 1:3 * j + 2], a_, ALU.mult, ALU.add)
                self.stt(a_, g_[:, 2:T + 2], wf[:, 3 * j + 2:3 * j + 3], a_, ALU.mult, ALU.add)
                self.act(e_, a_, AF.Gelu_apprx_tanh)
                if pend is not None:
                    self.tt(pend[0], pend[1], pend[2], ALU.mult)
                pend = (hbuf[j], e_, psu[:, 0:T])
        self.tt(pend[0], pend[1], pend[2], ALU.mult)
        for n in range(8):
            slot = self.slab(l, "dn%d" % n)
            ps = self.bank()
            for j in range(NFF):
                self.mm(ps[:, 0:T], slot[:, j * 128:(j + 1) * 128], hbuf[j], j == 0, j == NFF - 1)
            self.stt(X[n], X[n], ALPHA, ps[:, 0:T], ALU.mult, ALU.add)

    def even_mixer(self, X, l, T, st):
        A, Ab = self.af, self.ab
        NCH = T // 64
        xb = self.xb
        S_f, S_b, uhal = st["S_f"][l], st["S_b"][l], st["uhal"][l]
        slot = self.slab(l, "qk").re("p (k n) -> p k n", k=8)
        qT = A.alloc(64, [4, T])
        kT = A.alloc(64, [4, T])
        for i in range(8):
            ps = self.bank()
            for k in range(8):
                self.mm(ps[0:64, 0:T], slot[:, k, i * 64:(i + 1) * 64], xb[k][:, 0:T], k == 0, k == 7)
            if i < 4:
                self.act(qT[:, i, :], ps[0:64, 0:T], AF.Copy, scale=DK_A ** -0.5)
            else:
                self.cp(kT[:, i - 4, :], ps[0:64, 0:T])
        self.ck(1)
        lrT = Ab.alloc(17, [T])
        self.memset(lrT, 1.0)
        ps = self.bank()
        wl = self.wlrin[l].re("p (k n) -> p k n", k=8)
        for k in range(8):
            self.mm(ps[0:16, 0:T], wl[:, k, :], xb[k][:, 0:T], k == 0, k == 7)
        self.cp(lrT[0:16, :], ps[0:16, 0:T])
        self.ck(2)
        slot = self.slab(l, "v").re("p (k n) -> p k n", k=8)
        vtok = [Ab.alloc(64, [512]) for _ in range(NCH)]
        sp_tok = [A.alloc(64, [256]) for _ in range(2)]
        e1 = [A.alloc(64, [256]) for _ in range(2)]
        oT = A.alloc(128, [4, T])
        ep = [A.alloc(64, [4, 64]) for _ in range(2)]
        en = [A.alloc(64, [4, 64]) for _ in range(2)]
        qd = [Ab.alloc(64, [4, 64]) for _ in range(2)]
        kd = [Ab.alloc(64, [4, 64]) for _ in range(2)]
        kk = [Ab.alloc(64, [4, 64]) for _ in range(2)]
        scm = [Ab.alloc(64, [4, 64]) for _ in range(2)]
        kkt = [Ab.alloc(64, [256]) for _ in range(2)]
        for c in range(NCH):
            cs = slice(c * 64, (c + 1) * 64)
            r = c % 2
            ps = self.bank()
            for k in range(8):
                self.mm(ps[0:64, 0:512], xb[k][:, cs], slot[:, k, :], k == 0, k == 7)
            self.act(vtok[c], ps[0:64, 0:512], AF.Copy)
            self.ck(3)
            ps2 = self.bank()
            self.mm(ps2[0:64, 0:256], lrT[0:17, cs], self.wlraug[l], True, True)
            self.act(e1[r], ps2[0:64, 0:256], AF.Exp, scale=-1.0)
            self.act(sp_tok[r], e1[r], AF.Ln, bias=1.0)
            self.ck(4)
            ps3 = self.bank()
            for h in range(4):
                self.mm(ps3[0:64, h * 64:(h + 1) * 64], sp_tok[r][:, h * 64:(h + 1) * 64], self.U_f, True, True)
            p3 = ps3[0:64, 0:256].re("p (a b) -> p a b", a=4)
            self.act(ep[r], p3, AF.Exp, scale=-1.0 / 16.0)
            self.act(en[r], p3, AF.Exp, scale=1.0 / 16.0)
            self.ck(5)
            self.tt(qd[r], qT[:, :, cs], ep[r], ALU.mult)
            self.tt(kd[r], kT[:, :, cs], en[r], ALU.mult)
            for h in range(4):
                self.ts(kk[r][:, h, :], kd[r][:, h, :], ep[r][:, h, 63:64], None, ALU.mult)
            self.ck(6)
            ps4 = self.bank()
            for h in range(4):
                self.mm(ps4[0:64, h * 64:(h + 1) * 64], kd[r][:, h, :], qd[r][:, h, :], True, True)
            self.tt(scm[r], ps4[0:64, 0:256].re("p (a b) -> p a b", a=4), self.mask4.re("p (a b) -> p a b", a=4), ALU.mult)
            ps5 = self.bank()
            for h in range(4):
                self.mm(ps5[0:64, h * 64:(h + 1) * 64], kk[r][:, h, :], self.ident_b[0:64, 0:64], True, True)
            self.act(kkt[r], ps5[0:64, 0:256], AF.Copy)
            self.ck(7)
            ps6 = self.bank()
            for h in range(4):
                self.mm(ps6[:, h * 64:(h + 1) * 64], S_b[:, h, :], qd[r][:, h, :], True, False)
                self.mm(ps6[:, h * 64:(h + 1) * 64], vtok[c][:, h * 128:(h + 1) * 128], scm[r][:, h, :], False, True)
            self.act(oT[:, :, cs], ps6[:, 0:256].re("p (a b) -> p a b", a=4), AF.Copy)
            ps7 = self.bank()
            for h in range(4):
                self.mm(ps7[0:64, h * 128:(h + 1) * 128], kkt[r][:, h * 64:(h + 1) * 64], vtok[c][:, h * 128:(h + 1) * 128], True, True)
            for h in range(4):
                self.stt(S_f[:, h, :], S_f[:, h, :], ep[r][:, h, 63:64], ps7[0:64, h * 128:(h + 1) * 128], ALU.mult, ALU.add)
            self.act(S_b, S_f, AF.Copy)
            self.ck(8)
        slot = self.slab(l, "gate").re("p (k n) -> p k n", k=8)
        sg = A.alloc(128, [4, T])
        for h in range(4):
            ps = self.bank()
            for k in range(8):
                self.mm(ps[:, 0:T], slot[:, k, h * 128:(h + 1) * 128], xb[k][:, 0:T], k == 0, k == 7)
            self.act(sg[:, h, :], ps[:, 0:T], AF.Silu)
        gg = self.pfv("g_gla", l)
        R = [Ab.alloc(128, [T]) for _ in range(8)]
        sq = [Ab.alloc(128, [T]) for _ in range(2)]
        sd = [A.alloc(128, [T]) for _ in range(2)]
        for h in range(4):
            self.act(sq[h % 2], oT[:, h, :], AF.Square)
            ps = self.bank()
            self.mm(ps[:, 0:T], self.ones_b, sq[h % 2], True, True)
            self.act(sd[h % 2], ps[:, 0:T], AF.Sqrt, scale=1.0 / DV_A, bias=self.eps_c)
            self.recip(sd[h % 2], sd[h % 2])
            self.stt(sd[h % 2], oT[:, h, :], gg[:, h:h + 1], sd[h % 2], ALU.mult, ALU.mult)
            self.tt(R[h], sd[h % 2], sg[:, h, :], ALU.mult)
        self.ck(9)
        up = A.alloc(128, [4, T + 30])
        self.cp(up[:, :, 0:30], uhal, eng="pool")
        self.ck(91)
        slot = self.slab(l, "glua").re("p (k n) -> p k n", k=8)
        for j in range(4):
            ps = self.bank()
            for k in range(8):
                self.mm(ps[:, 0:T], slot[:, k, j * 128:(j + 1) * 128], xb[k][:, 0:T], k == 0, k == 7)
            self.act(up[:, j, 30:30 + T], ps[:, 0:T], AF.Copy)
        slot = self.slab(l, "glub").re("p (k n) -> p k n", k=8)
        sgb = [A.alloc(128, [T]) for _ in range(2)]
        for j in range(4):
            ps = self.bank()
            for k in range(8):
                self.mm(ps[:, 0:T], slot[:, k, j * 128:(j + 1) * 128], xb[k][:, 0:T], k == 0, k == 7)
            self.act(sgb[j % 2], ps[:, 0:T], AF.Sigmoid)
            self.tt(up[:, j, 30:30 + T], up[:, j, 30:30 + T], sgb[j % 2], ALU.mult)
        self.cp(uhal, up[:, :, T:T + 30], eng="pool")
        self.ck(92)
        wdw = self.pfv("w_dw", l)
        bdw = self.pfv("b_dw", l)
        cv = [A.alloc(128, [T]) for _ in range(4)]
        pss = self.bank()
        psm, psq = pss[:, 0:T], pss[:, T:2 * T]
        cvz = [Ab.alloc(128, [2, T]) for _ in range(2)]
        for j in range(4):
            ce = "dve"
            self.ts(cv[j], up[:, j, 0:T], wdw[:, j * 31:j * 31 + 1], bdw[:, j:j + 1], ALU.mult, ALU.add, eng=ce)
            for tap in range(1, W_B):
                self.stt(cv[j], up[:, j, tap:tap + T], wdw[:, j * 31 + tap:j * 31 + tap + 1], cv[j], ALU.mult, ALU.add, eng=ce)
        for idx, j in enumerate((0, 2, 1, 3)):
            self.act(cvz[idx % 2][:, 0, :], cv[j], AF.Copy)
            self.act(cvz[idx % 2][:, 1, :], cv[j], AF.Square)
            self.mm(pss[:, 0:2 * T], self.ones_b, cvz[idx % 2].re("p a b -> p (a b)"), idx == 0, idx == 3)
        self.ck(95)
        mu = A.alloc(128, [T])
        msq = A.alloc(128, [T])
        var = A.alloc(128, [T])
        self.act(mu, psm, AF.Copy, scale=1.0 / D_B)
        self.tt(msq, mu, mu, ALU.mult)
        self.stt(var, psq, 1.0 / D_B, msq, ALU.mult, ALU.subtract)
        self.act(var, var, AF.Sqrt, bias=self.eps_c)
        self.recip(var, var)
        gcn = self.pfv("g_cn", l)
        self.ck(96)
        bcn = self.pfv("b_cn", l)
        for j in range(4):
            self.tt(cv[j], cv[j], mu, ALU.subtract)
            self.tt(cv[j], cv[j], var, ALU.mult)
            self.act(R[4 + j], cv[j], AF.Silu, scale=gcn[:, j:j + 1], bias=bcn[:, j:j + 1])
        self.ck(10)
        self.out_proj(X, l, T, R)

    def out_proj(self, X, l, T, R):
        for s in range(2):
            slot = self.slab(l, "out%d" % s).re("p (k n) -> p k n", k=8)
            for jj in range(4):
                n = s * 4 + jj
                ps = self.bank()
                for k in range(8):
                    self.mm(ps[:, 0:T], slot[:, k, jj * 128:(jj + 1) * 128], R[k], k == 0, k == 7)
                self.stt(X[n], X[n], ALPHA, ps[:, 0:T], ALU.mult, ALU.add)

    def log1p_series(self, A, e, parts, shape):
        den = A.alloc(parts, shape)
        s_ = A.alloc(parts, shape)
        s2 = A.alloc(parts, shape)
        p = A.alloc(parts, shape)
        self.ts(den, e, 2.0, None, ALU.add)
        self.recip(den, den)
        self.tt(s_, e, den, ALU.mult)
        self.tt(s2, s_, s_, ALU.mult)
        self.ts(p, s2, 1.0 / 11.0, 1.0 / 9.0, ALU.mult, ALU.add)
        for cst in (1.0 / 7.0, 1.0 / 5.0, 1.0 / 3.0, 1.0):
            self.tt(p, p, s2, ALU.mult)
            self.ts(p, p, cst, None, ALU.add)
        self.tt(p, p, s_, ALU.mult)
        return p

    def odd_prologue(self, l, sbf):
        A = self.af
        lam = self.pfv("lam", l)
        e = A.alloc(128, [4])
        self.act(e, lam, AF.Exp, scale=-1.0)
        p = self.log1p_series(A, e, 128, [4])
        L4 = newV(sbf("L4_%d" % l, [128, 4], F32)[:, :])
        self.ts(L4, p, -8.0, None, ALU.mult)
        self.L4[l] = L4
        hd = self.pfv("hd", l)
        negA = newV(sbf("negA_%d" % l, [4, 1], F32)[:, :])
        self.act(negA, hd[0:4, 0:1], AF.Exp)
        self.ts(negA, negA, -1.0, None, ALU.mult)
        self.negA[l] = negA

    def rms_gate(self, oT, gname, slabname, l, T, R, base, dv):
        A, Ab = self.af, self.ab
        slot = self.slab(l, slabname).re("p (k n) -> p k n", k=8)
        sg = A.alloc(128, [4, T])
        for h in range(4):
            ps = self.bank()
            for k in range(8):
                self.mm(ps[:, 0:T], slot[:, k, h * 128:(h + 1) * 128], self.xb[k][:, 0:T], k == 0, k == 7)
            self.act(sg[:, h, :], ps[:, 0:T], AF.Silu)
        gg = self.pfv(gname, l)
        sq = [Ab.alloc(128, [T]) for _ in range(2)]
        sd = [A.alloc(128, [T]) for _ in range(2)]
        for h in range(4):
            self.act(sq[h % 2], oT[:, h, :], AF.Square)
            ps = self.bank()
            self.mm(ps[:, 0:T], self.ones_b, sq[h % 2], True, True)
            self.act(sd[h % 2], ps[:, 0:T], AF.Sqrt, scale=1.0 / dv, bias=self.eps_c)
            self.recip(sd[h % 2], sd[h % 2])
            self.stt(sd[h % 2], oT[:, h, :], gg[:, h:h + 1], sd[h % 2], ALU.mult, ALU.mult)
            self.tt(R[base + h], sd[h % 2], sg[:, h, :], ALU.mult)

    def odd_mixer(self, X, l, T, st):
        A, Ab = self.af, self.ab
        NCH = T // 64
        xb = self.xb
        chal, hl, Sd = st["chal"][l], st["h_lru"][l], st["Sd_f"][l]
        wcv = self.pfv("w_cv", l)
        bcv = self.pfv("b_cv", l)
        R = [Ab.alloc(128, [T]) for _ in range(8)]
        cvo = {nm: A.alloc(128, [4, T]) for nm in ("xl", "q", "k", "v")}
        KA = A.alloc(128, [4, T])
        KBN = A.alloc(128, [4, T])
        QA = A.alloc(128, [4, T])
        oT = A.alloc(128, [4, T])
        eG = A.alloc(128, [4, NCH])
        Gb = A.alloc(64, [4, T])
        Gbn = A.alloc(64, [4, T])
        gcol = A.alloc(64, [NCH, 4])
        m0 = A.mark()
        cin = [A.alloc(128, [4, T + 3]) for _ in range(2)]
        for si, nm in enumerate(("xl", "q", "k", "v")):
            slot = self.slab(l, nm).re("p (k n) -> p k n", k=8)
            ci = cin[si % 2]
            self.cp(ci[:, :, 0:3], chal[:, si * 4:(si + 1) * 4, :], eng="pool")
            for j in range(4):
                ps = self.bank()
                for k in range(8):
                    self.mm(ps[:, 0:T], slot[:, k, j * 128:(j + 1) * 128], xb[k][:, 0:T], k == 0, k == 7)
                self.act(ci[:, j, 3:3 + T], ps[:, 0:T], AF.Copy)
            self.cp(chal[:, si * 4:(si + 1) * 4, :], ci[:, :, T:T + 3], eng="pool")
            dst = cvo[nm]
            for j in range(4):
                jj = si * 4 + j
                ce = "dve"
                self.ts(dst[:, j, :], ci[:, j, 0:T], wcv[:, jj * 4:jj * 4 + 1], bcv[:, jj:jj + 1], ALU.mult, ALU.add, eng=ce)
                for tap in range(1, W_S):
                    self.stt(dst[:, j, :], ci[:, j, tap:tap + T], wcv[:, jj * 4 + tap:jj * 4 + tap + 1], dst[:, j, :], ALU.mult, ALU.add, eng=ce)
                if si > 0:
                    self.act(dst[:, j, :], dst[:, j, :], AF.Silu)
        self.ck(21)
        xc = cvo["xl"]
        xcb = Ab.alloc(128, [4, T])
        self.cp(xcb, xc)
        L4 = self.L4[l]
        wrg = self.wrg[l].re("p (c n) -> p c n", c=4)
        wig = self.wig[l].re("p (c n) -> p c n", c=4)
        brg = self.pfv("b_rg", l)
        big = self.pfv("b_ig", l)
        slot = self.slab(l, "gc").re("p (k n) -> p k n", k=8)
        tb = [[A.alloc(128, [T]) for _ in range(2)] for _ in range(7)]
        for c in range(4):
            r_, ig_, t_, rd_, a_, om_, h_ = [tb[i][c % 2] for i in range(7)]
            ps = self.bank()
            self.mm(ps[:, 0:T], wrg[:, c, :], xcb[:, c, :], True, True)
            self.act(r_, ps[:, 0:T], AF.Sigmoid, bias=brg[:, c:c + 1])
            ps = self.bank()
            self.mm(ps[:, 0:T], wig[:, c, :], xcb[:, c, :], True, True)
            self.act(ig_, ps[:, 0:T], AF.Sigmoid, bias=big[:, c:c + 1])
            self.act(t_, r_, AF.Tanh, scale=L4[:, c:c + 1])
            self.ts(rd_, t_, -1.0, 1.0, ALU.mult, ALU.add)
            self.recip(rd_, rd_)
            self.stt(a_, t_, 1.0, rd_, ALU.add, ALU.mult)
            self.stt(om_, t_, -4.0, rd_, ALU.mult, ALU.mult)
            self.tt(om_, om_, rd_, ALU.mult)
            self.act(om_, om_, AF.Sqrt)
            self.tt(om_, om_, ig_, ALU.mult)
            self.tt(om_, om_, xc[:, c, :], ALU.mult)
            self.scan(h_, a_, om_, hl[:, c:c + 1], ALU.mult, ALU.add)
            self.cp(hl[:, c:c + 1], h_[:, T - 1:T])
            ps = self.bank()
            for k in range(8):
                self.mm(ps[:, 0:T], slot[:, k, c * 128:(c + 1) * 128], xb[k][:, 0:T], k == 0, k == 7)
            self.act(r_, ps[:, 0:T], AF.Gelu_apprx_tanh)
            self.tt(R[c], h_, r_, ALU.mult)
        self.ck(22)
        self.P.fence()
        A.release(m0)
        qs, ks, vs = cvo["q"], cvo["k"], cvo["v"]
        wba = self.wbain[l].re("p (k n) -> p k n", k=8)
        psb, psa = self.bank(), self.bank()
        for k in range(8):
            self.mm(psb[0:4, 0:T], wba[:, k, 0:4], xb[k][:, 0:T], k == 0, k == 7)
        for k in range(8):
            self.mm(psa[0:4, 0:T], wba[:, k, 4:8], xb[k][:, 0:T], k == 0, k == 7)
        ROWS = A.alloc(4, [4, T])
        hd = self.pfv("hd", l)
        self.act(ROWS[:, 3, :], psb[0:4, 0:T], AF.Sigmoid)
        y = A.alloc(4, [T])
        ay = A.alloc(4, [T])
        self.ts(y, psa[0:4, 0:T], hd[0:4, 1:2], None, ALU.add)
        self.act(ay, y, AF.Abs)
        self.act(ay, ay, AF.Exp, scale=-1.0)
        p = self.log1p_series(A, ay, 4, [T])
        self.ts(y, y, 0.0, None, ALU.max)
        self.stt(y, p, 2.0, y, ALU.mult, ALU.add)
        self.ts(y, y, self.negA[l][0:4, 0:1], None, ALU.mult)
        gc = ROWS[:, 1, :]
        self.scan(gc, self.cmask[:, 0:T], y, 0.0, ALU.mult, ALU.add)
        self.act(ROWS[:, 0, :], gc, AF.Exp)
        self.tt(ROWS[:, 2, :], ROWS[:, 3, :], ROWS[:, 0, :], ALU.mult)
        self.ck(23)
        for c in range(NCH):
            psT = self.bank()
            self.mm(psT[0:64, 0:4], ROWS[:, 1, c * 64:(c + 1) * 64], self.ident_f[0:4, 0:4], True, True)
            self.cp(gcol[:, c, :], psT[0:64, 0:4])
        sqb = [Ab.alloc(128, [T]) for _ in range(2)]
        rn = [A.alloc(128, [T]) for _ in range(2)]
        for h in range(4):
            selh = self.sel[:, h * 128:(h + 1) * 128]
            for i, src in enumerate((qs, ks)):
                self.act(sqb[i], src[:, h, :], AF.Square)
                ps = self.bank()
                self.mm(ps[:, 0:T], self.ones_b, sqb[i], True, True)
                self.act(rn[i], ps[:, 0:T], AF.Sqrt, bias=self.eps6_c)
                self.recip(rn[i], rn[i])
            self.stt(qs[:, h, :], qs[:, h, :], DK_D ** -0.5, rn[0], ALU.mult, ALU.mult)
            self.tt(ks[:, h, :], ks[:, h, :], rn[1], ALU.mult)
            psE = self.bank()
            self.mm(psE[:, 0:T], selh, ROWS[:, 0, :], True, True)
            self.act(eG[:, h, :], psE[:, 0:T].re("p (c t) -> p c t", t=64)[:, :, 63], AF.Copy)
            self.tt(QA[:, h, :], qs[:, h, :], psE[:, 0:T], ALU.mult)
            psB = self.bank()
            self.mm(psB[:, 0:T], selh, ROWS[:, 2, :], True, True)
            self.tt(KA[:, h, :], ks[:, h, :], psB[:, 0:T], ALU.mult)
            psb2 = self.bank()
            self.mm(psb2[:, 0:T], selh, ROWS[:, 3, :], True, True)
            self.tt(vs[:, h, :], vs[:, h, :], psb2[:, 0:T], ALU.mult)
            self.tt(KBN[:, h, :], ks[:, h, :], psb2[:, 0:T], ALU.mult)
            psG = self.bank()
            self.mm(psG[:, 0:T], selh, ROWS[:, 1, :], True, True)
            self.act(Gb[:, h, :], psG[0:64, 0:T], AF.Copy)
            self.act(Gbn[:, h, :], psG[0:64, 0:T], AF.Copy, scale=-1.0)
        self.ck(24)
        self.P.fence()
        A.release(m0)
        NB = 16
        nb = [A.alloc(64, [256]) for _ in range(NB)]
        big_ = [A.alloc(64, [512]) for _ in range(5)]
        WT = A.alloc(128, [256])
        nbi = [0]
        BV, KN, QN = vs, ks, qs
        DT2 = [A.alloc(64, [256]) for _ in range(2)]
        AT2 = [A.alloc(64, [256]) for _ in range(2)]

        def nbuf():
            v = nb[nbi[0] % NB]
            nbi[0] += 1
            return v
        for c in range(NCH):
            cs = slice(c * 64, (c + 1) * 64)
            DT, D, DTs, Ds = DT2[c % 2], nbuf(), nbuf(), nbuf()
            for h in range(4):
                hc = slice(h * 64, (h + 1) * 64)
                self.stt(DT[:, hc], Gb[:, h, cs], gcol[:, c, h:h + 1], self.negU4[:, hc], ALU.subtract, ALU.add)
                self.stt(D[:, hc], Gbn[:, h, cs], gcol[:, c, h:h + 1], self.negL4[:, hc], ALU.add, ALU.add)
            self.act(DT, DT, AF.Exp)
            self.act(D, D, AF.Exp)
            self.tt(DTs, DT, self.ident4, ALU.subtract)
            self.tt(Ds, D, self.ident4, ALU.subtract)
            psNT, psN, psAT = self.bank(), self.bank(), self.bank()
            for h in range(4):
                hc = slice(h * 64, (h + 1) * 64)
                self.mm(psNT[0:64, hc], KN[:, h, cs], KBN[:, h, cs], True, True)
                self.mm(psN[0:64, hc], KBN[:, h, cs], KN[:, h, cs], True, True)
                self.mm(psAT[0:64, hc], KN[:, h, cs], QN[:, h, cs], True, True)
            NT, N, AT, PT = nbuf(), nbuf(), AT2[c % 2], nbuf()
            self.stt(NT, psNT[0:64, 0:256], -1.0, DTs, ALU.mult, ALU.mult)
            self.stt(N, psN[0:64, 0:256], -1.0, Ds, ALU.mult, ALU.mult)
            self.tt(AT, psAT[0:64, 0:256], DT, ALU.mult)
            self.tt(PT, NT, self.ident4, ALU.add)
            for lev in range(5):
                psN2 = self.bank()
                for h in range(4):
                    hc = slice(h * 64, (h + 1) * 64)
                    self.mm(psN2[0:64, hc], NT[:, hc], N[:, hc], True, True)
                N2 = nbuf()
                self.act(N2, psN2[0:64, 0:256], AF.Copy)
                if lev < 4:
                    psNT2 = self.bank()
                    for h in range(4):
                        hc = slice(h * 64, (h + 1) * 64)
                        self.mm(psNT2[0:64, hc], N[:, hc], NT[:, hc], True, True)
                    NT2 = nbuf()
                    self.cp(NT2, psNT2[0:64, 0:256])
                else:
                    NT2 = None
                psP = self.bank()
                for h in range(4):
                    hc = slice(h * 64, (h + 1) * 64)
                    self.mm(psP[0:64, hc], N2[:, hc], PT[:, hc], True, True)
                PT2 = nbuf()
                self.tt(PT2, PT, psP[0:64, 0:256], ALU.add)
                N, NT, PT = N2, NT2, PT2
            BVt, KAt, KBt, U_sb, VN = big_
            for src, dst, eng in ((BV, BVt, "act"), (KA, KAt, "dve")):
                psT = self.bank()
                for h in range(4):
                    self.mm(psT[0:64, h * 128:(h + 1) * 128], src[:, h, cs], self.ident_f, True, True)
                if eng == "act":
                    self.act(dst, psT[0:64, 0:512], AF.Copy)
                else:
                    self.cp(dst, psT[0:64, 0:512])
            psT = self.bank()
            for h in range(4):
                self.mm(psT[0:64, h * 128:(h + 1) * 128], KN[:, h, cs], self.ident_f, True, True)
            for h in range(4):
                self.ts(KBt[:, h * 128:(h + 1) * 128], psT[0:64, h * 128:(h + 1) * 128], DT[:, h * 64 + 63:h * 64 + 64], None, ALU.mult)
            psU = self.bank()
            for h in range(4):
                self.mm(psU[0:64, h * 128:(h + 1) * 128], PT[:, h * 64:(h + 1) * 64], BVt[:, h * 128:(h + 1) * 128], True, True)
            self.act(U_sb, psU[0:64, 0:512], AF.Copy)
            psW = self.bank()
            for h in range(4):
                self.mm(psW[:, h * 64:(h + 1) * 64], KAt[:, h * 128:(h + 1) * 128], PT[:, h * 64:(h + 1) * 64], True, True)
            self.cp(WT, psW[:, 0:256])
            psWS = self.bank()
            for h in range(4):
                self.mm(psWS[0:64, h * 128:(h + 1) * 128], WT[:, h * 64:(h + 1) * 64], Sd[:, h, :], True, True)
            self.tt(VN, U_sb, psWS[0:64, 0:512], ALU.subtract)
            psO, psO2 = self.bank(), self.bank()
            for h in range(4):
                hc = slice(h * 64, (h + 1) * 64)
                self.mm(psO[:, hc], Sd[:, h, :], QA[:, h, cs], True, True)
                self.mm(psO2[:, hc], VN[:, h * 128:(h + 1) * 128], AT[:, hc], True, True)
            self.act(oT[:, :, cs], psO[:, 0:256].re("p (a b) -> p a b", a=4), AF.Copy)
            self.tt(oT[:, :, cs], oT[:, :, cs], psO2[:, 0:256].re("p (a b) -> p a b", a=4), ALU.add)
            psS = self.bank()
            for h in range(4):
                self.mm(psS[:, h * 128:(h + 1) * 128], KBt[:, h * 128:(h + 1) * 128], VN[:, h * 128:(h + 1) * 128], True, True)
            for h in range(4):
                self.stt(Sd[:, h, :], Sd[:, h, :], eG[:, h, c:c + 1], psS[:, h * 128:(h + 1) * 128], ALU.mult, ALU.add)
            self.ck(25)
        self.P.fence()
        A.release(m0)
        self._dbg_oT = oT
        self._dbg_QA = QA
        self.rms_gate(oT, "g_dl", "z", l, T, R, 4, DV_D)
        self.ck(26)
        if self.cfg.get('dbg'):
            for n in range(4):
                self.cp(self.dbgbuf[:, n, 0:T], R[n])
            for n in range(4):
                self.cp(self.dbgbuf[:, 4 + n, 0:T], self._dbg_oT[:, n, :])
            self.dma('sp', self.dram['dbg'], self.dbgbuf, is_output=True)
        self.out_proj(X, l, T, R)

    def build(self):
        import contextlib
        nc, T, depth = self.nc, self.T, self.depth
        ntiles = self.S // T
        self.plan_slabs(ntiles + (1 if self.has_sample else 0))
        with contextlib.ExitStack() as es:
            def sb(name, shape, dt):
                return es.enter_context(nc.sbuf_tensor("t_" + name, list(shape), dt))
            AF_SZ = self.cfg.get("arena_f", 18500 * self.T // 256)
            AB_SZ = self.cfg.get("arena_b", 9700 * self.T // 256)
            self.af = Arena(sb("arena_f", [128, AF_SZ], F32), AF_SZ)
            self.ab = Arena(sb("arena_b", [128, AB_SZ], BF16), AB_SZ)
            slots_t = sb("slots", [128, NSLOT, SLAB], BF16)
            self.slots = [newV(slots_t[:, i, :]) for i in range(NSLOT)]
            Xt = [sb("X%d" % i, [128, 8, T], F32) for i in range(2)]
            Xs = [[newV(Xt[i][:, n, :]) for n in range(8)] for i in range(2)]
            xb_t = sb("xb", [128, 8, T], BF16)
            self.xb = [newV(xb_t[:, n, :]) for n in range(8)]
            self.pf = newV(sb("pf", [128, self.npf], F32)[:, :])
            cst = newV(sb("cst", [128, NCONST], F32)[:, :])
            cst_b = newV(sb("cst_b", [128, NCONST], BF16)[:, :])
            self.ones_b = newV(sb("ones_b", [128, 128], BF16)[:, :])
            self.eps_c = newV(sb("eps_c", [128, 1], F32)[:, :])
            self.ident_f = cst[:, 0:128]
            self.ident_b = cst_b[:, 0:128]
            self.U_f = cst[0:64, 128:192]
            self.mask4 = cst[0:64, 128:384]
            self.smask4 = cst[0:64, 384:640]
            self.lmask4 = cst[0:64, 640:896]
            self.ident4 = cst[0:64, 896:1152]
            self.cmask = cst[0:4, 1152:1664]
            self.sel = cst[0:4, 1664:2176]
            self.negU4 = cst[0:64, 2176:2432]
            self.negL4 = cst[0:64, 2432:2688]
            self.eps6_c = newV(sb("eps6_c", [128, 1], F32)[:, :])
            if self.cfg.get('dbg'):
                self.dbgbuf = newV(sb('dbgbuf', [128, 8, T], F32)[:, :, :])
            self.banks = [newV(es.enter_context(nc.psum_tensor("ps%d" % i, [128, 512], F32))[:, :]) for i in range(8)]
            self.bank_rr = 0
            self._ln_zb = [None, None]
            self._ln_zs = [None, None]
            self.wlrin, self.wlraug, self.wbain, self.wrg, self.wig = {}, {}, {}, {}, {}
            self.L4, self.negA = {}, {}
            stage = {}
            for k, shp in self.small_shapes.items():
                stage[k] = newV(sb("st_" + k, list(shp), F32)[:, :])
            st = {"S_f": {}, "S_b": {}, "uhal": {}, "ghal": {}, "h_lru": {}, "Sd_f": {}, "Sd_b": {}, "chal": {}}
            for l in range(depth):
                st["ghal"][l] = newV(sb("ghal%d" % l, [128, NFF, 2], F32)[:, :, :])
                if l % 2 == 0:
                    st["S_f"][l] = newV(sb("S_f%d" % l, [64, 4, 128], F32)[:, :, :])
                    st["S_b"][l] = newV(sb("S_b%d" % l, [64, 4, 128], BF16)[:, :, :])
                    st["uhal"][l] = newV(sb("uhal%d" % l, [128, 4, 30], F32)[:, :, :])
                else:
                    st["h_lru"][l] = newV(sb("hlru%d" % l, [128, 4], F32)[:, :])
                    st["Sd_f"][l] = newV(sb("Sd_f%d" % l, [128, 4, 128], F32)[:, :, :])
                    st["Sd_b"][l] = newV(sb("Sd_b%d" % l, [128, 4, 128], BF16)[:, :, :])
                    st["chal"][l] = newV(sb("chal%d" % l, [128, 16, 3], F32)[:, :, :])
            self.st = st
            d = self.dram
            self.sbuf_left = nc.sbuf_bytes_remaining
            for g in range(self.nslab_total):
                self.dma("pool", self.wbf_v[g], d["wslabs"][g])
            self.dma("sp", self.pf, d["pf"])
            self.dma("sp", cst, d["consts"])
            self.cp(cst_b, cst)
            self.memset(self.ones_b, 1.0)
            self.memset(self.eps_c, EPS)
            self.memset(self.eps6_c, 1e-6)
            for k in self.small_shapes:
                self.dma("sp", stage[k], d[k])
                shp = self.small_shapes[k]
                bt = newV(sb("sb_" + k, list(shp), BF16)[:, :])
                self.cp(bt, stage[k])
                l = int(k[-1])
                if k.startswith("wlrin"):
                    self.wlrin[l] = bt
                elif k.startswith("wlraug"):
                    self.wlraug[l] = bt
                elif k.startswith("wbain"):
                    self.wbain[l] = bt
                elif k.startswith("wrg"):
                    self.wrg[l] = bt
                elif k.startswith("wig"):
                    self.wig[l] = bt

            for l in range(depth):
                if l % 2 == 1:
                    self.odd_prologue(l, sb)

            def zero_states():
                for l in range(depth):
                    self.memset(st["ghal"][l], 0.0)
                    if l % 2 == 0:
                        self.memset(st["S_f"][l], 0.0)
                        self.memset(st["S_b"][l], 0.0)
                        self.memset(st["uhal"][l], 0.0)
                    else:
                        self.memset(st["h_lru"][l], 0.0)
                        self.memset(st["Sd_f"][l], 0.0)
                        self.memset(st["Sd_b"][l], 0.0)
                        self.memset(st["chal"][l], 0.0)

            def load_states():
                for l in range(depth):
                    self.dma("sp", st["ghal"][l], d["i%d_ffn" % l])
                    if l % 2 == 0:
                        self.dma("sp", st["S_f"][l], d["i%d_gla" % l].rearrange("h k v -> k h v"))
                        self.act(st["S_b"][l], st["S_f"][l], AF.Copy)
                        self.dma("sp", st["uhal"][l], d["i%d_dw" % l])
                    else:
                        self.dma("sp", st["h_lru"][l], d["i%d_lru" % l])
                        self.dma("sp", st["Sd_f"][l], d["i%d_delta" % l].rearrange("h k v -> k h v"))
                        self.act(st["Sd_b"][l], st["Sd_f"][l], AF.Copy)
                        self.dma("sp", st["chal"][l], d["i%d_conv" % l])

            def store_states(grp):
                for l in range(depth):
                    self.dma("sp", d["%s%d_ffn" % (grp, l)], st["ghal"][l], is_output=True)
                    if l % 2 == 0:
                        self.dma("sp", d["%s%d_gla" % (grp, l)].rearrange("h k v -> k h v"), st["S_f"][l], is_output=True)
                        self.dma("sp", d["%s%d_dw" % (grp, l)], st["uhal"][l], is_output=True)
                    else:
                        self.dma("sp", d["%s%d_lru" % (grp, l)], st["h_lru"][l], is_output=True)
                        self.dma("sp", d["%s%d_delta" % (grp, l)].rearrange("h k v -> k h v"), st["Sd_f"][l], is_output=True)
                        self.dma("sp", d["%s%d_conv" % (grp, l)], st["chal"][l], is_output=True)

            def run_tile(X, Tt):
                for n in range(8):
                    self.cp(self.xb[n][:, 0:Tt], X[n][:, 0:Tt])
                for l in range(depth):
                    self.P.fence()
                    self.af.reset()
                    self.ab.reset()
                    Xv = [x[:, 0:Tt] for x in X]
                    if l % 2 == 0:
                        self.even_mixer(Xv, l, Tt, st)
                    else:
                        self.odd_mixer(Xv, l, Tt, st)
                    self.ck(11)
                    self.layernorm(Xv, l, "ln1", Tt)
                    self.ck(12)
                    self.P.fence()
                    self.af.reset()
                    self.ab.reset()
                    self.ffn(Xv, l, Tt, st)
                    self.ck(13)
                    self.layernorm(Xv, l, "ln2", Tt)

            tcount = 0
            try:
                self.main_body(ntiles, d, Xt, Xs, zero_states, load_states, store_states, run_tile)
            except Cut:
                pass
            self.P.finish()
            self.P.emit()
        return nc

    def main_body(self, ntiles, d, Xt, Xs, zero_states, load_states, store_states, run_tile):
        T = self.T
        tcount = 0
        if True:
            if ntiles:
                zero_states()
                xp = d["xp"].rearrange("(k p) s -> p k s", p=128)
                yp = d["yp"].rearrange("(k p) s -> p k s", p=128)
                Xall = [V(Xt[i][:, :, :], [u for x in Xs[i] for u in x.us]) for i in range(2)]
                self.dma("pool", Xall[0], xp[:, :, 0:T])
                for i in range(ntiles):
                    if i + 1 < ntiles:
                        self.dma("pool", Xall[(i + 1) % 2], xp[:, :, (i + 1) * T:(i + 2) * T])
                    if i > 0 and i % 6 == 0:
                        self.P.new_epoch()
                    run_tile(Xs[i % 2], T)
                    self.dma("pool", yp[:, :, i * T:(i + 1) * T], Xall[i % 2], is_output=True)
                    tcount += 1
                store_states("p")
            if self.has_sample:
                xs = d["xs"].rearrange("(k p) s -> p k s", p=128)
                ys = d["ys"].rearrange("(k p) s -> p k s", p=128)
                Xi = tcount % 2
                Xsv = V(Xt[Xi][:, :, 0:64], [u for x in Xs[Xi] for u in x.us])
                load_states()
                self.dma("sp", Xsv, xs)
                run_tile(Xs[Xi], 64)
                self.dma("sp", ys, Xsv, is_output=True)
                store_states("s")


def run_config(inp, cfg, xp_list, xs_list, states_list):
    depth = cfg["depth"]
    wslabs, pp, small = host_weights(inp, depth)
    pf = pp.array()
    small_shapes = {k: v.shape for k, v in small.items()}
    b = Builder(cfg, pp.off, pf.shape[1], wslabs.shape[0], small_shapes)
    nc = b.build()
    consts = host_consts()
    ncores = len(xp_list)
    in_maps = []
    for c in range(ncores):
        m = {"wslabs": wslabs, "pf": pf, "consts": consts}
        m.update(small)
        if cfg["seq"]:
            m["xp"] = np.ascontiguousarray(xp_list[c].T)
        if cfg["sample"]:
            m["xs"] = np.ascontiguousarray(xs_list[c].T)
            stt = states_list[c]
            for l in range(depth):
                if l % 2 == 0:
                    m["i%d_gla" % l] = np.ascontiguousarray(stt["gla%d" % l])
                    m["i%d_dw" % l] = np.ascontiguousarray(stt["dw%d" % l].T.reshape(4, 128, W_B - 1).transpose(1, 0, 2))
                else:
                    m["i%d_lru" % l] = _fm(stt["lru%d" % l])
                    m["i%d_delta" % l] = np.ascontiguousarray(stt["delta%d" % l])
                    m["i%d_conv" % l] = np.ascontiguousarray(stt["conv%d" % l].T.reshape(16, 128, W_S - 1).transpose(1, 0, 2))
                m["i%d_ffn" % l] = np.ascontiguousarray(stt["ffn%d" % l].T.reshape(NFF, 128, W_F - 1).transpose(1, 0, 2))
        in_maps.append(m)
    res = run_bass_kernel_spmd(nc, in_maps, core_ids=list(range(ncores)))
    return res.results, b


def unpack_state(r, grp, l):
    out = {}
    if l % 2 == 0:
        out["gla"] = r["%s%d_gla" % (grp, l)]
        out["dw"] = np.ascontiguousarray(r["%s%d_dw" % (grp, l)].transpose(2, 1, 0).reshape(W_B - 1, D_B))
    else:
        out["lru"] = np.ascontiguousarray(r["%s%d_lru" % (grp, l)].T.reshape(D_C))
        out["delta"] = r["%s%d_delta" % (grp, l)]
        out["conv"] = np.ascontiguousarray(r["%s%d_conv" % (grp, l)].transpose(2, 1, 0).reshape(W_S - 1, 2048))
    out["ffn"] = np.ascontiguousarray(r["%s%d_ffn" % (grp, l)].transpose(2, 1, 0).reshape(W_F - 1, D_FF))
    return out


def kernel(**inputs):
    inp = {k: np.asarray(v) for k, v in inputs.items()}
    cfg = {"depth": DEPTH, "T": 256, "seq": 8192, "sample": True}
    xp_list = [inp["x_prompt"][c % 4] for c in range(8)]
    xs_list = [inp["x_sample"][c] for c in range(8)]
    states = []
    for c in range(8):
        s = {}
        for l in range(DEPTH):
            if l % 2 == 0:
                s["gla%d" % l] = inp["state_l%d_gla" % l][c]
                s["dw%d" % l] = inp["cache_l%d_dwconv" % l][c]
            else:
                s["lru%d" % l] = inp["state_l%d_lru" % l][c]
                s["delta%d" % l] = inp["state_l%d_delta" % l][c]
                s["conv%d" % l] = inp["cache_l%d_conv" % l][c]
            s["ffn%d" % l] = inp["cache_l%d_ffn" % l][c]
        states.append(s)
    results, _ = run_config(inp, cfg, xp_list, xs_list, states)
    y_prompt = np.stack([results[c]["yp"].T for c in range(4)], 0)
    y_sample = np.stack([results[c]["ys"].T for c in range(8)], 0)
    outs = [y_prompt, y_sample]
    for grp, cores in (("p", range(4)), ("s", range(8))):
        per = [[unpack_state(results[c], grp, l) for l in range(DEPTH)] for c in cores]
        for l in range(DEPTH):
            keys = ("gla", "dw", "ffn") if l % 2 == 0 else ("lru", "delta", "conv", "ffn")
            for k in keys:
                outs.append(np.stack([per[i][l][k] for i in range(len(per))], 0))
    return tuple(np.ascontiguousarray(o, dtype=np.float32) for o in outs)
```

```python
import numpy as np
import concourse.bass as bass
import concourse.mybir as mybir
from concourse.bass_utils import run_bass_kernel_spmd

F32 = mybir.dt.float32
BF16 = mybir.dt.bfloat16
AF = mybir.ActivationFunctionType
ALU = mybir.AluOpType

D_MODEL = 1024
DEPTH = 4
H_A, DK_A, DV_A, R_A = 4, 64, 128, 16
D_B, W_B = 512, 31
D_C, H_C, DH_C = 512, 8, 64
H_D, DK_D, DV_D = 4, 128, 128
W_S = 4
D_FF, W_F = 2688, 3
NFF = D_FF // 128
ALPHA = (2 * DEPTH) ** 0.25
EPS = 1e-5
LRU_C = 8.0
SLAB = 4096
NSLOT = 5
NDMASEM = 40


class Cut(Exception):
    pass


class Unit:
    __slots__ = ("w", "r")

    def __init__(self):
        self.w = None
        self.r = {}


class V:
    __slots__ = ("ap", "us")

    def __init__(self, ap, us):
        self.ap = ap
        self.us = tuple(us)

    def __getitem__(self, idx):
        return V(self.ap[idx], self.us)

    def re(self, s, **kw):
        return V(self.ap.rearrange(s, **kw), self.us)


def newV(ap):
    return V(ap, (Unit(),))


class Prog:
    ENG = ("pe", "act", "dve", "pool", "sp")

    def __init__(self, nc):
        self.nc = nc
        self.q = {e: [] for e in self.ENG}
        self.cnt = {e: 0 for e in self.ENG}
        self.waited = {e: {} for e in self.ENG}
        self.dma_val = [0] * NDMASEM
        self.dma_rr = 0
        self.out_tokens = []
        self.ninstr = 0
        self.epoch = 0

    def _wait(self, eng, key, val):
        if self.waited[eng].get(key, 0) >= val:
            return
        self.waited[eng][key] = val
        self.q[eng].append(("w", key, val))

    def _deps(self, eng, reads, writes):
        for v in reads:
            for u in v.us:
                if u.w is not None:
                    self._wait(eng, u.w[0], u.w[1])
        for v in writes:
            for u in v.us:
                if u.w is not None and u.w[0][0] != eng:
                    self._wait(eng, u.w[0], u.w[1])
                for k, val in u.r.items():
                    if k[0] != eng:
                        self._wait(eng, k, val)

    def _mark(self, tok, reads, writes):
        for v in reads:
            for u in v.us:
                if u.r.get(tok[0], 0) < tok[1]:
                    u.r[tok[0]] = tok[1]
        for v in writes:
            for u in v.us:
                u.w = tok
                u.r = {}

    def op(self, eng, fn, reads, writes, inc=True):
        self._deps(eng, reads, writes)
        key = (eng, self.epoch)
        if inc:
            self.cnt[eng] += 1
            tok = (key, self.cnt[eng])
        else:
            tok = (key, self.cnt[eng] + 1)
        self.q[eng].append(("i", fn, inc, key))
        self._mark(tok, reads, writes)
        self.ninstr += 1

    def new_epoch(self):
        self.fence()
        self.epoch += 1
        for e in self.ENG:
            self.cnt[e] = 0

    def dma(self, eng, out, in_, reads, writes, is_output=False, **kw):
        i = self.dma_rr
        self.dma_rr = (self.dma_rr + 1) % NDMASEM
        key = ("d", i)
        if self.dma_val[i] > 0:
            self._wait(eng, key, self.dma_val[i])
        self._deps(eng, reads, writes)
        self.dma_val[i] += 16
        tok = (key, self.dma_val[i])
        self.q[eng].append(("d", out, in_, i, kw))
        self._mark(tok, reads, writes)
        if is_output:
            self.out_tokens.append(tok)
        self.ninstr += 1

    def fence(self):
        comp = ("pe", "act", "dve", "pool")
        for e in comp:
            for f in comp:
                if e != f and self.cnt[f] > 0:
                    self._wait(e, (f, self.epoch), self.cnt[f])

    def finish(self):
        for key, val in self.out_tokens:
            self._wait("sp", key, val)

    def emit(self):
        nc = self.nc
        handles = {"pe": nc.tensor, "act": nc.scalar, "dve": nc.vector, "pool": nc.gpsimd, "sp": nc.sync}
        import contextlib
        with contextlib.ExitStack() as st:
            sems = {}
            for e in self.ENG:
                for ep in range(self.epoch + 1):
                    sems[(e, ep)] = st.enter_context(nc.semaphore("s_%s_%d" % (e, ep)))
            for i in range(NDMASEM):
                sems[("d", i)] = st.enter_context(nc.semaphore("sd%d" % i))
            block = st.enter_context(nc.Block())

            def run(e, h):
                for it in self.q[e]:
                    if it[0] == "w":
                        h.wait_ge(sems[it[1]], it[2])
                    elif it[0] == "i":
                        ins = it[1](h)
                        if it[2]:
                            ins.then_inc(sems[it[3]], 1)
                    else:
                        h.dma_start(out=it[1], in_=it[2], **it[4]).then_inc(sems[("d", it[3])], 16)

            @block.tensor
            def _(h):
                run("pe", h)

            @block.scalar
            def _(h):
                run("act", h)

            @block.vector
            def _(h):
                run("dve", h)

            @block.gpsimd
            def _(h):
                run("pool", h)

            @block.sync
            def _(h):
                run("sp", h)


class Arena:
    def __init__(self, tens, size):
        self.t = tens
        self.size = size
        self.off = 0

    def reset(self):
        self.off = 0

    def mark(self):
        return self.off

    def release(self, m):
        self.off = m

    def alloc(self, parts, shape):
        n = int(np.prod(shape))
        assert self.off + n <= self.size, ("arena overflow", self.off, n, self.size)
        ap = self.t[0:parts, self.off:self.off + n]
        self.off += n
        if len(shape) == 2:
            ap = ap.rearrange("p (a b) -> p a b", a=shape[0])
        elif len(shape) == 3:
            ap = ap.rearrange("p (a b c) -> p a b c", a=shape[0], b=shape[1])
        return newV(ap)


def _slab_in(w_cols):
    n = w_cols.shape[1]
    a = np.zeros((8, 128, 512), np.float32)
    a[:, :, :n] = w_cols.reshape(8, 128, n)
    return np.ascontiguousarray(a.transpose(1, 0, 2)).reshape(128, SLAB)


def _slab_down(w_cols):
    a = np.zeros((128, SLAB), np.float32)
    a[:, :NFF * 128] = w_cols.reshape(NFF, 128, 128).transpose(1, 0, 2).reshape(128, NFF * 128)
    return a


def _fm(vec):
    return np.ascontiguousarray(vec.reshape(-1, 128).T)


class ParamPack:
    def __init__(self):
        self.cols = []
        self.off = {}
        self.n = 0

    def add(self, name, arr):
        arr = np.asarray(arr, np.float32)
        assert arr.shape[0] == 128
        arr = arr.reshape(128, -1)
        self.off[name] = (self.n, arr.shape[1])
        self.cols.append(arr)
        self.n += arr.shape[1]

    def array(self):
        return np.ascontiguousarray(np.concatenate(self.cols, axis=1))


def layer_slab_names(l):
    names = []
    if l % 2 == 0:
        names += ["qk", "v", "gate", "glua", "glub", "out0", "out1"]
    else:
        names += ["xl", "q", "k", "v", "gc", "z", "out0", "out1"]
    names += ["up%d" % s for s in range(11)]
    names += ["dn%d" % n for n in range(8)]
    return names


def host_weights(inp, depth):
    slabs = []
    pp = ParamPack()
    small = {}
    for l in range(depth):
        if l % 2 == 0:
            e = l // 2
            w = inp["we_in"][e]
            slabs += [_slab_in(w[:, 0:512]), _slab_in(w[:, 512:1024]), _slab_in(w[:, 1024:1536]),
                      _slab_in(w[:, 1552:2064]), _slab_in(w[:, 2064:2576])]
            wo = inp["we_out"][e]
            slabs += [_slab_in(wo[:, 0:512]), _slab_in(wo[:, 512:1024])]
            small["wlrin%d" % l] = np.ascontiguousarray(
                w[:, 1536:1552].reshape(8, 128, 16).transpose(1, 0, 2)).reshape(128, 128)
            small["wlraug%d" % l] = np.ascontiguousarray(
                np.concatenate([inp["we_lr"][e], inp["be_lr"][e][None, :]], axis=0))
            pp.add("g_gla%d" % l, _fm(inp["ge_gla"][e]))
            pp.add("w_dw%d" % l, inp["we_dw"][e].T.reshape(4, 128, W_B).transpose(1, 0, 2))
            pp.add("b_dw%d" % l, _fm(inp["be_dw"][e]))
            pp.add("g_cn%d" % l, _fm(inp["ge_cn"][e]))
            pp.add("b_cn%d" % l, _fm(inp["be_cn"][e]))
        else:
            o = l // 2
            w = inp["wo_in"][o]
            slabs += [_slab_in(w[:, 0:512]), _slab_in(w[:, 512:1024]), _slab_in(w[:, 1024:1536]),
                      _slab_in(w[:, 1536:2048]), _slab_in(w[:, 2048:2560]), _slab_in(w[:, 2560:3072])]
            wo = inp["wo_out"][o]
            slabs += [_slab_in(wo[:, 0:512]), _slab_in(wo[:, 512:1024])]
            small["wbain%d" % l] = np.ascontiguousarray(
                w[:, 3072:3080].reshape(8, 128, 8).transpose(1, 0, 2)).reshape(128, 64)
            for nm, key in (("wrg", "wo_rg"), ("wig", "wo_ig")):
                g = inp[key][o]
                bd = np.zeros((4, 128, 128), np.float32)
                for hh in range(8):
                    c, r = hh // 2, (hh % 2) * 64
                    bd[c, r:r + 64, r:r + 64] = g[hh]
                small["%s%d" % (nm, l)] = np.ascontiguousarray(bd.transpose(1, 0, 2)).reshape(128, 512)
            pp.add("w_cv%d" % l, inp["wo_conv"][o].T.reshape(16, 128, W_S).transpose(1, 0, 2))
            pp.add("b_cv%d" % l, _fm(inp["bo_conv"][o]))
            pp.add("b_rg%d" % l, _fm(inp["bo_rg"][o]))
            pp.add("b_ig%d" % l, _fm(inp["bo_ig"][o]))
            pp.add("lam%d" % l, _fm(inp["lam_lru"][o]))
            col = np.zeros((128, 2), np.float32)
            col[0:H_D, 0] = inp["a_log"][o]
            col[0:H_D, 1] = inp["dt_bias"][o]
            pp.add("hd%d" % l, col)
            pp.add("g_dl%d" % l, _fm(inp["go_delta"][o]))
        wu = inp["w_up"][l]
        for s in range(11):
            cols = np.zeros((1024, 512), np.float32)
            for jj in range(2):
                j = 2 * s + jj
                if j < NFF:
                    cols[:, jj * 128:(jj + 1) * 128] = wu[:, j * 128:(j + 1) * 128]
                    cols[:, (2 + jj) * 128:(3 + jj) * 128] = wu[:, D_FF + j * 128:D_FF + (j + 1) * 128]
            slabs.append(_slab_in(cols))
        wd = inp["w_down"][l]
        for n in range(8):
            slabs.append(_slab_down(wd[:, n * 128:(n + 1) * 128]))
        pp.add("w_fdw%d" % l, inp["w_fdw"][l].T.reshape(NFF, 128, W_F).transpose(1, 0, 2))
        pp.add("b_fdw%d" % l, _fm(inp["b_fdw"][l]))
        for nm in ("ln1_g", "ln1_b", "ln2_g", "ln2_b"):
            pp.add("%s%d" % (nm, l), _fm(inp[nm][l]))
    return np.stack(slabs, 0), pp, small


NCONST = 128 + 256 * 4 + 512 * 2 + 512


def host_consts():
    ident = np.eye(128, dtype=np.float32)
    s = np.arange(64)
    U = (s[:, None] <= s[None, :]).astype(np.float32)
    Us = (s[:, None] < s[None, :]).astype(np.float32)
    Ls = (s[:, None] > s[None, :]).astype(np.float32)
    c = np.zeros((128, NCONST), np.float32)
    c[:, 0:128] = ident
    c[0:64, 128:384] = np.tile(U, (1, 4))
    c[0:64, 384:640] = np.tile(Us, (1, 4))
    c[0:64, 640:896] = np.tile(Ls, (1, 4))
    c[0:64, 896:1152] = np.tile(np.eye(64, dtype=np.float32), (1, 4))
    cm = np.ones(512, np.float32)
    cm[::64] = 0.0
    c[0:4, 1152:1664] = cm[None, :]
    for h in range(4):
        c[h, 1664 + h * 128:1664 + (h + 1) * 128] = 1.0
    c[0:64, 2176:2432] = np.tile((U - 1.0) * 30000.0, (1, 4))
    c[0:64, 2432:2688] = np.tile((U.T - 1.0) * 30000.0, (1, 4))
    return c


class Builder:
    def __init__(self, cfg, pp_off, npf, nslab_total, small_shapes):
        self.cfg = cfg
        self.depth = cfg["depth"]
        self.T = cfg["T"]
        self.S = cfg["seq"]
        self.has_sample = cfg["sample"]
        self.pp_off = pp_off
        nc = self.nc = bass.Bass("TRN2", target_bir_lowering=False)
        self.P = Prog(nc)
        T = self.T
        d = self.dram = {}
        depth = self.depth

        def din(name, shape):
            d[name] = nc.dram_tensor(name, list(shape), F32, kind="ExternalInput").ap()

        def dout(name, shape):
            d[name] = nc.dram_tensor(name, list(shape), F32, kind="ExternalOutput").ap()
        self.outs = []
        din("wslabs", (nslab_total, 128, SLAB))
        self.wbf = nc.dram_tensor("wbf", [nslab_total, 128, SLAB], BF16, kind="Internal").ap()
        self.wbf_v = [newV(self.wbf[g]) for g in range(nslab_total)]
        din("pf", (128, npf))
        din("consts", (128, NCONST))
        for k, shp in small_shapes.items():
            din(k, shp)
        if self.S:
            din("xp", (D_MODEL, self.S))
            dout("yp", (D_MODEL, self.S))
        if self.has_sample:
            din("xs", (D_MODEL, 64))
            dout("ys", (D_MODEL, 64))
        for grp in (["p"] if self.S else []) + (["s"] if self.has_sample else []):
            for l in range(depth):
                if l % 2 == 0:
                    dout("%s%d_gla" % (grp, l), (H_A, DK_A, DV_A))
                    dout("%s%d_dw" % (grp, l), (128, 4, W_B - 1))
                else:
                    dout("%s%d_lru" % (grp, l), (128, 4))
                    dout("%s%d_delta" % (grp, l), (H_D, DK_D, DV_D))
                    dout("%s%d_conv" % (grp, l), (128, 16, W_S - 1))
                dout("%s%d_ffn" % (grp, l), (128, NFF, W_F - 1))
        if self.has_sample:
            for l in range(depth):
                if l % 2 == 0:
                    din("i%d_gla" % l, (H_A, DK_A, DV_A))
                    din("i%d_dw" % l, (128, 4, W_B - 1))
                else:
                    din("i%d_lru" % l, (128, 4))
                    din("i%d_delta" % l, (H_D, DK_D, DV_D))
                    din("i%d_conv" % l, (128, 16, W_S - 1))
                din("i%d_ffn" % l, (128, NFF, W_F - 1))
        if cfg.get('dbg'):
            dout('dbg', (128, 8, self.T))
        self.small_shapes = small_shapes
        self.npf = npf
        self.nslab_total = nslab_total

    def mm(self, out, lhsT, rhs, start, stop):
        self.P.op("pe", lambda h, o=out.ap, a=lhsT.ap, b=rhs.ap, s=start, e=stop: h.matmul(o, a, b, start=s, stop=e),
                  [lhsT, rhs], [out], inc=True)

    def act(self, out, in_, func, scale=1.0, bias=0.0, extra=()):
        sc = scale.ap if isinstance(scale, V) else scale
        bi = bias.ap if isinstance(bias, V) else bias
        rd = [in_] + [x for x in (scale, bias) if isinstance(x, V)] + list(extra)
        self.P.op("act", lambda h, o=out.ap, i=in_.ap, f=func, s=sc, b=bi: h.activation(out=o, in_=i, func=f, bias=b, scale=s),
                  rd, [out])

    def tt(self, out, in0, in1, op, eng="dve"):
        self.P.op(eng, lambda h, o=out.ap, a=in0.ap, b=in1.ap, p=op: h.tensor_tensor(out=o, in0=a, in1=b, op=p),
                  [in0, in1], [out])

    def ts(self, out, in0, s1, s2, op0, op1=None, eng="dve"):
        a1 = s1.ap if isinstance(s1, V) else s1
        a2 = s2.ap if isinstance(s2, V) else s2
        rd = [in0] + [x for x in (s1, s2) if isinstance(x, V)]
        if op1 is None:
            self.P.op(eng, lambda h, o=out.ap, a=in0.ap, x=a1, p=op0: h.tensor_scalar(out=o, in0=a, scalar1=x, scalar2=None, op0=p),
                      rd, [out])
        else:
            self.P.op(eng, lambda h, o=out.ap, a=in0.ap, x=a1, y=a2, p=op0, q=op1: h.tensor_scalar(out=o, in0=a, scalar1=x, scalar2=y, op0=p, op1=q),
                      rd, [out])

    def stt(self, out, in0, sc, in1, op0, op1, eng="dve"):
        a1 = sc.ap if isinstance(sc, V) else sc
        rd = [in0, in1] + ([sc] if isinstance(sc, V) else [])
        self.P.op(eng, lambda h, o=out.ap, a=in0.ap, x=a1, b=in1.ap, p=op0, q=op1: h.scalar_tensor_tensor(out=o, in0=a, scalar=x, in1=b, op0=p, op1=q),
                  rd, [out])

    def cp(self, out, in_, eng="dve"):
        self.P.op(eng, lambda h, o=out.ap, i=in_.ap: h.tensor_copy(out=o, in_=i), [in_], [out])

    def recip(self, out, in_):
        self.P.op("dve", lambda h, o=out.ap, i=in_.ap: h.reciprocal(out=o, in_=i), [in_], [out])

    def memset(self, out, val, eng="dve"):
        self.P.op(eng, lambda h, o=out.ap, v=val: h.memset(o, v), [], [out])

    def scan(self, out, d0, d1, init, op0, op1):
        ia = init.ap if isinstance(init, V) else init
        rd = [d0, d1] + ([init] if isinstance(init, V) else [])
        self.P.op("dve", lambda h, o=out.ap, a=d0.ap, b=d1.ap, i=ia, p=op0, q=op1: h.tensor_tensor_scan(out=o, data0=a, data1=b, initial=i, op0=p, op1=q),
                  rd, [out])

    def dma(self, eng, out, in_, reads=(), writes=(), is_output=False, **kw):
        oa = out.ap if isinstance(out, V) else out
        ia = in_.ap if isinstance(in_, V) else in_
        rd = list(reads) + ([in_] if isinstance(in_, V) else [])
        wr = list(writes) + ([out] if isinstance(out, V) else [])
        self.P.dma(eng, oa, ia, rd, wr, is_output=is_output, **kw)

    def ck(self, lvl):
        if self.cfg.get('cut', 99) == lvl:
            raise Cut()

    def bank(self):
        b = self.banks[self.bank_rr]
        self.bank_rr = (self.bank_rr + 1) % 8
        return b

    def pfv(self, name, l):
        off, n = self.pp_off["%s%d" % (name, l)]
        return self.pf[:, off:off + n]

    def plan_slabs(self, ntile_calls):
        order = []
        base = 0
        self.layer_base = []
        for l in range(self.depth):
            self.layer_base.append(base)
            base += len(layer_slab_names(l))
        for _ in range(ntile_calls):
            for l in range(self.depth):
                for i, nm in enumerate(layer_slab_names(l)):
                    order.append((l, nm, self.layer_base[l] + i))
        self.slab_order = order
        self.slab_issued = 0
        self.slab_next = 0

    def slab(self, l, name):
        k = self.slab_next
        ol, onm, _ = self.slab_order[k]
        assert (ol, onm) == (l, name), ((ol, onm), (l, name))
        lim = min(len(self.slab_order), k + NSLOT)
        while self.slab_issued < lim:
            n = self.slab_issued
            _, _, gi = self.slab_order[n]
            self.dma("sp", self.slots[n % NSLOT], self.wbf_v[gi])
            self.slab_issued += 1
        self.slab_next += 1
        return self.slots[k % NSLOT]

    def layernorm(self, X, l, which, T):
        g = self.pfv(which + "_g", l)
        bb = self.pfv(which + "_b", l)
        A, Ab = self.af, self.ab
        assert 2 * T <= 512
        pss = self.bank()
        psm, psq = pss[:, 0:T], pss[:, T:2 * T]
        zzs = [Ab.alloc(128, [2, T]) for _ in range(2)]
        for n in range(8):
            zz = zzs[n % 2]
            self.act(zz[:, 0, :], X[n], AF.Copy)
            self.act(zz[:, 1, :], X[n], AF.Square)
            self.mm(pss[:, 0:2 * T], self.ones_b, zz.re("p a b -> p (a b)"), n == 0, n == 7)
        mu = A.alloc(128, [T])
        msq = A.alloc(128, [T])
        var = A.alloc(128, [T])
        rs = A.alloc(128, [T])
        self.act(mu, psm, AF.Copy, scale=1.0 / D_MODEL)
        self.tt(msq, mu, mu, ALU.mult)
        self.stt(var, psq, 1.0 / D_MODEL, msq, ALU.mult, ALU.subtract)
        self.act(var, var, AF.Sqrt, bias=self.eps_c)
        self.recip(rs, var)
        t1 = [A.alloc(128, [T]) for _ in range(2)]
        for n in range(8):
            t = t1[n % 2]
            self.tt(t, X[n], mu, ALU.subtract)
            self.tt(t, t, rs, ALU.mult)
            self.act(self.xb[n][:, 0:T], t, AF.Identity, scale=g[:, n:n + 1], bias=bb[:, n:n + 1])
            self.act(X[n], t, AF.Identity, scale=g[:, n:n + 1], bias=bb[:, n:n + 1])

    def ffn(self, X, l, T, st):
        A, Ab = self.af, self.ab
        wf = self.pfv("w_fdw", l)
        bf = self.pfv("b_fdw", l)
        ghal = st["ghal"][l]
        hbuf = [Ab.alloc(128, [T]) for _ in range(NFF)]
        gb = [A.alloc(128, [T + 2]) for _ in range(2)]
        acc = [A.alloc(128, [T]) for _ in range(2)]
        ge = [A.alloc(128, [T]) for _ in range(3)]
        pend = None
        for s in range(11):
            slot = self.slab(l, "up%d" % s).re("p (k n) -> p k n", k=8)
            for jj in range(2):
                j = 2 * s + jj
                if j >= NFF:
                    continue
                psg, psu = self.bank(), self.bank()
                for k in range(8):
                    self.mm(psg[:, 0:T], slot[:, k, jj * 128:(jj + 1) * 128], self.xb[k][:, 0:T], k == 0, k == 7)
                for k in range(8):
                    self.mm(psu[:, 0:T], slot[:, k, (2 + jj) * 128:(3 + jj) * 128], self.xb[k][:, 0:T], k == 0, k == 7)
                g_, a_, e_ = gb[j % 2], acc[j % 2], ge[j % 3]
                self.cp(g_[:, 0:2], ghal[:, j, :], eng="pool")
                self.act(g_[:, 2:T + 2], psg[:, 0:T], AF.Copy)
                self.cp(ghal[:, j, :], g_[:, T:T + 2], eng="pool")
                self.act(a_, g_[:, 0:T], AF.Identity, scale=wf[:, 3 * j:3 * j + 1], bias=bf[:, j:j + 1])
                self.stt(a_, g_[:, 1:T + 1], wf[:, 3 * j + 1:3 * j + 2], a_, ALU.mult, ALU.add)
                self.stt(a_, g_[:, 2:T + 2], wf[:, 3 * j + 2:3 * j + 3], a_, ALU.mult, ALU.add)
                self.act(e_, a_, AF.Gelu_apprx_tanh)
                if pend is not None:
                    self.tt(pend[0], pend[1], pend[2], ALU.mult)
                pend = (hbuf[j], e_, psu[:, 0:T])
        self.tt(pend[0], pend[1], pend[2], ALU.mult)
        for n in range(8):
            slot = self.slab(l, "dn%d" % n)
            ps = self.bank()
            for j in range(NFF):
                self.mm(ps[:, 0:T], slot[:, j * 128:(j + 1) * 128], hbuf[j], j == 0, j == NFF - 1)
            self.stt(X[n], X[n], ALPHA, ps[:, 0:T], ALU.mult, ALU.add)

    def even_mixer(self, X, l, T, st):
        A, Ab = self.af, self.ab
        NCH = T // 64
        xb = self.xb
        S_f, S_b, uhal = st["S_f"][l], st["S_b"][l], st["uhal"][l]
        slot = self.slab(l, "qk").re("p (k n) -> p k n", k=8)
        qT = A.alloc(64, [4, T])
        kT = A.alloc(64, [4, T])
        for i in range(8):
            ps = self.bank()
            for k in range(8):
                self.mm(ps[0:64, 0:T], slot[:, k, i * 64:(i + 1) * 64], xb[k][:, 0:T], k == 0, k == 7)
            if i < 4:
                self.act(qT[:, i, :], ps[0:64, 0:T], AF.Copy, scale=DK_A ** -0.5)
            else:
                self.cp(kT[:, i - 4, :], ps[0:64, 0:T])
        self.ck(1)
        lrT = Ab.alloc(17, [T])
        self.memset(lrT, 1.0)
        ps = self.bank()
        wl = self.wlrin[l].re("p (k n) -> p k n", k=8)
        for k in range(8):
            self.mm(ps[0:16, 0:T], wl[:, k, :], xb[k][:, 0:T], k == 0, k == 7)
        self.cp(lrT[0:16, :], ps[0:16, 0:T])
        self.ck(2)
        slot = self.slab(l, "v").re("p (k n) -> p k n", k=8)
        vtok = [Ab.alloc(64, [512]) for _ in range(NCH)]
        sp_tok = [A.alloc(64, [256]) for _ in range(2)]
        e1 = [A.alloc(64, [256]) for _ in range(2)]
        oT = A.alloc(128, [4, T])
        ep = [A.alloc(64, [4, 64]) for _ in range(2)]
        en = [A.alloc(64, [4, 64]) for _ in range(2)]
        qd = [Ab.alloc(64, [4, 64]) for _ in range(2)]
        kd = [Ab.alloc(64, [4, 64]) for _ in range(2)]
        kk = [Ab.alloc(64, [4, 64]) for _ in range(2)]
        scm = [Ab.alloc(64, [4, 64]) for _ in range(2)]
        kkt = [Ab.alloc(64, [256]) for _ in range(2)]
        def gla_a(c):
            cs = slice(c * 64, (c + 1) * 64)
            r = c % 2
            ps = self.bank()
            for k in range(8):
                self.mm(ps[0:64, 0:512], xb[k][:, cs], slot[:, k, :], k == 0, k == 7)
            self.act(vtok[c], ps[0:64, 0:512], AF.Copy)
            ps2 = self.bank()
            self.mm(ps2[0:64, 0:256], lrT[0:17, cs], self.wlraug[l], True, True)
            self.act(e1[r], ps2[0:64, 0:256], AF.Exp, scale=-1.0)
            yield
            self.act(sp_tok[r], e1[r], AF.Ln, bias=1.0)
            yield
            ps3 = self.bank()
            for h in range(4):
                self.mm(ps3[0:64, h * 64:(h + 1) * 64], sp_tok[r][:, h * 64:(h + 1) * 64], self.U_f, True, True)
            p3 = ps3[0:64, 0:256].re("p (a b) -> p a b", a=4)
            self.act(ep[r], p3, AF.Exp, scale=-1.0 / 16.0)
            self.act(en[r], p3, AF.Exp, scale=1.0 / 16.0)
            yield
            self.tt(qd[r], qT[:, :, cs], ep[r], ALU.mult)
            self.tt(kd[r], kT[:, :, cs], en[r], ALU.mult)
            yield
            for h in range(4):
                self.ts(kk[r][:, h, :], kd[r][:, h, :], ep[r][:, h, 63:64], None, ALU.mult)
            ps4 = self.bank()
            for h in range(4):
                self.mm(ps4[0:64, h * 64:(h + 1) * 64], kd[r][:, h, :], qd[r][:, h, :], True, True)
            self.tt(scm[r], ps4[0:64, 0:256].re("p (a b) -> p a b", a=4), self.mask4.re("p (a b) -> p a b", a=4), ALU.mult)
            yield
            ps5 = self.bank()
            for h in range(4):
                self.mm(ps5[0:64, h * 64:(h + 1) * 64], kk[r][:, h, :], self.ident_b[0:64, 0:64], True, True)
            self.act(kkt[r], ps5[0:64, 0:256], AF.Copy)
            yield

        def gla_b(c):
            cs = slice(c * 64, (c + 1) * 64)
            r = c % 2
            ps6 = self.bank()
            for h in range(4):
                self.mm(ps6[:, h * 64:(h + 1) * 64], S_b[:, h, :], qd[r][:, h, :], True, False)
                self.mm(ps6[:, h * 64:(h + 1) * 64], vtok[c][:, h * 128:(h + 1) * 128], scm[r][:, h, :], False, True)
            self.act(oT[:, :, cs], ps6[:, 0:256].re("p (a b) -> p a b", a=4), AF.Copy)
            ps7 = self.bank()
            for h in range(4):
                self.mm(ps7[0:64, h * 128:(h + 1) * 128], kkt[r][:, h * 64:(h + 1) * 64], vtok[c][:, h * 128:(h + 1) * 128], True, True)
            for h in range(4):
                self.stt(S_f[:, h, :], S_f[:, h, :], ep[r][:, h, 63:64], ps7[0:64, h * 128:(h + 1) * 128], ALU.mult, ALU.add)
            self.act(S_b, S_f, AF.Copy)

        for c0 in range(0, NCH, 2):
            cl = [c for c in (c0, c0 + 1) if c < NCH]
            alive = [gla_a(c) for c in cl]
            while alive:
                for g_ in list(alive):
                    try:
                        next(g_)
                    except StopIteration:
                        alive.remove(g_)
            for c in cl:
                gla_b(c)
            self.ck(8)
        slot = self.slab(l, "gate").re("p (k n) -> p k n", k=8)
        sg = A.alloc(128, [4, T])
        for h in range(4):
            ps = self.bank()
            for k in range(8):
                self.mm(ps[:, 0:T], slot[:, k, h * 128:(h + 1) * 128], xb[k][:, 0:T], k == 0, k == 7)
            self.act(sg[:, h, :], ps[:, 0:T], AF.Silu)
        gg = self.pfv("g_gla", l)
        R = [Ab.alloc(128, [T]) for _ in range(8)]
        sq = [Ab.alloc(128, [T]) for _ in range(2)]
        sd = [A.alloc(128, [T]) for _ in range(2)]
        for h in range(4):
            self.act(sq[h % 2], oT[:, h, :], AF.Square)
            ps = self.bank()
            self.mm(ps[:, 0:T], self.ones_b, sq[h % 2], True, True)
            self.act(sd[h % 2], ps[:, 0:T], AF.Sqrt, scale=1.0 / DV_A, bias=self.eps_c)
            self.recip(sd[h % 2], sd[h % 2])
            self.stt(sd[h % 2], oT[:, h, :], gg[:, h:h + 1], sd[h % 2], ALU.mult, ALU.mult)
            self.tt(R[h], sd[h % 2], sg[:, h, :], ALU.mult)
        self.ck(9)
        up = A.alloc(128, [4, T + 30])
        self.cp(up[:, :, 0:30], uhal, eng="pool")
        self.ck(91)
        slot = self.slab(l, "glua").re("p (k n) -> p k n", k=8)
        for j in range(4):
            ps = self.bank()
            for k in range(8):
                self.mm(ps[:, 0:T], slot[:, k, j * 128:(j + 1) * 128], xb[k][:, 0:T], k == 0, k == 7)
            self.act(up[:, j, 30:30 + T], ps[:, 0:T], AF.Copy)
        slot = self.slab(l, "glub").re("p (k n) -> p k n", k=8)
        sgb = [A.alloc(128, [T]) for _ in range(2)]
        for j in range(4):
            ps = self.bank()
            for k in range(8):
                self.mm(ps[:, 0:T], slot[:, k, j * 128:(j + 1) * 128], xb[k][:, 0:T], k == 0, k == 7)
            self.act(sgb[j % 2], ps[:, 0:T], AF.Sigmoid)
            self.tt(up[:, j, 30:30 + T], up[:, j, 30:30 + T], sgb[j % 2], ALU.mult)
        self.cp(uhal, up[:, :, T:T + 30], eng="pool")
        self.ck(92)
        wdw = self.pfv("w_dw", l)
        bdw = self.pfv("b_dw", l)
        cv = [A.alloc(128, [T]) for _ in range(4)]
        pss = self.bank()
        psm, psq = pss[:, 0:T], pss[:, T:2 * T]
        cvz = [Ab.alloc(128, [2, T]) for _ in range(2)]
        for j in range(4):
            ce = "dve"
            self.ts(cv[j], up[:, j, 0:T], wdw[:, j * 31:j * 31 + 1], bdw[:, j:j + 1], ALU.mult, ALU.add, eng=ce)
            for tap in range(1, W_B):
                self.stt(cv[j], up[:, j, tap:tap + T], wdw[:, j * 31 + tap:j * 31 + tap + 1], cv[j], ALU.mult, ALU.add, eng=ce)
        for idx, j in enumerate((0, 2, 1, 3)):
            self.act(cvz[idx % 2][:, 0, :], cv[j], AF.Copy)
            self.act(cvz[idx % 2][:, 1, :], cv[j], AF.Square)
            self.mm(pss[:, 0:2 * T], self.ones_b, cvz[idx % 2].re("p a b -> p (a b)"), idx == 0, idx == 3)
        self.ck(95)
        mu = A.alloc(128, [T])
        msq = A.alloc(128, [T])
        var = A.alloc(128, [T])
        self.act(mu, psm, AF.Copy, scale=1.0 / D_B)
        self.tt(msq, mu, mu, ALU.mult)
        self.stt(var, psq, 1.0 / D_B, msq, ALU.mult, ALU.subtract)
        self.act(var, var, AF.Sqrt, bias=self.eps_c)
        self.recip(var, var)
        gcn = self.pfv("g_cn", l)
        self.ck(96)
        bcn = self.pfv("b_cn", l)
        for j in range(4):
            self.tt(cv[j], cv[j], mu, ALU.subtract)
            self.tt(cv[j], cv[j], var, ALU.mult)
            self.act(R[4 + j], cv[j], AF.Silu, scale=gcn[:, j:j + 1], bias=bcn[:, j:j + 1])
        self.ck(10)
        self.out_proj(X, l, T, R)

    def out_proj(self, X, l, T, R):
        for s in range(2):
            slot = self.slab(l, "out%d" % s).re("p (k n) -> p k n", k=8)
            for jj in range(4):
                n = s * 4 + jj
                ps = self.bank()
                for k in range(8):
                    self.mm(ps[:, 0:T], slot[:, k, jj * 128:(jj + 1) * 128], R[k], k == 0, k == 7)
                self.stt(X[n], X[n], ALPHA, ps[:, 0:T], ALU.mult, ALU.add)

    def log1p_series(self, A, e, parts, shape):
        den = A.alloc(parts, shape)
        s_ = A.alloc(parts, shape)
        s2 = A.alloc(parts, shape)
        p = A.alloc(parts, shape)
        self.ts(den, e, 2.0, None, ALU.add)
        self.recip(den, den)
        self.tt(s_, e, den, ALU.mult)
        self.tt(s2, s_, s_, ALU.mult)
        self.ts(p, s2, 1.0 / 11.0, 1.0 / 9.0, ALU.mult, ALU.add)
        for cst in (1.0 / 7.0, 1.0 / 5.0, 1.0 / 3.0, 1.0):
            self.tt(p, p, s2, ALU.mult)
            self.ts(p, p, cst, None, ALU.add)
        self.tt(p, p, s_, ALU.mult)
        return p

    def odd_prologue(self, l, sbf):
        A = self.af
        lam = self.pfv("lam", l)
        e = A.alloc(128, [4])
        self.act(e, lam, AF.Exp, scale=-1.0)
        p = self.log1p_series(A, e, 128, [4])
        L4 = newV(sbf("L4_%d" % l, [128, 4], F32)[:, :])
        self.ts(L4, p, -8.0, None, ALU.mult)
        self.L4[l] = L4
        hd = self.pfv("hd", l)
        negA = newV(sbf("negA_%d" % l, [4, 1], F32)[:, :])
        self.act(negA, hd[0:4, 0:1], AF.Exp)
        self.ts(negA, negA, -1.0, None, ALU.mult)
        self.negA[l] = negA

    def rms_gate(self, oT, gname, slabname, l, T, R, base, dv):
        A, Ab = self.af, self.ab
        slot = self.slab(l, slabname).re("p (k n) -> p k n", k=8)
        sg = A.alloc(128, [4, T])
        for h in range(4):
            ps = self.bank()
            for k in range(8):
                self.mm(ps[:, 0:T], slot[:, k, h * 128:(h + 1) * 128], self.xb[k][:, 0:T], k == 0, k == 7)
            self.act(sg[:, h, :], ps[:, 0:T], AF.Silu)
        gg = self.pfv(gname, l)
        sq = [Ab.alloc(128, [T]) for _ in range(2)]
        sd = [A.alloc(128, [T]) for _ in range(2)]
        for h in range(4):
            self.act(sq[h % 2], oT[:, h, :], AF.Square)
            ps = self.bank()
            self.mm(ps[:, 0:T], self.ones_b, sq[h % 2], True, True)
            self.act(sd[h % 2], ps[:, 0:T], AF.Sqrt, scale=1.0 / dv, bias=self.eps_c)
            self.recip(sd[h % 2], sd[h % 2])
            self.stt(sd[h % 2], oT[:, h, :], gg[:, h:h + 1], sd[h % 2], ALU.mult, ALU.mult)
            self.tt(R[base + h], sd[h % 2], sg[:, h, :], ALU.mult)

    def odd_mixer(self, X, l, T, st):
        A, Ab = self.af, self.ab
        NCH = T // 64
        xb = self.xb
        chal, hl, Sd = st["chal"][l], st["h_lru"][l], st["Sd_f"][l]
        wcv = self.pfv("w_cv", l)
        bcv = self.pfv("b_cv", l)
        R = [Ab.alloc(128, [T]) for _ in range(8)]
        cvo = {nm: A.alloc(128, [4, T]) for nm in ("q", "k", "v")}
        KA = A.alloc(128, [4, T])
        KBN = A.alloc(128, [4, T])
        QA = A.alloc(128, [4, T])
        oT = A.alloc(128, [4, T])
        eG = A.alloc(128, [4, NCH])
        Gb = A.alloc(64, [4, T])
        Gbn = A.alloc(64, [4, T])
        gcol = A.alloc(64, [NCH, 4])
        m0 = A.mark()
        cin = [A.alloc(128, [4, T + 3]) for _ in range(2)]
        cvo["xl"] = A.alloc(128, [4, T])
        for si, nm in enumerate(("xl", "q", "k", "v")):
            slot = self.slab(l, nm).re("p (k n) -> p k n", k=8)
            ci = cin[si % 2]
            self.cp(ci[:, :, 0:3], chal[:, si * 4:(si + 1) * 4, :], eng="pool")
            for j in range(4):
                ps = self.bank()
                for k in range(8):
                    self.mm(ps[:, 0:T], slot[:, k, j * 128:(j + 1) * 128], xb[k][:, 0:T], k == 0, k == 7)
                self.act(ci[:, j, 3:3 + T], ps[:, 0:T], AF.Copy)
            self.cp(chal[:, si * 4:(si + 1) * 4, :], ci[:, :, T:T + 3], eng="pool")
            dst = cvo[nm]
            for j in range(4):
                jj = si * 4 + j
                ce = "dve"
                self.ts(dst[:, j, :], ci[:, j, 0:T], wcv[:, jj * 4:jj * 4 + 1], bcv[:, jj:jj + 1], ALU.mult, ALU.add, eng=ce)
                for tap in range(1, W_S):
                    self.stt(dst[:, j, :], ci[:, j, tap:tap + T], wcv[:, jj * 4 + tap:jj * 4 + tap + 1], dst[:, j, :], ALU.mult, ALU.add, eng=ce)
                if si > 0:
                    self.act(dst[:, j, :], dst[:, j, :], AF.Silu)
        self.ck(21)
        xc = cvo["xl"]
        xcb = Ab.alloc(128, [4, T])
        self.cp(xcb, xc)
        L4 = self.L4[l]
        wrg = self.wrg[l].re("p (c n) -> p c n", c=4)
        wig = self.wig[l].re("p (c n) -> p c n", c=4)
        brg = self.pfv("b_rg", l)
        big = self.pfv("b_ig", l)
        slot = self.slab(l, "gc").re("p (k n) -> p k n", k=8)
        tb = [[A.alloc(128, [T]) for _ in range(2)] for _ in range(7)]
        for c in range(4):
            r_, ig_, t_, rd_, a_, om_, h_ = [tb[i][c % 2] for i in range(7)]
            ps = self.bank()
            self.mm(ps[:, 0:T], wrg[:, c, :], xcb[:, c, :], True, True)
            self.act(r_, ps[:, 0:T], AF.Sigmoid, bias=brg[:, c:c + 1])
            ps = self.bank()
            self.mm(ps[:, 0:T], wig[:, c, :], xcb[:, c, :], True, True)
            self.act(ig_, ps[:, 0:T], AF.Sigmoid, bias=big[:, c:c + 1])
            self.act(t_, r_, AF.Tanh, scale=L4[:, c:c + 1])
            self.ts(rd_, t_, -1.0, 1.0, ALU.mult, ALU.add)
            self.recip(rd_, rd_)
            self.stt(a_, t_, 1.0, rd_, ALU.add, ALU.mult)
            self.stt(om_, t_, -4.0, rd_, ALU.mult, ALU.mult)
            self.tt(om_, om_, rd_, ALU.mult)
            self.act(om_, om_, AF.Sqrt)
            self.tt(om_, om_, ig_, ALU.mult)
            self.tt(om_, om_, xc[:, c, :], ALU.mult)
            self.scan(h_, a_, om_, hl[:, c:c + 1], ALU.mult, ALU.add)
            self.cp(hl[:, c:c + 1], h_[:, T - 1:T])
            ps = self.bank()
            for k in range(8):
                self.mm(ps[:, 0:T], slot[:, k, c * 128:(c + 1) * 128], xb[k][:, 0:T], k == 0, k == 7)
            self.act(r_, ps[:, 0:T], AF.Gelu_apprx_tanh)
            self.tt(R[c], h_, r_, ALU.mult)
        self.ck(22)
        self.P.fence()
        A.release(m0)
        qs, ks, vs = cvo["q"], cvo["k"], cvo["v"]
        wba = self.wbain[l].re("p (k n) -> p k n", k=8)
        psb, psa = self.bank(), self.bank()
        for k in range(8):
            self.mm(psb[0:4, 0:T], wba[:, k, 0:4], xb[k][:, 0:T], k == 0, k == 7)
        for k in range(8):
            self.mm(psa[0:4, 0:T], wba[:, k, 4:8], xb[k][:, 0:T], k == 0, k == 7)
        ROWS = A.alloc(4, [4, T])
        hd = self.pfv("hd", l)
        self.act(ROWS[:, 3, :], psb[0:4, 0:T], AF.Sigmoid)
        y = A.alloc(4, [T])
        ay = A.alloc(4, [T])
        self.ts(y, psa[0:4, 0:T], hd[0:4, 1:2], None, ALU.add)
        self.act(ay, y, AF.Abs)
        self.act(ay, ay, AF.Exp, scale=-1.0)
        p = self.log1p_series(A, ay, 4, [T])
        self.ts(y, y, 0.0, None, ALU.max)
        self.stt(y, p, 2.0, y, ALU.mult, ALU.add)
        self.ts(y, y, self.negA[l][0:4, 0:1], None, ALU.mult)
        gc = ROWS[:, 1, :]
        self.scan(gc, self.cmask[:, 0:T], y, 0.0, ALU.mult, ALU.add)
        self.act(ROWS[:, 0, :], gc, AF.Exp)
        self.tt(ROWS[:, 2, :], ROWS[:, 3, :], ROWS[:, 0, :], ALU.mult)
        self.ck(23)
        for c in range(NCH):
            psT = self.bank()
            self.mm(psT[0:64, 0:4], ROWS[:, 1, c * 64:(c + 1) * 64], self.ident_f[0:4, 0:4], True, True)
            self.cp(gcol[:, c, :], psT[0:64, 0:4])
        sqb = [Ab.alloc(128, [T]) for _ in range(2)]
        rn = [A.alloc(128, [T]) for _ in range(2)]
        for h in range(4):
            selh = self.sel[:, h * 128:(h + 1) * 128]
            for i, src in enumerate((qs, ks)):
                self.act(sqb[i], src[:, h, :], AF.Square)
                ps = self.bank()
                self.mm(ps[:, 0:T], self.ones_b, sqb[i], True, True)
                self.act(rn[i], ps[:, 0:T], AF.Sqrt, bias=self.eps6_c)
                self.recip(rn[i], rn[i])
            self.stt(qs[:, h, :], qs[:, h, :], DK_D ** -0.5, rn[0], ALU.mult, ALU.mult)
            self.tt(ks[:, h, :], ks[:, h, :], rn[1], ALU.mult)
            psE = self.bank()
            self.mm(psE[:, 0:T], selh, ROWS[:, 0, :], True, True)
            self.act(eG[:, h, :], psE[:, 0:T].re("p (c t) -> p c t", t=64)[:, :, 63], AF.Copy)
            self.tt(QA[:, h, :], qs[:, h, :], psE[:, 0:T], ALU.mult)
            psB = self.bank()
            self.mm(psB[:, 0:T], selh, ROWS[:, 2, :], True, True)
            self.tt(KA[:, h, :], ks[:, h, :], psB[:, 0:T], ALU.mult)
            psb2 = self.bank()
            self.mm(psb2[:, 0:T], selh, ROWS[:, 3, :], True, True)
            self.tt(vs[:, h, :], vs[:, h, :], psb2[:, 0:T], ALU.mult)
            self.tt(KBN[:, h, :], ks[:, h, :], psb2[:, 0:T], ALU.mult)
            psG = self.bank()
            self.mm(psG[:, 0:T], selh, ROWS[:, 1, :], True, True)
            self.act(Gb[:, h, :], psG[0:64, 0:T], AF.Copy)
            self.act(Gbn[:, h, :], psG[0:64, 0:T], AF.Copy, scale=-1.0)
        self.ck(24)
        self.P.fence()
        A.release(m0)
        BV, KN, QN = vs, ks, qs
        slots2 = []
        for _ in range(2):
            slots2.append({
                "DT": A.alloc(64, [256]), "AT": A.alloc(64, [256]),
                "ring": [A.alloc(64, [256]) for _ in range(8)], "ri": 0,
                "BVt": A.alloc(64, [512]), "KAt": A.alloc(64, [512]), "KBt": A.alloc(64, [512]),
                "U": A.alloc(64, [512]), "WT": A.alloc(128, [256])})
        VN = A.alloc(64, [512])

        def part_a(c, sb_):
            cs = slice(c * 64, (c + 1) * 64)

            def nbuf():
                v = sb_["ring"][sb_["ri"] % 8]
                sb_["ri"] += 1
                return v
            DT, AT = sb_["DT"], sb_["AT"]
            D, DTs, Ds = nbuf(), nbuf(), nbuf()
            for h in range(4):
                hc = slice(h * 64, (h + 1) * 64)
                self.stt(DT[:, hc], Gb[:, h, cs], gcol[:, c, h:h + 1], self.negU4[:, hc], ALU.subtract, ALU.add)
                self.stt(D[:, hc], Gbn[:, h, cs], gcol[:, c, h:h + 1], self.negL4[:, hc], ALU.add, ALU.add)
            yield
            self.act(DT, DT, AF.Exp)
            self.act(D, D, AF.Exp)
            yield
            self.tt(DTs, DT, self.ident4, ALU.subtract)
            self.tt(Ds, D, self.ident4, ALU.subtract)
            psNT, psN, psAT = self.bank(), self.bank(), self.bank()
            for h in range(4):
                hc = slice(h * 64, (h + 1) * 64)
                self.mm(psNT[0:64, hc], KN[:, h, cs], KBN[:, h, cs], True, True)
                self.mm(psN[0:64, hc], KBN[:, h, cs], KN[:, h, cs], True, True)
                self.mm(psAT[0:64, hc], KN[:, h, cs], QN[:, h, cs], True, True)
            NT, N, PT = nbuf(), nbuf(), nbuf()
            self.stt(NT, psNT[0:64, 0:256], -1.0, DTs, ALU.mult, ALU.mult)
            self.stt(N, psN[0:64, 0:256], -1.0, Ds, ALU.mult, ALU.mult)
            self.tt(AT, psAT[0:64, 0:256], DT, ALU.mult)
            self.tt(PT, NT, self.ident4, ALU.add)
            yield
            for lev in range(5):
                psN2 = self.bank()
                for h in range(4):
                    hc = slice(h * 64, (h + 1) * 64)
                    self.mm(psN2[0:64, hc], NT[:, hc], N[:, hc], True, True)
                N2 = nbuf()
                self.act(N2, psN2[0:64, 0:256], AF.Copy)
                if lev < 4:
                    psNT2 = self.bank()
                    for h in range(4):
                        hc = slice(h * 64, (h + 1) * 64)
                        self.mm(psNT2[0:64, hc], N[:, hc], NT[:, hc], True, True)
                    NT2 = nbuf()
                    self.cp(NT2, psNT2[0:64, 0:256])
                else:
                    NT2 = None
                yield
                psP = self.bank()
                for h in range(4):
                    hc = slice(h * 64, (h + 1) * 64)
                    self.mm(psP[0:64, hc], N2[:, hc], PT[:, hc], True, True)
                PT2 = nbuf()
                self.tt(PT2, PT, psP[0:64, 0:256], ALU.add)
                N, NT, PT = N2, NT2, PT2
                yield
            BVt, KAt, KBt, U_sb, WT = sb_["BVt"], sb_["KAt"], sb_["KBt"], sb_["U"], sb_["WT"]
            for src, dst, eng in ((BV, BVt, "act"), (KA, KAt, "dve")):
                psT = self.bank()
                for h in range(4):
                    self.mm(psT[0:64, h * 128:(h + 1) * 128], src[:, h, cs], self.ident_f, True, True)
                if eng == "act":
                    self.act(dst, psT[0:64, 0:512], AF.Copy)
                else:
                    self.cp(dst, psT[0:64, 0:512])
            psT = self.bank()
            for h in range(4):
                self.mm(psT[0:64, h * 128:(h + 1) * 128], KN[:, h, cs], self.ident_f, True, True)
            for h in range(4):
                self.ts(KBt[:, h * 128:(h + 1) * 128], psT[0:64, h * 128:(h + 1) * 128], DT[:, h * 64 + 63:h * 64 + 64], None, ALU.mult)
            yield
            psU = self.bank()
            for h in range(4):
                self.mm(psU[0:64, h * 128:(h + 1) * 128], PT[:, h * 64:(h + 1) * 64], BVt[:, h * 128:(h + 1) * 128], True, True)
            self.act(U_sb, psU[0:64, 0:512], AF.Copy)
            psW = self.bank()
            for h in range(4):
                self.mm(psW[:, h * 64:(h + 1) * 64], KAt[:, h * 128:(h + 1) * 128], PT[:, h * 64:(h + 1) * 64], True, True)
            self.cp(WT, psW[:, 0:256])
            yield

        def part_b(c, sb_):
            cs = slice(c * 64, (c + 1) * 64)
            AT, KBt, U_sb, WT = sb_["AT"], sb_["KBt"], sb_["U"], sb_["WT"]
            psWS = self.bank()
            for h in range(4):
                self.mm(psWS[0:64, h * 128:(h + 1) * 128], WT[:, h * 64:(h + 1) * 64], Sd[:, h, :], True, True)
            self.tt(VN, U_sb, psWS[0:64, 0:512], ALU.subtract)
            psO, psO2 = self.bank(), self.bank()
            for h in range(4):
                hc = slice(h * 64, (h + 1) * 64)
                self.mm(psO[:, hc], Sd[:, h, :], QA[:, h, cs], True, True)
                self.mm(psO2[:, hc], VN[:, h * 128:(h + 1) * 128], AT[:, hc], True, True)
            self.act(oT[:, :, cs], psO[:, 0:256].re("p (a b) -> p a b", a=4), AF.Copy)
            self.tt(oT[:, :, cs], oT[:, :, cs], psO2[:, 0:256].re("p (a b) -> p a b", a=4), ALU.add)
            psS = self.bank()
            for h in range(4):
                self.mm(psS[:, h * 128:(h + 1) * 128], KBt[:, h * 128:(h + 1) * 128], VN[:, h * 128:(h + 1) * 128], True, True)
            for h in range(4):
                self.stt(Sd[:, h, :], Sd[:, h, :], eG[:, h, c:c + 1], psS[:, h * 128:(h + 1) * 128], ALU.mult, ALU.add)

        for c0 in range(0, NCH, 2):
            cl = [c for c in (c0, c0 + 1) if c < NCH]
            gens = [part_a(c, slots2[c - c0]) for c in cl]
            alive = list(gens)
            while alive:
                for g_ in list(alive):
                    try:
                        next(g_)
                    except StopIteration:
                        alive.remove(g_)
            for c in cl:
                part_b(c, slots2[c - c0])
            self.ck(25)
        self.P.fence()
        A.release(m0)
        self._dbg_oT = oT
        self._dbg_QA = QA
        self.rms_gate(oT, "g_dl", "z", l, T, R, 4, DV_D)
        self.ck(26)
        if self.cfg.get('dbg'):
            for n in range(4):
                self.cp(self.dbgbuf[:, n, 0:T], R[n])
            for n in range(4):
                self.cp(self.dbgbuf[:, 4 + n, 0:T], self._dbg_oT[:, n, :])
            self.dma('sp', self.dram['dbg'], self.dbgbuf, is_output=True)
        self.out_proj(X, l, T, R)

    def build(self):
        import contextlib
        nc, T, depth = self.nc, self.T, self.depth
        ntiles = self.S // T
        self.plan_slabs(ntiles + (1 if self.has_sample else 0))
        with contextlib.ExitStack() as es:
            def sb(name, shape, dt):
                return es.enter_context(nc.sbuf_tensor("t_" + name, list(shape), dt))
            AF_SZ = self.cfg.get("arena_f", 19800 * self.T // 256)
            AB_SZ = self.cfg.get("arena_b", 9700 * self.T // 256)
            self.af = Arena(sb("arena_f", [128, AF_SZ], F32), AF_SZ)
            self.ab = Arena(sb("arena_b", [128, AB_SZ], BF16), AB_SZ)
            slots_t = sb("slots", [128, NSLOT, SLAB], BF16)
            self.slots = [newV(slots_t[:, i, :]) for i in range(NSLOT)]
            Xt = [sb("X%d" % i, [128, 8, T], F32) for i in range(2)]
            Xs = [[newV(Xt[i][:, n, :]) for n in range(8)] for i in range(2)]
            xb_t = sb("xb", [128, 8, T], BF16)
            self.xb = [newV(xb_t[:, n, :]) for n in range(8)]
            self.pf = newV(sb("pf", [128, self.npf], F32)[:, :])
            cst = newV(sb("cst", [128, NCONST], F32)[:, :])
            cst_b = newV(sb("cst_b", [128, NCONST], BF16)[:, :])
            self.ones_b = newV(sb("ones_b", [128, 128], BF16)[:, :])
            self.eps_c = newV(sb("eps_c", [128, 1], F32)[:, :])
            self.ident_f = cst[:, 0:128]
            self.ident_b = cst_b[:, 0:128]
            self.U_f = cst[0:64, 128:192]
            self.mask4 = cst[0:64, 128:384]
            self.smask4 = cst[0:64, 384:640]
            self.lmask4 = cst[0:64, 640:896]
            self.ident4 = cst[0:64, 896:1152]
            self.cmask = cst[0:4, 1152:1664]
            self.sel = cst[0:4, 1664:2176]
            self.negU4 = cst[0:64, 2176:2432]
            self.negL4 = cst[0:64, 2432:2688]
            self.eps6_c = newV(sb("eps6_c", [128, 1], F32)[:, :])
            if self.cfg.get('dbg'):
                self.dbgbuf = newV(sb('dbgbuf', [128, 8, T], F32)[:, :, :])
            self.banks = [newV(es.enter_context(nc.psum_tensor("ps%d" % i, [128, 512], F32))[:, :]) for i in range(8)]
            self.bank_rr = 0
            self._ln_zb = [None, None]
            self._ln_zs = [None, None]
            self.wlrin, self.wlraug, self.wbain, self.wrg, self.wig = {}, {}, {}, {}, {}
            self.L4, self.negA = {}, {}
            stage = {}
            for k, shp in self.small_shapes.items():
                stage[k] = newV(sb("st_" + k, list(shp), F32)[:, :])
            st = {"S_f": {}, "S_b": {}, "uhal": {}, "ghal": {}, "h_lru": {}, "Sd_f": {}, "Sd_b": {}, "chal": {}}
            for l in range(depth):
                st["ghal"][l] = newV(sb("ghal%d" % l, [128, NFF, 2], F32)[:, :, :])
                if l % 2 == 0:
                    st["S_f"][l] = newV(sb("S_f%d" % l, [64, 4, 128], F32)[:, :, :])
                    st["S_b"][l] = newV(sb("S_b%d" % l, [64, 4, 128], BF16)[:, :, :])
                    st["uhal"][l] = newV(sb("uhal%d" % l, [128, 4, 30], F32)[:, :, :])
                else:
                    st["h_lru"][l] = newV(sb("hlru%d" % l, [128, 4], F32)[:, :])
                    st["Sd_f"][l] = newV(sb("Sd_f%d" % l, [128, 4, 128], F32)[:, :, :])
                    st["Sd_b"][l] = newV(sb("Sd_b%d" % l, [128, 4, 128], BF16)[:, :, :])
                    st["chal"][l] = newV(sb("chal%d" % l, [128, 16, 3], F32)[:, :, :])
            self.st = st
            d = self.dram
            self.sbuf_left = nc.sbuf_bytes_remaining
            for g in range(self.nslab_total):
                self.dma("pool", self.wbf_v[g], d["wslabs"][g])
            self.dma("sp", self.pf, d["pf"])
            self.dma("sp", cst, d["consts"])
            self.cp(cst_b, cst)
            self.memset(self.ones_b, 1.0)
            self.memset(self.eps_c, EPS)
            self.memset(self.eps6_c, 1e-6)
            for k in self.small_shapes:
                self.dma("sp", stage[k], d[k])
                shp = self.small_shapes[k]
                bt = newV(sb("sb_" + k, list(shp), BF16)[:, :])
                self.cp(bt, stage[k])
                l = int(k[-1])
                if k.startswith("wlrin"):
                    self.wlrin[l] = bt
                elif k.startswith("wlraug"):
                    self.wlraug[l] = bt
                elif k.startswith("wbain"):
                    self.wbain[l] = bt
                elif k.startswith("wrg"):
                    self.wrg[l] = bt
                elif k.startswith("wig"):
                    self.wig[l] = bt

            for l in range(depth):
                if l % 2 == 1:
                    self.odd_prologue(l, sb)

            def zero_states():
                for l in range(depth):
                    self.memset(st["ghal"][l], 0.0)
                    if l % 2 == 0:
                        self.memset(st["S_f"][l], 0.0)
                        self.memset(st["S_b"][l], 0.0)
                        self.memset(st["uhal"][l], 0.0)
                    else:
                        self.memset(st["h_lru"][l], 0.0)
                        self.memset(st["Sd_f"][l], 0.0)
                        self.memset(st["Sd_b"][l], 0.0)
                        self.memset(st["chal"][l], 0.0)

            def load_states():
                for l in range(depth):
                    self.dma("sp", st["ghal"][l], d["i%d_ffn" % l])
                    if l % 2 == 0:
                        self.dma("sp", st["S_f"][l], d["i%d_gla" % l].rearrange("h k v -> k h v"))
                        self.act(st["S_b"][l], st["S_f"][l], AF.Copy)
                        self.dma("sp", st["uhal"][l], d["i%d_dw" % l])
                    else:
                        self.dma("sp", st["h_lru"][l], d["i%d_lru" % l])
                        self.dma("sp", st["Sd_f"][l], d["i%d_delta" % l].rearrange("h k v -> k h v"))
                        self.act(st["Sd_b"][l], st["Sd_f"][l], AF.Copy)
                        self.dma("sp", st["chal"][l], d["i%d_conv" % l])

            def store_states(grp):
                for l in range(depth):
                    self.dma("sp", d["%s%d_ffn" % (grp, l)], st["ghal"][l], is_output=True)
                    if l % 2 == 0:
                        self.dma("sp", d["%s%d_gla" % (grp, l)].rearrange("h k v -> k h v"), st["S_f"][l], is_output=True)
                        self.dma("sp", d["%s%d_dw" % (grp, l)], st["uhal"][l], is_output=True)
                    else:
                        self.dma("sp", d["%s%d_lru" % (grp, l)], st["h_lru"][l], is_output=True)
                        self.dma("sp", d["%s%d_delta" % (grp, l)].rearrange("h k v -> k h v"), st["Sd_f"][l], is_output=True)
                        self.dma("sp", d["%s%d_conv" % (grp, l)], st["chal"][l], is_output=True)

            def run_tile(X, Tt):
                for n in range(8):
                    self.cp(self.xb[n][:, 0:Tt], X[n][:, 0:Tt])
                for l in range(depth):
                    self.P.fence()
                    self.af.reset()
                    self.ab.reset()
                    Xv = [x[:, 0:Tt] for x in X]
                    if l % 2 == 0:
                        self.even_mixer(Xv, l, Tt, st)
                    else:
                        self.odd_mixer(Xv, l, Tt, st)
                    self.ck(11)
                    self.layernorm(Xv, l, "ln1", Tt)
                    self.ck(12)
                    self.P.fence()
                    self.af.reset()
                    self.ab.reset()
                    self.ffn(Xv, l, Tt, st)
                    self.ck(13)
                    self.layernorm(Xv, l, "ln2", Tt)

            tcount = 0
            try:
                self.main_body(ntiles, d, Xt, Xs, zero_states, load_states, store_states, run_tile)
            except Cut:
                pass
            self.P.finish()
            self.P.emit()
        return nc

    def main_body(self, ntiles, d, Xt, Xs, zero_states, load_states, store_states, run_tile):
        T = self.T
        tcount = 0
        if True:
            if ntiles:
                zero_states()
                xp = d["xp"].rearrange("(k p) s -> p k s", p=128)
                yp = d["yp"].rearrange("(k p) s -> p k s", p=128)
                Xall = [V(Xt[i][:, :, :], [u for x in Xs[i] for u in x.us]) for i in range(2)]
                self.dma("pool", Xall[0], xp[:, :, 0:T])
                for i in range(ntiles):
                    if i + 1 < ntiles:
                        self.dma("pool", Xall[(i + 1) % 2], xp[:, :, (i + 1) * T:(i + 2) * T])
                    if i > 0 and i % 6 == 0:
                        self.P.new_epoch()
                    run_tile(Xs[i % 2], T)
                    self.dma("pool", yp[:, :, i * T:(i + 1) * T], Xall[i % 2], is_output=True)
                    tcount += 1
                store_states("p")
            if self.has_sample:
                xs = d["xs"].rearrange("(k p) s -> p k s", p=128)
                ys = d["ys"].rearrange("(k p) s -> p k s", p=128)
                Xi = tcount % 2
                Xsv = V(Xt[Xi][:, :, 0:64], [u for x in Xs[Xi] for u in x.us])
                load_states()
                self.dma("sp", Xsv, xs)
                run_tile(Xs[Xi], 64)
                self.dma("sp", ys, Xsv, is_output=True)
                store_states("s")


def run_config(inp, cfg, xp_list, xs_list, states_list):
    depth = cfg["depth"]
    wslabs, pp, small = host_weights(inp, depth)
    pf = pp.array()
    small_shapes = {k: v.shape for k, v in small.items()}
    b = Builder(cfg, pp.off, pf.shape[1], wslabs.shape[0], small_shapes)
    nc = b.build()
    consts = host_consts()
    ncores = len(xp_list)
    in_maps = []
    for c in range(ncores):
        m = {"wslabs": wslabs, "pf": pf, "consts": consts}
        m.update(small)
        if cfg["seq"]:
            m["xp"] = np.ascontiguousarray(xp_list[c].T)
        if cfg["sample"]:
            m["xs"] = np.ascontiguousarray(xs_list[c].T)
            stt = states_list[c]
            for l in range(depth):
                if l % 2 == 0:
                    m["i%d_gla" % l] = np.ascontiguousarray(stt["gla%d" % l])
                    m["i%d_dw" % l] = np.ascontiguousarray(stt["dw%d" % l].T.reshape(4, 128, W_B - 1).transpose(1, 0, 2))
                else:
                    m["i%d_lru" % l] = _fm(stt["lru%d" % l])
                    m["i%d_delta" % l] = np.ascontiguousarray(stt["delta%d" % l])
                    m["i%d_conv" % l] = np.ascontiguousarray(stt["conv%d" % l].T.reshape(16, 128, W_S - 1).transpose(1, 0, 2))
                m["i%d_ffn" % l] = np.ascontiguousarray(stt["ffn%d" % l].T.reshape(NFF, 128, W_F - 1).transpose(1, 0, 2))
        in_maps.append(m)
    res = run_bass_kernel_spmd(nc, in_maps, core_ids=list(range(ncores)))
    return res.results, b


def unpack_state(r, grp, l):
    out = {}
    if l % 2 == 0:
        out["gla"] = r["%s%d_gla" % (grp, l)]
        out["dw"] = np.ascontiguousarray(r["%s%d_dw" % (grp, l)].transpose(2, 1, 0).reshape(W_B - 1, D_B))
    else:
        out["lru"] = np.ascontiguousarray(r["%s%d_lru" % (grp, l)].T.reshape(D_C))
        out["delta"] = r["%s%d_delta" % (grp, l)]
        out["conv"] = np.ascontiguousarray(r["%s%d_conv" % (grp, l)].transpose(2, 1, 0).reshape(W_S - 1, 2048))
    out["ffn"] = np.ascontiguousarray(r["%s%d_ffn" % (grp, l)].transpose(2, 1, 0).reshape(W_F - 1, D_FF))
    return out


def kernel(**inputs):
    inp = {k: np.asarray(v) for k, v in inputs.items()}
    cfg = {"depth": DEPTH, "T": 256, "seq": 8192, "sample": True}
    xp_list = [inp["x_prompt"][c % 4] for c in range(8)]
    xs_list = [inp["x_sample"][c] for c in range(8)]
    states = []
    for c in range(8):
        s = {}
        for l in range(DEPTH):
            if l % 2 == 0:
                s["gla%d" % l] = inp["state_l%d_gla" % l][c]
                s["dw%d" % l] = inp["cache_l%d_dwconv" % l][c]
            else:
                s["lru%d" % l] = inp["state_l%d_lru" % l][c]
                s["delta%d" % l] = inp["state_l%d_delta" % l][c]
                s["conv%d" % l] = inp["cache_l%d_conv" % l][c]
            s["ffn%d" % l] = inp["cache_l%d_ffn" % l][c]
        states.append(s)
    results, _ = run_config(inp, cfg, xp_list, xs_list, states)
    y_prompt = np.stack([results[c]["yp"].T for c in range(4)], 0)
    y_sample = np.stack([results[c]["ys"].T for c in range(8)], 0)
    outs = [y_prompt, y_sample]
    for grp, cores in (("p", range(4)), ("s", range(8))):
        per = [[unpack_state(results[c], grp, l) for l in range(DEPTH)] for c in cores]
        for l in range(DEPTH):
            keys = ("gla", "dw", "ffn") if l % 2 == 0 else ("lru", "delta", "conv", "ffn")
            for k in keys:
                outs.append(np.stack([per[i][l][k] for i in range(len(per))], 0))
    return tuple(np.ascontiguousarray(o, dtype=np.float32) for o in outs)
```

```python
import numpy as np
import concourse.bass as bass
import concourse.mybir as mybir
from concourse.bass_utils import run_bass_kernel_spmd

F32 = mybir.dt.float32
BF16 = mybir.dt.bfloat16
AF = mybir.ActivationFunctionType
ALU = mybir.AluOpType

D_MODEL = 1024
DEPTH = 4
H_A, DK_A, DV_A, R_A = 4, 64, 128, 16
D_B, W_B = 512, 31
D_C, H_C, DH_C = 512, 8, 64
H_D, DK_D, DV_D = 4, 128, 128
W_S = 4
D_FF, W_F = 2688, 3
NFF = D_FF // 128
ALPHA = (2 * DEPTH) ** 0.25
EPS = 1e-5
LRU_C = 8.0
SLAB = 4096
NSLOT = 5
NDMASEM = 40


class Cut(Exception):
    pass


class Unit:
    __slots__ = ("w", "r")

    def __init__(self):
        self.w = None
        self.r = {}


class V:
    __slots__ = ("ap", "us")

    def __init__(self, ap, us):
        self.ap = ap
        self.us = tuple(us)

    def __getitem__(self, idx):
        return V(self.ap[idx], self.us)

    def re(self, s, **kw):
        return V(self.ap.rearrange(s, **kw), self.us)


def newV(ap):
    return V(ap, (Unit(),))


class Prog:
    ENG = ("pe", "act", "dve", "pool", "sp")

    def __init__(self, nc):
        self.nc = nc
        self.q = {e: [] for e in self.ENG}
        self.cnt = {e: 0 for e in self.ENG}
        self.waited = {e: {} for e in self.ENG}
        self.dma_val = [0] * NDMASEM
        self.dma_rr = 0
        self.out_tokens = []
        self.ninstr = 0
        self.epoch = 0

    def _wait(self, eng, key, val):
        if self.waited[eng].get(key, 0) >= val:
            return
        self.waited[eng][key] = val
        self.q[eng].append(("w", key, val))

    def _deps(self, eng, reads, writes):
        for v in reads:
            for u in v.us:
                if u.w is not None:
                    self._wait(eng, u.w[0], u.w[1])
        for v in writes:
            for u in v.us:
                if u.w is not None and u.w[0][0] != eng:
                    self._wait(eng, u.w[0], u.w[1])
                for k, val in u.r.items():
                    if k[0] != eng:
                        self._wait(eng, k, val)

    def _mark(self, tok, reads, writes):
        for v in reads:
            for u in v.us:
                if u.r.get(tok[0], 0) < tok[1]:
                    u.r[tok[0]] = tok[1]
        for v in writes:
            for u in v.us:
                u.w = tok
                u.r = {}

    def op(self, eng, fn, reads, writes, inc=True):
        self._deps(eng, reads, writes)
        key = (eng, self.epoch)
        if inc:
            self.cnt[eng] += 1
            tok = (key, self.cnt[eng])
        else:
            tok = (key, self.cnt[eng] + 1)
        self.q[eng].append(("i", fn, inc, key))
        self._mark(tok, reads, writes)
        self.ninstr += 1

    def new_epoch(self):
        self.fence()
        self.epoch += 1
        for e in self.ENG:
            self.cnt[e] = 0

    def dma(self, eng, out, in_, reads, writes, is_output=False, **kw):
        i = self.dma_rr
        self.dma_rr = (self.dma_rr + 1) % NDMASEM
        key = ("d", i)
        if self.dma_val[i] > 0:
            self._wait(eng, key, self.dma_val[i])
        self._deps(eng, reads, writes)
        self.dma_val[i] += 16
        tok = (key, self.dma_val[i])
        self.q[eng].append(("d", out, in_, i, kw))
        self._mark(tok, reads, writes)
        if is_output:
            self.out_tokens.append(tok)
        self.ninstr += 1

    def fence(self):
        comp = ("pe", "act", "dve", "pool")
        for e in comp:
            for f in comp:
                if e != f and self.cnt[f] > 0:
                    self._wait(e, (f, self.epoch), self.cnt[f])

    def finish(self):
        for key, val in self.out_tokens:
            self._wait("sp", key, val)

    def emit(self):
        nc = self.nc
        handles = {"pe": nc.tensor, "act": nc.scalar, "dve": nc.vector, "pool": nc.gpsimd, "sp": nc.sync}
        import contextlib
        with contextlib.ExitStack() as st:
            sems = {}
            for e in self.ENG:
                for ep in range(self.epoch + 1):
                    sems[(e, ep)] = st.enter_context(nc.semaphore("s_%s_%d" % (e, ep)))
            for i in range(NDMASEM):
                sems[("d", i)] = st.enter_context(nc.semaphore("sd%d" % i))
            block = st.enter_context(nc.Block())

            def run(e, h):
                for it in self.q[e]:
                    if it[0] == "w":
                        h.wait_ge(sems[it[1]], it[2])
                    elif it[0] == "i":
                        ins = it[1](h)
                        if it[2]:
                            ins.then_inc(sems[it[3]], 1)
                    else:
                        h.dma_start(out=it[1], in_=it[2], **it[4]).then_inc(sems[("d", it[3])], 16)

            @block.tensor
            def _(h):
                run("pe", h)

            @block.scalar
            def _(h):
                run("act", h)

            @block.vector
            def _(h):
                run("dve", h)

            @block.gpsimd
            def _(h):
                run("pool", h)

            @block.sync
            def _(h):
                run("sp", h)


class Arena:
    def __init__(self, tens, size):
        self.t = tens
        self.size = size
        self.off = 0

    def reset(self):
        self.off = 0

    def mark(self):
        return self.off

    def release(self, m):
        self.off = m

    def alloc(self, parts, shape):
        n = int(np.prod(shape))
        assert self.off + n <= self.size, ("arena overflow", self.off, n, self.size)
        ap = self.t[0:parts, self.off:self.off + n]
        self.off += n
        if len(shape) == 2:
            ap = ap.rearrange("p (a b) -> p a b", a=shape[0])
        elif len(shape) == 3:
            ap = ap.rearrange("p (a b c) -> p a b c", a=shape[0], b=shape[1])
        return newV(ap)


def _slab_in(w_cols):
    n = w_cols.shape[1]
    a = np.zeros((8, 128, 512), np.float32)
    a[:, :, :n] = w_cols.reshape(8, 128, n)
    return np.ascontiguousarray(a.transpose(1, 0, 2)).reshape(128, SLAB)


def _slab_down(w_cols):
    a = np.zeros((128, SLAB), np.float32)
    a[:, :NFF * 128] = w_cols.reshape(NFF, 128, 128).transpose(1, 0, 2).reshape(128, NFF * 128)
    return a


def _fm(vec):
    return np.ascontiguousarray(vec.reshape(-1, 128).T)


class ParamPack:
    def __init__(self):
        self.cols = []
        self.off = {}
        self.n = 0

    def add(self, name, arr):
        arr = np.asarray(arr, np.float32)
        assert arr.shape[0] == 128
        arr = arr.reshape(128, -1)
        self.off[name] = (self.n, arr.shape[1])
        self.cols.append(arr)
        self.n += arr.shape[1]

    def array(self):
        return np.ascontiguousarray(np.concatenate(self.cols, axis=1))


def layer_slab_names(l):
    names = []
    if l % 2 == 0:
        names += ["qk", "v", "gate", "glua", "glub", "out0", "out1"]
    else:
        names += ["xl", "q", "k", "v", "gc", "z", "out0", "out1"]
    names += ["up%d" % s for s in range(11)]
    names += ["dn%d" % n for n in range(8)]
    return names


def host_weights(inp, depth):
    slabs = []
    pp = ParamPack()
    small = {}
    for l in range(depth):
        if l % 2 == 0:
            e = l // 2
            w = inp["we_in"][e]
            slabs += [_slab_in(w[:, 0:512]), _slab_in(w[:, 512:1024]), _slab_in(w[:, 1024:1536]),
                      _slab_in(w[:, 1552:2064]), _slab_in(w[:, 2064:2576])]
            wo = inp["we_out"][e]
            slabs += [_slab_in(wo[:, 0:512]), _slab_in(wo[:, 512:1024])]
            small["wlrin%d" % l] = np.ascontiguousarray(
                w[:, 1536:1552].reshape(8, 128, 16).transpose(1, 0, 2)).reshape(128, 128)
            small["wlraug%d" % l] = np.ascontiguousarray(
                np.concatenate([inp["we_lr"][e], inp["be_lr"][e][None, :]], axis=0))
            pp.add("g_gla%d" % l, _fm(inp["ge_gla"][e]))
            pp.add("w_dw%d" % l, inp["we_dw"][e].T.reshape(4, 128, W_B).transpose(1, 0, 2))
            pp.add("b_dw%d" % l, _fm(inp["be_dw"][e]))
            pp.add("g_cn%d" % l, _fm(inp["ge_cn"][e]))
            pp.add("b_cn%d" % l, _fm(inp["be_cn"][e]))
        else:
            o = l // 2
            w = inp["wo_in"][o]
            slabs += [_slab_in(w[:, 0:512]), _slab_in(w[:, 512:1024]), _slab_in(w[:, 1024:1536]),
                      _slab_in(w[:, 1536:2048]), _slab_in(w[:, 2048:2560]), _slab_in(w[:, 2560:3072])]
            wo = inp["wo_out"][o]
            slabs += [_slab_in(wo[:, 0:512]), _slab_in(wo[:, 512:1024])]
            small["wbain%d" % l] = np.ascontiguousarray(
                w[:, 3072:3080].reshape(8, 128, 8).transpose(1, 0, 2)).reshape(128, 64)
            for nm, key in (("wrg", "wo_rg"), ("wig", "wo_ig")):
                g = inp[key][o]
                bd = np.zeros((4, 128, 128), np.float32)
                for hh in range(8):
                    c, r = hh // 2, (hh % 2) * 64
                    bd[c, r:r + 64, r:r + 64] = g[hh]
                small["%s%d" % (nm, l)] = np.ascontiguousarray(bd.transpose(1, 0, 2)).reshape(128, 512)
            pp.add("w_cv%d" % l, inp["wo_conv"][o].T.reshape(16, 128, W_S).transpose(1, 0, 2))
            pp.add("b_cv%d" % l, _fm(inp["bo_conv"][o]))
            pp.add("b_rg%d" % l, _fm(inp["bo_rg"][o]))
            pp.add("b_ig%d" % l, _fm(inp["bo_ig"][o]))
            pp.add("lam%d" % l, _fm(inp["lam_lru"][o]))
            col = np.zeros((128, 2), np.float32)
            col[0:H_D, 0] = inp["a_log"][o]
            col[0:H_D, 1] = inp["dt_bias"][o]
            pp.add("hd%d" % l, col)
            pp.add("g_dl%d" % l, _fm(inp["go_delta"][o]))
        wu = inp["w_up"][l]
        for s in range(11):
            cols = np.zeros((1024, 512), np.float32)
            for jj in range(2):
                j = 2 * s + jj
                if j < NFF:
                    cols[:, jj * 128:(jj + 1) * 128] = wu[:, j * 128:(j + 1) * 128]
                    cols[:, (2 + jj) * 128:(3 + jj) * 128] = wu[:, D_FF + j * 128:D_FF + (j + 1) * 128]
            slabs.append(_slab_in(cols))
        wd = inp["w_down"][l]
        for n in range(8):
            slabs.append(_slab_down(wd[:, n * 128:(n + 1) * 128]))
        pp.add("w_fdw%d" % l, inp["w_fdw"][l].T.reshape(NFF, 128, W_F).transpose(1, 0, 2))
        pp.add("b_fdw%d" % l, _fm(inp["b_fdw"][l]))
        for nm in ("ln1_g", "ln1_b", "ln2_g", "ln2_b"):
            pp.add("%s%d" % (nm, l), _fm(inp[nm][l]))
    return np.stack(slabs, 0), pp, small


NCONST = 128 + 256 * 4 + 512 * 2 + 512


def host_consts():
    ident = np.eye(128, dtype=np.float32)
    s = np.arange(64)
    U = (s[:, None] <= s[None, :]).astype(np.float32)
    Us = (s[:, None] < s[None, :]).astype(np.float32)
    Ls = (s[:, None] > s[None, :]).astype(np.float32)
    c = np.zeros((128, NCONST), np.float32)
    c[:, 0:128] = ident
    c[0:64, 128:384] = np.tile(U, (1, 4))
    c[0:64, 384:640] = np.tile(Us, (1, 4))
    c[0:64, 640:896] = np.tile(Ls, (1, 4))
    c[0:64, 896:1152] = np.tile(np.eye(64, dtype=np.float32), (1, 4))
    cm = np.ones(512, np.float32)
    cm[::64] = 0.0
    c[0:4, 1152:1664] = cm[None, :]
    for h in range(4):
        c[h, 1664 + h * 128:1664 + (h + 1) * 128] = 1.0
    c[0:64, 2176:2432] = np.tile((U - 1.0) * 30000.0, (1, 4))
    c[0:64, 2432:2688] = np.tile((U.T - 1.0) * 30000.0, (1, 4))
    return c


class Builder:
    def __init__(self, cfg, pp_off, npf, nslab_total, small_shapes):
        self.cfg = cfg
        self.depth = cfg["depth"]
        self.T = cfg["T"]
        self.S = cfg["seq"]
        self.has_sample = cfg["sample"]
        self.pp_off = pp_off
        nc = self.nc = bass.Bass("TRN2", target_bir_lowering=False)
        self.P = Prog(nc)
        T = self.T
        d = self.dram = {}
        depth = self.depth

        def din(name, shape):
            d[name] = nc.dram_tensor(name, list(shape), F32, kind="ExternalInput").ap()

        def dout(name, shape):
            d[name] = nc.dram_tensor(name, list(shape), F32, kind="ExternalOutput").ap()
        self.outs = []
        din("wslabs", (nslab_total, 128, SLAB))
        self.wbf = nc.dram_tensor("wbf", [nslab_total, 128, SLAB], BF16, kind="Internal").ap()
        self.wbf_v = [newV(self.wbf[g]) for g in range(nslab_total)]
        din("pf", (128, npf))
        din("consts", (128, NCONST))
        for k, shp in small_shapes.items():
            din(k, shp)
        if self.S:
            din("xp", (D_MODEL, self.S))
            dout("yp", (D_MODEL, self.S))
        if self.has_sample:
            din("xs", (D_MODEL, 64))
            dout("ys", (D_MODEL, 64))
        for grp in (["p"] if self.S else []) + (["s"] if self.has_sample else []):
            for l in range(depth):
                if l % 2 == 0:
                    dout("%s%d_gla" % (grp, l), (H_A, DK_A, DV_A))
                    dout("%s%d_dw" % (grp, l), (128, 4, W_B - 1))
                else:
                    dout("%s%d_lru" % (grp, l), (128, 4))
                    dout("%s%d_delta" % (grp, l), (H_D, DK_D, DV_D))
                    dout("%s%d_conv" % (grp, l), (128, 16, W_S - 1))
                dout("%s%d_ffn" % (grp, l), (128, NFF, W_F - 1))
        if self.has_sample:
            for l in range(depth):
                if l % 2 == 0:
                    din("i%d_gla" % l, (H_A, DK_A, DV_A))
                    din("i%d_dw" % l, (128, 4, W_B - 1))
                else:
                    din("i%d_lru" % l, (128, 4))
                    din("i%d_delta" % l, (H_D, DK_D, DV_D))
                    din("i%d_conv" % l, (128, 16, W_S - 1))
                din("i%d_ffn" % l, (128, NFF, W_F - 1))
        if cfg.get('dbg'):
            dout('dbg', (128, 8, self.T))
        self.small_shapes = small_shapes
        self.npf = npf
        self.nslab_total = nslab_total

    def mm(self, out, lhsT, rhs, start, stop, inc=None):
        self.P.op("pe", lambda h, o=out.ap, a=lhsT.ap, b=rhs.ap, s=start, e=stop: h.matmul(o, a, b, start=s, stop=e),
                  [lhsT, rhs], [out], inc=(stop if inc is None else inc))

    def act(self, out, in_, func, scale=1.0, bias=0.0, extra=()):
        sc = scale.ap if isinstance(scale, V) else scale
        bi = bias.ap if isinstance(bias, V) else bias
        rd = [in_] + [x for x in (scale, bias) if isinstance(x, V)] + list(extra)
        self.P.op("act", lambda h, o=out.ap, i=in_.ap, f=func, s=sc, b=bi: h.activation(out=o, in_=i, func=f, bias=b, scale=s),
                  rd, [out])

    def tt(self, out, in0, in1, op, eng="dve"):
        self.P.op(eng, lambda h, o=out.ap, a=in0.ap, b=in1.ap, p=op: h.tensor_tensor(out=o, in0=a, in1=b, op=p),
                  [in0, in1], [out])

    def ts(self, out, in0, s1, s2, op0, op1=None, eng="dve"):
        a1 = s1.ap if isinstance(s1, V) else s1
        a2 = s2.ap if isinstance(s2, V) else s2
        rd = [in0] + [x for x in (s1, s2) if isinstance(x, V)]
        if op1 is None:
            self.P.op(eng, lambda h, o=out.ap, a=in0.ap, x=a1, p=op0: h.tensor_scalar(out=o, in0=a, scalar1=x, scalar2=None, op0=p),
                      rd, [out])
        else:
            self.P.op(eng, lambda h, o=out.ap, a=in0.ap, x=a1, y=a2, p=op0, q=op1: h.tensor_scalar(out=o, in0=a, scalar1=x, scalar2=y, op0=p, op1=q),
                      rd, [out])

    def stt(self, out, in0, sc, in1, op0, op1, eng="dve"):
        a1 = sc.ap if isinstance(sc, V) else sc
        rd = [in0, in1] + ([sc] if isinstance(sc, V) else [])
        self.P.op(eng, lambda h, o=out.ap, a=in0.ap, x=a1, b=in1.ap, p=op0, q=op1: h.scalar_tensor_tensor(out=o, in0=a, scalar=x, in1=b, op0=p, op1=q),
                  rd, [out])

    def cp(self, out, in_, eng="dve"):
        self.P.op(eng, lambda h, o=out.ap, i=in_.ap: h.tensor_copy(out=o, in_=i), [in_], [out])

    def recip(self, out, in_):
        self.P.op("dve", lambda h, o=out.ap, i=in_.ap: h.reciprocal(out=o, in_=i), [in_], [out])

    def memset(self, out, val, eng="dve"):
        self.P.op(eng, lambda h, o=out.ap, v=val: h.memset(o, v), [], [out])

    def scan(self, out, d0, d1, init, op0, op1):
        ia = init.ap if isinstance(init, V) else init
        rd = [d0, d1] + ([init] if isinstance(init, V) else [])
        self.P.op("dve", lambda h, o=out.ap, a=d0.ap, b=d1.ap, i=ia, p=op0, q=op1: h.tensor_tensor_scan(out=o, data0=a, data1=b, initial=i, op0=p, op1=q),
                  rd, [out])

    def dma(self, eng, out, in_, reads=(), writes=(), is_output=False, **kw):
        oa = out.ap if isinstance(out, V) else out
        ia = in_.ap if isinstance(in_, V) else in_
        rd = list(reads) + ([in_] if isinstance(in_, V) else [])
        wr = list(writes) + ([out] if isinstance(out, V) else [])
        self.P.dma(eng, oa, ia, rd, wr, is_output=is_output, **kw)

    def ck(self, lvl):
        if self.cfg.get('cut', 99) == lvl:
            raise Cut()

    def bank(self):
        b = self.banks[self.bank_rr]
        self.bank_rr = (self.bank_rr + 1) % 8
        return b

    def pfv(self, name, l):
        off, n = self.pp_off["%s%d" % (name, l)]
        return self.pf[:, off:off + n]

    def plan_slabs(self, ntile_calls):
        order = []
        base = 0
        self.layer_base = []
        for l in range(self.depth):
            self.layer_base.append(base)
            base += len(layer_slab_names(l))
        for _ in range(ntile_calls):
            for l in range(self.depth):
                for i, nm in enumerate(layer_slab_names(l)):
                    order.append((l, nm, self.layer_base[l] + i))
        self.slab_order = order
        self.slab_issued = 0
        self.slab_next = 0

    def slab(self, l, name):
        k = self.slab_next
        ol, onm, _ = self.slab_order[k]
        assert (ol, onm) == (l, name), ((ol, onm), (l, name))
        lim = min(len(self.slab_order), k + NSLOT)
        while self.slab_issued < lim:
            n = self.slab_issued
            _, _, gi = self.slab_order[n]
            self.dma("sp", self.slots[n % NSLOT], self.wbf_v[gi])
            self.slab_issued += 1
        self.slab_next += 1
        return self.slots[k % NSLOT]

    def layernorm(self, X, l, which, T):
        g = self.pfv(which + "_g", l)
        bb = self.pfv(which + "_b", l)
        A, Ab = self.af, self.ab
        assert 2 * T <= 512
        pss = self.bank()
        psm, psq = pss[:, 0:T], pss[:, T:2 * T]
        zzs = [Ab.alloc(128, [2, T]) for _ in range(2)]
        for n in range(8):
            zz = zzs[n % 2]
            self.act(zz[:, 0, :], X[n], AF.Copy)
            self.act(zz[:, 1, :], X[n], AF.Square)
            self.mm(pss[:, 0:2 * T], self.ones_b, zz.re("p a b -> p (a b)"), n == 0, n == 7, inc=True)
        mu = A.alloc(128, [T])
        msq = A.alloc(128, [T])
        var = A.alloc(128, [T])
        rs = A.alloc(128, [T])
        self.act(mu, psm, AF.Copy, scale=1.0 / D_MODEL)
        self.tt(msq, mu, mu, ALU.mult)
        self.stt(var, psq, 1.0 / D_MODEL, msq, ALU.mult, ALU.subtract)
        self.act(var, var, AF.Sqrt, bias=self.eps_c)
        self.recip(rs, var)
        t1 = [A.alloc(128, [T]) for _ in range(2)]
        for n in range(8):
            t = t1[n % 2]
            self.tt(t, X[n], mu, ALU.subtract)
            self.tt(t, t, rs, ALU.mult)
            self.act(self.xb[n][:, 0:T], t, AF.Identity, scale=g[:, n:n + 1], bias=bb[:, n:n + 1])
            self.act(X[n], t, AF.Identity, scale=g[:, n:n + 1], bias=bb[:, n:n + 1])

    def ffn(self, X, l, T, st):
        A, Ab = self.af, self.ab
        wf = self.pfv("w_fdw", l)
        bf = self.pfv("b_fdw", l)
        ghal = st["ghal"][l]
        hbuf = [Ab.alloc(128, [T]) for _ in range(NFF)]
        gb = [A.alloc(128, [T + 2]) for _ in range(2)]
        acc = [A.alloc(128, [T]) for _ in range(2)]
        ge = [A.alloc(128, [T]) for _ in range(3)]
        pend = None
        for s in range(11):
            slot = self.slab(l, "up%d" % s).re("p (k n) -> p k n", k=8)
            for jj in range(2):
                j = 2 * s + jj
                if j >= NFF:
                    continue
                psg, psu = self.bank(), self.bank()
                for k in range(8):
                    self.mm(psg[:, 0:T], slot[:, k, jj * 128:(jj + 1) * 128], self.xb[k][:, 0:T], k == 0, k == 7)
                for k in range(8):
                    self.mm(psu[:, 0:T], slot[:, k, (2 + jj) * 128:(3 + jj) * 128], self.xb[k][:, 0:T], k == 0, k == 7)
                g_, a_, e_ = gb[j % 2], acc[j % 2], ge[j % 3]
                self.cp(g_[:, 0:2], ghal[:, j, :], eng="pool")
                self.act(g_[:, 2:T + 2], psg[:, 0:T], AF.Copy)
                self.cp(ghal[:, j, :], g_[:, T:T + 2], eng="pool")
                self.act(a_, g_[:, 0:T], AF.Identity, scale=wf[:, 3 * j:3 * j + 1], bias=bf[:, j:j + 1])
                self.stt(a_, g_[:, 1:T + 1], wf[:, 3 * j + 1:3 * j + 2], a_, ALU.mult, ALU.add)
                self.stt(a_, g_[:, 2:T + 2], wf[:, 3 * j + 2:3 * j + 3], a_, ALU.mult, ALU.add)
                self.act(e_, a_, AF.Gelu_apprx_tanh)
                if pend is not None:
                    self.tt(pend[0], pend[1], pend[2], ALU.mult)
                pend = (hbuf[j], e_, psu[:, 0:T])
        self.tt(pend[0], pend[1], pend[2], ALU.mult)
        for n in range(8):
            slot = self.slab(l, "dn%d" % n)
            ps = self.bank()
            for j in range(NFF):
                self.mm(ps[:, 0:T], slot[:, j * 128:(j + 1) * 128], hbuf[j], j == 0, j == NFF - 1)
            self.stt(X[n], X[n], ALPHA, ps[:, 0:T], ALU.mult, ALU.add)

    def even_mixer(self, X, l, T, st):
        A, Ab = self.af, self.ab
        NCH = T // 64
        xb = self.xb
        S_f, S_b, uhal = st["S_f"][l], st["S_b"][l], st["uhal"][l]
        slot = self.slab(l, "qk").re("p (k n) -> p k n", k=8)
        qT = A.alloc(64, [4, T])
        kT = A.alloc(64, [4, T])
        for i in range(8):
            ps = self.bank()
            for k in range(8):
                self.mm(ps[0:64, 0:T], slot[:, k, i * 64:(i + 1) * 64], xb[k][:, 0:T], k == 0, k == 7)
            if i < 4:
                self.act(qT[:, i, :], ps[0:64, 0:T], AF.Copy, scale=DK_A ** -0.5)
            else:
                self.cp(kT[:, i - 4, :], ps[0:64, 0:T])
        self.ck(1)
        lrT = Ab.alloc(17, [T])
        self.memset(lrT, 1.0)
        ps = self.bank()
        wl = self.wlrin[l].re("p (k n) -> p k n", k=8)
        for k in range(8):
            self.mm(ps[0:16, 0:T], wl[:, k, :], xb[k][:, 0:T], k == 0, k == 7)
        self.cp(lrT[0:16, :], ps[0:16, 0:T])
        self.ck(2)
        slot = self.slab(l, "v").re("p (k n) -> p k n", k=8)
        vtok = [Ab.alloc(64, [512]) for _ in range(NCH)]
        sp_tok = [A.alloc(64, [256]) for _ in range(2)]
        e1 = [A.alloc(64, [256]) for _ in range(2)]
        oT = A.alloc(128, [4, T])
        ep = [A.alloc(64, [4, 64]) for _ in range(2)]
        en = [A.alloc(64, [4, 64]) for _ in range(2)]
        qd = [Ab.alloc(64, [4, 64]) for _ in range(2)]
        kd = [Ab.alloc(64, [4, 64]) for _ in range(2)]
        kk = [Ab.alloc(64, [4, 64]) for _ in range(2)]
        scm = [Ab.alloc(64, [4, 64]) for _ in range(2)]
        kkt = [Ab.alloc(64, [256]) for _ in range(2)]
        def gla_a(c):
            cs = slice(c * 64, (c + 1) * 64)
            r = c % 2
            ps = self.bank()
            for k in range(8):
                self.mm(ps[0:64, 0:512], xb[k][:, cs], slot[:, k, :], k == 0, k == 7)
            self.act(vtok[c], ps[0:64, 0:512], AF.Copy)
            ps2 = self.bank()
            self.mm(ps2[0:64, 0:256], lrT[0:17, cs], self.wlraug[l], True, True)
            self.act(e1[r], ps2[0:64, 0:256], AF.Exp, scale=-1.0)
            yield
            self.act(sp_tok[r], e1[r], AF.Ln, bias=1.0)
            yield
            ps3 = self.bank()
            for h in range(4):
                self.mm(ps3[0:64, h * 64:(h + 1) * 64], sp_tok[r][:, h * 64:(h + 1) * 64], self.U_f, True, True)
            p3 = ps3[0:64, 0:256].re("p (a b) -> p a b", a=4)
            self.act(ep[r], p3, AF.Exp, scale=-1.0 / 16.0)
            self.act(en[r], p3, AF.Exp, scale=1.0 / 16.0)
            yield
            self.tt(qd[r], qT[:, :, cs], ep[r], ALU.mult)
            self.tt(kd[r], kT[:, :, cs], en[r], ALU.mult)
            yield
            for h in range(4):
                self.ts(kk[r][:, h, :], kd[r][:, h, :], ep[r][:, h, 63:64], None, ALU.mult)
            ps4 = self.bank()
            for h in range(4):
                self.mm(ps4[0:64, h * 64:(h + 1) * 64], kd[r][:, h, :], qd[r][:, h, :], True, True)
            self.tt(scm[r], ps4[0:64, 0:256].re("p (a b) -> p a b", a=4), self.mask4.re("p (a b) -> p a b", a=4), ALU.mult)
            yield
            ps5 = self.bank()
            for h in range(4):
                self.mm(ps5[0:64, h * 64:(h + 1) * 64], kk[r][:, h, :], self.ident_b[0:64, 0:64], True, True)
            self.act(kkt[r], ps5[0:64, 0:256], AF.Copy)
            yield

        def gla_b(c):
            cs = slice(c * 64, (c + 1) * 64)
            r = c % 2
            ps6 = self.bank()
            for h in range(4):
                self.mm(ps6[:, h * 64:(h + 1) * 64], S_b[:, h, :], qd[r][:, h, :], True, False)
                self.mm(ps6[:, h * 64:(h + 1) * 64], vtok[c][:, h * 128:(h + 1) * 128], scm[r][:, h, :], False, True)
            self.act(oT[:, :, cs], ps6[:, 0:256].re("p (a b) -> p a b", a=4), AF.Copy)
            ps7 = self.bank()
            for h in range(4):
                self.mm(ps7[0:64, h * 128:(h + 1) * 128], kkt[r][:, h * 64:(h + 1) * 64], vtok[c][:, h * 128:(h + 1) * 128], True, True)
            for h in range(4):
                self.stt(S_f[:, h, :], S_f[:, h, :], ep[r][:, h, 63:64], ps7[0:64, h * 128:(h + 1) * 128], ALU.mult, ALU.add)
            self.act(S_b, S_f, AF.Copy)

        for c0 in range(0, NCH, 2):
            cl = [c for c in (c0, c0 + 1) if c < NCH]
            alive = [gla_a(c) for c in cl]
            while alive:
                for g_ in list(alive):
                    try:
                        next(g_)
                    except StopIteration:
                        alive.remove(g_)
            for c in cl:
                gla_b(c)
            self.ck(8)
        slot = self.slab(l, "gate").re("p (k n) -> p k n", k=8)
        sg = A.alloc(128, [4, T])
        for h in range(4):
            ps = self.bank()
            for k in range(8):
                self.mm(ps[:, 0:T], slot[:, k, h * 128:(h + 1) * 128], xb[k][:, 0:T], k == 0, k == 7)
            self.act(sg[:, h, :], ps[:, 0:T], AF.Silu)
        gg = self.pfv("g_gla", l)
        R = [Ab.alloc(128, [T]) for _ in range(8)]
        sq = [Ab.alloc(128, [T]) for _ in range(2)]
        sd = [A.alloc(128, [T]) for _ in range(2)]
        for h in range(4):
            self.act(sq[h % 2], oT[:, h, :], AF.Square)
            ps = self.bank()
            self.mm(ps[:, 0:T], self.ones_b, sq[h % 2], True, True)
            self.act(sd[h % 2], ps[:, 0:T], AF.Sqrt, scale=1.0 / DV_A, bias=self.eps_c)
            self.recip(sd[h % 2], sd[h % 2])
            self.stt(sd[h % 2], oT[:, h, :], gg[:, h:h + 1], sd[h % 2], ALU.mult, ALU.mult)
            self.tt(R[h], sd[h % 2], sg[:, h, :], ALU.mult)
        self.ck(9)
        up = A.alloc(128, [4, T + 30])
        self.cp(up[:, :, 0:30], uhal, eng="pool")
        self.ck(91)
        slot = self.slab(l, "glua").re("p (k n) -> p k n", k=8)
        for j in range(4):
            ps = self.bank()
            for k in range(8):
                self.mm(ps[:, 0:T], slot[:, k, j * 128:(j + 1) * 128], xb[k][:, 0:T], k == 0, k == 7)
            self.act(up[:, j, 30:30 + T], ps[:, 0:T], AF.Copy)
        slot = self.slab(l, "glub").re("p (k n) -> p k n", k=8)
        sgb = [A.alloc(128, [T]) for _ in range(2)]
        for j in range(4):
            ps = self.bank()
            for k in range(8):
                self.mm(ps[:, 0:T], slot[:, k, j * 128:(j + 1) * 128], xb[k][:, 0:T], k == 0, k == 7)
            self.act(sgb[j % 2], ps[:, 0:T], AF.Sigmoid)
            self.tt(up[:, j, 30:30 + T], up[:, j, 30:30 + T], sgb[j % 2], ALU.mult)
        self.cp(uhal, up[:, :, T:T + 30], eng="pool")
        self.ck(92)
        wdw = self.pfv("w_dw", l)
        bdw = self.pfv("b_dw", l)
        cv = [A.alloc(128, [T]) for _ in range(4)]
        pss = self.bank()
        psm, psq = pss[:, 0:T], pss[:, T:2 * T]
        cvz = [Ab.alloc(128, [2, T]) for _ in range(2)]
        for j in range(4):
            ce = "dve"
            self.ts(cv[j], up[:, j, 0:T], wdw[:, j * 31:j * 31 + 1], bdw[:, j:j + 1], ALU.mult, ALU.add, eng=ce)
            for tap in range(1, W_B):
                self.stt(cv[j], up[:, j, tap:tap + T], wdw[:, j * 31 + tap:j * 31 + tap + 1], cv[j], ALU.mult, ALU.add, eng=ce)
        for idx, j in enumerate((0, 2, 1, 3)):
            self.act(cvz[idx % 2][:, 0, :], cv[j], AF.Copy)
            self.act(cvz[idx % 2][:, 1, :], cv[j], AF.Square)
            self.mm(pss[:, 0:2 * T], self.ones_b, cvz[idx % 2].re("p a b -> p (a b)"), idx == 0, idx == 3, inc=True)
        self.ck(95)
        mu = A.alloc(128, [T])
        msq = A.alloc(128, [T])
        var = A.alloc(128, [T])
        self.act(mu, psm, AF.Copy, scale=1.0 / D_B)
        self.tt(msq, mu, mu, ALU.mult)
        self.stt(var, psq, 1.0 / D_B, msq, ALU.mult, ALU.subtract)
        self.act(var, var, AF.Sqrt, bias=self.eps_c)
        self.recip(var, var)
        gcn = self.pfv("g_cn", l)
        self.ck(96)
        bcn = self.pfv("b_cn", l)
        for j in range(4):
            self.tt(cv[j], cv[j], mu, ALU.subtract)
            self.tt(cv[j], cv[j], var, ALU.mult)
            self.act(R[4 + j], cv[j], AF.Silu, scale=gcn[:, j:j + 1], bias=bcn[:, j:j + 1])
        self.ck(10)
        self.out_proj(X, l, T, R)

    def out_proj(self, X, l, T, R):
        for s in range(2):
            slot = self.slab(l, "out%d" % s).re("p (k n) -> p k n", k=8)
            for jj in range(4):
                n = s * 4 + jj
                ps = self.bank()
                for k in range(8):
                    self.mm(ps[:, 0:T], slot[:, k, jj * 128:(jj + 1) * 128], R[k], k == 0, k == 7)
                self.stt(X[n], X[n], ALPHA, ps[:, 0:T], ALU.mult, ALU.add)

    def log1p_series(self, A, e, parts, shape):
        den = A.alloc(parts, shape)
        s_ = A.alloc(parts, shape)
        s2 = A.alloc(parts, shape)
        p = A.alloc(parts, shape)
        self.ts(den, e, 2.0, None, ALU.add)
        self.recip(den, den)
        self.tt(s_, e, den, ALU.mult)
        self.tt(s2, s_, s_, ALU.mult)
        self.ts(p, s2, 1.0 / 11.0, 1.0 / 9.0, ALU.mult, ALU.add)
        for cst in (1.0 / 7.0, 1.0 / 5.0, 1.0 / 3.0, 1.0):
            self.tt(p, p, s2, ALU.mult)
            self.ts(p, p, cst, None, ALU.add)
        self.tt(p, p, s_, ALU.mult)
        return p

    def odd_prologue(self, l, sbf):
        A = self.af
        lam = self.pfv("lam", l)
        e = A.alloc(128, [4])
        self.act(e, lam, AF.Exp, scale=-1.0)
        p = self.log1p_series(A, e, 128, [4])
        L4 = newV(sbf("L4_%d" % l, [128, 4], F32)[:, :])
        self.ts(L4, p, -8.0, None, ALU.mult)
        self.L4[l] = L4
        hd = self.pfv("hd", l)
        negA = newV(sbf("negA_%d" % l, [4, 1], F32)[:, :])
        self.act(negA, hd[0:4, 0:1], AF.Exp)
        self.ts(negA, negA, -1.0, None, ALU.mult)
        self.negA[l] = negA

    def rms_gate(self, oT, gname, slabname, l, T, R, base, dv):
        A, Ab = self.af, self.ab
        slot = self.slab(l, slabname).re("p (k n) -> p k n", k=8)
        sg = A.alloc(128, [4, T])
        for h in range(4):
            ps = self.bank()
            for k in range(8):
                self.mm(ps[:, 0:T], slot[:, k, h * 128:(h + 1) * 128], self.xb[k][:, 0:T], k == 0, k == 7)
            self.act(sg[:, h, :], ps[:, 0:T], AF.Silu)
        gg = self.pfv(gname, l)
        sq = [Ab.alloc(128, [T]) for _ in range(2)]
        sd = [A.alloc(128, [T]) for _ in range(2)]
        for h in range(4):
            self.act(sq[h % 2], oT[:, h, :], AF.Square)
            ps = self.bank()
            self.mm(ps[:, 0:T], self.ones_b, sq[h % 2], True, True)
            self.act(sd[h % 2], ps[:, 0:T], AF.Sqrt, scale=1.0 / dv, bias=self.eps_c)
            self.recip(sd[h % 2], sd[h % 2])
            self.stt(sd[h % 2], oT[:, h, :], gg[:, h:h + 1], sd[h % 2], ALU.mult, ALU.mult)
            self.tt(R[base + h], sd[h % 2], sg[:, h, :], ALU.mult)

    def odd_mixer(self, X, l, T, st):
        A, Ab = self.af, self.ab
        NCH = T // 64
        xb = self.xb
        chal, hl, Sd = st["chal"][l], st["h_lru"][l], st["Sd_f"][l]
        wcv = self.pfv("w_cv", l)
        bcv = self.pfv("b_cv", l)
        R = [Ab.alloc(128, [T]) for _ in range(8)]
        cvo = {nm: A.alloc(128, [4, T]) for nm in ("q", "k", "v")}
        KA = A.alloc(128, [4, T])
        KBN = A.alloc(128, [4, T])
        QA = A.alloc(128, [4, T])
        oT = A.alloc(128, [4, T])
        eG = A.alloc(128, [4, NCH])
        Gb = A.alloc(64, [4, T])
        Gbn = A.alloc(64, [4, T])
        gcol = A.alloc(64, [NCH, 4])
        m0 = A.mark()
        cin = [A.alloc(128, [4, T + 3]) for _ in range(2)]
        cvo["xl"] = A.alloc(128, [4, T])
        for si, nm in enumerate(("xl", "q", "k", "v")):
            slot = self.slab(l, nm).re("p (k n) -> p k n", k=8)
            ci = cin[si % 2]
            self.cp(ci[:, :, 0:3], chal[:, si * 4:(si + 1) * 4, :], eng="pool")
            for j in range(4):
                ps = self.bank()
                for k in range(8):
                    self.mm(ps[:, 0:T], slot[:, k, j * 128:(j + 1) * 128], xb[k][:, 0:T], k == 0, k == 7)
                self.act(ci[:, j, 3:3 + T], ps[:, 0:T], AF.Copy)
            self.cp(chal[:, si * 4:(si + 1) * 4, :], ci[:, :, T:T + 3], eng="pool")
            dst = cvo[nm]
            for j in range(4):
                jj = si * 4 + j
                ce = "dve"
                self.ts(dst[:, j, :], ci[:, j, 0:T], wcv[:, jj * 4:jj * 4 + 1], bcv[:, jj:jj + 1], ALU.mult, ALU.add, eng=ce)
                for tap in range(1, W_S):
                    self.stt(dst[:, j, :], ci[:, j, tap:tap + T], wcv[:, jj * 4 + tap:jj * 4 + tap + 1], dst[:, j, :], ALU.mult, ALU.add, eng=ce)
                if si > 0:
                    self.act(dst[:, j, :], dst[:, j, :], AF.Silu)
        self.ck(21)
        xc = cvo["xl"]
        xcb = Ab.alloc(128, [4, T])
        self.cp(xcb, xc)
        L4 = self.L4[l]
        wrg = self.wrg[l].re("p (c n) -> p c n", c=4)
        wig = self.wig[l].re("p (c n) -> p c n", c=4)
        brg = self.pfv("b_rg", l)
        big = self.pfv("b_ig", l)
        slot = self.slab(l, "gc").re("p (k n) -> p k n", k=8)
        tb = [[A.alloc(128, [T]) for _ in range(2)] for _ in range(7)]
        for c in range(4):
            r_, ig_, t_, rd_, a_, om_, h_ = [tb[i][c % 2] for i in range(7)]
            ps = self.bank()
            self.mm(ps[:, 0:T], wrg[:, c, :], xcb[:, c, :], True, True)
            self.act(r_, ps[:, 0:T], AF.Sigmoid, bias=brg[:, c:c + 1])
            ps = self.bank()
            self.mm(ps[:, 0:T], wig[:, c, :], xcb[:, c, :], True, True)
            self.act(ig_, ps[:, 0:T], AF.Sigmoid, bias=big[:, c:c + 1])
            self.act(t_, r_, AF.Tanh, scale=L4[:, c:c + 1])
            self.ts(rd_, t_, -1.0, 1.0, ALU.mult, ALU.add)
            self.recip(rd_, rd_)
            self.stt(a_, t_, 1.0, rd_, ALU.add, ALU.mult)
            self.stt(om_, t_, -4.0, rd_, ALU.mult, ALU.mult)
            self.tt(om_, om_, rd_, ALU.mult)
            self.act(om_, om_, AF.Sqrt)
            self.tt(om_, om_, ig_, ALU.mult)
            self.tt(om_, om_, xc[:, c, :], ALU.mult)
            self.scan(h_, a_, om_, hl[:, c:c + 1], ALU.mult, ALU.add)
            self.cp(hl[:, c:c + 1], h_[:, T - 1:T])
            ps = self.bank()
            for k in range(8):
                self.mm(ps[:, 0:T], slot[:, k, c * 128:(c + 1) * 128], xb[k][:, 0:T], k == 0, k == 7)
            self.act(r_, ps[:, 0:T], AF.Gelu_apprx_tanh)
            self.tt(R[c], h_, r_, ALU.mult)
        self.ck(22)
        self.P.fence()
        A.release(m0)
        qs, ks, vs = cvo["q"], cvo["k"], cvo["v"]
        wba = self.wbain[l].re("p (k n) -> p k n", k=8)
        psb, psa = self.bank(), self.bank()
        for k in range(8):
            self.mm(psb[0:4, 0:T], wba[:, k, 0:4], xb[k][:, 0:T], k == 0, k == 7)
        for k in range(8):
            self.mm(psa[0:4, 0:T], wba[:, k, 4:8], xb[k][:, 0:T], k == 0, k == 7)
        ROWS = A.alloc(4, [4, T])
        hd = self.pfv("hd", l)
        self.act(ROWS[:, 3, :], psb[0:4, 0:T], AF.Sigmoid)
        y = A.alloc(4, [T])
        ay = A.alloc(4, [T])
        self.ts(y, psa[0:4, 0:T], hd[0:4, 1:2], None, ALU.add)
        self.act(ay, y, AF.Abs)
        self.act(ay, ay, AF.Exp, scale=-1.0)
        p = self.log1p_series(A, ay, 4, [T])
        self.ts(y, y, 0.0, None, ALU.max)
        self.stt(y, p, 2.0, y, ALU.mult, ALU.add)
        self.ts(y, y, self.negA[l][0:4, 0:1], None, ALU.mult)
        gc = ROWS[:, 1, :]
        self.scan(gc, self.cmask[:, 0:T], y, 0.0, ALU.mult, ALU.add)
        self.act(ROWS[:, 0, :], gc, AF.Exp)
        self.tt(ROWS[:, 2, :], ROWS[:, 3, :], ROWS[:, 0, :], ALU.mult)
        self.ck(23)
        for c in range(NCH):
            psT = self.bank()
            self.mm(psT[0:64, 0:4], ROWS[:, 1, c * 64:(c + 1) * 64], self.ident_f[0:4, 0:4], True, True)
            self.cp(gcol[:, c, :], psT[0:64, 0:4])
        sqb = [Ab.alloc(128, [T]) for _ in range(2)]
        rn = [A.alloc(128, [T]) for _ in range(2)]
        for h in range(4):
            selh = self.sel[:, h * 128:(h + 1) * 128]
            for i, src in enumerate((qs, ks)):
                self.act(sqb[i], src[:, h, :], AF.Square)
                ps = self.bank()
                self.mm(ps[:, 0:T], self.ones_b, sqb[i], True, True)
                self.act(rn[i], ps[:, 0:T], AF.Sqrt, bias=self.eps6_c)
                self.recip(rn[i], rn[i])
            self.stt(qs[:, h, :], qs[:, h, :], DK_D ** -0.5, rn[0], ALU.mult, ALU.mult)
            self.tt(ks[:, h, :], ks[:, h, :], rn[1], ALU.mult)
            psE = self.bank()
            self.mm(psE[:, 0:T], selh, ROWS[:, 0, :], True, True)
            self.act(eG[:, h, :], psE[:, 0:T].re("p (c t) -> p c t", t=64)[:, :, 63], AF.Copy)
            self.tt(QA[:, h, :], qs[:, h, :], psE[:, 0:T], ALU.mult)
            psB = self.bank()
            self.mm(psB[:, 0:T], selh, ROWS[:, 2, :], True, True)
            self.tt(KA[:, h, :], ks[:, h, :], psB[:, 0:T], ALU.mult)
            psb2 = self.bank()
            self.mm(psb2[:, 0:T], selh, ROWS[:, 3, :], True, True)
            self.tt(vs[:, h, :], vs[:, h, :], psb2[:, 0:T], ALU.mult)
            self.tt(KBN[:, h, :], ks[:, h, :], psb2[:, 0:T], ALU.mult)
            psG = self.bank()
            self.mm(psG[:, 0:T], selh, ROWS[:, 1, :], True, True)
            self.act(Gb[:, h, :], psG[0:64, 0:T], AF.Copy)
            self.act(Gbn[:, h, :], psG[0:64, 0:T], AF.Copy, scale=-1.0)
        self.ck(24)
        self.P.fence()
        A.release(m0)
        BV, KN, QN = vs, ks, qs
        slots2 = []
        for _ in range(2):
            slots2.append({
                "DT": A.alloc(64, [256]), "AT": A.alloc(64, [256]),
                "ring": [A.alloc(64, [256]) for _ in range(8)], "ri": 0,
                "BVt": A.alloc(64, [512]), "KAt": A.alloc(64, [512]), "KBt": A.alloc(64, [512]),
                "U": A.alloc(64, [512]), "WT": A.alloc(128, [256])})
        VN = A.alloc(64, [512])

        def part_a(c, sb_):
            cs = slice(c * 64, (c + 1) * 64)

            def nbuf():
                v = sb_["ring"][sb_["ri"] % 8]
                sb_["ri"] += 1
                return v
            DT, AT = sb_["DT"], sb_["AT"]
            D, DTs, Ds = nbuf(), nbuf(), nbuf()
            for h in range(4):
                hc = slice(h * 64, (h + 1) * 64)
                self.stt(DT[:, hc], Gb[:, h, cs], gcol[:, c, h:h + 1], self.negU4[:, hc], ALU.subtract, ALU.add)
                self.stt(D[:, hc], Gbn[:, h, cs], gcol[:, c, h:h + 1], self.negL4[:, hc], ALU.add, ALU.add)
            yield
            self.act(DT, DT, AF.Exp)
            self.act(D, D, AF.Exp)
            yield
            self.tt(DTs, DT, self.ident4, ALU.subtract)
            self.tt(Ds, D, self.ident4, ALU.subtract)
            psNT, psN, psAT = self.bank(), self.bank(), self.bank()
            for h in range(4):
                hc = slice(h * 64, (h + 1) * 64)
                self.mm(psNT[0:64, hc], KN[:, h, cs], KBN[:, h, cs], True, True)
                self.mm(psN[0:64, hc], KBN[:, h, cs], KN[:, h, cs], True, True)
                self.mm(psAT[0:64, hc], KN[:, h, cs], QN[:, h, cs], True, True)
            NT, N, PT = nbuf(), nbuf(), nbuf()
            self.stt(NT, psNT[0:64, 0:256], -1.0, DTs, ALU.mult, ALU.mult)
            self.stt(N, psN[0:64, 0:256], -1.0, Ds, ALU.mult, ALU.mult)
            self.tt(AT, psAT[0:64, 0:256], DT, ALU.mult)
            self.tt(PT, NT, self.ident4, ALU.add)
            yield
            for lev in range(5):
                psN2 = self.bank()
                for h in range(4):
                    hc = slice(h * 64, (h + 1) * 64)
                    self.mm(psN2[0:64, hc], NT[:, hc], N[:, hc], True, True)
                N2 = nbuf()
                self.act(N2, psN2[0:64, 0:256], AF.Copy)
                if lev < 4:
                    psNT2 = self.bank()
                    for h in range(4):
                        hc = slice(h * 64, (h + 1) * 64)
                        self.mm(psNT2[0:64, hc], N[:, hc], NT[:, hc], True, True)
                    NT2 = nbuf()
                    self.cp(NT2, psNT2[0:64, 0:256])
                else:
                    NT2 = None
                yield
                psP = self.bank()
                for h in range(4):
                    hc = slice(h * 64, (h + 1) * 64)
                    self.mm(psP[0:64, hc], N2[:, hc], PT[:, hc], True, True)
                PT2 = nbuf()
                self.tt(PT2, PT, psP[0:64, 0:256], ALU.add)
                N, NT, PT = N2, NT2, PT2
                yield
            BVt, KAt, KBt, U_sb, WT = sb_["BVt"], sb_["KAt"], sb_["KBt"], sb_["U"], sb_["WT"]
            for src, dst, eng in ((BV, BVt, "act"), (KA, KAt, "dve")):
                psT = self.bank()
                for h in range(4):
                    self.mm(psT[0:64, h * 128:(h + 1) * 128], src[:, h, cs], self.ident_f, True, True)
                if eng == "act":
                    self.act(dst, psT[0:64, 0:512], AF.Copy)
                else:
                    self.cp(dst, psT[0:64, 0:512])
            psT = self.bank()
            for h in range(4):
                self.mm(psT[0:64, h * 128:(h + 1) * 128], KN[:, h, cs], self.ident_f, True, True)
            for h in range(4):
                self.ts(KBt[:, h * 128:(h + 1) * 128], psT[0:64, h * 128:(h + 1) * 128], DT[:, h * 64 + 63:h * 64 + 64], None, ALU.mult)
            yield
            psU = self.bank()
            for h in range(4):
                self.mm(psU[0:64, h * 128:(h + 1) * 128], PT[:, h * 64:(h + 1) * 64], BVt[:, h * 128:(h + 1) * 128], True, True)
            self.act(U_sb, psU[0:64, 0:512], AF.Copy)
            psW = self.bank()
            for h in range(4):
                self.mm(psW[:, h * 64:(h + 1) * 64], KAt[:, h * 128:(h + 1) * 128], PT[:, h * 64:(h + 1) * 64], True, True)
            self.cp(WT, psW[:, 0:256])
            yield

        def part_b(c, sb_):
            cs = slice(c * 64, (c + 1) * 64)
            AT, KBt, U_sb, WT = sb_["AT"], sb_["KBt"], sb_["U"], sb_["WT"]
            psWS = self.bank()
            for h in range(4):
                self.mm(psWS[0:64, h * 128:(h + 1) * 128], WT[:, h * 64:(h + 1) * 64], Sd[:, h, :], True, True)
            self.tt(VN, U_sb, psWS[0:64, 0:512], ALU.subtract)
            psO, psO2 = self.bank(), self.bank()
            for h in range(4):
                hc = slice(h * 64, (h + 1) * 64)
                self.mm(psO[:, hc], Sd[:, h, :], QA[:, h, cs], True, True)
                self.mm(psO2[:, hc], VN[:, h * 128:(h + 1) * 128], AT[:, hc], True, True)
            self.act(oT[:, :, cs], psO[:, 0:256].re("p (a b) -> p a b", a=4), AF.Copy)
            self.tt(oT[:, :, cs], oT[:, :, cs], psO2[:, 0:256].re("p (a b) -> p a b", a=4), ALU.add)
            psS = self.bank()
            for h in range(4):
                self.mm(psS[:, h * 128:(h + 1) * 128], KBt[:, h * 128:(h + 1) * 128], VN[:, h * 128:(h + 1) * 128], True, True)
            for h in range(4):
                self.stt(Sd[:, h, :], Sd[:, h, :], eG[:, h, c:c + 1], psS[:, h * 128:(h + 1) * 128], ALU.mult, ALU.add)

        for c0 in range(0, NCH, 2):
            cl = [c for c in (c0, c0 + 1) if c < NCH]
            gens = [part_a(c, slots2[c - c0]) for c in cl]
            alive = list(gens)
            while alive:
                for g_ in list(alive):
                    try:
                        next(g_)
                    except StopIteration:
                        alive.remove(g_)
            for c in cl:
                part_b(c, slots2[c - c0])
            self.ck(25)
        self.P.fence()
        A.release(m0)
        self._dbg_oT = oT
        self._dbg_QA = QA
        self.rms_gate(oT, "g_dl", "z", l, T, R, 4, DV_D)
        self.ck(26)
        if self.cfg.get('dbg'):
            for n in range(4):
                self.cp(self.dbgbuf[:, n, 0:T], R[n])
            for n in range(4):
                self.cp(self.dbgbuf[:, 4 + n, 0:T], self._dbg_oT[:, n, :])
            self.dma('sp', self.dram['dbg'], self.dbgbuf, is_output=True)
        self.out_proj(X, l, T, R)

    def build(self):
        import contextlib
        nc, T, depth = self.nc, self.T, self.depth
        ntiles = self.S // T
        self.plan_slabs(ntiles + (1 if self.has_sample else 0))
        with contextlib.ExitStack() as es:
            def sb(name, shape, dt):
                return es.enter_context(nc.sbuf_tensor("t_" + name, list(shape), dt))
            AF_SZ = self.cfg.get("arena_f", 19800 * self.T // 256)
            AB_SZ = self.cfg.get("arena_b", 9700 * self.T // 256)
            self.af = Arena(sb("arena_f", [128, AF_SZ], F32), AF_SZ)
            self.ab = Arena(sb("arena_b", [128, AB_SZ], BF16), AB_SZ)
            slots_t = sb("slots", [128, NSLOT, SLAB], BF16)
            self.slots = [newV(slots_t[:, i, :]) for i in range(NSLOT)]
            Xt = [sb("X%d" % i, [128, 8, T], F32) for i in range(2)]
            Xs = [[newV(Xt[i][:, n, :]) for n in range(8)] for i in range(2)]
            xb_t = sb("xb", [128, 8, T], BF16)
            self.xb = [newV(xb_t[:, n, :]) for n in range(8)]
            self.pf = newV(sb("pf", [128, self.npf], F32)[:, :])
            cst = newV(sb("cst", [128, NCONST], F32)[:, :])
            cst_b = newV(sb("cst_b", [128, NCONST], BF16)[:, :])
            self.ones_b = newV(sb("ones_b", [128, 128], BF16)[:, :])
            self.eps_c = newV(sb("eps_c", [128, 1], F32)[:, :])
            self.ident_f = cst[:, 0:128]
            self.ident_b = cst_b[:, 0:128]
            self.U_f = cst[0:64, 128:192]
            self.mask4 = cst[0:64, 128:384]
            self.smask4 = cst[0:64, 384:640]
            self.lmask4 = cst[0:64, 640:896]
            self.ident4 = cst[0:64, 896:1152]
            self.cmask = cst[0:4, 1152:1664]
            self.sel = cst[0:4, 1664:2176]
            self.negU4 = cst[0:64, 2176:2432]
            self.negL4 = cst[0:64, 2432:2688]
            self.eps6_c = newV(sb("eps6_c", [128, 1], F32)[:, :])
            if self.cfg.get('dbg'):
                self.dbgbuf = newV(sb('dbgbuf', [128, 8, T], F32)[:, :, :])
            self.banks = [newV(es.enter_context(nc.psum_tensor("ps%d" % i, [128, 512], F32))[:, :]) for i in range(8)]
            self.bank_rr = 0
            self._ln_zb = [None, None]
            self._ln_zs = [None, None]
            self.wlrin, self.wlraug, self.wbain, self.wrg, self.wig = {}, {}, {}, {}, {}
            self.L4, self.negA = {}, {}
            stage = {}
            for k, shp in self.small_shapes.items():
                stage[k] = newV(sb("st_" + k, list(shp), F32)[:, :])
            st = {"S_f": {}, "S_b": {}, "uhal": {}, "ghal": {}, "h_lru": {}, "Sd_f": {}, "Sd_b": {}, "chal": {}}
            for l in range(depth):
                st["ghal"][l] = newV(sb("ghal%d" % l, [128, NFF, 2], F32)[:, :, :])
                if l % 2 == 0:
                    st["S_f"][l] = newV(sb("S_f%d" % l, [64, 4, 128], F32)[:, :, :])
                    st["S_b"][l] = newV(sb("S_b%d" % l, [64, 4, 128], BF16)[:, :, :])
                    st["uhal"][l] = newV(sb("uhal%d" % l, [128, 4, 30], F32)[:, :, :])
                else:
                    st["h_lru"][l] = newV(sb("hlru%d" % l, [128, 4], F32)[:, :])
                    st["Sd_f"][l] = newV(sb("Sd_f%d" % l, [128, 4, 128], F32)[:, :, :])
                    st["Sd_b"][l] = newV(sb("Sd_b%d" % l, [128, 4, 128], BF16)[:, :, :])
                    st["chal"][l] = newV(sb("chal%d" % l, [128, 16, 3], F32)[:, :, :])
            self.st = st
            d = self.dram
            self.sbuf_left = nc.sbuf_bytes_remaining
            for g in range(self.nslab_total):
                self.dma("pool", self.wbf_v[g], d["wslabs"][g])
            self.dma("sp", self.pf, d["pf"])
            self.dma("sp", cst, d["consts"])
            self.cp(cst_b, cst)
            self.memset(self.ones_b, 1.0)
            self.memset(self.eps_c, EPS)
            self.memset(self.eps6_c, 1e-6)
            for k in self.small_shapes:
                self.dma("sp", stage[k], d[k])
                shp = self.small_shapes[k]
                bt = newV(sb("sb_" + k, list(shp), BF16)[:, :])
                self.cp(bt, stage[k])
                l = int(k[-1])
                if k.startswith("wlrin"):
                    self.wlrin[l] = bt
                elif k.startswith("wlraug"):
                    self.wlraug[l] = bt
                elif k.startswith("wbain"):
                    self.wbain[l] = bt
                elif k.startswith("wrg"):
                    self.wrg[l] = bt
                elif k.startswith("wig"):
                    self.wig[l] = bt

            for l in range(depth):
                if l % 2 == 1:
                    self.odd_prologue(l, sb)

            def zero_states():
                for l in range(depth):
                    self.memset(st["ghal"][l], 0.0)
                    if l % 2 == 0:
                        self.memset(st["S_f"][l], 0.0)
                        self.memset(st["S_b"][l], 0.0)
                        self.memset(st["uhal"][l], 0.0)
                    else:
                        self.memset(st["h_lru"][l], 0.0)
                        self.memset(st["Sd_f"][l], 0.0)
                        self.memset(st["Sd_b"][l], 0.0)
                        self.memset(st["chal"][l], 0.0)

            def load_states():
                for l in range(depth):
                    self.dma("sp", st["ghal"][l], d["i%d_ffn" % l])
                    if l % 2 == 0:
                        self.dma("sp", st["S_f"][l], d["i%d_gla" % l].rearrange("h k v -> k h v"))
                        self.act(st["S_b"][l], st["S_f"][l], AF.Copy)
                        self.dma("sp", st["uhal"][l], d["i%d_dw" % l])
                    else:
                        self.dma("sp", st["h_lru"][l], d["i%d_lru" % l])
                        self.dma("sp", st["Sd_f"][l], d["i%d_delta" % l].rearrange("h k v -> k h v"))
                        self.act(st["Sd_b"][l], st["Sd_f"][l], AF.Copy)
                        self.dma("sp", st["chal"][l], d["i%d_conv" % l])

            def store_states(grp):
                for l in range(depth):
                    self.dma("sp", d["%s%d_ffn" % (grp, l)], st["ghal"][l], is_output=True)
                    if l % 2 == 0:
                        self.dma("sp", d["%s%d_gla" % (grp, l)].rearrange("h k v -> k h v"), st["S_f"][l], is_output=True)
                        self.dma("sp", d["%s%d_dw" % (grp, l)], st["uhal"][l], is_output=True)
                    else:
                        self.dma("sp", d["%s%d_lru" % (grp, l)], st["h_lru"][l], is_output=True)
                        self.dma("sp", d["%s%d_delta" % (grp, l)].rearrange("h k v -> k h v"), st["Sd_f"][l], is_output=True)
                        self.dma("sp", d["%s%d_conv" % (grp, l)], st["chal"][l], is_output=True)

            def run_tile(X, Tt):
                for n in range(8):
                    self.cp(self.xb[n][:, 0:Tt], X[n][:, 0:Tt])
                for l in range(depth):
                    self.P.fence()
                    self.af.reset()
                    self.ab.reset()
                    Xv = [x[:, 0:Tt] for x in X]
                    if l % 2 == 0:
                        self.even_mixer(Xv, l, Tt, st)
                    else:
                        self.odd_mixer(Xv, l, Tt, st)
                    self.ck(11)
                    self.layernorm(Xv, l, "ln1", Tt)
                    self.ck(12)
                    self.P.fence()
                    self.af.reset()
                    self.ab.reset()
                    self.ffn(Xv, l, Tt, st)
                    self.ck(13)
                    self.layernorm(Xv, l, "ln2", Tt)

            tcount = 0
            try:
                self.main_body(ntiles, d, Xt, Xs, zero_states, load_states, store_states, run_tile)
            except Cut:
                pass
            self.P.finish()
            self.P.emit()
        return nc

    def main_body(self, ntiles, d, Xt, Xs, zero_states, load_states, store_states, run_tile):
        T = self.T
        tcount = 0
        if True:
            if ntiles:
                zero_states()
                xp = d["xp"].rearrange("(k p) s -> p k s", p=128)
                yp = d["yp"].rearrange("(k p) s -> p k s", p=128)
                Xall = [V(Xt[i][:, :, :], [u for x in Xs[i] for u in x.us]) for i in range(2)]
                self.dma("pool", Xall[0], xp[:, :, 0:T])
                for i in range(ntiles):
                    if i + 1 < ntiles:
                        self.dma("pool", Xall[(i + 1) % 2], xp[:, :, (i + 1) * T:(i + 2) * T])
                    if i > 0 and i % 6 == 0:
                        self.P.new_epoch()
                    run_tile(Xs[i % 2], T)
                    self.dma("pool", yp[:, :, i * T:(i + 1) * T], Xall[i % 2], is_output=True)
                    tcount += 1
                store_states("p")
            if self.has_sample:
                xs = d["xs"].rearrange("(k p) s -> p k s", p=128)
                ys = d["ys"].rearrange("(k p) s -> p k s", p=128)
                Xi = tcount % 2
                Xsv = V(Xt[Xi][:, :, 0:64], [u for x in Xs[Xi] for u in x.us])
                load_states()
                self.dma("sp", Xsv, xs)
                run_tile(Xs[Xi], 64)
                self.dma("sp", ys, Xsv, is_output=True)
                store_states("s")


def run_config(inp, cfg, xp_list, xs_list, states_list):
    depth = cfg["depth"]
    wslabs, pp, small = host_weights(inp, depth)
    pf = pp.array()
    small_shapes = {k: v.shape for k, v in small.items()}
    b = Builder(cfg, pp.off, pf.shape[1], wslabs.shape[0], small_shapes)
    nc = b.build()
    consts = host_consts()
    ncores = len(xp_list)
    in_maps = []
    for c in range(ncores):
        m = {"wslabs": wslabs, "pf": pf, "consts": consts}
        m.update(small)
        if cfg["seq"]:
            m["xp"] = np.ascontiguousarray(xp_list[c].T)
        if cfg["sample"]:
            m["xs"] = np.ascontiguousarray(xs_list[c].T)
            stt = states_list[c]
            for l in range(depth):
                if l % 2 == 0:
                    m["i%d_gla" % l] = np.ascontiguousarray(stt["gla%d" % l])
                    m["i%d_dw" % l] = np.ascontiguousarray(stt["dw%d" % l].T.reshape(4, 128, W_B - 1).transpose(1, 0, 2))
                else:
                    m["i%d_lru" % l] = _fm(stt["lru%d" % l])
                    m["i%d_delta" % l] = np.ascontiguousarray(stt["delta%d" % l])
                    m["i%d_conv" % l] = np.ascontiguousarray(stt["conv%d" % l].T.reshape(16, 128, W_S - 1).transpose(1, 0, 2))
                m["i%d_ffn" % l] = np.ascontiguousarray(stt["ffn%d" % l].T.reshape(NFF, 128, W_F - 1).transpose(1, 0, 2))
        in_maps.append(m)
    res = run_bass_kernel_spmd(nc, in_maps, core_ids=list(range(ncores)))
    return res.results, b


def unpack_state(r, grp, l):
    out = {}
    if l % 2 == 0:
        out["gla"] = r["%s%d_gla" % (grp, l)]
        out["dw"] = np.ascontiguousarray(r["%s%d_dw" % (grp, l)].transpose(2, 1, 0).reshape(W_B - 1, D_B))
    else:
        out["lru"] = np.ascontiguousarray(r["%s%d_lru" % (grp, l)].T.reshape(D_C))
        out["delta"] = r["%s%d_delta" % (grp, l)]
        out["conv"] = np.ascontiguousarray(r["%s%d_conv" % (grp, l)].transpose(2, 1, 0).reshape(W_S - 1, 2048))
    out["ffn"] = np.ascontiguousarray(r["%s%d_ffn" % (grp, l)].transpose(2, 1, 0).reshape(W_F - 1, D_FF))
    return out


def kernel(**inputs):
    inp = {k: np.asarray(v) for k, v in inputs.items()}
    cfg = {"depth": DEPTH, "T": 256, "seq": 8192, "sample": True}
    xp_list = [inp["x_prompt"][c % 4] for c in range(8)]
    xs_list = [inp["x_sample"][c] for c in range(8)]
    states = []
    for c in range(8):
        s = {}
        for l in range(DEPTH):
            if l % 2 == 0:
                s["gla%d" % l] = inp["state_l%d_gla" % l][c]
                s["dw%d" % l] = inp["cache_l%d_dwconv" % l][c]
            else:
                s["lru%d" % l] = inp["state_l%d_lru" % l][c]
                s["delta%d" % l] = inp["state_l%d_delta" % l][c]
                s["conv%d" % l] = inp["cache_l%d_conv" % l][c]
            s["ffn%d" % l] = inp["cache_l%d_ffn" % l][c]
        states.append(s)
    results, _ = run_config(inp, cfg, xp_list, xs_list, states)
    y_prompt = np.stack([results[c]["yp"].T for c in range(4)], 0)
    y_sample = np.stack([results[c]["ys"].T for c in range(8)], 0)
    outs = [y_prompt, y_sample]
    for grp, cores in (("p", range(4)), ("s", range(8))):
        per = [[unpack_state(results[c], grp, l) for l in range(DEPTH)] for c in cores]
        for l in range(DEPTH):
            keys = ("gla", "dw", "ffn") if l % 2 == 0 else ("lru", "delta", "conv", "ffn")
            for k in keys:
                outs.append(np.stack([per[i][l][k] for i in range(len(per))], 0))
    return tuple(np.ascontiguousarray(o, dtype=np.float32) for o in outs)
```

```python
import numpy as np
import concourse.bass as bass
import concourse.mybir as mybir
from concourse.bass_utils import run_bass_kernel_spmd

F32 = mybir.dt.float32
BF16 = mybir.dt.bfloat16
AF = mybir.ActivationFunctionType
ALU = mybir.AluOpType

D_MODEL = 1024
DEPTH = 4
H_A, DK_A, DV_A, R_A = 4, 64, 128, 16
D_B, W_B = 512, 31
D_C, H_C, DH_C = 512, 8, 64
H_D, DK_D, DV_D = 4, 128, 128
W_S = 4
D_FF, W_F = 2688, 3
NFF = D_FF // 128
ALPHA = (2 * DEPTH) ** 0.25
EPS = 1e-5
LRU_C = 8.0
SLAB = 4096
NSLOT = 5
NDMASEM = 40


class Cut(Exception):
    pass


class Unit:
    __slots__ = ("w", "r")

    def __init__(self):
        self.w = None
        self.r = {}


class V:
    __slots__ = ("ap", "us")

    def __init__(self, ap, us):
        self.ap = ap
        self.us = tuple(us)

    def __getitem__(self, idx):
        return V(self.ap[idx], self.us)

    def re(self, s, **kw):
        return V(self.ap.rearrange(s, **kw), self.us)


def newV(ap):
    return V(ap, (Unit(),))


class Prog:
    ENG = ("pe", "act", "dve", "pool", "sp")

    def __init__(self, nc):
        self.nc = nc
        self.q = {e: [] for e in self.ENG}
        self.cnt = {e: 0 for e in self.ENG}
        self.waited = {e: {} for e in self.ENG}
        self.dma_val = [0] * NDMASEM
        self.dma_rr = 0
        self.out_tokens = []
        self.ninstr = 0
        self.epoch = 0

    def _wait(self, eng, key, val):
        if self.waited[eng].get(key, 0) >= val:
            return
        self.waited[eng][key] = val
        self.q[eng].append(("w", key, val))

    def _deps(self, eng, reads, writes):
        for v in reads:
            for u in v.us:
                if u.w is not None:
                    self._wait(eng, u.w[0], u.w[1])
        for v in writes:
            for u in v.us:
                if u.w is not None and u.w[0][0] != eng:
                    self._wait(eng, u.w[0], u.w[1])
                for k, val in u.r.items():
                    if k[0] != eng:
                        self._wait(eng, k, val)

    def _mark(self, tok, reads, writes):
        for v in reads:
            for u in v.us:
                if u.r.get(tok[0], 0) < tok[1]:
                    u.r[tok[0]] = tok[1]
        for v in writes:
            for u in v.us:
                u.w = tok
                u.r = {}

    def op(self, eng, fn, reads, writes, inc=True):
        self._deps(eng, reads, writes)
        key = (eng, self.epoch)
        if inc:
            self.cnt[eng] += 1
            tok = (key, self.cnt[eng])
        else:
            tok = (key, self.cnt[eng] + 1)
        self.q[eng].append(("i", fn, inc, key))
        self._mark(tok, reads, writes)
        self.ninstr += 1

    def new_epoch(self):
        self.fence()
        self.epoch += 1
        for e in self.ENG:
            self.cnt[e] = 0

    def dma(self, eng, out, in_, reads, writes, is_output=False, **kw):
        i = self.dma_rr
        self.dma_rr = (self.dma_rr + 1) % NDMASEM
        key = ("d", i)
        if self.dma_val[i] > 0:
            self._wait(eng, key, self.dma_val[i])
        self._deps(eng, reads, writes)
        self.dma_val[i] += 16
        tok = (key, self.dma_val[i])
        self.q[eng].append(("d", out, in_, i, kw))
        self._mark(tok, reads, writes)
        if is_output:
            self.out_tokens.append(tok)
        self.ninstr += 1

    def fence(self):
        comp = ("pe", "act", "dve", "pool")
        for e in comp:
            for f in comp:
                if e != f and self.cnt[f] > 0:
                    self._wait(e, (f, self.epoch), self.cnt[f])

    def finish(self):
        for key, val in self.out_tokens:
            self._wait("sp", key, val)

    def emit(self):
        nc = self.nc
        handles = {"pe": nc.tensor, "act": nc.scalar, "dve": nc.vector, "pool": nc.gpsimd, "sp": nc.sync}
        import contextlib
        with contextlib.ExitStack() as st:
            sems = {}
            for e in self.ENG:
                for ep in range(self.epoch + 1):
                    sems[(e, ep)] = st.enter_context(nc.semaphore("s_%s_%d" % (e, ep)))
            for i in range(NDMASEM):
                sems[("d", i)] = st.enter_context(nc.semaphore("sd%d" % i))
            block = st.enter_context(nc.Block())

            def run(e, h):
                for it in self.q[e]:
                    if it[0] == "w":
                        h.wait_ge(sems[it[1]], it[2])
                    elif it[0] == "i":
                        ins = it[1](h)
                        if it[2]:
                            ins.then_inc(sems[it[3]], 1)
                    else:
                        h.dma_start(out=it[1], in_=it[2], **it[4]).then_inc(sems[("d", it[3])], 16)

            @block.tensor
            def _(h):
                run("pe", h)

            @block.scalar
            def _(h):
                run("act", h)

            @block.vector
            def _(h):
                run("dve", h)

            @block.gpsimd
            def _(h):
                run("pool", h)

            @block.sync
            def _(h):
                run("sp", h)


class Arena:
    def __init__(self, tens, size):
        self.t = tens
        self.size = size
        self.off = 0

    def reset(self):
        self.off = 0

    def mark(self):
        return self.off

    def release(self, m):
        self.off = m

    def alloc(self, parts, shape):
        n = int(np.prod(shape))
        assert self.off + n <= self.size, ("arena overflow", self.off, n, self.size)
        ap = self.t[0:parts, self.off:self.off + n]
        self.off += n
        if len(shape) == 2:
            ap = ap.rearrange("p (a b) -> p a b", a=shape[0])
        elif len(shape) == 3:
            ap = ap.rearrange("p (a b c) -> p a b c", a=shape[0], b=shape[1])
        return newV(ap)


def _slab_in(w_cols):
    n = w_cols.shape[1]
    a = np.zeros((8, 128, 512), np.float32)
    a[:, :, :n] = w_cols.reshape(8, 128, n)
    return np.ascontiguousarray(a.transpose(1, 0, 2)).reshape(128, SLAB)


def _slab_down(w_cols):
    a = np.zeros((128, SLAB), np.float32)
    a[:, :NFF * 128] = w_cols.reshape(NFF, 128, 128).transpose(1, 0, 2).reshape(128, NFF * 128)
    return a


def _fm(vec):
    return np.ascontiguousarray(vec.reshape(-1, 128).T)


class ParamPack:
    def __init__(self):
        self.cols = []
        self.off = {}
        self.n = 0

    def add(self, name, arr):
        arr = np.asarray(arr, np.float32)
        assert arr.shape[0] == 128
        arr = arr.reshape(128, -1)
        self.off[name] = (self.n, arr.shape[1])
        self.cols.append(arr)
        self.n += arr.shape[1]

    def array(self):
        return np.ascontiguousarray(np.concatenate(self.cols, axis=1))


def layer_slab_names(l):
    names = []
    if l % 2 == 0:
        names += ["qk", "v", "gate", "glua", "glub", "out0", "out1"]
    else:
        names += ["xl", "q", "k", "v", "gc", "z", "out0", "out1"]
    names += ["up%d" % s for s in range(11)]
    names += ["dn%d" % n for n in range(8)]
    return names


def host_weights(inp, depth):
    slabs = []
    pp = ParamPack()
    small = {}
    for l in range(depth):
        if l % 2 == 0:
            e = l // 2
            w = inp["we_in"][e]
            slabs += [_slab_in(w[:, 0:512]), _slab_in(w[:, 512:1024]), _slab_in(w[:, 1024:1536]),
                      _slab_in(w[:, 1552:2064]), _slab_in(w[:, 2064:2576])]
            wo = inp["we_out"][e]
            slabs += [_slab_in(wo[:, 0:512]), _slab_in(wo[:, 512:1024])]
            small["wlrin%d" % l] = np.ascontiguousarray(
                w[:, 1536:1552].reshape(8, 128, 16).transpose(1, 0, 2)).reshape(128, 128)
            small["wlraug%d" % l] = np.ascontiguousarray(
                np.concatenate([inp["we_lr"][e], inp["be_lr"][e][None, :]], axis=0))
            pp.add("g_gla%d" % l, _fm(inp["ge_gla"][e]))
            pp.add("w_dw%d" % l, inp["we_dw"][e].T.reshape(4, 128, W_B).transpose(1, 0, 2))
            pp.add("b_dw%d" % l, _fm(inp["be_dw"][e]))
            pp.add("g_cn%d" % l, _fm(inp["ge_cn"][e]))
            pp.add("b_cn%d" % l, _fm(inp["be_cn"][e]))
        else:
            o = l // 2
            w = inp["wo_in"][o]
            slabs += [_slab_in(w[:, 0:512]), _slab_in(w[:, 512:1024]), _slab_in(w[:, 1024:1536]),
                      _slab_in(w[:, 1536:2048]), _slab_in(w[:, 2048:2560]), _slab_in(w[:, 2560:3072])]
            wo = inp["wo_out"][o]
            slabs += [_slab_in(wo[:, 0:512]), _slab_in(wo[:, 512:1024])]
            small["wbain%d" % l] = np.ascontiguousarray(
                w[:, 3072:3080].reshape(8, 128, 8).transpose(1, 0, 2)).reshape(128, 64)
            for nm, key in (("wrg", "wo_rg"), ("wig", "wo_ig")):
                g = inp[key][o]
                bd = np.zeros((4, 128, 128), np.float32)
                for hh in range(8):
                    c, r = hh // 2, (hh % 2) * 64
                    bd[c, r:r + 64, r:r + 64] = g[hh]
                small["%s%d" % (nm, l)] = np.ascontiguousarray(bd.transpose(1, 0, 2)).reshape(128, 512)
            pp.add("w_cv%d" % l, inp["wo_conv"][o].T.reshape(16, 128, W_S).transpose(1, 0, 2))
            pp.add("b_cv%d" % l, _fm(inp["bo_conv"][o]))
            pp.add("b_rg%d" % l, _fm(inp["bo_rg"][o]))
            pp.add("b_ig%d" % l, _fm(inp["bo_ig"][o]))
            pp.add("lam%d" % l, _fm(inp["lam_lru"][o]))
            col = np.zeros((128, 2), np.float32)
            col[0:H_D, 0] = inp["a_log"][o]
            col[0:H_D, 1] = inp["dt_bias"][o]
            pp.add("hd%d" % l, col)
            pp.add("g_dl%d" % l, _fm(inp["go_delta"][o]))
        wu = inp["w_up"][l]
        for s in range(11):
            cols = np.zeros((1024, 512), np.float32)
            for jj in range(2):
                j = 2 * s + jj
                if j < NFF:
                    cols[:, jj * 128:(jj + 1) * 128] = wu[:, j * 128:(j + 1) * 128]
                    cols[:, (2 + jj) * 128:(3 + jj) * 128] = wu[:, D_FF + j * 128:D_FF + (j + 1) * 128]
            slabs.append(_slab_in(cols))
        wd = inp["w_down"][l]
        for n in range(8):
            slabs.append(_slab_down(wd[:, n * 128:(n + 1) * 128]))
        pp.add("w_fdw%d" % l, inp["w_fdw"][l].T.reshape(NFF, 128, W_F).transpose(1, 0, 2))
        pp.add("b_fdw%d" % l, _fm(inp["b_fdw"][l]))
        for nm in ("ln1_g", "ln1_b", "ln2_g", "ln2_b"):
            pp.add("%s%d" % (nm, l), _fm(inp[nm][l]))
    return np.stack(slabs, 0), pp, small


NCONST = 128 + 256 * 4 + 512 * 2 + 512


def host_consts():
    ident = np.eye(128, dtype=np.float32)
    s = np.arange(64)
    U = (s[:, None] <= s[None, :]).astype(np.float32)
    Us = (s[:, None] < s[None, :]).astype(np.float32)
    Ls = (s[:, None] > s[None, :]).astype(np.float32)
    c = np.zeros((128, NCONST), np.float32)
    c[:, 0:128] = ident
    c[0:64, 128:384] = np.tile(U, (1, 4))
    c[0:64, 384:640] = np.tile(Us, (1, 4))
    c[0:64, 640:896] = np.tile(Ls, (1, 4))
    c[0:64, 896:1152] = np.tile(np.eye(64, dtype=np.float32), (1, 4))
    cm = np.ones(512, np.float32)
    cm[::64] = 0.0
    c[0:4, 1152:1664] = cm[None, :]
    for h in range(4):
        c[h, 1664 + h * 128:1664 + (h + 1) * 128] = 1.0
    c[0:64, 2176:2432] = np.tile((U - 1.0) * 30000.0, (1, 4))
    c[0:64, 2432:2688] = np.tile((U.T - 1.0) * 30000.0, (1, 4))
    return c


class Builder:
    def __init__(self, cfg, pp_off, npf, nslab_total, small_shapes):
        self.cfg = cfg
        self.depth = cfg["depth"]
        self.T = cfg["T"]
        self.S = cfg["seq"]
        self.has_sample = cfg["sample"]
        self.pp_off = pp_off
        nc = self.nc = bass.Bass("TRN2", target_bir_lowering=False)
        self.P = Prog(nc)
        T = self.T
        d = self.dram = {}
        depth = self.depth

        def din(name, shape):
            d[name] = nc.dram_tensor(name, list(shape), F32, kind="ExternalInput").ap()

        def dout(name, shape):
            d[name] = nc.dram_tensor(name, list(shape), F32, kind="ExternalOutput").ap()
        self.outs = []
        din("wslabs", (nslab_total, 128, SLAB))
        self.wbf = nc.dram_tensor("wbf", [nslab_total, 128, SLAB], BF16, kind="Internal").ap()
        self.wbf_v = [newV(self.wbf[g]) for g in range(nslab_total)]
        din("pf", (128, npf))
        din("consts", (128, NCONST))
        for k, shp in small_shapes.items():
            din(k, shp)
        if self.S:
            din("xp", (D_MODEL, self.S))
            dout("yp", (D_MODEL, self.S))
        if self.has_sample:
            din("xs", (D_MODEL, 64))
            dout("ys", (D_MODEL, 64))
        for grp in (["p"] if self.S else []) + (["s"] if self.has_sample else []):
            for l in range(depth):
                if l % 2 == 0:
                    dout("%s%d_gla" % (grp, l), (H_A, DK_A, DV_A))
                    dout("%s%d_dw" % (grp, l), (128, 4, W_B - 1))
                else:
                    dout("%s%d_lru" % (grp, l), (128, 4))
                    dout("%s%d_delta" % (grp, l), (H_D, DK_D, DV_D))
                    dout("%s%d_conv" % (grp, l), (128, 16, W_S - 1))
                dout("%s%d_ffn" % (grp, l), (128, NFF, W_F - 1))
        if self.has_sample:
            for l in range(depth):
                if l % 2 == 0:
                    din("i%d_gla" % l, (H_A, DK_A, DV_A))
                    din("i%d_dw" % l, (128, 4, W_B - 1))
                else:
                    din("i%d_lru" % l, (128, 4))
                    din("i%d_delta" % l, (H_D, DK_D, DV_D))
                    din("i%d_conv" % l, (128, 16, W_S - 1))
                din("i%d_ffn" % l, (128, NFF, W_F - 1))
        if cfg.get('dbg'):
            dout('dbg', (128, 8, self.T))
        self.small_shapes = small_shapes
        self.npf = npf
        self.nslab_total = nslab_total

    def mm(self, out, lhsT, rhs, start, stop, inc=None):
        self.P.op("pe", lambda h, o=out.ap, a=lhsT.ap, b=rhs.ap, s=start, e=stop: h.matmul(o, a, b, start=s, stop=e),
                  [lhsT, rhs], [out], inc=(stop if inc is None else inc))

    def act(self, out, in_, func, scale=1.0, bias=0.0, extra=()):
        sc = scale.ap if isinstance(scale, V) else scale
        bi = bias.ap if isinstance(bias, V) else bias
        rd = [in_] + [x for x in (scale, bias) if isinstance(x, V)] + list(extra)
        self.P.op("act", lambda h, o=out.ap, i=in_.ap, f=func, s=sc, b=bi: h.activation(out=o, in_=i, func=f, bias=b, scale=s),
                  rd, [out])

    def tt(self, out, in0, in1, op, eng="dve"):
        self.P.op(eng, lambda h, o=out.ap, a=in0.ap, b=in1.ap, p=op: h.tensor_tensor(out=o, in0=a, in1=b, op=p),
                  [in0, in1], [out])

    def ts(self, out, in0, s1, s2, op0, op1=None, eng="dve"):
        a1 = s1.ap if isinstance(s1, V) else s1
        a2 = s2.ap if isinstance(s2, V) else s2
        rd = [in0] + [x for x in (s1, s2) if isinstance(x, V)]
        if op1 is None:
            self.P.op(eng, lambda h, o=out.ap, a=in0.ap, x=a1, p=op0: h.tensor_scalar(out=o, in0=a, scalar1=x, scalar2=None, op0=p),
                      rd, [out])
        else:
            self.P.op(eng, lambda h, o=out.ap, a=in0.ap, x=a1, y=a2, p=op0, q=op1: h.tensor_scalar(out=o, in0=a, scalar1=x, scalar2=y, op0=p, op1=q),
                      rd, [out])

    def stt(self, out, in0, sc, in1, op0, op1, eng="dve"):
        a1 = sc.ap if isinstance(sc, V) else sc
        rd = [in0, in1] + ([sc] if isinstance(sc, V) else [])
        self.P.op(eng, lambda h, o=out.ap, a=in0.ap, x=a1, b=in1.ap, p=op0, q=op1: h.scalar_tensor_tensor(out=o, in0=a, scalar=x, in1=b, op0=p, op1=q),
                  rd, [out])

    def cp(self, out, in_, eng="dve"):
        self.P.op(eng, lambda h, o=out.ap, i=in_.ap: h.tensor_copy(out=o, in_=i), [in_], [out])

    def recip(self, out, in_):
        self.P.op("dve", lambda h, o=out.ap, i=in_.ap: h.reciprocal(out=o, in_=i), [in_], [out])

    def memset(self, out, val, eng="dve"):
        self.P.op(eng, lambda h, o=out.ap, v=val: h.memset(o, v), [], [out])

    def scan(self, out, d0, d1, init, op0, op1):
        ia = init.ap if isinstance(init, V) else init
        rd = [d0, d1] + ([init] if isinstance(init, V) else [])
        self.P.op("dve", lambda h, o=out.ap, a=d0.ap, b=d1.ap, i=ia, p=op0, q=op1: h.tensor_tensor_scan(out=o, data0=a, data1=b, initial=i, op0=p, op1=q),
                  rd, [out])

    def dma(self, eng, out, in_, reads=(), writes=(), is_output=False, **kw):
        oa = out.ap if isinstance(out, V) else out
        ia = in_.ap if isinstance(in_, V) else in_
        rd = list(reads) + ([in_] if isinstance(in_, V) else [])
        wr = list(writes) + ([out] if isinstance(out, V) else [])
        self.P.dma(eng, oa, ia, rd, wr, is_output=is_output, **kw)

    def ck(self, lvl):
        if self.cfg.get('cut', 99) == lvl:
            raise Cut()

    def bank(self):
        b = self.banks[self.bank_rr]
        self.bank_rr = (self.bank_rr + 1) % 8
        return b

    def pfv(self, name, l):
        off, n = self.pp_off["%s%d" % (name, l)]
        return self.pf[:, off:off + n]

    def plan_slabs(self, ntile_calls):
        order = []
        base = 0
        self.layer_base = []
        for l in range(self.depth):
            self.layer_base.append(base)
            base += len(layer_slab_names(l))
        for _ in range(ntile_calls):
            for l in range(self.depth):
                for i, nm in enumerate(layer_slab_names(l)):
                    order.append((l, nm, self.layer_base[l] + i))
        self.slab_order = order
        self.slab_issued = 0
        self.slab_next = 0

    def slab(self, l, name):
        k = self.slab_next
        ol, onm, _ = self.slab_order[k]
        assert (ol, onm) == (l, name), ((ol, onm), (l, name))
        lim = min(len(self.slab_order), k + NSLOT)
        while self.slab_issued < lim:
            n = self.slab_issued
            _, _, gi = self.slab_order[n]
            self.dma("sp", self.slots[n % NSLOT], self.wbf_v[gi])
            self.slab_issued += 1
        self.slab_next += 1
        return self.slots[k % NSLOT]

    def layernorm(self, X, l, which, T):
        g = self.pfv(which + "_g", l)
        bb = self.pfv(which + "_b", l)
        A, Ab = self.af, self.ab
        assert 2 * T <= 512
        pss = self.bank()
        psm, psq = pss[:, 0:T], pss[:, T:2 * T]
        zzs = [Ab.alloc(128, [2, T]) for _ in range(2)]
        for n in range(8):
            zz = zzs[n % 2]
            self.act(zz[:, 0, :], X[n], AF.Copy)
            self.act(zz[:, 1, :], X[n], AF.Square)
            self.mm(pss[:, 0:2 * T], self.ones_b, zz.re("p a b -> p (a b)"), n == 0, n == 7, inc=True)
        mu = A.alloc(128, [T])
        msq = A.alloc(128, [T])
        var = A.alloc(128, [T])
        rs = A.alloc(128, [T])
        self.act(mu, psm, AF.Copy, scale=1.0 / D_MODEL)
        self.tt(msq, mu, mu, ALU.mult)
        self.stt(var, psq, 1.0 / D_MODEL, msq, ALU.mult, ALU.subtract)
        self.act(var, var, AF.Ln, bias=self.eps_c)
        self.act(rs, var, AF.Exp, scale=-0.5)
        t1 = [A.alloc(128, [T]) for _ in range(2)]
        for n in range(8):
            t = t1[n % 2]
            self.tt(t, X[n], mu, ALU.subtract)
            self.tt(t, t, rs, ALU.mult)
            self.act(self.xb[n][:, 0:T], t, AF.Identity, scale=g[:, n:n + 1], bias=bb[:, n:n + 1])
            self.act(X[n], t, AF.Identity, scale=g[:, n:n + 1], bias=bb[:, n:n + 1])

    def ffn(self, X, l, T, st):
        A, Ab = self.af, self.ab
        wf = self.pfv("w_fdw", l)
        bf = self.pfv("b_fdw", l)
        ghal = st["ghal"][l]
        hbuf = [Ab.alloc(128, [T]) for _ in range(NFF)]
        gb = [A.alloc(128, [T + 2]) for _ in range(2)]
        acc = [A.alloc(128, [T]) for _ in range(2)]
        ge = [A.alloc(128, [T]) for _ in range(3)]
        pend = None
        for s in range(11):
            slot = self.slab(l, "up%d" % s).re("p (k n) -> p k n", k=8)
            for jj in range(2):
                j = 2 * s + jj
                if j >= NFF:
                    continue
                psg, psu = self.bank(), self.bank()
                for k in range(8):
                    self.mm(psg[:, 0:T], slot[:, k, jj * 128:(jj + 1) * 128], self.xb[k][:, 0:T], k == 0, k == 7)
                for k in range(8):
                    self.mm(psu[:, 0:T], slot[:, k, (2 + jj) * 128:(3 + jj) * 128], self.xb[k][:, 0:T], k == 0, k == 7)
                g_, a_, e_ = gb[j % 2], acc[j % 2], ge[j % 3]
                self.cp(g_[:, 0:2], ghal[:, j, :], eng="pool")
                self.act(g_[:, 2:T + 2], psg[:, 0:T], AF.Copy)
                self.cp(ghal[:, j, :], g_[:, T:T + 2], eng="pool")
                self.act(a_, g_[:, 0:T], AF.Identity, scale=wf[:, 3 * j:3 * j + 1], bias=bf[:, j:j + 1])
                self.stt(a_, g_[:, 1:T + 1], wf[:, 3 * j + 1:3 * j + 2], a_, ALU.mult, ALU.add)
                self.stt(a_, g_[:, 2:T + 2], wf[:, 3 * j + 2:3 * j + 3], a_, ALU.mult, ALU.add)
                self.act(e_, a_, AF.Gelu_apprx_tanh)
                if pend is not None:
                    self.tt(pend[0], pend[1], pend[2], ALU.mult)
                pend = (hbuf[j], e_, psu[:, 0:T])
        self.tt(pend[0], pend[1], pend[2], ALU.mult)
        for n in range(8):
            slot = self.slab(l, "dn%d" % n)
            ps = self.bank()
            for j in range(NFF):
                self.mm(ps[:, 0:T], slot[:, j * 128:(j + 1) * 128], hbuf[j], j == 0, j == NFF - 1)
            self.stt(X[n], X[n], ALPHA, ps[:, 0:T], ALU.mult, ALU.add)

    def even_mixer(self, X, l, T, st):
        A, Ab = self.af, self.ab
        NCH = T // 64
        xb = self.xb
        S_f, S_b, uhal = st["S_f"][l], st["S_b"][l], st["uhal"][l]
        slot = self.slab(l, "qk").re("p (k n) -> p k n", k=8)
        qT = A.alloc(64, [4, T])
        kT = A.alloc(64, [4, T])
        for i in range(8):
            ps = self.bank()
            for k in range(8):
                self.mm(ps[0:64, 0:T], slot[:, k, i * 64:(i + 1) * 64], xb[k][:, 0:T], k == 0, k == 7)
            if i < 4:
                self.act(qT[:, i, :], ps[0:64, 0:T], AF.Copy, scale=DK_A ** -0.5)
            else:
                self.cp(kT[:, i - 4, :], ps[0:64, 0:T])
        self.ck(1)
        lrT = Ab.alloc(17, [T])
        self.memset(lrT, 1.0)
        ps = self.bank()
        wl = self.wlrin[l].re("p (k n) -> p k n", k=8)
        for k in range(8):
            self.mm(ps[0:16, 0:T], wl[:, k, :], xb[k][:, 0:T], k == 0, k == 7)
        self.cp(lrT[0:16, :], ps[0:16, 0:T])
        self.ck(2)
        slot = self.slab(l, "v").re("p (k n) -> p k n", k=8)
        vtok = [Ab.alloc(64, [512]) for _ in range(NCH)]
        sp_tok = [A.alloc(64, [256]) for _ in range(2)]
        e1 = [A.alloc(64, [256]) for _ in range(2)]
        oT = A.alloc(128, [4, T])
        ep = [A.alloc(64, [4, 64]) for _ in range(2)]
        en = [A.alloc(64, [4, 64]) for _ in range(2)]
        qd = [Ab.alloc(64, [4, 64]) for _ in range(2)]
        kd = [Ab.alloc(64, [4, 64]) for _ in range(2)]
        kk = [Ab.alloc(64, [4, 64]) for _ in range(2)]
        scm = [Ab.alloc(64, [4, 64]) for _ in range(2)]
        kkt = [Ab.alloc(64, [256]) for _ in range(2)]
        def gla_a(c):
            cs = slice(c * 64, (c + 1) * 64)
            r = c % 2
            ps = self.bank()
            for k in range(8):
                self.mm(ps[0:64, 0:512], xb[k][:, cs], slot[:, k, :], k == 0, k == 7)
            self.act(vtok[c], ps[0:64, 0:512], AF.Copy)
            ps2 = self.bank()
            self.mm(ps2[0:64, 0:256], lrT[0:17, cs], self.wlraug[l], True, True)
            self.act(e1[r], ps2[0:64, 0:256], AF.Exp, scale=-1.0)
            yield
            self.act(sp_tok[r], e1[r], AF.Ln, bias=1.0)
            yield
            ps3 = self.bank()
            for h in range(4):
                self.mm(ps3[0:64, h * 64:(h + 1) * 64], sp_tok[r][:, h * 64:(h + 1) * 64], self.U_f, True, True)
            p3 = ps3[0:64, 0:256].re("p (a b) -> p a b", a=4)
            self.act(ep[r], p3, AF.Exp, scale=-1.0 / 16.0)
            self.act(en[r], p3, AF.Exp, scale=1.0 / 16.0)
            yield
            self.tt(qd[r], qT[:, :, cs], ep[r], ALU.mult)
            self.tt(kd[r], kT[:, :, cs], en[r], ALU.mult)
            yield
            for h in range(4):
                self.ts(kk[r][:, h, :], kd[r][:, h, :], ep[r][:, h, 63:64], None, ALU.mult)
            ps4 = self.bank()
            for h in range(4):
                self.mm(ps4[0:64, h * 64:(h + 1) * 64], kd[r][:, h, :], qd[r][:, h, :], True, True)
            self.tt(scm[r], ps4[0:64, 0:256].re("p (a b) -> p a b", a=4), self.mask4.re("p (a b) -> p a b", a=4), ALU.mult)
            yield
            ps5 = self.bank()
            for h in range(4):
                self.mm(ps5[0:64, h * 64:(h + 1) * 64], kk[r][:, h, :], self.ident_b[0:64, 0:64], True, True)
            self.act(kkt[r], ps5[0:64, 0:256], AF.Copy)
            yield

        def gla_b(c):
            cs = slice(c * 64, (c + 1) * 64)
            r = c % 2
            ps6 = self.bank()
            for h in range(4):
                self.mm(ps6[:, h * 64:(h + 1) * 64], S_b[:, h, :], qd[r][:, h, :], True, False)
                self.mm(ps6[:, h * 64:(h + 1) * 64], vtok[c][:, h * 128:(h + 1) * 128], scm[r][:, h, :], False, True)
            self.act(oT[:, :, cs], ps6[:, 0:256].re("p (a b) -> p a b", a=4), AF.Copy)
            ps7 = self.bank()
            for h in range(4):
                self.mm(ps7[0:64, h * 128:(h + 1) * 128], kkt[r][:, h * 64:(h + 1) * 64], vtok[c][:, h * 128:(h + 1) * 128], True, True)
            for h in range(4):
                self.stt(S_f[:, h, :], S_f[:, h, :], ep[r][:, h, 63:64], ps7[0:64, h * 128:(h + 1) * 128], ALU.mult, ALU.add)
            self.act(S_b, S_f, AF.Copy)

        for c0 in range(0, NCH, 2):
            cl = [c for c in (c0, c0 + 1) if c < NCH]
            alive = [gla_a(c) for c in cl]
            while alive:
                for g_ in list(alive):
                    try:
                        next(g_)
                    except StopIteration:
                        alive.remove(g_)
            for c in cl:
                gla_b(c)
            self.ck(8)
        slot = self.slab(l, "gate").re("p (k n) -> p k n", k=8)
        sg = A.alloc(128, [4, T])
        for h in range(4):
            ps = self.bank()
            for k in range(8):
                self.mm(ps[:, 0:T], slot[:, k, h * 128:(h + 1) * 128], xb[k][:, 0:T], k == 0, k == 7)
            self.act(sg[:, h, :], ps[:, 0:T], AF.Silu)
        gg = self.pfv("g_gla", l)
        R = [Ab.alloc(128, [T]) for _ in range(8)]
        sq = [Ab.alloc(128, [T]) for _ in range(2)]
        sd = [A.alloc(128, [T]) for _ in range(2)]
        for h in range(4):
            self.act(sq[h % 2], oT[:, h, :], AF.Square)
            ps = self.bank()
            self.mm(ps[:, 0:T], self.ones_b, sq[h % 2], True, True)
            self.act(sd[h % 2], ps[:, 0:T], AF.Ln, scale=1.0 / DV_A, bias=self.eps_c)
            self.act(sd[h % 2], sd[h % 2], AF.Exp, scale=-0.5)
            self.stt(sd[h % 2], oT[:, h, :], gg[:, h:h + 1], sd[h % 2], ALU.mult, ALU.mult)
            self.tt(R[h], sd[h % 2], sg[:, h, :], ALU.mult)
        self.ck(9)
        up = A.alloc(128, [4, T + 30])
        self.cp(up[:, :, 0:30], uhal, eng="pool")
        self.ck(91)
        slot = self.slab(l, "glua").re("p (k n) -> p k n", k=8)
        for j in range(4):
            ps = self.bank()
            for k in range(8):
                self.mm(ps[:, 0:T], slot[:, k, j * 128:(j + 1) * 128], xb[k][:, 0:T], k == 0, k == 7)
            self.act(up[:, j, 30:30 + T], ps[:, 0:T], AF.Copy)
        slot = self.slab(l, "glub").re("p (k n) -> p k n", k=8)
        sgb = [A.alloc(128, [T]) for _ in range(2)]
        for j in range(4):
            ps = self.bank()
            for k in range(8):
                self.mm(ps[:, 0:T], slot[:, k, j * 128:(j + 1) * 128], xb[k][:, 0:T], k == 0, k == 7)
            self.act(sgb[j % 2], ps[:, 0:T], AF.Sigmoid)
            self.tt(up[:, j, 30:30 + T], up[:, j, 30:30 + T], sgb[j % 2], ALU.mult)
        self.cp(uhal, up[:, :, T:T + 30], eng="pool")
        self.ck(92)
        wdw = self.pfv("w_dw", l)
        bdw = self.pfv("b_dw", l)
        cv = [A.alloc(128, [T]) for _ in range(4)]
        pss = self.bank()
        psm, psq = pss[:, 0:T], pss[:, T:2 * T]
        cvz = [Ab.alloc(128, [2, T]) for _ in range(2)]
        for j in range(4):
            ce = "dve"
            self.ts(cv[j], up[:, j, 0:T], wdw[:, j * 31:j * 31 + 1], bdw[:, j:j + 1], ALU.mult, ALU.add, eng=ce)
            for tap in range(1, W_B):
                self.stt(cv[j], up[:, j, tap:tap + T], wdw[:, j * 31 + tap:j * 31 + tap + 1], cv[j], ALU.mult, ALU.add, eng=ce)
        for idx, j in enumerate((0, 2, 1, 3)):
            self.act(cvz[idx % 2][:, 0, :], cv[j], AF.Copy)
            self.act(cvz[idx % 2][:, 1, :], cv[j], AF.Square)
            self.mm(pss[:, 0:2 * T], self.ones_b, cvz[idx % 2].re("p a b -> p (a b)"), idx == 0, idx == 3, inc=True)
        self.ck(95)
        mu = A.alloc(128, [T])
        msq = A.alloc(128, [T])
        var = A.alloc(128, [T])
        self.act(mu, psm, AF.Copy, scale=1.0 / D_B)
        self.tt(msq, mu, mu, ALU.mult)
        self.stt(var, psq, 1.0 / D_B, msq, ALU.mult, ALU.subtract)
        self.act(var, var, AF.Ln, bias=self.eps_c)
        self.act(var, var, AF.Exp, scale=-0.5)
        gcn = self.pfv("g_cn", l)
        self.ck(96)
        bcn = self.pfv("b_cn", l)
        for j in range(4):
            self.tt(cv[j], cv[j], mu, ALU.subtract)
            self.tt(cv[j], cv[j], var, ALU.mult)
            self.act(R[4 + j], cv[j], AF.Silu, scale=gcn[:, j:j + 1], bias=bcn[:, j:j + 1])
        self.ck(10)
        self.out_proj(X, l, T, R)

    def out_proj(self, X, l, T, R):
        for s in range(2):
            slot = self.slab(l, "out%d" % s).re("p (k n) -> p k n", k=8)
            for jj in range(4):
                n = s * 4 + jj
                ps = self.bank()
                for k in range(8):
                    self.mm(ps[:, 0:T], slot[:, k, jj * 128:(jj + 1) * 128], R[k], k == 0, k == 7)
                self.stt(X[n], X[n], ALPHA, ps[:, 0:T], ALU.mult, ALU.add)

    def log1p_series(self, A, e, parts, shape):
        den = A.alloc(parts, shape)
        s_ = A.alloc(parts, shape)
        s2 = A.alloc(parts, shape)
        p = A.alloc(parts, shape)
        self.ts(den, e, 2.0, None, ALU.add)
        self.recip(den, den)
        self.tt(s_, e, den, ALU.mult)
        self.tt(s2, s_, s_, ALU.mult)
        self.ts(p, s2, 1.0 / 11.0, 1.0 / 9.0, ALU.mult, ALU.add)
        for cst in (1.0 / 7.0, 1.0 / 5.0, 1.0 / 3.0, 1.0):
            self.tt(p, p, s2, ALU.mult)
            self.ts(p, p, cst, None, ALU.add)
        self.tt(p, p, s_, ALU.mult)
        return p

    def odd_prologue(self, l, sbf):
        A = self.af
        lam = self.pfv("lam", l)
        e = A.alloc(128, [4])
        self.act(e, lam, AF.Exp, scale=-1.0)
        p = self.log1p_series(A, e, 128, [4])
        L4 = newV(sbf("L4_%d" % l, [128, 4], F32)[:, :])
        self.ts(L4, p, -8.0, None, ALU.mult)
        self.L4[l] = L4
        hd = self.pfv("hd", l)
        negA = newV(sbf("negA_%d" % l, [4, 1], F32)[:, :])
        self.act(negA, hd[0:4, 0:1], AF.Exp)
        self.ts(negA, negA, -1.0, None, ALU.mult)
        self.negA[l] = negA

    def rms_gate(self, oT, gname, slabname, l, T, R, base, dv):
        A, Ab = self.af, self.ab
        slot = self.slab(l, slabname).re("p (k n) -> p k n", k=8)
        sg = A.alloc(128, [4, T])
        for h in range(4):
            ps = self.bank()
            for k in range(8):
                self.mm(ps[:, 0:T], slot[:, k, h * 128:(h + 1) * 128], self.xb[k][:, 0:T], k == 0, k == 7)
            self.act(sg[:, h, :], ps[:, 0:T], AF.Silu)
        gg = self.pfv(gname, l)
        sq = [Ab.alloc(128, [T]) for _ in range(2)]
        sd = [A.alloc(128, [T]) for _ in range(2)]
        for h in range(4):
            self.act(sq[h % 2], oT[:, h, :], AF.Square)
            ps = self.bank()
            self.mm(ps[:, 0:T], self.ones_b, sq[h % 2], True, True)
            self.act(sd[h % 2], ps[:, 0:T], AF.Ln, scale=1.0 / dv, bias=self.eps_c)
            self.act(sd[h % 2], sd[h % 2], AF.Exp, scale=-0.5)
            self.stt(sd[h % 2], oT[:, h, :], gg[:, h:h + 1], sd[h % 2], ALU.mult, ALU.mult)
            self.tt(R[base + h], sd[h % 2], sg[:, h, :], ALU.mult)

    def odd_mixer(self, X, l, T, st):
        A, Ab = self.af, self.ab
        NCH = T // 64
        xb = self.xb
        chal, hl, Sd = st["chal"][l], st["h_lru"][l], st["Sd_f"][l]
        wcv = self.pfv("w_cv", l)
        bcv = self.pfv("b_cv", l)
        R = [Ab.alloc(128, [T]) for _ in range(8)]
        cvo = {nm: A.alloc(128, [4, T]) for nm in ("q", "k", "v")}
        KA = A.alloc(128, [4, T])
        KBN = A.alloc(128, [4, T])
        QA = A.alloc(128, [4, T])
        oT = A.alloc(128, [4, T])
        eG = A.alloc(128, [4, NCH])
        Gb = A.alloc(64, [4, T])
        Gbn = A.alloc(64, [4, T])
        gcol = A.alloc(64, [NCH, 4])
        m0 = A.mark()
        cin = [A.alloc(128, [4, T + 3]) for _ in range(2)]
        cvo["xl"] = A.alloc(128, [4, T])
        for si, nm in enumerate(("xl", "q", "k", "v")):
            slot = self.slab(l, nm).re("p (k n) -> p k n", k=8)
            ci = cin[si % 2]
            self.cp(ci[:, :, 0:3], chal[:, si * 4:(si + 1) * 4, :], eng="pool")
            for j in range(4):
                ps = self.bank()
                for k in range(8):
                    self.mm(ps[:, 0:T], slot[:, k, j * 128:(j + 1) * 128], xb[k][:, 0:T], k == 0, k == 7)
                self.act(ci[:, j, 3:3 + T], ps[:, 0:T], AF.Copy)
            self.cp(chal[:, si * 4:(si + 1) * 4, :], ci[:, :, T:T + 3], eng="pool")
            dst = cvo[nm]
            for j in range(4):
                jj = si * 4 + j
                ce = "dve"
                self.ts(dst[:, j, :], ci[:, j, 0:T], wcv[:, jj * 4:jj * 4 + 1], bcv[:, jj:jj + 1], ALU.mult, ALU.add, eng=ce)
                for tap in range(1, W_S):
                    self.stt(dst[:, j, :], ci[:, j, tap:tap + T], wcv[:, jj * 4 + tap:jj * 4 + tap + 1], dst[:, j, :], ALU.mult, ALU.add, eng=ce)
                if si > 0:
                    self.act(dst[:, j, :], dst[:, j, :], AF.Silu)
        self.ck(21)
        xc = cvo["xl"]
        xcb = Ab.alloc(128, [4, T])
        self.cp(xcb, xc)
        L4 = self.L4[l]
        wrg = self.wrg[l].re("p (c n) -> p c n", c=4)
        wig = self.wig[l].re("p (c n) -> p c n", c=4)
        brg = self.pfv("b_rg", l)
        big = self.pfv("b_ig", l)
        slot = self.slab(l, "gc").re("p (k n) -> p k n", k=8)
        tb = [[A.alloc(128, [T]) for _ in range(2)] for _ in range(7)]
        for c in range(4):
            r_, ig_, t_, rd_, a_, om_, h_ = [tb[i][c % 2] for i in range(7)]
            ps = self.bank()
            self.mm(ps[:, 0:T], wrg[:, c, :], xcb[:, c, :], True, True)
            self.act(r_, ps[:, 0:T], AF.Sigmoid, bias=brg[:, c:c + 1])
            ps = self.bank()
            self.mm(ps[:, 0:T], wig[:, c, :], xcb[:, c, :], True, True)
            self.act(ig_, ps[:, 0:T], AF.Sigmoid, bias=big[:, c:c + 1])
            self.act(t_, r_, AF.Tanh, scale=L4[:, c:c + 1])
            self.ts(rd_, t_, -1.0, 1.0, ALU.mult, ALU.add)
            self.recip(rd_, rd_)
            self.stt(a_, t_, 1.0, rd_, ALU.add, ALU.mult)
            self.stt(om_, t_, -4.0, rd_, ALU.mult, ALU.mult)
            self.tt(om_, om_, rd_, ALU.mult)
            self.act(om_, om_, AF.Sqrt)
            self.tt(om_, om_, ig_, ALU.mult)
            self.tt(om_, om_, xc[:, c, :], ALU.mult)
            self.scan(h_, a_, om_, hl[:, c:c + 1], ALU.mult, ALU.add)
            self.cp(hl[:, c:c + 1], h_[:, T - 1:T])
            ps = self.bank()
            for k in range(8):
                self.mm(ps[:, 0:T], slot[:, k, c * 128:(c + 1) * 128], xb[k][:, 0:T], k == 0, k == 7)
            self.act(r_, ps[:, 0:T], AF.Gelu_apprx_tanh)
            self.tt(R[c], h_, r_, ALU.mult)
        self.ck(22)
        self.P.fence()
        A.release(m0)
        qs, ks, vs = cvo["q"], cvo["k"], cvo["v"]
        wba = self.wbain[l].re("p (k n) -> p k n", k=8)
        psb, psa = self.bank(), self.bank()
        for k in range(8):
            self.mm(psb[0:4, 0:T], wba[:, k, 0:4], xb[k][:, 0:T], k == 0, k == 7)
        for k in range(8):
            self.mm(psa[0:4, 0:T], wba[:, k, 4:8], xb[k][:, 0:T], k == 0, k == 7)
        ROWS = A.alloc(4, [4, T])
        hd = self.pfv("hd", l)
        self.act(ROWS[:, 3, :], psb[0:4, 0:T], AF.Sigmoid)
        y = A.alloc(4, [T])
        ay = A.alloc(4, [T])
        self.ts(y, psa[0:4, 0:T], hd[0:4, 1:2], None, ALU.add)
        self.act(ay, y, AF.Abs)
        self.act(ay, ay, AF.Exp, scale=-1.0)
        p = self.log1p_series(A, ay, 4, [T])
        self.ts(y, y, 0.0, None, ALU.max)
        self.stt(y, p, 2.0, y, ALU.mult, ALU.add)
        self.ts(y, y, self.negA[l][0:4, 0:1], None, ALU.mult)
        gc = ROWS[:, 1, :]
        self.scan(gc, self.cmask[:, 0:T], y, 0.0, ALU.mult, ALU.add)
        self.act(ROWS[:, 0, :], gc, AF.Exp)
        self.tt(ROWS[:, 2, :], ROWS[:, 3, :], ROWS[:, 0, :], ALU.mult)
        self.ck(23)
        for c in range(NCH):
            psT = self.bank()
            self.mm(psT[0:64, 0:4], ROWS[:, 1, c * 64:(c + 1) * 64], self.ident_f[0:4, 0:4], True, True)
            self.cp(gcol[:, c, :], psT[0:64, 0:4])
        sqb = [Ab.alloc(128, [T]) for _ in range(2)]
        rn = [A.alloc(128, [T]) for _ in range(2)]
        for h in range(4):
            selh = self.sel[:, h * 128:(h + 1) * 128]
            for i, src in enumerate((qs, ks)):
                self.act(sqb[i], src[:, h, :], AF.Square)
                ps = self.bank()
                self.mm(ps[:, 0:T], self.ones_b, sqb[i], True, True)
                self.act(rn[i], ps[:, 0:T], AF.Ln, bias=self.eps6_c)
                self.act(rn[i], rn[i], AF.Exp, scale=-0.5)
            self.stt(qs[:, h, :], qs[:, h, :], DK_D ** -0.5, rn[0], ALU.mult, ALU.mult)
            self.tt(ks[:, h, :], ks[:, h, :], rn[1], ALU.mult)
            psE = self.bank()
            self.mm(psE[:, 0:T], selh, ROWS[:, 0, :], True, True)
            self.act(eG[:, h, :], psE[:, 0:T].re("p (c t) -> p c t", t=64)[:, :, 63], AF.Copy)
            self.tt(QA[:, h, :], qs[:, h, :], psE[:, 0:T], ALU.mult)
            psB = self.bank()
            self.mm(psB[:, 0:T], selh, ROWS[:, 2, :], True, True)
            self.tt(KA[:, h, :], ks[:, h, :], psB[:, 0:T], ALU.mult)
            psb2 = self.bank()
            self.mm(psb2[:, 0:T], selh, ROWS[:, 3, :], True, True)
            self.tt(vs[:, h, :], vs[:, h, :], psb2[:, 0:T], ALU.mult)
            self.tt(KBN[:, h, :], ks[:, h, :], psb2[:, 0:T], ALU.mult)
            psG = self.bank()
            self.mm(psG[:, 0:T], selh, ROWS[:, 1, :], True, True)
            self.act(Gb[:, h, :], psG[0:64, 0:T], AF.Copy)
            self.act(Gbn[:, h, :], psG[0:64, 0:T], AF.Copy, scale=-1.0)
        self.ck(24)
        self.P.fence()
        A.release(m0)
        BV, KN, QN = vs, ks, qs
        slots2 = []
        for _ in range(2):
            slots2.append({
                "DT": A.alloc(64, [256]), "AT": A.alloc(64, [256]),
                "ring": [A.alloc(64, [256]) for _ in range(8)], "ri": 0,
                "BVt": A.alloc(64, [512]), "KAt": A.alloc(64, [512]), "KBt": A.alloc(64, [512]),
                "U": A.alloc(64, [512]), "WT": A.alloc(128, [256])})
        VN = A.alloc(64, [512])

        def part_a(c, sb_):
            cs = slice(c * 64, (c + 1) * 64)

            def nbuf():
                v = sb_["ring"][sb_["ri"] % 8]
                sb_["ri"] += 1
                return v
            DT, AT = sb_["DT"], sb_["AT"]
            D, DTs, Ds = nbuf(), nbuf(), nbuf()
            for h in range(4):
                hc = slice(h * 64, (h + 1) * 64)
                self.stt(DT[:, hc], Gb[:, h, cs], gcol[:, c, h:h + 1], self.negU4[:, hc], ALU.subtract, ALU.add)
                self.stt(D[:, hc], Gbn[:, h, cs], gcol[:, c, h:h + 1], self.negL4[:, hc], ALU.add, ALU.add)
            yield
            self.act(DT, DT, AF.Exp)
            self.act(D, D, AF.Exp)
            yield
            self.tt(DTs, DT, self.ident4, ALU.subtract)
            self.tt(Ds, D, self.ident4, ALU.subtract)
            psNT, psN, psAT = self.bank(), self.bank(), self.bank()
            for h in range(4):
                hc = slice(h * 64, (h + 1) * 64)
                self.mm(psNT[0:64, hc], KN[:, h, cs], KBN[:, h, cs], True, True)
                self.mm(psN[0:64, hc], KBN[:, h, cs], KN[:, h, cs], True, True)
                self.mm(psAT[0:64, hc], KN[:, h, cs], QN[:, h, cs], True, True)
            NT, N, PT = nbuf(), nbuf(), nbuf()
            self.stt(NT, psNT[0:64, 0:256], -1.0, DTs, ALU.mult, ALU.mult)
            self.stt(N, psN[0:64, 0:256], -1.0, Ds, ALU.mult, ALU.mult)
            self.tt(AT, psAT[0:64, 0:256], DT, ALU.mult)
            self.tt(PT, NT, self.ident4, ALU.add)
            yield
            for lev in range(5):
                psN2 = self.bank()
                for h in range(4):
                    hc = slice(h * 64, (h + 1) * 64)
                    self.mm(psN2[0:64, hc], NT[:, hc], N[:, hc], True, True)
                N2 = nbuf()
                self.act(N2, psN2[0:64, 0:256], AF.Copy)
                if lev < 4:
                    psNT2 = self.bank()
                    for h in range(4):
                        hc = slice(h * 64, (h + 1) * 64)
                        self.mm(psNT2[0:64, hc], N[:, hc], NT[:, hc], True, True)
                    NT2 = nbuf()
                    self.cp(NT2, psNT2[0:64, 0:256])
                else:
                    NT2 = None
                yield
                psP = self.bank()
                for h in range(4):
                    hc = slice(h * 64, (h + 1) * 64)
                    self.mm(psP[0:64, hc], N2[:, hc], PT[:, hc], True, True)
                PT2 = nbuf()
                self.tt(PT2, PT, psP[0:64, 0:256], ALU.add)
                N, NT, PT = N2, NT2, PT2
                yield
            BVt, KAt, KBt, U_sb, WT = sb_["BVt"], sb_["KAt"], sb_["KBt"], sb_["U"], sb_["WT"]
            for src, dst, eng in ((BV, BVt, "act"), (KA, KAt, "dve")):
                psT = self.bank()
                for h in range(4):
                    self.mm(psT[0:64, h * 128:(h + 1) * 128], src[:, h, cs], self.ident_f, True, True)
                if eng == "act":
                    self.act(dst, psT[0:64, 0:512], AF.Copy)
                else:
                    self.cp(dst, psT[0:64, 0:512])
            psT = self.bank()
            for h in range(4):
                self.mm(psT[0:64, h * 128:(h + 1) * 128], KN[:, h, cs], self.ident_f, True, True)
            for h in range(4):
                self.ts(KBt[:, h * 128:(h + 1) * 128], psT[0:64, h * 128:(h + 1) * 128], DT[:, h * 64 + 63:h * 64 + 64], None, ALU.mult)
            yield
            psU = self.bank()
            for h in range(4):
                self.mm(psU[0:64, h * 128:(h + 1) * 128], PT[:, h * 64:(h + 1) * 64], BVt[:, h * 128:(h + 1) * 128], True, True)
            self.act(U_sb, psU[0:64, 0:512], AF.Copy)
            psW = self.bank()
            for h in range(4):
                self.mm(psW[:, h * 64:(h + 1) * 64], KAt[:, h * 128:(h + 1) * 128], PT[:, h * 64:(h + 1) * 64], True, True)
            self.cp(WT, psW[:, 0:256])
            yield

        def part_b(c, sb_):
            cs = slice(c * 64, (c + 1) * 64)
            AT, KBt, U_sb, WT = sb_["AT"], sb_["KBt"], sb_["U"], sb_["WT"]
            psWS = self.bank()
            for h in range(4):
                self.mm(psWS[0:64, h * 128:(h + 1) * 128], WT[:, h * 64:(h + 1) * 64], Sd[:, h, :], True, True)
            self.tt(VN, U_sb, psWS[0:64, 0:512], ALU.subtract)
            psO, psO2 = self.bank(), self.bank()
            for h in range(4):
                hc = slice(h * 64, (h + 1) * 64)
                self.mm(psO[:, hc], Sd[:, h, :], QA[:, h, cs], True, True)
                self.mm(psO2[:, hc], VN[:, h * 128:(h + 1) * 128], AT[:, hc], True, True)
            self.act(oT[:, :, cs], psO[:, 0:256].re("p (a b) -> p a b", a=4), AF.Copy)
            self.tt(oT[:, :, cs], oT[:, :, cs], psO2[:, 0:256].re("p (a b) -> p a b", a=4), ALU.add)
            psS = self.bank()
            for h in range(4):
                self.mm(psS[:, h * 128:(h + 1) * 128], KBt[:, h * 128:(h + 1) * 128], VN[:, h * 128:(h + 1) * 128], True, True)
            for h in range(4):
                self.stt(Sd[:, h, :], Sd[:, h, :], eG[:, h, c:c + 1], psS[:, h * 128:(h + 1) * 128], ALU.mult, ALU.add)

        for c0 in range(0, NCH, 2):
            cl = [c for c in (c0, c0 + 1) if c < NCH]
            gens = [part_a(c, slots2[c - c0]) for c in cl]
            alive = list(gens)
            while alive:
                for g_ in list(alive):
                    try:
                        next(g_)
                    except StopIteration:
                        alive.remove(g_)
            for c in cl:
                part_b(c, slots2[c - c0])
            self.ck(25)
        self.P.fence()
        A.release(m0)
        self._dbg_oT = oT
        self._dbg_QA = QA
        self.rms_gate(oT, "g_dl", "z", l, T, R, 4, DV_D)
        self.ck(26)
        if self.cfg.get('dbg'):
            for n in range(4):
                self.cp(self.dbgbuf[:, n, 0:T], R[n])
            for n in range(4):
                self.cp(self.dbgbuf[:, 4 + n, 0:T], self._dbg_oT[:, n, :])
            self.dma('sp', self.dram['dbg'], self.dbgbuf, is_output=True)
        self.out_proj(X, l, T, R)

    def build(self):
        import contextlib
        nc, T, depth = self.nc, self.T, self.depth
        ntiles = self.S // T
        self.plan_slabs(ntiles + (1 if self.has_sample else 0))
        with contextlib.ExitStack() as es:
            def sb(name, shape, dt):
                return es.enter_context(nc.sbuf_tensor("t_" + name, list(shape), dt))
            AF_SZ = self.cfg.get("arena_f", 19800 * self.T // 256)
            AB_SZ = self.cfg.get("arena_b", 9700 * self.T // 256)
            self.af = Arena(sb("arena_f", [128, AF_SZ], F32), AF_SZ)
            self.ab = Arena(sb("arena_b", [128, AB_SZ], BF16), AB_SZ)
            slots_t = sb("slots", [128, NSLOT, SLAB], BF16)
            self.slots = [newV(slots_t[:, i, :]) for i in range(NSLOT)]
            Xt = [sb("X%d" % i, [128, 8, T], F32) for i in range(2)]
            Xs = [[newV(Xt[i][:, n, :]) for n in range(8)] for i in range(2)]
            xb_t = sb("xb", [128, 8, T], BF16)
            self.xb = [newV(xb_t[:, n, :]) for n in range(8)]
            self.pf = newV(sb("pf", [128, self.npf], F32)[:, :])
            cst = newV(sb("cst", [128, NCONST], F32)[:, :])
            cst_b = newV(sb("cst_b", [128, NCONST], BF16)[:, :])
            self.ones_b = newV(sb("ones_b", [128, 128], BF16)[:, :])
            self.eps_c = newV(sb("eps_c", [128, 1], F32)[:, :])
            self.ident_f = cst[:, 0:128]
            self.ident_b = cst_b[:, 0:128]
            self.U_f = cst[0:64, 128:192]
            self.mask4 = cst[0:64, 128:384]
            self.smask4 = cst[0:64, 384:640]
            self.lmask4 = cst[0:64, 640:896]
            self.ident4 = cst[0:64, 896:1152]
            self.cmask = cst[0:4, 1152:1664]
            self.sel = cst[0:4, 1664:2176]
            self.negU4 = cst[0:64, 2176:2432]
            self.negL4 = cst[0:64, 2432:2688]
            self.eps6_c = newV(sb("eps6_c", [128, 1], F32)[:, :])
            if self.cfg.get('dbg'):
                self.dbgbuf = newV(sb('dbgbuf', [128, 8, T], F32)[:, :, :])
            self.banks = [newV(es.enter_context(nc.psum_tensor("ps%d" % i, [128, 512], F32))[:, :]) for i in range(8)]
            self.bank_rr = 0
            self._ln_zb = [None, None]
            self._ln_zs = [None, None]
            self.wlrin, self.wlraug, self.wbain, self.wrg, self.wig = {}, {}, {}, {}, {}
            self.L4, self.negA = {}, {}
            stage = {}
            for k, shp in self.small_shapes.items():
                stage[k] = newV(sb("st_" + k, list(shp), F32)[:, :])
            st = {"S_f": {}, "S_b": {}, "uhal": {}, "ghal": {}, "h_lru": {}, "Sd_f": {}, "Sd_b": {}, "chal": {}}
            for l in range(depth):
                st["ghal"][l] = newV(sb("ghal%d" % l, [128, NFF, 2], F32)[:, :, :])
                if l % 2 == 0:
                    st["S_f"][l] = newV(sb("S_f%d" % l, [64, 4, 128], F32)[:, :, :])
                    st["S_b"][l] = newV(sb("S_b%d" % l, [64, 4, 128], BF16)[:, :, :])
                    st["uhal"][l] = newV(sb("uhal%d" % l, [128, 4, 30], F32)[:, :, :])
                else:
                    st["h_lru"][l] = newV(sb("hlru%d" % l, [128, 4], F32)[:, :])
                    st["Sd_f"][l] = newV(sb("Sd_f%d" % l, [128, 4, 128], F32)[:, :, :])
                    st["Sd_b"][l] = newV(sb("Sd_b%d" % l, [128, 4, 128], BF16)[:, :, :])
                    st["chal"][l] = newV(sb("chal%d" % l, [128, 16, 3], F32)[:, :, :])
            self.st = st
            d = self.dram
            self.sbuf_left = nc.sbuf_bytes_remaining
            for g in range(self.nslab_total):
                self.dma("pool", self.wbf_v[g], d["wslabs"][g])
            self.dma("sp", self.pf, d["pf"])
            self.dma("sp", cst, d["consts"])
            self.cp(cst_b, cst)
            self.memset(self.ones_b, 1.0)
            self.memset(self.eps_c, EPS)
            self.memset(self.eps6_c, 1e-6)
            for k in self.small_shapes:
                self.dma("sp", stage[k], d[k])
                shp = self.small_shapes[k]
                bt = newV(sb("sb_" + k, list(shp), BF16)[:, :])
                self.cp(bt, stage[k])
                l = int(k[-1])
                if k.startswith("wlrin"):
                    self.wlrin[l] = bt
                elif k.startswith("wlraug"):
                    self.wlraug[l] = bt
                elif k.startswith("wbain"):
                    self.wbain[l] = bt
                elif k.startswith("wrg"):
                    self.wrg[l] = bt
                elif k.startswith("wig"):
                    self.wig[l] = bt

            for l in range(depth):
                if l % 2 == 1:
                    self.odd_prologue(l, sb)

            def zero_states():
                for l in range(depth):
                    self.memset(st["ghal"][l], 0.0)
                    if l % 2 == 0:
                        self.memset(st["S_f"][l], 0.0)
                        self.memset(st["S_b"][l], 0.0)
                        self.memset(st["uhal"][l], 0.0)
                    else:
                        self.memset(st["h_lru"][l], 0.0)
                        self.memset(st["Sd_f"][l], 0.0)
                        self.memset(st["Sd_b"][l], 0.0)
                        self.memset(st["chal"][l], 0.0)

            def load_states():
                for l in range(depth):
                    self.dma("sp", st["ghal"][l], d["i%d_ffn" % l])
                    if l % 2 == 0:
                        self.dma("sp", st["S_f"][l], d["i%d_gla" % l].rearrange("h k v -> k h v"))
                        self.act(st["S_b"][l], st["S_f"][l], AF.Copy)
                        self.dma("sp", st["uhal"][l], d["i%d_dw" % l])
                    else:
                        self.dma("sp", st["h_lru"][l], d["i%d_lru" % l])
                        self.dma("sp", st["Sd_f"][l], d["i%d_delta" % l].rearrange("h k v -> k h v"))
                        self.act(st["Sd_b"][l], st["Sd_f"][l], AF.Copy)
                        self.dma("sp", st["chal"][l], d["i%d_conv" % l])

            def store_states(grp):
                for l in range(depth):
                    self.dma("sp", d["%s%d_ffn" % (grp, l)], st["ghal"][l], is_output=True)
                    if l % 2 == 0:
                        self.dma("sp", d["%s%d_gla" % (grp, l)].rearrange("h k v -> k h v"), st["S_f"][l], is_output=True)
                        self.dma("sp", d["%s%d_dw" % (grp, l)], st["uhal"][l], is_output=True)
                    else:
                        self.dma("sp", d["%s%d_lru" % (grp, l)], st["h_lru"][l], is_output=True)
                        self.dma("sp", d["%s%d_delta" % (grp, l)].rearrange("h k v -> k h v"), st["Sd_f"][l], is_output=True)
                        self.dma("sp", d["%s%d_conv" % (grp, l)], st["chal"][l], is_output=True)

            def run_tile(X, Tt):
                for n in range(8):
                    self.cp(self.xb[n][:, 0:Tt], X[n][:, 0:Tt])
                for l in range(depth):
                    self.P.fence()
                    self.af.reset()
                    self.ab.reset()
                    Xv = [x[:, 0:Tt] for x in X]
                    if l % 2 == 0:
                        self.even_mixer(Xv, l, Tt, st)
                    else:
                        self.odd_mixer(Xv, l, Tt, st)
                    self.ck(11)
                    self.layernorm(Xv, l, "ln1", Tt)
                    self.ck(12)
                    self.P.fence()
                    self.af.reset()
                    self.ab.reset()
                    self.ffn(Xv, l, Tt, st)
                    self.ck(13)
                    self.layernorm(Xv, l, "ln2", Tt)

            tcount = 0
            try:
                self.main_body(ntiles, d, Xt, Xs, zero_states, load_states, store_states, run_tile)
            except Cut:
                pass
            self.P.finish()
            self.P.emit()
        return nc

    def main_body(self, ntiles, d, Xt, Xs, zero_states, load_states, store_states, run_tile):
        T = self.T
        tcount = 0
        if True:
            if ntiles:
                zero_states()
                xp = d["xp"].rearrange("(k p) s -> p k s", p=128)
                yp = d["yp"].rearrange("(k p) s -> p k s", p=128)
                Xall = [V(Xt[i][:, :, :], [u for x in Xs[i] for u in x.us]) for i in range(2)]
                self.dma("pool", Xall[0], xp[:, :, 0:T])
                for i in range(ntiles):
                    if i + 1 < ntiles:
                        self.dma("pool", Xall[(i + 1) % 2], xp[:, :, (i + 1) * T:(i + 2) * T])
                    if i > 0 and i % 6 == 0:
                        self.P.new_epoch()
                    run_tile(Xs[i % 2], T)
                    self.dma("pool", yp[:, :, i * T:(i + 1) * T], Xall[i % 2], is_output=True)
                    tcount += 1
                store_states("p")
            if self.has_sample:
                xs = d["xs"].rearrange("(k p) s -> p k s", p=128)
                ys = d["ys"].rearrange("(k p) s -> p k s", p=128)
                Xi = tcount % 2
                Xsv = V(Xt[Xi][:, :, 0:64], [u for x in Xs[Xi] for u in x.us])
                load_states()
                self.dma("sp", Xsv, xs)
                run_tile(Xs[Xi], 64)
                self.dma("sp", ys, Xsv, is_output=True)
                store_states("s")


def run_config(inp, cfg, xp_list, xs_list, states_list):
    depth = cfg["depth"]
    wslabs, pp, small = host_weights(inp, depth)
    pf = pp.array()
    small_shapes = {k: v.shape for k, v in small.items()}
    b = Builder(cfg, pp.off, pf.shape[1], wslabs.shape[0], small_shapes)
    nc = b.build()
    consts = host_consts()
    ncores = len(xp_list)
    in_maps = []
    for c in range(ncores):
        m = {"wslabs": wslabs, "pf": pf, "consts": consts}
        m.update(small)
        if cfg["seq"]:
            m["xp"] = np.ascontiguousarray(xp_list[c].T)
        if cfg["sample"]:
            m["xs"] = np.ascontiguousarray(xs_list[c].T)
            stt = states_list[c]
            for l in range(depth):
                if l % 2 == 0:
                    m["i%d_gla" % l] = np.ascontiguousarray(stt["gla%d" % l])
                    m["i%d_dw" % l] = np.ascontiguousarray(stt["dw%d" % l].T.reshape(4, 128, W_B - 1).transpose(1, 0, 2))
                else:
                    m["i%d_lru" % l] = _fm(stt["lru%d" % l])
                    m["i%d_delta" % l] = np.ascontiguousarray(stt["delta%d" % l])
                    m["i%d_conv" % l] = np.ascontiguousarray(stt["conv%d" % l].T.reshape(16, 128, W_S - 1).transpose(1, 0, 2))
                m["i%d_ffn" % l] = np.ascontiguousarray(stt["ffn%d" % l].T.reshape(NFF, 128, W_F - 1).transpose(1, 0, 2))
        in_maps.append(m)
    res = run_bass_kernel_spmd(nc, in_maps, core_ids=list(range(ncores)))
    return res.results, b


def unpack_state(r, grp, l):
    out = {}
    if l % 2 == 0:
        out["gla"] = r["%s%d_gla" % (grp, l)]
        out["dw"] = np.ascontiguousarray(r["%s%d_dw" % (grp, l)].transpose(2, 1, 0).reshape(W_B - 1, D_B))
    else:
        out["lru"] = np.ascontiguousarray(r["%s%d_lru" % (grp, l)].T.reshape(D_C))
        out["delta"] = r["%s%d_delta" % (grp, l)]
        out["conv"] = np.ascontiguousarray(r["%s%d_conv" % (grp, l)].transpose(2, 1, 0).reshape(W_S - 1, 2048))
    out["ffn"] = np.ascontiguousarray(r["%s%d_ffn" % (grp, l)].transpose(2, 1, 0).reshape(W_F - 1, D_FF))
    return out


def kernel(**inputs):
    inp = {k: np.asarray(v) for k, v in inputs.items()}
    cfg = {"depth": DEPTH, "T": 256, "seq": 8192, "sample": True}
    xp_list = [inp["x_prompt"][c % 4] for c in range(8)]
    xs_list = [inp["x_sample"][c] for c in range(8)]
    states = []
    for c in range(8):
        s = {}
        for l in range(DEPTH):
            if l % 2 == 0:
                s["gla%d" % l] = inp["state_l%d_gla" % l][c]
                s["dw%d" % l] = inp["cache_l%d_dwconv" % l][c]
            else:
                s["lru%d" % l] = inp["state_l%d_lru" % l][c]
                s["delta%d" % l] = inp["state_l%d_delta" % l][c]
                s["conv%d" % l] = inp["cache_l%d_conv" % l][c]
            s["ffn%d" % l] = inp["cache_l%d_ffn" % l][c]
        states.append(s)
    results, _ = run_config(inp, cfg, xp_list, xs_list, states)
    y_prompt = np.stack([results[c]["yp"].T for c in range(4)], 0)
    y_sample = np.stack([results[c]["ys"].T for c in range(8)], 0)
    outs = [y_prompt, y_sample]
    for grp, cores in (("p", range(4)), ("s", range(8))):
        per = [[unpack_state(results[c], grp, l) for l in range(DEPTH)] for c in cores]
        for l in range(DEPTH):
            keys = ("gla", "dw", "ffn") if l % 2 == 0 else ("lru", "delta", "conv", "ffn")
            for k in keys:
                outs.append(np.stack([per[i][l][k] for i in range(len(per))], 0))
    return tuple(np.ascontiguousarray(o, dtype=np.float32) for o in outs)
```
